# Optimizing a Trainium2 kernel written in Bass

```python
import functools
import jax, jax.numpy as jnp
from jax import lax
import numpy as np

D_MODEL = 1024
BATCH = 2
SEQ = 8192
DEPTH = 2
DEC_BATCH = 32
DEC_SEQ = 1
PAST_LEN = 8192
PAGE_SIZE = 128

BRANCH_W = 1024
N_BRANCH = 3
LRU_W = BRANCH_W
LRU_BLOCKS = 16
LRU_BD = LRU_W // LRU_BLOCKS
CONV_W = 4
LRU_C = 8.0
POOL_W = BRANCH_W
POOL_WINDOWS = (2, 4, 8, 16)
POOL_G = len(POOL_WINDOWS)
POOL_GD = POOL_W // POOL_G
POOL_BUF = max(POOL_WINDOWS) - 1
N_HEADS = 16
HEAD_DIM = 64
NSA_W = N_HEADS * HEAD_DIM
KV_HEADS = 4
Q_PER_KV = N_HEADS // KV_HEADS
KV_W = KV_HEADS * HEAD_DIM
CMP_STRIDE = 16
CMP_BLK = 2 * CMP_STRIDE
SEL_BLK = 64
N_SEL = 16
WINDOW = 512
Q_BLK = 128
FORCE = 1e4
NEG = -1e30
EPS = 1e-6

SPLIT_SIZES = (LRU_W, LRU_W, POOL_W, POOL_W, NSA_W, NSA_W, 6 * KV_W, N_BRANCH * N_HEADS, N_BRANCH * D_MODEL)
IN_W = sum(SPLIT_SIZES)
SPLIT_IDX = tuple(sum(SPLIT_SIZES[:i + 1]) for i in range(len(SPLIT_SIZES) - 1))

kernel_name = 'hybrid_rglru_pool_nsa_decoder_step'

f32 = jnp.float32


def _rmsnorm(x, g):
    xf = x.astype(f32)
    y = xf * lax.rsqrt(jnp.mean(xf * xf, axis=-1, keepdims=True) + EPS) * g.astype(f32)
    return y.astype(x.dtype)


def _alibi_slopes():
    s = 2.0 ** (-8.0 * np.arange(1, N_HEADS + 1) / N_HEADS)
    return jnp.asarray(s, dtype=f32).reshape(KV_HEADS, Q_PER_KV)


def _rglru(xb, conv_buf, h0, conv_w, conv_b, w_a, b_a, w_x, b_x, lam):
    B, T, W = xb.shape
    xp = jnp.concatenate([conv_buf.astype(xb.dtype), xb], axis=1)
    xc = conv_b
    for k in range(CONV_W):
        xc = xc + xp[:, k:k + T] * conv_w[k]
    xf = xc.astype(f32)
    xh = xf.reshape(B, T, LRU_BLOCKS, LRU_BD)
    r = jax.nn.sigmoid(jnp.einsum('btnd,nde->btne', xh, w_a.astype(f32)).reshape(B, T, W) + b_a.astype(f32))
    i = jax.nn.sigmoid(jnp.einsum('btnd,nde->btne', xh, w_x.astype(f32)).reshape(B, T, W) + b_x.astype(f32))
    log_a = -LRU_C * r * jax.nn.softplus(-lam.astype(f32))
    a = jnp.exp(log_a)
    b = jnp.sqrt(-jnp.expm1(2.0 * log_a)) * (i * xf)

    def step(h, ab):
        h = ab[0] * h + ab[1]
        return h, h

    hT, hs = lax.scan(step, h0.astype(f32), (a.swapaxes(0, 1), b.swapaxes(0, 1)))
    return hs.swapaxes(0, 1).astype(xb.dtype), xp[:, -(CONV_W - 1):], hT.astype(xb.dtype)


def _pool(xb, buf, pos0, w_pool, scale):
    B, T, W = xb.shape
    xcat = jnp.concatenate([buf.astype(xb.dtype), xb], axis=1)
    xf = xcat.astype(f32)
    cs = jnp.pad(jnp.cumsum(xf, axis=1), ((0, 0), (1, 0), (0, 0)))
    pos = pos0 + jnp.arange(T)
    outs = []
    for g, w in enumerate(POOL_WINDOWS):
        sl = slice(g * POOL_GD, (g + 1) * POOL_GD)
        wsum = cs[:, POOL_BUF + 1:POOL_BUF + 1 + T, sl] - cs[:, POOL_BUF + 1 - w:POOL_BUF + 1 - w + T, sl]
        cnt = jnp.minimum(w, pos + 1).astype(f32)[None, :, None]
        outs.append(wsum / cnt)
    pooled = jnp.concatenate(outs, axis=-1) - xf[:, POOL_BUF:]
    mixed = jnp.einsum('btgd,gde->btge', pooled.reshape(B, T, POOL_G, POOL_GD), w_pool.astype(f32)).reshape(B, T, W)
    y = (mixed * scale.astype(f32)).astype(xb.dtype)
    return y, xcat[:, -POOL_BUF:]


def _compress(rows, w_pos, w_phi):
    B, L = rows.shape[:2]
    nch = L // CMP_STRIDE
    ch = rows[:, :nch * CMP_STRIDE].reshape(B, nch, CMP_STRIDE, KV_HEADS, HEAD_DIM)
    first = jnp.einsum('bcjgd,jd->bcgd', ch, w_pos[:CMP_STRIDE])
    second = jnp.einsum('bcjgd,jd->bcgd', ch, w_pos[CMP_STRIDE:])
    blk = first[:, :-1] + second[:, 1:]
    return jnp.einsum('bngd,de->bnge', blk, w_phi)


def _nsa_attend(q, q_pos, kc, vc, c_end, gather_sel, n_blocks, kw, vw, w_pos, gates):
    B, T = q.shape[:2]
    slopes = _alibi_slopes()
    qg = q.astype(f32).reshape(B, T, KV_HEADS, Q_PER_KV, HEAD_DIM) * (HEAD_DIM ** -0.5)
    dist_c = q_pos[:, None] - c_end[None, :]
    ok_c = (dist_c >= 0)[:, None, None, :]
    s_c = jnp.einsum('btgrd,bngd->btgrn', qg, kc.astype(f32)) - slopes[:, :, None] * dist_c[:, None, None, :].astype(f32)
    p_c = jnp.where(ok_c, jax.nn.softmax(jnp.where(ok_c, s_c, NEG), axis=-1), 0.0)
    o_c = jnp.einsum('btgrn,bngd->btgrd', p_c, vc.astype(f32))
    per = SEL_BLK // CMP_STRIDE
    ps = p_c.sum(axis=3)
    ps = jnp.pad(ps, ((0, 0), (0, 0), (0, 0), (0, per * n_blocks - ps.shape[-1])))
    ps = ps.reshape(B, T, KV_HEADS, n_blocks, per).sum(-1)
    j = jnp.arange(n_blocks)[None, :]
    jb = (q_pos // SEL_BLK)[:, None]
    ok_b = (j * SEL_BLK) <= q_pos[:, None]
    forced = (j == 0) | (j == jb) | (j == jb - 1)
    score = jnp.where(ok_b[:, None, :], ps + FORCE * forced[:, None, :].astype(f32), NEG)
    _, idx = lax.top_k(score, min(N_SEL, n_blocks))
    ks, vs = gather_sel(idx)
    kpos = idx[..., None] * SEL_BLK + jnp.arange(SEL_BLK)
    dist_s = (q_pos[None, :, None, None, None] - kpos)[:, :, :, None]
    s_s = jnp.einsum('btgrd,btgnkd->btgrnk', qg, ks.astype(f32)) - slopes[None, None, :, :, None, None] * dist_s.astype(f32)
    s_s = jnp.where(dist_s >= 0, s_s, NEG)
    p_s = jax.nn.softmax(s_s.reshape(B, T, KV_HEADS, Q_PER_KV, -1), axis=-1).reshape(s_s.shape)
    o_s = jnp.einsum('btgrnk,btgnkd->btgrd', p_s, vs.astype(f32))
    dist_w = q_pos[:, None] - w_pos[None, :]
    ok_w = ((dist_w >= 0) & (dist_w < WINDOW) & (w_pos >= 0)[None, :])[:, None, None, :]
    s_w = jnp.einsum('btgrd,blgd->btgrl', qg, kw.astype(f32)) - slopes[:, :, None] * dist_w[:, None, None, :].astype(f32)
    p_w = jax.nn.softmax(jnp.where(ok_w, s_w, NEG), axis=-1)
    o_w = jnp.einsum('btgrl,blgd->btgrd', p_w, vw.astype(f32))
    o = jnp.stack([o_c, o_s, o_w], axis=-1).reshape(B, T, N_HEADS, HEAD_DIM, N_BRANCH)
    o = (o * gates.astype(f32)[:, :, :, None, :]).sum(-1)
    return o.reshape(B, T, NSA_W).astype(q.dtype)


def _nsa_prompt(q, k_c, v_c, k_s, v_s, k_w, v_w, gates, pk, fk, pv, fv):
    B, S = q.shape[:2]
    kc = _compress(k_c, pk, fk)
    vc = _compress(v_c, pv, fv)
    c_end = jnp.arange(kc.shape[1]) * CMP_STRIDE + CMP_BLK - 1
    nb = S // SEL_BLK
    ksb = k_s.reshape(B, nb, SEL_BLK, KV_HEADS, HEAD_DIM)
    vsb = v_s.reshape(B, nb, SEL_BLK, KV_HEADS, HEAD_DIM)
    bi = jnp.arange(B)[:, None, None, None]
    gi = jnp.arange(KV_HEADS)[None, None, :, None]

    def gather(idx):
        return ksb[bi, idx, :, gi], vsb[bi, idx, :, gi]

    pad = ((0, 0), (WINDOW, 0), (0, 0), (0, 0))
    kwp, vwp = jnp.pad(k_w, pad), jnp.pad(v_w, pad)
    nqb = S // Q_BLK
    qb = q.reshape(B, nqb, Q_BLK, N_HEADS, HEAD_DIM).swapaxes(0, 1)
    gb = gates.reshape(B, nqb, Q_BLK, N_HEADS, N_BRANCH).swapaxes(0, 1)

    def one(args):
        jq, qj, gj = args
        s0 = jq * Q_BLK
        kw = lax.dynamic_slice_in_dim(kwp, s0, WINDOW + Q_BLK, axis=1)
        vw = lax.dynamic_slice_in_dim(vwp, s0, WINDOW + Q_BLK, axis=1)
        w_pos = s0 - WINDOW + jnp.arange(WINDOW + Q_BLK)
        return _nsa_attend(qj, s0 + jnp.arange(Q_BLK), kc, vc, c_end, gather, nb, kw, vw, w_pos, gj)

    y = lax.map(one, (jnp.arange(nqb), qb, gb))
    y = y.swapaxes(0, 1).reshape(B, S, NSA_W)
    wb = min(WINDOW, S)
    return y, (k_c, v_c, k_s, v_s, k_w[:, -wb:], v_w[:, -wb:])


def _nsa_sample(q, k_c, v_c, k_s, v_s, k_w, v_w, gates, pk, fk, pv, fv,
                pool_ck, pool_cv, pool_sk, pool_sv, buf_k, buf_v, page_table):
    B, T = q.shape[:2]
    P = page_table.shape[1] * PAGE_SIZE

    def past_rows(pool):
        return pool[page_table].reshape(B, P, KV_HEADS, HEAD_DIM)

    kc = _compress(jnp.concatenate([past_rows(pool_ck), k_c.astype(pool_ck.dtype)], axis=1), pk, fk)
    vc = _compress(jnp.concatenate([past_rows(pool_cv), v_c.astype(pool_cv.dtype)], axis=1), pv, fv)
    c_end = jnp.arange(kc.shape[1]) * CMP_STRIDE + CMP_BLK - 1
    L = P + T
    nb = -(-L // SEL_BLK)
    npb = P // SEL_BLK
    nnew = nb - npb
    subs = PAGE_SIZE // SEL_BLK

    def new_blocks(rows):
        rows = jnp.pad(rows, ((0, 0), (0, nnew * SEL_BLK - T), (0, 0), (0, 0)))
        return rows.reshape(B, nnew, SEL_BLK, KV_HEADS, HEAD_DIM)

    ksn, vsn = new_blocks(k_s), new_blocks(v_s)
    ksp = pool_sk.reshape(pool_sk.shape[0], subs, SEL_BLK, KV_HEADS, HEAD_DIM)
    vsp = pool_sv.reshape(pool_sv.shape[0], subs, SEL_BLK, KV_HEADS, HEAD_DIM)
    bi = jnp.arange(B)[:, None, None, None]
    gi = jnp.arange(KV_HEADS)[None, None, :, None]

    def gather(idx):
        past = (idx < npb)[..., None, None]
        ip = jnp.minimum(idx, npb - 1)
        phys = page_table[bi, ip // subs]
        sub = ip % subs
        inew = jnp.clip(idx - npb, 0, nnew - 1)
        k = jnp.where(past, ksp[phys, sub, :, gi].astype(k_s.dtype), ksn[bi, inew, :, gi])
        v = jnp.where(past, vsp[phys, sub, :, gi].astype(v_s.dtype), vsn[bi, inew, :, gi])
        return k, v

    kw = jnp.concatenate([buf_k.astype(k_w.dtype), k_w], axis=1)
    vw = jnp.concatenate([buf_v.astype(v_w.dtype), v_w], axis=1)
    wb = buf_k.shape[1]
    w_pos = P - wb + jnp.arange(wb + T)
    y = _nsa_attend(q, P + jnp.arange(T), kc, vc, c_end, gather, nb, kw, vw, w_pos, gates)
    return y, (k_c, v_c, k_s, v_s, kw[:, -wb:], vw[:, -wb:])


def _layer(x, pos0, conv_buf, h0, pool_buf, nsa_fn, lw):
    (g_pre, g_post, w_in, conv_w, conv_b, w_a, b_a, w_x, b_x, lam, w_pool, pool_scale,
     pk, fk, pv, fv, w_branch, w_out) = lw
    B, T = x.shape[:2]
    u = _rmsnorm(x, g_pre)
    z = jnp.einsum('btd,de->bte', u, w_in)
    lru_x, lru_g, pool_x, pool_g, q, nsa_g, kv, bg, mg = jnp.split(z, SPLIT_IDX, axis=-1)
    y_lru, conv_new, h_new = _rglru(lru_x, conv_buf, h0, conv_w, conv_b, w_a, b_a, w_x, b_x, lam)
    y_pool, pool_new = _pool(pool_x, pool_buf, pos0, w_pool, pool_scale)
    k_c, v_c, k_s, v_s, k_w, v_w = [a.reshape(B, T, KV_HEADS, HEAD_DIM) for a in jnp.split(kv, 6, axis=-1)]
    gates = jax.nn.sigmoid(bg.reshape(B, T, N_HEADS, N_BRANCH))
    y_nsa, nsa_state = nsa_fn(q.reshape(B, T, N_HEADS, HEAD_DIM), k_c, v_c, k_s, v_s, k_w, v_w, gates, pk, fk, pv, fv)
    zb = jnp.stack([y_lru * jax.nn.silu(lru_g), y_pool * jax.nn.silu(pool_g), y_nsa * jax.nn.silu(nsa_g)], axis=2)
    br = jnp.einsum('btnw,nwd->btnd', zb, w_branch)
    m = jax.nn.sigmoid(mg.reshape(B, T, N_BRANCH, D_MODEL))
    out = jnp.einsum('btd,de->bte', (m * br).sum(axis=2), w_out)
    return x + _rmsnorm(out, g_post), nsa_state + (conv_new, h_new, pool_new)


def setup_inputs(seed: int = 0) -> dict:
    key = jax.random.key(seed)
    ks = jax.random.split(key, 32)
    n_pages = PAST_LEN // PAGE_SIZE
    used = DEC_BATCH * n_pages
    n_phys = used + max(1, used // 4)
    win_buf = min(WINDOW, PAST_LEN)

    def nrm(k, shape, s=1.0):
        return s * jax.random.normal(k, shape, f32)

    u = jax.random.uniform(ks[0], (DEPTH, LRU_W), f32, minval=0.9, maxval=0.999)
    sg = u ** (1.0 / LRU_C)
    lam = jnp.log(sg) - jnp.log1p(-sg)
    page_table = jax.random.permutation(ks[1], n_phys)[:used].reshape(DEC_BATCH, n_pages).astype(jnp.int32)
    pool_shape = (DEPTH, n_phys, PAGE_SIZE, KV_HEADS, HEAD_DIM)
    win_shape = (DEPTH, DEC_BATCH, win_buf, KV_HEADS, HEAD_DIM)
    return {
        'x_prompt': nrm(ks[2], (BATCH, SEQ, D_MODEL)),
        'x_sample': nrm(ks[3], (DEC_BATCH, DEC_SEQ, D_MODEL)),
        'cache_cmp_k': nrm(ks[4], pool_shape),
        'cache_cmp_v': nrm(ks[5], pool_shape),
        'cache_sel_k': nrm(ks[6], pool_shape),
        'cache_sel_v': nrm(ks[7], pool_shape),
        'cache_win_k': nrm(ks[8], win_shape),
        'cache_win_v': nrm(ks[9], win_shape),
        'state_conv': nrm(ks[10], (DEPTH, DEC_BATCH, CONV_W - 1, LRU_W)),
        'state_lru': nrm(ks[11], (DEPTH, DEC_BATCH, LRU_W), 0.3),
        'state_pool': nrm(ks[12], (DEPTH, DEC_BATCH, POOL_BUF, POOL_W)),
        'page_table': page_table,
        'g_pre': 1.0 + nrm(ks[13], (DEPTH, D_MODEL), 0.05),
        'g_post': 1.0 + nrm(ks[14], (DEPTH, D_MODEL), 0.05),
        'w_in': nrm(ks[15], (DEPTH, D_MODEL, IN_W), D_MODEL ** -0.5),
        'conv_w': nrm(ks[16], (DEPTH, CONV_W, LRU_W), CONV_W ** -0.5),
        'conv_b': nrm(ks[17], (DEPTH, LRU_W), 0.01),
        'w_rg_a': nrm(ks[18], (DEPTH, LRU_BLOCKS, LRU_BD, LRU_BD), LRU_BD ** -0.5),
        'b_rg_a': nrm(ks[19], (DEPTH, LRU_W), 0.1),
        'w_rg_x': nrm(ks[20], (DEPTH, LRU_BLOCKS, LRU_BD, LRU_BD), LRU_BD ** -0.5),
        'b_rg_x': nrm(ks[21], (DEPTH, LRU_W), 0.1),
        'lru_lambda': lam,
        'w_pool': nrm(ks[22], (DEPTH, POOL_G, POOL_GD, POOL_GD), POOL_GD ** -0.5),
        'pool_scale': 1.0 + nrm(ks[23], (DEPTH, POOL_W), 0.1),
        'cmp_pos_k': (1.0 + nrm(ks[24], (DEPTH, CMP_BLK, HEAD_DIM), 0.1)) * CMP_BLK ** -0.5,
        'cmp_phi_k': nrm(ks[25], (DEPTH, HEAD_DIM, HEAD_DIM), HEAD_DIM ** -0.5),
        'cmp_pos_v': (1.0 + nrm(ks[26], (DEPTH, CMP_BLK, HEAD_DIM), 0.1)) * CMP_BLK ** -0.5,
        'cmp_phi_v': nrm(ks[27], (DEPTH, HEAD_DIM, HEAD_DIM), HEAD_DIM ** -0.5),
        'w_branch': nrm(ks[28], (DEPTH, N_BRANCH, BRANCH_W, D_MODEL), BRANCH_W ** -0.5),
        'w_out': nrm(ks[29], (DEPTH, D_MODEL, D_MODEL), D_MODEL ** -0.5),
    }


def reference(x_prompt, x_sample, cache_cmp_k, cache_cmp_v, cache_sel_k, cache_sel_v, cache_win_k, cache_win_v,
              state_conv, state_lru, state_pool, page_table, g_pre, g_post, w_in, conv_w, conv_b,
              w_rg_a, b_rg_a, w_rg_x, b_rg_x, lru_lambda, w_pool, pool_scale,
              cmp_pos_k, cmp_phi_k, cmp_pos_v, cmp_phi_v, w_branch, w_out):
    past_len = page_table.shape[1] * PAGE_SIZE
    Bp = x_prompt.shape[0]
    dt = x_prompt.dtype
    xp, xs = x_prompt, x_sample
    pr = [[] for _ in range(9)]
    sm = [[] for _ in range(9)]
    for l in range(DEPTH):
        lw = (g_pre[l], g_post[l], w_in[l], conv_w[l], conv_b[l], w_rg_a[l], b_rg_a[l], w_rg_x[l], b_rg_x[l],
              lru_lambda[l], w_pool[l], pool_scale[l], cmp_pos_k[l], cmp_phi_k[l], cmp_pos_v[l], cmp_phi_v[l],
              w_branch[l], w_out[l])
        xp, st_p = _layer(xp, 0, jnp.zeros((Bp, CONV_W - 1, LRU_W), dt), jnp.zeros((Bp, LRU_W), dt),
                          jnp.zeros((Bp, POOL_BUF, POOL_W), dt), _nsa_prompt, lw)
        nsa_s = functools.partial(_nsa_sample, pool_ck=cache_cmp_k[l], pool_cv=cache_cmp_v[l],
                                  pool_sk=cache_sel_k[l], pool_sv=cache_sel_v[l],
                                  buf_k=cache_win_k[l], buf_v=cache_win_v[l], page_table=page_table)
        xs, st_s = _layer(xs, past_len, state_conv[l], state_lru[l], state_pool[l], nsa_s, lw)
        for i in range(9):
            pr[i].append(st_p[i])
            sm[i].append(st_s[i])
    return (xp, xs,
            jnp.stack(pr[0]), jnp.stack(sm[0]), jnp.stack(pr[1]), jnp.stack(sm[1]),
            jnp.stack(pr[2]), jnp.stack(sm[2]), jnp.stack(pr[3]), jnp.stack(sm[3]),
            jnp.stack(pr[4]), jnp.stack(sm[4]), jnp.stack(pr[5]), jnp.stack(sm[5]),
            jnp.stack(pr[6]), jnp.stack(sm[6]), jnp.stack(pr[7]), jnp.stack(sm[7]),
            jnp.stack(pr[8]), jnp.stack(sm[8]))
```

```python
import numpy as np
import concourse.bass as bass
import concourse.mybir as mybir
from concourse.bass_utils import run_bass_kernel_spmd

F32 = mybir.dt.float32
BF16 = mybir.dt.bfloat16
I32 = mybir.dt.int32
AF = mybir.ActivationFunctionType
ALU = mybir.AluOpType
AX = mybir.AxisListType

D = 1024
IN_W = 10800
NEGB = -30000.0
THR_SKIP = 100.0
HPERM = [0, 4, 1, 5, 2, 6, 3, 7, 8, 12, 9, 13, 10, 14, 11, 15]
SLOPES = [2.0 ** (-8.0 * (h + 1) / 16.0) for h in range(16)]
POOL_WIN = (2, 4, 8, 16)


class Ev:
    __slots__ = ("sem", "val", "home")

    def __init__(self, sem, val, home=None):
        self.sem = sem
        self.val = val
        self.home = home


class Buf:
    __slots__ = ("name", "t", "w", "r", "dsem", "dcnt", "waited")

    def __init__(self, name, t=None):
        self.name = name
        self.t = t
        self.w = None
        self.r = {}
        self.dsem = None
        self.dcnt = 0
        self.waited = 0

    def __getitem__(self, idx):
        return self.t[idx]


class Eng:
    def __init__(self, K, eng, name, is_pe=False):
        self.K = K
        self.e = eng
        self.name = name
        self.sem = K.newsem("e_" + name)
        self.cnt = 0
        self.seen = {}
        self.is_pe = is_pe

    def wait(self, ev):
        if ev is None:
            return
        if self.is_pe and ev.sem is self.sem:
            return
        k = ev.sem.num
        val = ev.val if ev.home is None else ev.home.dcnt
        if ev.home is not None:
            ev.home.waited = max(ev.home.waited, val)
        if self.seen.get(k, 0) >= val:
            return
        self.e.wait_ge(ev.sem, val)
        self.seen[k] = val

    def pre(self, reads, writes):
        for b in reads:
            self.wait(b.w)
        for b in writes:
            self.wait(b.w)
            for ev in b.r.values():
                self.wait(ev)

    def post(self, ins, reads, writes):
        self.cnt += 1
        ins.then_inc(self.sem, 1)
        ev = Ev(self.sem, self.cnt)
        for b in reads:
            b.r[self.sem.num] = ev
        for b in writes:
            b.w = ev
            b.r = {}
        return ev

    def op(self, reads, writes, fn):
        self.pre(reads, writes)
        ins = fn(self.e)
        return self.post(ins, reads, writes)

    def dma(self, out_ap, in_ap, reads, writes, home, **kw):
        self.pre(reads, writes)
        if home.dsem is None:
            home.dsem = self.K.newsem("d_" + home.name)
            self.K.homes.append(home)
        if home.waited:
            self.wait(Ev(home.dsem, home.dcnt, home))
        ins = self.e.dma_start(out=out_ap, in_=in_ap, **kw)
        home.dcnt += 16
        ins.then_inc(home.dsem, 16)
        ev = Ev(home.dsem, home.dcnt, home)
        for b in reads:
            b.r[("d", home.dsem.num)] = ev
        for b in writes:
            b.w = ev
            b.r = {}
        return ev


def idma(eng, out_ap, in_ap, idx_ap, reads, writes, home):
    eng.pre(reads, writes)
    if home.dsem is None:
        home.dsem = eng.K.newsem("d_" + home.name)
        eng.K.homes.append(home)
    if home.waited:
        eng.wait(Ev(home.dsem, home.dcnt, home))
    ins = eng.e.indirect_dma_start(out=out_ap, out_offset=None, in_=in_ap, in_offset=bass.IndirectOffsetOnAxis(ap=idx_ap, axis=0))
    home.dcnt += 16
    ins.then_inc(home.dsem, 16)
    ev = Ev(home.dsem, home.dcnt, home)
    for b in reads:
        b.r[("d", home.dsem.num)] = ev
    for b in writes:
        b.w = ev
        b.r = {}
    return ev


class Kern:
    def __init__(self):
        self.nc = bass.Bass("TRN2", target_bir_lowering=False)
        nc = self.nc
        self.nsem = 0
        self.pe = Eng(self, nc.tensor, "pe", is_pe=True)
        self.dve = Eng(self, nc.vector, "dve")
        self.act = Eng(self, nc.scalar, "act")
        self.pool = Eng(self, nc.gpsimd, "pool")
        self.sp = Eng(self, nc.sync, "sp")
        self.outs = []
        self.cms = []
        self.homes = []

    def newsem(self, name):
        self.nsem += 1
        return self.nc.alloc_semaphore(name="%s_%d" % (name, self.nsem))

    def sb(self, name, shape, dt):
        cm = self.nc.sbuf_tensor(name, list(shape), dt)
        t = cm.__enter__()
        self.cms.append(cm)
        return Buf(name, t)

    def release_to(self, mark):
        while len(self.cms) > mark:
            self.cms.pop().__exit__(None, None, None)

    def barrier(self):
        engs = (self.pe, self.dve, self.act, self.pool, self.sp)
        for e in engs:
            for f in engs:
                if f.cnt and not (f is e and e.is_pe):
                    if f is e:
                        e.e.wait_ge(f.sem, f.cnt)
                        e.seen[f.sem.num] = f.cnt
                    else:
                        e.wait(Ev(f.sem, f.cnt))
            for hb in self.homes:
                e.wait(Ev(hb.dsem, hb.dcnt, hb))

    def ps(self, name, shape, dt=F32):
        t = self.nc.psum_tensor(name, list(shape), dt).__enter__()
        return Buf(name, t)

    def dram_in(self, name, shape, dt):
        t = self.nc.dram_tensor(name, list(shape), dt, kind="ExternalInput")
        return Buf(name, t.ap())

    def dram_out(self, name, shape, dt):
        t = self.nc.dram_tensor(name, list(shape), dt, kind="ExternalOutput")
        b = Buf(name, t.ap())
        self.outs.append(b)
        return b

    def dram_tmp(self, name, shape, dt):
        t = self.nc.dram_tensor(name, list(shape), dt, kind="Internal")
        return Buf(name, t.ap())

    def finish(self):
        for b in self.outs:
            self.sp.wait(b.w)
        for e in (self.pe, self.dve, self.act, self.pool):
            if e.cnt:
                self.sp.wait(Ev(e.sem, e.cnt))


class Cfg:
    def __init__(self, S=8192, NS=16, P=8192, NPHYS=2560, NCORES=2):
        self.S = S
        self.NS = NS
        self.P = P
        self.NPHYS = NPHYS
        self.NCORES = NCORES


CH_LRUX, CH_LRUG, CH_POOLX, CH_POOLG, CH_Q, CH_NSAG, CH_KVF, CH_BG, CH_MG = 0, 8, 16, 24, 32, 40, 48, 56, 57
NCHUNK = 81


def make_consts(cfg):
    S = cfg.S
    NKT = S // 128
    NMT = max(1, (S // 16) // 128)
    p = np.arange(128)
    ident = np.eye(128, dtype=np.float32)
    kk, qq = np.meshgrid(p, p, indexing="ij")
    tric = np.where(kk <= qq, 0.0, NEGB).astype(np.float32)
    triw = np.where(kk > qq, 0.0, NEGB).astype(np.float32)
    row0b = np.zeros((128, 512), np.float32)
    row0b[0, :] = NEGB
    x = np.arange(256)
    m = x[None, :] - 128
    jbrel = (p // 64)[:, None]
    patw = np.where(m > jbrel, -1.0e30, np.where((m == jbrel) | (m == jbrel - 1), 1.0e4, 0.0)).astype(np.float32)
    ratio = np.zeros((128, 4, 15), np.float32)
    for g, w in enumerate(POOL_WIN):
        ratio[:, g, :] = (w / np.minimum(w, np.arange(15) + 1.0))[None, :]
    qrel = np.arange(512)
    cmpb = np.zeros((128, 4, 512), np.float32)
    for v in range(4):
        cmpb[:, v, :] = np.where((16 * p[:, None] + 15 - 512 * v) <= qrel[None, :], 0.0, NEGB)
    selh = np.zeros((128, 32, 128), np.float32)
    k = np.arange(128)
    for v in range(32):
        selh[:, v, :] = ((p % 64)[:, None] == (2 * v + k // 64)[None, :]).astype(np.float32)
    gagg = np.zeros((128, NMT, 128), np.float32)
    j = np.arange(128)
    for mt in range(NMT):
        mm_ = 128 * mt + p
        gagg[:, mt, :] = ((mm_[:, None] >= 1) & (((mm_[:, None] - 1) // 4) == j[None, :])).astype(np.float32)
    NAB = NKT + 3
    OFF = NKT - 1
    NABC = NKT + 16 * (NMT - 1) + 1
    ab = np.zeros((128, 16, NAB), np.float64)
    abc = np.zeros((128, 16, NABC), np.float64)
    for h in range(16):
        W = 128 if h <= 3 else 512
        idx = np.arange(NAB)
        ab[:, h, :] = SLOPES[h] * (p[:, None] + 128.0 * (idx[None, :] - OFF) - W / 2.0)
        idx = np.arange(NABC)
        abc[:, h, :] = SLOPES[h] * (16.0 * p[:, None] + 15.0 + 128.0 * (idx[None, :] - OFF) - W / 2.0)
    f32c = np.concatenate([ident, tric, triw, row0b, patw, ratio.reshape(128, -1),
                           ab.reshape(128, -1).astype(np.float32), abc.reshape(128, -1).astype(np.float32)], axis=1)
    bfc = np.concatenate([cmpb.reshape(128, -1), selh.reshape(128, -1), gagg.reshape(128, -1)], axis=1)
    P_ = cfg.P
    NPG = P_ // 128
    NMS = max(1, (P_ // 16) // 128)
    absb = np.zeros((128, NMS, 16), np.float64)
    abk = np.zeros((128, NPG, 16), np.float64)
    abw = np.zeros((128, 4, 16), np.float64)
    for h in range(16):
        for mt in range(NMS):
            mm_ = 128 * mt + p
            absb[:, mt, h] = -SLOPES[h] * (P_ - (16.0 * mm_ + 15.0))
        for pg in range(NPG):
            abk[:, pg, h] = -SLOPES[h] * (P_ - (128.0 * pg + p))
        for t in range(4):
            abw[:, t, h] = -SLOPES[h] * (512.0 - (128.0 * t + p))
    absb[0, 0, :] = NEGB
    abw[0, 0, :] = NEGB
    ind = (p[:, None] // 16 == np.arange(8)[None, :]).astype(np.float32)
    fb = np.zeros((128, 128), np.float32)
    fb[:, 0] = 1.0e4
    fb[:, NPG * 2 - 1] = 1.0e4
    gs = np.zeros((128, NMS, 128), np.float32)
    for mt in range(NMS):
        mm_ = 128 * mt + p
        gs[:, mt, :] = ((mm_[:, None] >= 1) & (((mm_[:, None] - 1) // 4) == j[None, :])).astype(np.float32)
    c_s = np.concatenate([p[:, None].astype(np.float32), absb.reshape(128, -1), abk.reshape(128, -1), abw.reshape(128, -1),
                          ind, fb, gs.reshape(128, -1)], axis=1)
    return {"c_f32": np.ascontiguousarray(f32c, dtype=np.float32), "c_bf": np.ascontiguousarray(bfc, dtype=np.float32),
            "c_s": np.ascontiguousarray(c_s, dtype=np.float32)}


def build(cfg):
    S = cfg.S
    NT = S // 512
    NKT = S // 128
    NBk = min(S // 64, 128)
    NMT = max(1, (S // 16) // 128)
    NAB = NKT + 3
    OFF = NKT - 1
    NABC = NKT + 16 * (NMT - 1) + 1
    NSL = [min(NKT, 24), NKT]

    K = Kern()
    nc = K.nc
    pe, dve, act, pool, sp = K.pe, K.dve, K.act, K.pool, K.sp

    x_in = K.dram_in("x", [S, D], F32)
    g_pre = K.dram_in("g_pre", [2, D], F32)
    g_post = K.dram_in("g_post", [2, D], F32)
    w_in = K.dram_in("w_in", [2, D, IN_W], F32)
    vecs = K.dram_in("vecs", [18, D], F32)
    w_rg = K.dram_in("w_rg", [2, 2, 16, 64, 64], F32)
    w_pool = K.dram_in("w_pool", [2, 4, 256, 256], F32)
    cmp_pos = K.dram_in("cmp_pos", [2, 2, 32, 64], F32)
    cmp_phi = K.dram_in("cmp_phi", [2, 2, 64, 64], F32)
    w_branch = K.dram_in("w_branch", [2, 3, D, D], F32)
    w_out = K.dram_in("w_out", [2, D, D], F32)
    c_f32 = K.dram_in("c_f32", [128, 128 * 3 + 512 + 256 + 60 + 16 * NAB + 16 * NABC], F32)
    c_bf = K.dram_in("c_bf", [128, 2048 + 4096 + NMT * 128], F32)

    y_out = K.dram_out("y", [S, D], F32)
    kv_out = [K.dram_out("kvo%d" % i, [2, S, 256], F32) for i in range(4)]
    win_out = [K.dram_out("wino%d" % i, [2, 512, 256], F32) for i in range(2)]
    st_out = K.dram_out("st", [2, 19, D], F32)

    NS, PL, NPHYS = cfg.NS, cfg.P, cfg.NPHYS
    NPG = PL // 128
    NMS = max(1, (PL // 16) // 128)
    WBUF = min(512, PL)
    xs_in = K.dram_in("xs_in", [NS, D], F32)
    pools = [K.dram_in("pool%d" % i, [2 * NPHYS * 128, 256], F32) for i in range(4)]
    winc = [K.dram_in("winc%d" % i, [2, NS, WBUF, 256], F32) for i in range(2)]
    st_conv = K.dram_in("st_conv", [2, NS * 3, D], F32)
    st_lru = K.dram_in("st_lru", [2, NS, D], F32)
    st_pool = K.dram_in("st_pool", [2, NS * 15, D], F32)
    ptab = K.dram_in("ptab", [NS, NPG], I32)
    c_s = K.dram_in("c_s", [128, 1 + NMS * 16 + NPG * 16 + 64 + 8 + 128 + NMS * 128], F32)
    ys_out = K.dram_out("ys", [NS, D], F32)
    skv_out = K.dram_out("skv", [2, NS, 1536], F32)
    swin_out = [K.dram_out("swin%d" % i, [2, NS, WBUF, 256], F32) for i in range(2)]
    sconv_out = K.dram_out("sconv", [2, NS * 3, D], F32)
    slru_out = K.dram_out("slru", [2, NS, D], F32)
    spool_out = K.dram_out("spool", [2, NS * 15, D], F32)
    xs1 = K.dram_tmp("xsamp1", [NS, D], F32)
    Wqt = [K.dram_tmp("Wqt%d" % l, [2, 128, 8, 512], BF16) for l in range(2)]

    x1 = K.dram_tmp("x1", [S, D], F32)
    Wc = [K.dram_tmp("Wc%d" % l, [NCHUNK, 128, 8, 128], BF16) for l in range(2)]
    Wkv = [K.dram_tmp("Wkv%d" % l, [3, 128, 8, 512], BF16) for l in range(2)]
    Wb = [K.dram_tmp("Wb%d" % l, [24, 128, 8, 128], BF16) for l in range(2)]
    Wo = [K.dram_tmp("Wo%d" % l, [128, 8, D], BF16) for l in range(2)]

    def prep_cols(l, ch, scol, n, dcol):
        src = w_in[l, :, scol:scol + n].rearrange("(k p) n -> p k n", p=128)
        pool.dma(Wc[l][ch, :, :, dcol:dcol + n], src, [w_in], [Wc[l]], Wc[l])

    def prep_layer(l):
        for c in range(8):
            prep_cols(l, CH_LRUX + c, 0 + c * 128, 128, 0)
            prep_cols(l, CH_LRUG + c, 1024 + c * 128, 128, 0)
            prep_cols(l, CH_POOLX + c, 2048 + c * 128, 128, 0)
            prep_cols(l, CH_POOLG + c, 3072 + c * 128, 128, 0)
            for e in range(2):
                h = HPERM[2 * c + e]
                prep_cols(l, CH_Q + c, 4096 + h * 64, 64, e * 64)
                prep_cols(l, CH_NSAG + c, 5120 + h * 64, 64, e * 64)
        for i, base in enumerate((6144, 6400, 6656, 7168)):
            for cc in range(2):
                prep_cols(l, CH_KVF + 2 * i + cc, base + cc * 128, 128, 0)
        prep_cols(l, CH_BG, 7680, 128, 0)
        for n in range(3):
            for dc in range(8):
                prep_cols(l, CH_MG + n * 8 + dc, 7728 + n * 1024 + dc * 128, 128, 0)
        for blk in range(3):
            src = w_in[l, :, 6144 + blk * 512:6144 + (blk + 1) * 512].rearrange("(k p) n -> p k n", p=128)
            pool.dma(Wkv[l][blk], src, [w_in], [Wkv[l]], Wkv[l])
        for blk in range(2):
            src = w_in[l, :, 4096 + blk * 512:4096 + (blk + 1) * 512].rearrange("(k p) n -> p k n", p=128)
            pool.dma(Wqt[l][blk], src, [w_in], [Wqt[l]], Wqt[l])
        for n in range(2):
            for dc in range(8):
                src = w_branch[l, n, :, dc * 128:(dc + 1) * 128].rearrange("(k p) n -> p k n", p=128)
                pool.dma(Wb[l][n * 8 + dc], src, [w_branch], [Wb[l]], Wb[l])
        for fc in range(8):
            for e in range(2):
                h = HPERM[2 * fc + e]
                src = w_branch[l, 2, h * 64:(h + 1) * 64, :].rearrange("p (dc n) -> dc p n", n=128)
                pool.dma(Wb[l][16:24, e * 64:(e + 1) * 64, fc, :], src, [w_branch], [Wb[l]], Wb[l])
        src = w_out[l].rearrange("(k p) n -> p k n", p=128)
        pool.dma(Wo[l][:, :, :], src, [w_out], [Wo[l]], Wo[l])

    cf = K.sb("cf", [128, 128 + 256 + 60 + 16 * NAB + 16 * NABC], F32)
    sp.dma(cf[:, 0:128], c_f32[:, 0:128], [c_f32], [cf], cf)
    sp.dma(cf[:, 128:], c_f32[:, 896:], [c_f32], [cf], cf)
    o_ = [0]

    def take(n):
        a = o_[0]
        o_[0] += n
        return a

    o_id, o_pw, o_ratio = take(128), take(256), take(60)
    o_ab, o_abc = take(16 * NAB), take(16 * NABC)
    cb = K.sb("cb", [128, 2048 + 4096 + NMT * 128], BF16)
    pool.dma(cb[:], c_bf[:, :], [c_bf], [cb], cb)
    o_cmpb, o_selh, o_gagg = 0, 2048, 2048 + 4096
    cbm = K.sb("cbm", [128, 128 * 3 + 512], BF16)
    pool.dma(cbm[:], c_f32[:, 0:896], [c_f32], [cbm], cbm)
    ones_bf = K.sb("ones_bf", [128, 128], BF16)
    dve.op([], [ones_bf], lambda e: e.memset(ones_bf[:], 1.0))
    zer_bf = K.sb("zer_bf", [128, 128], BF16)
    dve.op([], [zer_bf], lambda e: e.memset(zer_bf[:], 0.0))

    def ident_f(n):
        return cf[0:n, o_id:o_id + n]

    ident_b = cbm[:, 0:128]
    tric_b = cbm[:, 128:256]
    triw_b = cbm[:, 256:384]
    row0b_b = cbm[:, 384:896]

    psA = [K.ps("psA%d" % i, [128, 512]) for i in range(2)]
    psS = [K.ps("psS%d" % i, [128, 512]) for i in range(2)]
    psO = [K.ps("psO%d" % i, [128, 512]) for i in range(2)]
    psM = K.ps("psM", [128, 512])
    psT = K.ps("psT", [128, 1024], BF16)
    rr = {"A": 0, "S": 0, "O": 0}

    def nxt(kind):
        lst = {"A": psA, "S": psS, "O": psO}[kind]
        rr[kind] += 1
        return lst[rr[kind] % 2]

    vr = K.sb("vr", [18, D], F32)
    sp.dma(vr[:], vecs[:, :], [vecs], [vr], vr)
    CV = K.sb("CV", [128, 8, 20], F32)
    for c in range(8):
        pe.op([vr, cf], [psM], lambda e, c=c: e.transpose(out=psM[:, c * 18:(c + 1) * 18], in_=vr[0:18, c * 128:(c + 1) * 128], identity=ident_f(18)))
    act.op([psM], [CV], lambda e: e.copy(out=CV[:, :, 0:18], in_=psM[:, 0:144].rearrange("p (c v) -> p c v", v=18)))
    tmpc = K.sb("tmpc", [128, 8, 2], F32)
    for l in range(2):
        act.op([CV], [tmpc], lambda e, l=l: e.activation(out=tmpc[:, :, l], in_=CV[:, :, l * 9 + 7], func=AF.Exp, scale=-1.0))
        act.op([tmpc], [tmpc], lambda e, l=l: e.activation(out=tmpc[:, :, l], in_=tmpc[:, :, l], func=AF.Ln, bias=1.0, scale=1.0))
        dve.op([tmpc], [CV], lambda e, l=l: e.tensor_scalar(out=CV[:, :, 18 + l], in0=tmpc[:, :, l], scalar1=-8.0, scalar2=None, op0=ALU.mult))

    wpr = K.sb("wpr", [32, 4, 128], F32)
    for l in range(2):
        for kv in range(2):
            for half in range(2):
                sp.dma(wpr[:, l * 2 + kv, half * 64:(half + 1) * 64], cmp_pos[l, kv, :, :], [cmp_pos], [wpr], wpr)
    CWP = K.sb("CWP", [128, 4, 32], F32)
    for i in range(4):
        pe.op([wpr, cf], [psM], lambda e, i=i: e.transpose(out=psM[:, 256 + i * 32:256 + (i + 1) * 32], in_=wpr[0:32, i, :], identity=ident_f(32)))
    act.op([psM], [CWP], lambda e: e.copy(out=CWP[:, :, :], in_=psM[:, 256:384].rearrange("p (i j) -> p i j", j=32)))

    WRG = K.sb("WRG", [128, 2, 8, 128], BF16)
    PHI = K.sb("PHI", [128, 4, 128], BF16)
    WPL = K.sb("WPL", [128, 4, 2, 256], BF16)
    pool.op([], [WRG], lambda e: e.memset(WRG[:], 0.0))
    pool.op([], [PHI], lambda e: e.memset(PHI[:], 0.0))
    for l in range(2):
        for kv in range(2):
            for half in range(2):
                pool.dma(PHI[half * 64:(half + 1) * 64, l * 2 + kv, half * 64:(half + 1) * 64], cmp_phi[l, kv, :, :], [cmp_phi], [PHI], PHI)

    def load_layer_small(l):
        for ax in range(2):
            for c in range(8):
                for half in range(2):
                    pool.dma(WRG[half * 64:(half + 1) * 64, ax, c, half * 64:(half + 1) * 64], w_rg[l, ax, 2 * c + half, :, :], [w_rg], [WRG], WRG)
        for g in range(4):
            pool.dma(WPL[:, g, :, :], w_pool[l, g, :, :].rearrange("(dh p) e -> p dh e", p=128), [w_pool], [WPL], WPL)
        sp.dma(gbc[0][:], g_pre[l:l + 1, :].partition_broadcast(128), [g_pre], [gbc[0]], gbc[0])
        sp.dma(gbc[1][:], g_post[l:l + 1, :].partition_broadcast(128), [g_post], [gbc[1]], gbc[1])

    gbc = [K.sb("gbc%d" % i, [128, D], F32) for i in range(2)]

    prep_layer(0)
    prep_layer(1)

    MARK_PROMPT = len(K.cms)
    KT = [K.sb("KT%d" % cc, [128, NSL[cc] * 128], BF16) for cc in range(2)]
    VS = [K.sb("VS%d" % cc, [128, NSL[cc], 2, 65], BF16) for cc in range(2)]
    KWT = K.sb("KWT", [128, 2, 8 * 128], BF16)
    VW = K.sb("VW", [128, 8, 4, 65], BF16)
    KCT = K.sb("KCT", [128, 2, NMT * 128], BF16)
    VC = K.sb("VC", [128, NMT, 4, 65], BF16)
    lruc = K.sb("lruc", [128, 8, 4], F32)
    poolc = K.sb("poolc", [128, 8, 15], F32)
    cmpc = K.sb("cmpc", [128, 4, 1], F32)

    xs = [K.sb("xs%d" % i, [128, D], F32) for i in range(2)]
    ub = K.sb("ub", [128, D], BF16)
    uT = K.sb("uT", [128, 8, 512], BF16)
    WB = [K.sb("WB%d" % i, [128, 2, 8, 128], BF16) for i in range(2)]
    WKB = K.sb("WKB", [128, 8, 512], BF16)
    QT = K.sb("QT", [128, 8, 512], BF16)
    zb = K.sb("zb", [128, 8, 512], BF16)
    ACC = K.sb("ACC", [128, 8, 512], F32)
    gates = K.sb("gates", [48, 512], F32)
    MbT = K.sb("MbT", [128, 4, 512], BF16)
    MbS = K.sb("MbS", [128, 4, 512], BF16)
    merged = QT
    SC = [K.sb("SC%d" % i, [128, 528], F32) for i in range(6)]
    PT = [K.sb("PT%d" % i, [128, 512], BF16) for i in range(2)]
    PN = K.sb("PN", [128, NMT, 512], BF16)
    kvst = [K.sb("kvst%d" % i, [128, 512], F32) for i in range(1)]
    sm = K.sb("sm", [128, 64], F32)
    rrb = {"WB": 0, "PT": 0, "kvst": 0, "xs": 0}

    def rot(name, lst):
        rrb[name] += 1
        return lst[rrb[name] % len(lst)]

    wb_dma = {"n": 0}

    def mm(out_b, out_ap, l_b, l_ap, r_b, r_ap, start, stop):
        pe.op([l_b, r_b], [out_b], lambda e: e.matmul(out_ap, lhsT=l_ap, rhs=r_ap, start=start, stop=stop))

    def proj(l, chunks, consume, Wsrc=None, ncols=128):
        Wsrc = Wsrc if Wsrc is not None else Wc[l]
        groups = []
        i = 0
        while i < len(chunks):
            n = 1
            if i + 1 < len(chunks) and chunks[i + 1] == chunks[i] + 1:
                n = 2
            groups.append((i, chunks[i], n))
            i += n

        def load(g):
            _, ch0, n = g
            buf = rot("WB", WB)
            sp.dma(buf[:, 0:n, :, :], Wsrc[ch0:ch0 + n].rearrange("c p k n -> p c k n"), [Wsrc], [buf], buf)
            return buf

        cur = load(groups[0])
        for gi, g in enumerate(groups):
            nxtb = load(groups[gi + 1]) if gi + 1 < len(groups) else None
            i0, ch0, n = g
            for j in range(n):
                ps = nxt("A")
                for kc in range(8):
                    mm(ps, ps[0:ncols, :], cur, cur[:, j, kc, 0:ncols], uT, uT[:, kc, :], kc == 0, kc == 7)
                consume(i0 + j, ch0 + j, ps)
            cur = nxtb

    def attn_pairs(h, q0, W, branch, J):
        res = []
        qlo_t = q0 // 128
        nsub = W // 128
        if branch == "s":
            kts = range(0, qlo_t + nsub)
        else:
            kts = range(max(0, qlo_t - 4), qlo_t + nsub)
        for kt in kts:
            i_lo = max(0, kt - qlo_t)
            i_hi = nsub - 1
            if branch == "w":
                i_hi = min(nsub - 1, kt + 4 - qlo_t)
            if i_lo > i_hi:
                continue
            c0, c1 = i_lo * 128, (i_hi + 1) * 128
            mind = (q0 + c0) - (kt * 128 + 127)
            if mind > 0 and SLOPES[h] * mind > THR_SKIP:
                continue
            masks = []
            if kt - qlo_t >= 0:
                masks.append((tric_b, (kt - qlo_t) * 128))
            if branch == "w" and 0 <= kt + 4 - qlo_t <= nsub - 1:
                masks.append((triw_b, (kt + 4 - qlo_t) * 128))
            res.append((kt, c0, c1, masks))
        return res

    EPS = 1e-6
    m8 = K.sb("m8", [128, 16], F32)
    wk = K.sb("wk", [128, 128], F32)
    Mbq = K.sb("Mbq", [128, 128], BF16)
    blkb = K.sb("blkb", [128, 32], BF16)
    vcst = K.sb("vcst", [32, 128], BF16)

    def init_state():
        for b_ in (KCT, VC, VW, VS[0], VS[1], KT[0], KT[1], KWT, MbT, MbS):
            pool.op([], [b_], lambda e, b_=b_: e.memset(b_[:], 0.0))
        pool.op([], [VC], lambda e: e.memset(VC[:, :, :, 64:65], 1.0))
        pool.op([], [VW], lambda e: e.memset(VW[:, :, :, 64:65], 1.0))
        for cc in range(2):
            pool.op([], [VS[cc]], lambda e, cc=cc: e.memset(VS[cc][:, :, :, 64:65], 1.0))

    def compress(l, J, kv, cc, ps):
        i = l * 2 + kv
        kcs, fs = SC[0], SC[1]
        act.op([ps], [kcs], lambda e: e.copy(out=kcs[:, 0:512], in_=ps[:, :]))
        v3 = kcs[:, 0:512].rearrange("p (c j) -> p c j", j=16)
        for half, o in ((0, 0), (1, 32)):
            dve.op([kcs, CWP], [fs], lambda e: e.tensor_scalar(out=fs[:, o:o + 32], in0=v3[:, :, 0], scalar1=CWP[:, i, half * 16:half * 16 + 1], scalar2=None, op0=ALU.mult))
            for j in range(1, 16):
                dve.op([kcs, CWP, fs], [fs], lambda e, j=j: e.scalar_tensor_tensor(out=fs[:, o:o + 32], in0=v3[:, :, j], scalar=CWP[:, i, half * 16 + j:half * 16 + j + 1], in1=fs[:, o:o + 32], op0=ALU.mult, op1=ALU.add))
        ci = kv * 2 + cc
        dve.op([cmpc, fs], [blkb], lambda e: e.tensor_tensor(out=blkb[:, 0:1], in0=cmpc[:, ci, :], in1=fs[:, 32:33], op=ALU.add))
        dve.op([fs], [blkb], lambda e: e.tensor_tensor(out=blkb[:, 1:32], in0=fs[:, 0:31], in1=fs[:, 33:64], op=ALU.add))
        dve.op([fs], [cmpc], lambda e: e.tensor_copy(out=cmpc[:, ci, :], in_=fs[:, 31:32]))
        if kv == 0:
            mm(psM, psM[:, 0:32], PHI, PHI[:, i, :], blkb, blkb[:, :], True, True)
            act.op([psM], [KCT], lambda e: e.copy(out=KCT[:, cc, 32 * J:32 * J + 32], in_=psM[:, 0:32]))
        else:
            mm(psM, psM[0:32, 0:128], blkb, blkb[:, :], PHI, PHI[:, i, :], True, True)
            act.op([psM], [vcst], lambda e: e.copy(out=vcst[:, :], in_=psM[0:32, 0:128]))
            r0 = 32 * (J % 4)
            sp.dma(VC[r0:r0 + 32, J // 4, 2 * cc:2 * cc + 2, 0:64], vcst[:, :].rearrange("p (g d) -> p g d", d=64), [vcst], [VC], VC)

    def combine(h, pos, O, qc0, W, b):
        cp, half = pos // 2, pos % 2
        Oa, rec = SC[2], SC[3]
        act.op([O], [Oa], lambda e: e.copy(out=Oa[0:65, 0:W], in_=O[0:65, 0:W]))
        mm(psM, psM[0:64, 0:W], cf, cf[0:65, o_id + 64:o_id + 65].to_broadcast([65, 64]), Oa, Oa[0:65, 0:W], True, True)
        gb = nxt("S")
        gi = 3 * h + b
        mm(gb, gb[0:64, 0:W], cf, cf[0:48, o_id + gi:o_id + gi + 1].to_broadcast([48, 64]), gates, gates[0:48, qc0:qc0 + W], True, True)
        dve.op([psM], [rec], lambda e: e.tensor_scalar(out=rec[0:64, 0:W], in0=psM[0:64, 0:W], scalar1=1e-30, scalar2=None, op0=ALU.max))
        dve.op([rec], [rec], lambda e: e.reciprocal(out=rec[0:64, 0:W], in_=rec[0:64, 0:W]))
        dve.op([gb, rec], [rec], lambda e: e.tensor_tensor(out=rec[0:64, 0:W], in0=gb[0:64, 0:W], in1=rec[0:64, 0:W], op=ALU.mult))
        T2 = SC[4]
        hs_ = slice(half * 64, half * 64 + 64)
        dve.op([Oa, rec], [T2], lambda e: e.tensor_tensor(out=T2[hs_, 0:W], in0=Oa[0:64, 0:W], in1=rec[0:64, 0:W], op=ALU.mult))
        dve.op([T2, ACC], [ACC], lambda e: e.tensor_tensor(out=ACC[hs_, cp, qc0:qc0 + W], in0=T2[hs_, 0:W], in1=ACC[hs_, cp, qc0:qc0 + W], op=ALU.add))

    DBG = getattr(cfg, 'DBG', '')

    def attend(h, pos, q0, W, branch, J):
        t0 = J * 512
        qc0 = q0 - t0
        cp, half = pos // 2, pos % 2
        g = h // 4
        cc, gi = g // 2, g % 2
        pairs = attn_pairs(h, q0, W, branch, J)
        full = [p_ for p_ in pairs if p_[1] == 0 and p_[2] == W]
        assert full, (h, q0, W, branch)
        pairs = [full[0]] + [p_ for p_ in pairs if p_ is not full[0]]
        O = nxt("O")

        def kv_of(kt):
            if branch == "s":
                sl = kt % NSL[cc]
                return KT[cc], KT[cc][half * 64:half * 64 + 64, sl * 128:(sl + 1) * 128], VS[cc], VS[cc][:, sl, gi, :]
            sl = kt % 8
            return KWT, KWT[half * 64:half * 64 + 64, cc, sl * 128:(sl + 1) * 128], VW, VW[:, sl, g, :]

        def emit_scores(pi):
            kt, c0, c1, masks = pairs[pi]
            S_ = nxt("S")
            lk_b, lk, _, _ = kv_of(kt)
            if 't' in DBG:
                masks = []
            nmore = len(masks) + (1 if (branch == "s" and "m" not in DBG) else 0)
            mm(S_, S_[:, c0:c1], lk_b, lk, QT, QT[half * 64:half * 64 + 64, cp, qc0 + c0:qc0 + c1], True, nmore == 0)
            if branch == "s" and "m" not in DBG:
                a2 = ((2 * kt) // 64) % 2
                v = kt % 32
                nmore -= 1
                Mb_ = MbT if a2 == half else MbS
                mm(S_, S_[:, c0:c1], cb, cb[half * 64:half * 64 + 64, o_selh + v * 128:o_selh + (v + 1) * 128],
                   Mb_, Mb_[half * 64:half * 64 + 64, g, qc0 + c0:qc0 + c1], False, nmore == 0)
            for (mb_, mc0) in masks:
                nmore -= 1
                mm(S_, S_[:, mc0:mc0 + 128], cbm, ident_b, cbm, mb_, False, nmore == 0)
            return S_

        S_next = emit_scores(0)
        for pi, (kt, c0, c1, masks) in enumerate(pairs):
            S_ = S_next
            if pi + 1 < len(pairs):
                S_next = emit_scores(pi + 1)
            P_ = rot("PT", PT)
            _, _, vb, vap = kv_of(kt)
            idx = kt - q0 // 128 + OFF
            act.op([S_, cf], [P_], lambda e, S_=S_, P_=P_, c0=c0, c1=c1, idx=idx: e.activation(
                out=P_[:, c0:c1], in_=S_[:, c0:c1], func=AF.Exp, bias=cf[:, o_ab + h * NAB + idx:o_ab + h * NAB + idx + 1], scale=1.0))
            if 'p' not in DBG:
                mm(O, O[0:65, c0:c1], vb, vap, P_, P_[:, c0:c1], pi == 0, pi == len(pairs) - 1)
        if 'c' not in DBG:
            combine(h, pos, O, qc0, W, 1 if branch == "s" else 2)

    def attend_cmp(h, pos, q0, W, J, psacc, written, last_head):
        t0 = J * 512
        qc0 = q0 - t0
        cp, half = pos // 2, pos % 2
        g = h // 4
        cc = g // 2
        mtd = J // 4
        mts = []
        for mt in range(mtd + 1):
            mind = q0 - (16 * (128 * mt + 127) + 15)
            if mind > 0 and SLOPES[h] * mind > THR_SKIP:
                continue
            mts.append(mt)
        O = nxt("O")
        Dp = nxt("O")
        for mi, mt in enumerate(mts):
            S_ = nxt("S")
            nmore = (1 if mt == 0 else 0) + (1 if mt == mtd else 0)
            mm(S_, S_[:, 0:W], KCT, KCT[half * 64:half * 64 + 64, cc, mt * 128:(mt + 1) * 128], QT, QT[half * 64:half * 64 + 64, cp, qc0:qc0 + W], True, nmore == 0)
            if mt == 0:
                nmore -= 1
                mm(S_, S_[:, 0:W], cbm, ident_b, cbm, row0b_b[:, 0:W], False, nmore == 0)
            if mt == mtd:
                nmore -= 1
                v = J % 4
                mm(S_, S_[:, 0:W], cbm, ident_b, cb, cb[:, o_cmpb + v * 512 + qc0:o_cmpb + v * 512 + qc0 + W], False, nmore == 0)
            idx = 16 * mt - q0 // 128 + OFF
            act.op([S_, cf], [PN], lambda e, S_=S_, mt=mt, idx=idx: e.activation(
                out=PN[:, mt, qc0:qc0 + W], in_=S_[:, 0:W], func=AF.Exp, bias=cf[:, o_abc + h * NABC + idx:o_abc + h * NABC + idx + 1], scale=1.0))
            mm(Dp, Dp[:, 0:W], ones_bf, ones_bf[:, :], PN, PN[:, mt, qc0:qc0 + W], mi == 0, mi == len(mts) - 1)
            mm(O, O[0:65, 0:W], VC, VC[:, mt, g, :], PN, PN[:, mt, qc0:qc0 + W], mi == 0, mi == len(mts) - 1)
        rden = SC[4]
        dve.op([Dp], [rden], lambda e: e.tensor_scalar(out=rden[:, 0:W], in0=Dp[:, 0:W], scalar1=1e-30, scalar2=None, op0=ALU.max))
        dve.op([rden], [rden], lambda e: e.reciprocal(out=rden[:, 0:W], in_=rden[:, 0:W]))
        for mt in mts:
            dve.op([PN, rden], [PN], lambda e, mt=mt: e.tensor_tensor(out=PN[:, mt, qc0:qc0 + W], in0=PN[:, mt, qc0:qc0 + W], in1=rden[:, 0:W], op=ALU.mult))
        nsub = W // 128
        for si in range(nsub):
            qs = qc0 // 128 + si
            for mi, mt in enumerate(mts):
                mm(psacc, psacc[:, qs * 128:qs * 128 + NBk], PN, PN[:, mt, qs * 128:(qs + 1) * 128], cb, cb[:, o_gagg + mt * 128:o_gagg + mt * 128 + NBk],
                   False, last_head and si == nsub - 1 and mi == len(mts) - 1)
        if getattr(cfg, 'STAGE', 99) > 2.31:
            combine(h, pos, O, qc0, W, 0)

    def topk_group(g, J, psacc):
        t0 = J * 512
        for qs in range(4):
            s0 = t0 + qs * 128
            sc = SC[5]
            po = o_pw + 128 - s0 // 64
            dve.op([psacc, cf], [sc], lambda e: e.tensor_tensor(out=sc[:, 0:NBk], in0=psacc[:, qs * 128:qs * 128 + NBk], in1=cf[:, po:po + NBk], op=ALU.add))
            dve.op([sc], [sc], lambda e: e.tensor_scalar(out=sc[:, 0:1], in0=sc[:, 0:1], scalar1=1.0e4, scalar2=None, op0=ALU.add))
            dve.op([sc], [m8], lambda e: e.max(out=m8[:, 0:8], in_=sc[:, 0:NBk]))
            dve.op([sc, m8], [wk], lambda e: e.match_replace(out=wk[:, 0:NBk], in_to_replace=m8[:, 0:8], in_values=sc[:, 0:NBk], imm_value=-3.0e38))
            dve.op([wk], [m8], lambda e: e.max(out=m8[:, 8:16], in_=wk[:, 0:NBk]))
            dve.op([sc, m8], [Mbq], lambda e: e.tensor_scalar(out=Mbq[:, 0:NBk], in0=sc[:, 0:NBk], scalar1=m8[:, 15:16], scalar2=NEGB, op0=ALU.is_lt, op1=ALU.mult))
            pe.op([Mbq, cbm], [psT], lambda e, qs=qs: e.transpose(out=psT[0:NBk, qs * 128:(qs + 1) * 128], in_=Mbq[:, 0:NBk], identity=ident_b))
        act.op([psT], [MbT], lambda e: e.copy(out=MbT[0:NBk, g, :], in_=psT[0:NBk, 0:512]))
        n0 = min(NBk, 64)
        act.op([psT], [MbS], lambda e: e.copy(out=MbS[64:64 + n0, g, :], in_=psT[0:n0, 0:512]))
        if NBk > 64:
            act.op([psT], [MbS], lambda e: e.copy(out=MbS[0:NBk - 64, g, :], in_=psT[64:NBk, 0:512]))

    def merge_branch(l, n, first):
        def load(dc):
            buf = rot("WB", WB)
            sp.dma(buf[:, 0, :, :], Wb[l][n * 8 + dc], [Wb[l]], [buf], buf)
            sp.dma(buf[:, 1, :, :], Wc[l][CH_MG + n * 8 + dc], [Wc[l]], [buf], buf)
            return buf
        cur = load(0)
        for dc in range(8):
            nb_ = load(dc + 1) if dc < 7 else None
            pbr, pmg = nxt("A"), nxt("A")
            for fc in range(8):
                mm(pbr, pbr[:, :], cur, cur[:, 0, fc, :], zb, zb[:, fc, :], fc == 0, fc == 7)
            for kc in range(8):
                mm(pmg, pmg[:, :], cur, cur[:, 1, kc, :], uT, uT[:, kc, :], kc == 0, kc == 7)
            mt_ = SC[0]
            act.op([pmg], [mt_], lambda e: e.activation(out=mt_[:, 0:512], in_=pmg[:, :], func=AF.Sigmoid))
            if first:
                dve.op([pbr, mt_], [ACC], lambda e, dc=dc: e.tensor_tensor(out=ACC[:, dc, :], in0=pbr[:, :], in1=mt_[:, 0:512], op=ALU.mult))
            else:
                dve.op([pbr, mt_], [mt_], lambda e: e.tensor_tensor(out=mt_[:, 0:512], in0=pbr[:, :], in1=mt_[:, 0:512], op=ALU.mult))
                dve.op([ACC, mt_], [ACC], lambda e, dc=dc: e.tensor_tensor(out=ACC[:, dc, :], in0=ACC[:, dc, :], in1=mt_[:, 0:512], op=ALU.add))
            cur = nb_

    def tile_prompt(l, J, x_src, x_dst):
        stage = getattr(cfg, 'STAGE', 99)
        t0 = J * 512
        last = (J == NT - 1)
        for s in range(4):
            xb = rot("xs", xs)
            sp.dma(xb[:], x_src[t0 + s * 128:t0 + (s + 1) * 128, :], [x_src], [xb], xb)
            dve.op([], [sm], lambda e, s=s: e.memset(sm[:, s:s + 1], 0.0))
            act.op([xb, sm], [ub, sm], lambda e, s=s, xb=xb: e.activation(out=ub[:, :], in_=xb[:, :], func=AF.Square, accum_out=sm[:, s:s + 1]))
            dve.op([sm], [sm], lambda e, s=s: e.tensor_scalar(out=sm[:, 4 + s:5 + s], in0=sm[:, s:s + 1], scalar1=1.0 / D, scalar2=EPS, op0=ALU.mult, op1=ALU.add))
            act.op([sm], [sm], lambda e, s=s: e.sqrt(out=sm[:, 4 + s:5 + s], in_=sm[:, 4 + s:5 + s]))
            dve.op([sm], [sm], lambda e, s=s: e.reciprocal(out=sm[:, 8 + s:9 + s], in_=sm[:, 4 + s:5 + s]))
            dve.op([xb, sm, gbc[0]], [ub], lambda e, s=s, xb=xb: e.scalar_tensor_tensor(out=ub[:, :], in0=xb[:, :], scalar=sm[:, 8 + s:9 + s], in1=gbc[0][:, :], op0=ALU.mult, op1=ALU.mult))
            for c in range(8):
                pe.op([ub, cbm], [psT], lambda e, c=c: e.transpose(out=psT[:, c * 128:(c + 1) * 128], in_=ub[:, c * 128:(c + 1) * 128], identity=ident_b))
            act.op([psT], [uT], lambda e, s=s: e.copy(out=uT[:, :, s * 128:(s + 1) * 128], in_=psT[:, :].rearrange("p (c t) -> p c t", t=128)))

        if stage <= 2.1:
            return
        for blk in range(3):
            sp.dma(WKB[:], Wkv[l][blk], [Wkv[l]], [WKB], WKB)
            for s in range(4):
                ps = nxt("A")
                for kc in range(8):
                    mm(ps, ps[:, :], uT, uT[:, kc, s * 128:(s + 1) * 128], WKB, WKB[:, kc, :], kc == 0, kc == 7)
                st = rot("kvst", kvst)
                act.op([ps], [st], lambda e, ps=ps, st=st: e.copy(out=st[:, :], in_=ps[:, :]))
                r0 = t0 + s * 128
                kt = r0 // 128
                if blk < 2:
                    sp.dma(kv_out[2 * blk][l, r0:r0 + 128, :], st[:, 0:256], [st], [kv_out[2 * blk]], st)
                    sp.dma(kv_out[2 * blk + 1][l, r0:r0 + 128, :], st[:, 256:512], [st], [kv_out[2 * blk + 1]], st)
                elif r0 >= S - 512:
                    w0 = r0 - (S - 512)
                    sp.dma(win_out[0][l, w0:w0 + 128, :], st[:, 0:256], [st], [win_out[0]], st)
                    sp.dma(win_out[1][l, w0:w0 + 128, :], st[:, 256:512], [st], [win_out[1]], st)
                if blk == 1:
                    for cc in range(2):
                        dve.op([st], [VS[cc]], lambda e, cc=cc, st=st, kt=kt: e.tensor_copy(
                            out=VS[cc][:, kt % NSL[cc], :, 0:64], in_=st[:, 256 + cc * 128:256 + (cc + 1) * 128].rearrange("p (g d) -> p g d", d=64)))
                if blk == 2:
                    dve.op([st], [VW], lambda e, st=st, kt=kt: e.tensor_copy(out=VW[:, kt % 8, :, 0:64], in_=st[:, 256:512].rearrange("p (g d) -> p g d", d=64)))

        if stage <= 2.2:
            return
        def c_q(i, ch, ps):
            act.op([ps], [QT], lambda e: e.mul(out=QT[:, i, :], in_=ps[:, :], mul=0.125))
        proj(l, [CH_Q + c for c in range(8)], c_q)

        def c_kv(i, ch, ps):
            j = ch - CH_KVF
            cc = j % 2
            if j < 2:
                compress(l, J, 0, cc, ps)
            elif j < 4:
                compress(l, J, 1, cc, ps)
            elif j < 6:
                sl = (4 * J) % NSL[cc]
                act.op([ps], [KT[cc]], lambda e: e.copy(out=KT[cc][:, sl * 128:sl * 128 + 512], in_=ps[:, :]))
            else:
                sl = (4 * J) % 8
                act.op([ps], [KWT], lambda e: e.copy(out=KWT[:, cc, sl * 128:sl * 128 + 512], in_=ps[:, :]))
        proj(l, [CH_KVF + j for j in range(8)], c_kv)

        def c_bg(i, ch, ps):
            act.op([ps], [gates], lambda e: e.activation(out=gates[:, :], in_=ps[0:48, :], func=AF.Sigmoid))
        proj(l, [CH_BG], c_bg, ncols=48)

        if stage <= 2.3:
            return
        pool.op([], [ACC], lambda e: e.memset(ACC[:], 0.0))
        for g in range(4):
            heads = [(h, HPERM.index(h)) for h in range(4 * g, 4 * g + 4)]
            psacc = psA[g % 2]
            written = set()
            mm(psacc, psacc[:, :], zer_bf, zer_bf[:, :], cb, cb[:, 0:512], True, False)
            for hi, (h, pos) in enumerate(heads):
                W = 128 if h <= 3 else 512
                for q0 in range(t0, t0 + 512, W):
                    attend_cmp(h, pos, q0, W, J, psacc, written, hi == 3 and q0 + W == t0 + 512)
            if stage <= 2.32:
                continue
            topk_group(g, J, psacc)
            if stage <= 2.33:
                continue
            for (h, pos) in heads:
                W = 128 if h <= 3 else 512
                if 'h' in DBG and pos % 2 == 1:
                    continue
                for q0 in range(t0, t0 + 512, W):
                    attend(h, pos, q0, W, "s", J)
                    if stage > 2.34:
                        attend(h, pos, q0, W, "w", J)

        if stage <= 2.4:
            return
        def c_nsag(i, ch, ps):
            sg = SC[0]
            act.op([ps], [sg], lambda e: e.activation(out=sg[:, 0:512], in_=ps[:, :], func=AF.Silu))
            dve.op([ACC, sg], [zb], lambda e: e.tensor_tensor(out=zb[:, i, :], in0=ACC[:, i, :], in1=sg[:, 0:512], op=ALU.mult))
        proj(l, [CH_NSAG + c for c in range(8)], c_nsag)
        merge_branch(l, 2, True)

        if stage <= 2.5:
            return
        def c_lru(i, ch, ps):
            c = i // 2
            if i % 2 == 0:
                xp, xc, r_, i_, a_, hs = SC[0], SC[1], SC[2], SC[3], SC[4], SC[5]
                act.op([lruc], [xp], lambda e: e.copy(out=xp[:, 0:3], in_=lruc[:, c, 0:3]))
                act.op([ps], [xp], lambda e: e.copy(out=xp[:, 3:515], in_=ps[:, :]))
                act.op([xp], [lruc], lambda e: e.copy(out=lruc[:, c, 0:3], in_=xp[:, 512:515]))
                cv = l * 9
                dve.op([xp, CV], [xc], lambda e: e.tensor_scalar(out=xc[:, 0:512], in0=xp[:, 0:512], scalar1=CV[:, c, cv:cv + 1], scalar2=CV[:, c, cv + 4:cv + 5], op0=ALU.mult, op1=ALU.add))
                for k in range(1, 4):
                    dve.op([xp, CV, xc], [xc], lambda e, k=k: e.scalar_tensor_tensor(out=xc[:, 0:512], in0=xp[:, k:k + 512], scalar=CV[:, c, cv + k:cv + k + 1], in1=xc[:, 0:512], op0=ALU.mult, op1=ALU.add))
                xcb = rot("PT", PT)
                act.op([xc], [xcb], lambda e: e.copy(out=xcb[:, :], in_=xc[:, 0:512]))
                pr, pi_ = nxt("S"), nxt("S")
                mm(pr, pr[:, :], WRG, WRG[:, 0, c, :], xcb, xcb[:, :], True, True)
                mm(pi_, pi_[:, :], WRG, WRG[:, 1, c, :], xcb, xcb[:, :], True, True)
                act.op([pr, CV], [r_], lambda e: e.activation(out=r_[:, 0:512], in_=pr[:, :], func=AF.Sigmoid, bias=CV[:, c, cv + 5:cv + 6], scale=1.0))
                act.op([pi_, CV], [i_], lambda e: e.activation(out=i_[:, 0:512], in_=pi_[:, :], func=AF.Sigmoid, bias=CV[:, c, cv + 6:cv + 7], scale=1.0))
                act.op([r_, CV], [a_], lambda e: e.activation(out=a_[:, 0:512], in_=r_[:, 0:512], func=AF.Exp, scale=CV[:, c, 18 + l:19 + l]))
                dve.op([a_], [r_], lambda e: e.scalar_tensor_tensor(out=r_[:, 0:512], in0=a_[:, 0:512], scalar=-1.0, in1=a_[:, 0:512], op0=ALU.mult, op1=ALU.mult))
                act.op([r_], [r_], lambda e: e.activation(out=r_[:, 0:512], in_=r_[:, 0:512], func=AF.Sqrt, bias=1.0, scale=1.0))
                dve.op([i_, xc], [i_], lambda e: e.tensor_tensor(out=i_[:, 0:512], in0=i_[:, 0:512], in1=xc[:, 0:512], op=ALU.mult))
                dve.op([i_, r_], [i_], lambda e: e.tensor_tensor(out=i_[:, 0:512], in0=i_[:, 0:512], in1=r_[:, 0:512], op=ALU.mult))
                dve.op([a_, i_, lruc], [hs], lambda e: e.tensor_tensor_scan(out=hs[:, 0:512], data0=a_[:, 0:512], data1=i_[:, 0:512], initial=lruc[:, c, 3:4], op0=ALU.mult, op1=ALU.add))
                act.op([hs], [lruc], lambda e: e.copy(out=lruc[:, c, 3:4], in_=hs[:, 511:512]))
            else:
                sg, hs = SC[1], SC[5]
                act.op([ps], [sg], lambda e: e.activation(out=sg[:, 0:512], in_=ps[:, :], func=AF.Silu))
                dve.op([hs, sg], [zb], lambda e: e.tensor_tensor(out=zb[:, c, :], in0=hs[:, 0:512], in1=sg[:, 0:512], op=ALU.mult))
        chl = []
        for c in range(8):
            chl += [CH_LRUX + c, CH_LRUG + c]
        proj(l, chl, c_lru)
        merge_branch(l, 0, False)

        if stage <= 2.6:
            return
        PD = [PT[0], PT[1]]

        def c_pool(i, ch, ps):
            gq, k = i // 4, i % 4
            w = POOL_WIN[gq]
            if k < 2:
                c = 2 * gq + k
                X = SC[k]
                act.op([poolc], [X], lambda e: e.copy(out=X[:, 0:15], in_=poolc[:, c, :]))
                act.op([ps], [X], lambda e: e.copy(out=X[:, 15:527], in_=ps[:, :]))
                act.op([X], [poolc], lambda e: e.copy(out=poolc[:, c, :], in_=X[:, 512:527]))
                cur = X
                sh = 1
                tmps = [SC[4], SC[5]]
                ti = 0
                while sh < w:
                    nx = tmps[ti % 2]
                    ti += 1
                    dve.op([cur], [nx], lambda e, cur=cur, nx=nx, sh=sh: e.tensor_tensor(out=nx[:, sh:527], in0=cur[:, sh:527], in1=cur[:, 0:527 - sh], op=ALU.add))
                    cur = nx
                    sh *= 2
                if J == 0:
                    dve.op([cur, cf], [cur], lambda e, cur=cur: e.tensor_tensor(out=cur[:, 15:30], in0=cur[:, 15:30], in1=cf[:, o_ratio + gq * 15:o_ratio + gq * 15 + 15], op=ALU.mult))
                dve.op([cur, X], [PD[k]], lambda e, cur=cur: e.scalar_tensor_tensor(out=PD[k][:, :], in0=cur[:, 15:527], scalar=1.0 / w, in1=X[:, 15:527], op0=ALU.mult, op1=ALU.subtract))
            else:
                eh = k - 2
                c = 2 * gq + eh
                pm = nxt("S")
                for dh in range(2):
                    mm(pm, pm[:, :], WPL, WPL[:, gq, dh, eh * 128:(eh + 1) * 128], PD[dh], PD[dh][:, :], dh == 0, dh == 1)
                sg = SC[2]
                act.op([ps], [sg], lambda e: e.activation(out=sg[:, 0:512], in_=ps[:, :], func=AF.Silu))
                dve.op([pm, CV, sg], [zb], lambda e: e.scalar_tensor_tensor(out=zb[:, c, :], in0=pm[:, :], scalar=CV[:, c, l * 9 + 8:l * 9 + 9], in1=sg[:, 0:512], op0=ALU.mult, op1=ALU.mult))
        chl = []
        for gq in range(4):
            chl += [CH_POOLX + 2 * gq, CH_POOLX + 2 * gq + 1, CH_POOLG + 2 * gq, CH_POOLG + 2 * gq + 1]
        proj(l, chl, c_pool)
        merge_branch(l, 1, False)
        for dc in range(8):
            act.op([ACC], [merged], lambda e, dc=dc: e.copy(out=merged[:, dc, :], in_=ACC[:, dc, :]))

        if stage <= 2.7:
            return
        sp.dma(WKB[:], Wo[l][:, :, 0:512], [Wo[l]], [WKB], WKB)
        sp.dma(zb[:], Wo[l][:, :, 512:1024], [Wo[l]], [zb], zb)
        wo = [WKB, zb]
        for s in range(4):
            xr, yo = xs[0], xs[1]
            sp.dma(xr[:], x_src[t0 + s * 128:t0 + (s + 1) * 128, :], [x_src], [xr], xr)
            po = [nxt("A"), nxt("A")]
            dve.op([], [sm], lambda e: e.memset(sm[:, 16:18], 0.0))
            for eh in range(2):
                for dc in range(8):
                    mm(po[eh], po[eh][:, :], merged, merged[:, dc, s * 128:(s + 1) * 128], wo[eh], wo[eh][:, dc, :], dc == 0, dc == 7)
                act.op([po[eh], sm], [ub, sm], lambda e, eh=eh: e.activation(out=ub[:, eh * 512:(eh + 1) * 512], in_=po[eh][:, :], func=AF.Square, accum_out=sm[:, 16 + eh:17 + eh]))
            dve.op([sm], [sm], lambda e: e.tensor_tensor(out=sm[:, 18:19], in0=sm[:, 16:17], in1=sm[:, 17:18], op=ALU.add))
            dve.op([sm], [sm], lambda e: e.tensor_scalar(out=sm[:, 19:20], in0=sm[:, 18:19], scalar1=1.0 / D, scalar2=EPS, op0=ALU.mult, op1=ALU.add))
            act.op([sm], [sm], lambda e: e.sqrt(out=sm[:, 19:20], in_=sm[:, 19:20]))
            dve.op([sm], [sm], lambda e: e.reciprocal(out=sm[:, 20:21], in_=sm[:, 19:20]))
            for eh in range(2):
                dve.op([po[eh], sm, gbc[1]], [yo], lambda e, eh=eh: e.scalar_tensor_tensor(out=yo[:, eh * 512:(eh + 1) * 512], in0=po[eh][:, :], scalar=sm[:, 20:21], in1=gbc[1][:, eh * 512:(eh + 1) * 512], op0=ALU.mult, op1=ALU.mult))
            dve.op([yo, xr], [yo], lambda e: e.tensor_tensor(out=yo[:, :], in0=yo[:, :], in1=xr[:, :], op=ALU.add))
            sp.dma(x_dst[t0 + s * 128:t0 + (s + 1) * 128, :], yo[:], [yo], [x_dst], yo)

    def layer_prompt(l, x_src, x_dst):
        load_layer_small(l)
        if getattr(cfg, "STAGE", 99) == 1:
            return
        dve.op([], [lruc], lambda e: e.memset(lruc[:], 0.0))
        dve.op([], [poolc], lambda e: e.memset(poolc[:], 0.0))
        dve.op([], [cmpc], lambda e: e.memset(cmpc[:], 0.0))
        for J in range(NT):
            tile_prompt(l, J, x_src, x_dst)
        stt = SC[0]
        for c in range(8):
            act.op([lruc, poolc], [stt], lambda e, c=c: e.copy(out=stt[:, c * 19:c * 19 + 4], in_=lruc[:, c, :]))
            act.op([poolc], [stt], lambda e, c=c: e.copy(out=stt[:, c * 19 + 4:c * 19 + 19], in_=poolc[:, c, :]))
        psx = [psA[0], psA[1]]
        for c in range(8):
            pb = psx[c // 4]
            pe.op([stt, cf], [pb], lambda e, c=c, pb=pb: e.transpose(out=pb[0:19, (c % 4) * 128:(c % 4 + 1) * 128], in_=stt[:, c * 19:c * 19 + 19], identity=ident_f(128)))
        so = xs[0]
        for hh in range(2):
            act.op([psx[hh]], [so], lambda e, hh=hh: e.copy(out=so[0:19, hh * 512:(hh + 1) * 512], in_=psx[hh][0:19, :]))
        sp.dma(st_out[l, :, :], so[0:19, :], [so], [st_out], so)

    SST = getattr(cfg, 'SST', 99)

    def sample_phase():
        NCS = 1 + NMS * 16 + NPG * 16 + 64 + 8 + 128 + NMS * 128
        csb = K.sb("csb", [128, NCS - NMS * 128], F32)
        sp.dma(csb[:], c_s[:, 0:NCS - NMS * 128], [c_s], [csb], csb)
        oo = [0]

        def tk(n):
            a = oo[0]
            oo[0] += n
            return a
        o_p, o_abs, o_abk, o_abw, o_ind, o_fb = tk(1), tk(NMS * 16), tk(NPG * 16), tk(64), tk(8), tk(128)
        gsb = K.sb("gsb", [128, NMS * 128], BF16)
        pool.dma(gsb[:], c_s[:, NCS - NMS * 128:NCS], [c_s], [gsb], gsb)
        indb = K.sb("indb", [128, 8], BF16)
        pool.dma(indb[:], c_s[:, o_ind:o_ind + 8], [c_s], [indb], indb)
        NSB = 32
        xsb = K.sb("xsb", [NS, D], F32)
        ysb = K.sb("ysb", [NS, D], F32)
        ubs = K.sb("ubs", [NS, D], BF16)
        uTs = K.sb("uTs", [128, 8, NS], BF16)
        kvtok = K.sb("kvtok", [NS, 1536], F32)
        qtok = K.sb("qtok", [NS, D], F32)
        FM = {n_: K.sb("fm_" + n_, [128, 8, NS], F32) for n_ in ("xl", "lg", "px", "pg", "ng", "h", "t0", "t1", "t2", "t3")}
        mgs = K.sb("mgs", [128, 24, NS], F32)
        CS = K.sb("CS", [128, 8, NS * 3], F32)
        HS = K.sb("HS", [128, 8, NS], F32)
        PSs = K.sb("PSs", [128, 8, NS * 15], F32)
        stl = K.sb("stl", [128, D], F32)
        zbs = [K.sb("zbs%d" % n_, [128, 8, NS], BF16) for n_ in range(3)]
        xcbs = K.sb("xcbs", [128, 8, NS], BF16)
        pdb = K.sb("pdb", [128, 8, NS], BF16)
        gat_f = K.sb("gat_f", [48, NS], F32)
        gat_t = K.sb("gat_t", [NS, 48], F32)
        WBs = [K.sb("WBs%d" % i, [128, 2, 8, 128], BF16) for i in range(2)]
        WKs = K.sb("WKs", [128, 8, 512], BF16)
        WK2 = K.sb("WK2", [128, 8, 512], BF16)
        sms = K.sb("sms", [128, 32], F32)
        mergs = K.sb("mergs", [128, 8, NS], BF16)
        maccs = K.sb("maccs", [128, 8, NS], F32)
        ptb = K.sb("ptb", [128, NPG], I32)
        ptf = K.sb("ptf", [128, NPG], F32)
        IDX = K.sb("IDX", [128, NPG], I32)
        KP = [K.sb("KP%d" % i, [128, 256], F32) for i in range(4)]
        VP = [K.sb("VP%d" % i, [128, 256], F32) for i in range(4)]
        KPw = [K.sb("KPw%d" % i, [128, 256], F32) for i in range(2)]
        VPw = [K.sb("VPw%d" % i, [128, 256], F32) for i in range(2)]
        WT = K.sb("WT", [128, 2, 2, 256], F32)
        p12 = [K.sb("p12_%d" % i, [128, 2, 256], BF16) for i in range(3)]
        fsb = K.sb("fsb", [128, 4, NMS * 128], F32)
        blkT = K.sb("blkT", [128, 2, NMS * 128], BF16)
        KCs = K.sb("KCs", [128, 2, NMS * 128], BF16)
        VCs = K.sb("VCs", [128, NMS, 4, 65], BF16)
        QTs = K.sb("QTs", [128, 8, NS], BF16)
        S16 = K.sb("S16", [128, NMS, 16], F32)
        P16 = K.sb("P16", [128, NMS, 16], BF16)
        rd16 = K.sb("rd16", [128, 16], F32)
        Pn4 = K.sb("Pn4", [128, NMS, 4], BF16)
        Pn4f = K.sb("Pn4f", [128, NMS, 4], F32)
        sc4 = K.sb("sc4", [4, 128], F32)
        wk4 = K.sb("wk4", [4, 128], F32)
        m84 = K.sb("m84", [4, 16], F32)
        mb4 = K.sb("mb4", [4, 128], F32)
        MB4T = K.sb("MB4T", [128, 4], BF16)
        QBs = K.sb("QBs", [128, D], F32)
        prod = K.sb("prod", [128, D], F32)
        s16 = K.sb("s16", [128, 16], F32)
        mkb = K.sb("mkb", [128, 4], F32)
        Pk = [K.sb("Pk%d" % i, [128, 16], BF16) for i in range(3)]
        Vst = [K.sb("Vst%d" % i, [128, 4, 65], BF16) for i in range(3)]
        O16sb = K.sb("O16sb", [16, 3, 260], F32)
        OS = K.sb("OS", [NS, 3, 16, 65], F32)
        onat = K.sb("onat", [NS, 16, 64], F32)
        operm = K.sb("operm", [NS, 16, 64], BF16)
        tt = [K.sb("tt%d" % i, [NS, 16, 64], F32) for i in range(2)]
        pnew = K.sb("pnew", [NS, 2, 16], F32)
        cden = K.sb("cden", [NS, 3, 16], F32)
        for b_ in [VCs] + Vst:
            pool.op([], [b_], lambda e, b_=b_: e.memset(b_[:], 0.0))
        pool.op([], [VCs], lambda e: e.memset(VCs[:, :, :, 64:65], 1.0))
        for i in range(len(Vst)):
            pool.op([], [Vst[i]], lambda e, i=i: e.memset(Vst[i][:, :, 64:65], 1.0))
        pool.op([], [blkT], lambda e: e.memset(blkT[:], 0.0))
        pool.op([], [IDX], lambda e: e.memset(IDX[:], 0))
        rs = {"WBs": 0, "KP": 0, "VP": 0, "KPw": 0, "VPw": 0, "p12": 0, "Pk": 0, "Vst": 0}

        def rt(name, lst):
            rs[name] += 1
            return lst[rs[name] % len(lst)]

        def projS(l, chunks, dst, ncols=128, Wsrc=None, rhs_b=None):
            Wsrc = Wsrc if Wsrc is not None else Wc[l]
            rhs_b = rhs_b if rhs_b is not None else uTs
            groups = []
            i = 0
            while i < len(chunks):
                n = 2 if (i + 1 < len(chunks) and chunks[i + 1] == chunks[i] + 1) else 1
                groups.append((i, chunks[i], n))
                i += n

            def load(g):
                buf = rt("WBs", WBs)
                sp.dma(buf[:, 0:g[2], :, :], Wsrc[g[1]:g[1] + g[2]].rearrange("c p k n -> p c k n"), [Wsrc], [buf], buf)
                return buf
            ps = nxt("A")
            cur = load(groups[0])
            for gi, g in enumerate(groups):
                nb_ = load(groups[gi + 1]) if gi + 1 < len(groups) else None
                for j in range(g[2]):
                    ii = g[0] + j
                    for kc in range(8):
                        mm(ps, ps[0:ncols, ii * NS:(ii + 1) * NS], cur, cur[:, j, kc, 0:ncols], rhs_b, rhs_b[:, kc, :], kc == 0, kc == 7)
                cur = nb_
            n = len(chunks)
            act.op([ps], [dst], lambda e: e.copy(out=dst[0:ncols, 0:n, :], in_=ps[0:ncols, 0:n * NS].rearrange("p (c t) -> p c t", t=NS)))

        def bc3(ap2, n):
            return ap2.to_broadcast([128, 8, n])

        def tokmajor(src_fm, dst_rows):
            pb = nxt("A")
            for c in range(8):
                pe.op([src_fm, cf], [pb], lambda e, c=c: e.transpose(out=pb[0:NS, c * 128:(c + 1) * 128], in_=src_fm[:, c, :], identity=ident_f(128)))
            pb2 = nxt("A")
            for c in range(4, 8):
                pass
            act.op([pb], [stl], lambda e: e.copy(out=stl[0:NS, 0:512], in_=pb[0:NS, 0:512]))
            return pb

        def layer_sample(l, xsrc, xdst):
            load_layer_small(l)
            cv = l * 9
            sp.dma(xsb[:], xsrc[:, :], [xsrc], [xsb], xsb)
            dve.op([], [sms], lambda e: e.memset(sms[:, 0:1], 0.0))
            act.op([xsb, sms], [ubs, sms], lambda e: e.activation(out=ubs[:, :], in_=xsb[:, :], func=AF.Square, accum_out=sms[0:NS, 0:1]))
            dve.op([sms], [sms], lambda e: e.tensor_scalar(out=sms[0:NS, 1:2], in0=sms[0:NS, 0:1], scalar1=1.0 / D, scalar2=EPS, op0=ALU.mult, op1=ALU.add))
            act.op([sms], [sms], lambda e: e.sqrt(out=sms[0:NS, 1:2], in_=sms[0:NS, 1:2]))
            dve.op([sms], [sms], lambda e: e.reciprocal(out=sms[0:NS, 2:3], in_=sms[0:NS, 1:2]))
            dve.op([xsb, sms, gbc[0]], [ubs], lambda e: e.scalar_tensor_tensor(out=ubs[:, :], in0=xsb[:, :], scalar=sms[0:NS, 2:3], in1=gbc[0][0:NS, :], op0=ALU.mult, op1=ALU.mult))
            for c in range(8):
                pe.op([ubs, cbm], [psT], lambda e, c=c: e.transpose(out=psT[:, c * NS:(c + 1) * NS], in_=ubs[0:NS, c * 128:(c + 1) * 128], identity=cbm[0:NS, 0:NS]))
            act.op([psT], [uTs], lambda e: e.copy(out=uTs[:, :, :], in_=psT[:, 0:8 * NS].rearrange("p (c t) -> p c t", t=NS)))
            for blk in range(3):
                sp.dma(WKs[:], Wkv[l][blk], [Wkv[l]], [WKs], WKs)
                ps = nxt("A")
                for kc in range(8):
                    mm(ps, ps[0:NS, :], uTs, uTs[:, kc, :], WKs, WKs[:, kc, :], kc == 0, kc == 7)
                act.op([ps], [kvtok], lambda e, ps=ps, blk=blk: e.copy(out=kvtok[:, blk * 512:(blk + 1) * 512], in_=ps[0:NS, :]))
            sp.dma(skv_out[l, :, :], kvtok[:, :], [kvtok], [skv_out], kvtok)
            for blk in range(2):
                sp.dma(WKs[:], Wqt[l][blk], [Wqt[l]], [WKs], WKs)
                ps = nxt("A")
                for kc in range(8):
                    mm(ps, ps[0:NS, :], uTs, uTs[:, kc, :], WKs, WKs[:, kc, :], kc == 0, kc == 7)
                act.op([ps], [qtok], lambda e, ps=ps, blk=blk: e.mul(out=qtok[:, blk * 512:(blk + 1) * 512], in_=ps[0:NS, :], mul=0.125))
            projS(l, [CH_Q + c for c in range(8)], FM["t0"])
            act.op([FM["t0"]], [QTs], lambda e: e.mul(out=QTs[:, :, :], in_=FM["t0"][:, :, :], mul=0.125))
            projS(l, [CH_LRUX + c for c in range(8)], FM["xl"])
            projS(l, [CH_LRUG + c for c in range(8)], FM["lg"])
            projS(l, [CH_POOLX + c for c in range(8)], FM["px"])
            projS(l, [CH_POOLG + c for c in range(8)], FM["pg"])
            projS(l, [CH_NSAG + c for c in range(8)], FM["ng"])
            projS(l, [CH_BG], gat_f, ncols=48) if False else None
            psg = nxt("A")
            bw = rt("WBs", WBs)
            sp.dma(bw[:, 0:1, :, :], Wc[l][CH_BG:CH_BG + 1].rearrange("c p k n -> p c k n"), [Wc[l]], [bw], bw)
            for kc in range(8):
                mm(psg, psg[0:48, 0:NS], bw, bw[:, 0, kc, 0:48], uTs, uTs[:, kc, :], kc == 0, kc == 7)
            act.op([psg], [gat_f], lambda e: e.activation(out=gat_f[:, :], in_=psg[0:48, 0:NS], func=AF.Sigmoid))
            pe.op([gat_f, cf], [psM], lambda e: e.transpose(out=psM[0:NS, 0:48], in_=gat_f[0:48, 0:NS], identity=ident_f(48)))
            act.op([psM], [gat_t], lambda e: e.copy(out=gat_t[:, :], in_=psM[0:NS, 0:48]))
            for n_ in range(3):
                projS(l, [CH_MG + n_ * 8 + dc for dc in range(8)], FM["t0"])
                act.op([FM["t0"]], [mgs], lambda e, n_=n_: e.activation(out=mgs[:, n_ * 8:(n_ + 1) * 8, :], in_=FM["t0"][:, :, :], func=AF.Sigmoid))

            if SST <= 1:
                return
            def load_fm(src_ap, rows, dst, width):
                sp.dma(stl[0:rows, :], src_ap, [st_conv, st_lru, st_pool], [stl], stl)
                for c in range(8):
                    pe.op([stl, cf], [psM], lambda e, c=c: e.transpose(out=psM[:, 0:rows], in_=stl[0:rows, c * 128:(c + 1) * 128], identity=ident_f(rows)))
                    act.op([psM], [dst], lambda e, c=c: e.copy(out=dst[:, c, width[0]:width[0] + rows], in_=psM[:, 0:rows]))
            load_fm(st_conv[l, :, :], NS * 3, CS, (0,))
            load_fm(st_lru[l, :, :], NS, HS, (0,))
            half_ = (NS * 15 + 1) // 2 if NS * 15 > 128 else NS * 15
            r0_ = 0
            while r0_ < NS * 15:
                rws = min(half_, NS * 15 - r0_)
                load_fm(st_pool[l, r0_:r0_ + rws, :], rws, PSs, (r0_,))
                r0_ += rws

            xl, t0_, t1_, t2_, t3_, hN = FM["xl"], FM["t0"], FM["t1"], FM["t2"], FM["t3"], FM["h"]
            CS4 = CS[:, :, :].rearrange("p c (b k) -> p c b k", k=3)
            dve.op([xl, CV], [t0_], lambda e: e.tensor_tensor(out=t0_[:, :, :], in0=xl[:, :, :], in1=bc3(CV[:, :, cv + 3:cv + 4], NS), op=ALU.mult))
            for k in range(3):
                dve.op([CS, CV], [t1_], lambda e, k=k: e.tensor_tensor(out=t1_[:, :, :], in0=CS4[:, :, :, k], in1=bc3(CV[:, :, cv + k:cv + k + 1], NS), op=ALU.mult))
                dve.op([t0_, t1_], [t0_], lambda e: e.tensor_tensor(out=t0_[:, :, :], in0=t0_[:, :, :], in1=t1_[:, :, :], op=ALU.add))
            dve.op([t0_, CV], [t0_], lambda e: e.tensor_tensor(out=t0_[:, :, :], in0=t0_[:, :, :], in1=bc3(CV[:, :, cv + 4:cv + 5], NS), op=ALU.add))
            act.op([t0_], [xcbs], lambda e: e.copy(out=xcbs[:, :, :], in_=t0_[:, :, :]))
            pr = nxt("S")
            for ax in range(2):
                for c in range(8):
                    mm(pr, pr[:, (ax * 8 + c) * NS:(ax * 8 + c + 1) * NS], WRG, WRG[:, ax, c, :], xcbs, xcbs[:, c, :], True, True)
            prv = pr[:, 0:16 * NS].rearrange("p (a c t) -> p a c t", a=2, t=NS)
            dve.op([pr, CV], [t1_], lambda e: e.tensor_tensor(out=t1_[:, :, :], in0=prv[:, 0, :, :], in1=bc3(CV[:, :, cv + 5:cv + 6], NS), op=ALU.add))
            dve.op([pr, CV], [t2_], lambda e: e.tensor_tensor(out=t2_[:, :, :], in0=prv[:, 1, :, :], in1=bc3(CV[:, :, cv + 6:cv + 7], NS), op=ALU.add))
            act.op([t1_], [t1_], lambda e: e.activation(out=t1_[:, :, :], in_=t1_[:, :, :], func=AF.Sigmoid))
            act.op([t2_], [t2_], lambda e: e.activation(out=t2_[:, :, :], in_=t2_[:, :, :], func=AF.Sigmoid))
            dve.op([t1_, CV], [t1_], lambda e: e.tensor_tensor(out=t1_[:, :, :], in0=t1_[:, :, :], in1=bc3(CV[:, :, 18 + l:19 + l], NS), op=ALU.mult))
            act.op([t1_], [t1_], lambda e: e.activation(out=t1_[:, :, :], in_=t1_[:, :, :], func=AF.Exp))
            dve.op([t1_], [t3_], lambda e: e.scalar_tensor_tensor(out=t3_[:, :, :], in0=t1_[:, :, :], scalar=-1.0, in1=t1_[:, :, :], op0=ALU.mult, op1=ALU.mult))
            act.op([t3_], [t3_], lambda e: e.activation(out=t3_[:, :, :], in_=t3_[:, :, :], func=AF.Sqrt, bias=1.0, scale=1.0))
            dve.op([t2_, t0_], [t2_], lambda e: e.tensor_tensor(out=t2_[:, :, :], in0=t2_[:, :, :], in1=t0_[:, :, :], op=ALU.mult))
            dve.op([t2_, t3_], [t2_], lambda e: e.tensor_tensor(out=t2_[:, :, :], in0=t2_[:, :, :], in1=t3_[:, :, :], op=ALU.mult))
            dve.op([t1_, HS], [t1_], lambda e: e.tensor_tensor(out=t1_[:, :, :], in0=t1_[:, :, :], in1=HS[:, :, :], op=ALU.mult))
            dve.op([t1_, t2_], [hN], lambda e: e.tensor_tensor(out=hN[:, :, :], in0=t1_[:, :, :], in1=t2_[:, :, :], op=ALU.add))
            act.op([FM["lg"]], [t3_], lambda e: e.activation(out=t3_[:, :, :], in_=FM["lg"][:, :, :], func=AF.Silu))
            dve.op([hN, t3_], [zbs[0]], lambda e: e.tensor_tensor(out=zbs[0][:, :, :], in0=hN[:, :, :], in1=t3_[:, :, :], op=ALU.mult))

            px = FM["px"]
            PS4 = PSs[:, :, :].rearrange("p c (b k) -> p c b k", k=15)
            for gq, w in enumerate(POOL_WIN):
                cs_ = slice(2 * gq, 2 * gq + 2)
                dve.op([PSs], [t0_], lambda e, cs_=cs_, w=w: e.tensor_reduce(out=t0_[:, cs_, :], in_=PS4[:, cs_, :, 15 - (w - 1):15], axis=AX.X, op=ALU.add))
                dve.op([t0_, px], [t0_], lambda e, cs_=cs_: e.tensor_tensor(out=t0_[:, cs_, :], in0=t0_[:, cs_, :], in1=px[:, cs_, :], op=ALU.add))
                dve.op([t0_, px], [pdb], lambda e, cs_=cs_, w=w: e.scalar_tensor_tensor(out=pdb[:, cs_, :], in0=t0_[:, cs_, :], scalar=1.0 / w, in1=px[:, cs_, :], op0=ALU.mult, op1=ALU.subtract))
            pm = nxt("S")
            for c in range(8):
                gq, eh = c // 2, c % 2
                for dh in range(2):
                    mm(pm, pm[:, c * NS:(c + 1) * NS], WPL, WPL[:, gq, dh, eh * 128:(eh + 1) * 128], pdb, pdb[:, 2 * gq + dh, :], dh == 0, dh == 1)
            act.op([FM["pg"]], [t3_], lambda e: e.activation(out=t3_[:, :, :], in_=FM["pg"][:, :, :], func=AF.Silu))
            dve.op([pm, CV], [t0_], lambda e: e.tensor_tensor(out=t0_[:, :, :], in0=pm[:, 0:8 * NS].rearrange("p (c t) -> p c t", t=NS), in1=bc3(CV[:, :, cv + 8:cv + 9], NS), op=ALU.mult))
            dve.op([t0_, t3_], [zbs[1]], lambda e: e.tensor_tensor(out=zbs[1][:, :, :], in0=t0_[:, :, :], in1=t3_[:, :, :], op=ALU.mult))

            def rows_out(src_fm, dst_ap, dst_buf):
                for hh in range(2):
                    pb = nxt("A")
                    for c4 in range(4):
                        c = hh * 4 + c4
                        pe.op([src_fm, cf], [pb], lambda e, c=c, c4=c4, pb=pb: e.transpose(out=pb[0:NS, c4 * 128:(c4 + 1) * 128], in_=src_fm[:, c, :], identity=ident_f(128)))
                    act.op([pb], [stl], lambda e, pb=pb, hh=hh: e.copy(out=stl[0:NS, hh * 512:(hh + 1) * 512], in_=pb[0:NS, :]))
                sp.dma(dst_ap, stl[0:NS, :], [stl], [dst_buf], stl)
            sconv3 = sconv_out[l, :, :].rearrange("(b k) d -> b k d", k=3)
            sp.dma(sconv3[:, 0:2, :], st_conv[l, :, :].rearrange("(b k) d -> b k d", k=3)[:, 1:3, :], [st_conv], [sconv_out], stl)
            rows_out(xl, sconv3[:, 2, :], sconv_out)
            rows_out(hN, slru_out[l, :, :], slru_out)
            spool3 = spool_out[l, :, :].rearrange("(b k) d -> b k d", k=15)
            sp.dma(spool3[:, 0:14, :], st_pool[l, :, :].rearrange("(b k) d -> b k d", k=15)[:, 1:15, :], [st_pool], [spool_out], stl)
            rows_out(px, spool3[:, 14, :], spool_out)
            for i in range(2):
                sp.dma(swin_out[i][l, :, 0:WBUF - 1, :], winc[i][l, :, 1:WBUF, :], [winc[i]], [swin_out[i]], stl)
                sp.dma(swin_out[i][l, :, WBUF - 1, :], kvtok[:, 1024 + i * 256:1280 + i * 256], [kvtok], [swin_out[i]], kvtok)

            if SST <= 2:
                return
            for kvi in range(2):
                i_ = l * 2 + kvi
                for fsx in range(2):
                    src = cmp_pos[l, kvi, fsx * 16:(fsx + 1) * 16, :]
                    for ii in range(8):
                        for g in range(4):
                            sp.dma(WT[ii * 16:(ii + 1) * 16, kvi, fsx, g * 64:(g + 1) * 64], src, [cmp_pos], [WT], WT)
            if SST <= 3:
                return
            for b in range(NS):
                sample_attn(l, b)
            if SST <= 10:
                return
            sample_combine(l)
            merge_out(l, xsrc, xdst)

        def sample_attn(l, b):
            sp.dma(ptb[:], ptab[b:b + 1, :].partition_broadcast(128), [ptab], [ptb], ptb)
            dve.op([ptb], [ptf], lambda e: e.tensor_copy(out=ptf[:], in_=ptb[:]))
            dve.op([csb], [sms], lambda e: e.tensor_scalar(out=sms[:, 8:9], in0=csb[:, o_p:o_p + 1], scalar1=float(l * NPHYS * 128), scalar2=None, op0=ALU.add))
            dve.op([ptf, sms], [ptf], lambda e: e.tensor_scalar(out=ptf[:], in0=ptf[:], scalar1=128.0, scalar2=sms[:, 8:9], op0=ALU.mult, op1=ALU.add))
            dve.op([ptf], [IDX], lambda e: e.tensor_copy(out=IDX[:], in_=ptf[:]))
            if SST <= 4:
                return
            for kvi in range(2):
                i_ = l * 2 + kvi
                banks = [psA[0], psA[1], psS[0], psS[1]]
                for pg in range(NPG):
                    kp = rt("KP", KP)
                    idma(pool, kp[:, :], pools[kvi][:, :], IDX[:, pg:pg + 1], [pools[kvi], IDX], [kp], kp)
                    pp = rt("p12", p12)
                    for fsx in range(2):
                        dve.op([kp, WT], [pp], lambda e, fsx=fsx, kp=kp, pp=pp: e.tensor_tensor(out=pp[:, fsx, :], in0=kp[:, :], in1=WT[:, kvi, fsx, :], op=ALU.mult))
                    for fsx in range(2):
                        for cc in range(2):
                            bk = banks[fsx * 2 + cc]
                            mm(bk, bk[:, pg * 8:(pg + 1) * 8], pp, pp[:, fsx, cc * 128:(cc + 1) * 128], indb, indb[:, :], True, True)
                for q_ in range(4):
                    act.op([banks[q_]], [fsb], lambda e, q_=q_: e.copy(out=fsb[:, q_, :], in_=banks[q_][:, 0:NMS * 128]))
                NC_ = NMS * 128
                for cc in range(2):
                    dve.op([fsb], [blkT], lambda e, cc=cc: e.tensor_tensor(out=blkT[:, cc, 1:NC_], in0=fsb[:, cc, 0:NC_ - 1], in1=fsb[:, 2 + cc, 1:NC_], op=ALU.add))
                if kvi == 0:
                    for cc in range(2):
                        pk = nxt("O")
                        mm(pk, pk[:, 0:NC_], PHI, PHI[:, i_, :], blkT, blkT[:, cc, :], True, True)
                        act.op([pk], [KCs], lambda e, cc=cc, pk=pk: e.copy(out=KCs[:, cc, :], in_=pk[:, 0:NC_]))
                else:
                    for mt in range(NMS):
                        pk = nxt("O")
                        for cc in range(2):
                            mm(pk, pk[:, cc * 128:(cc + 1) * 128], blkT, blkT[:, cc, mt * 128:(mt + 1) * 128], PHI, PHI[:, i_, :], True, True)
                        act.op([pk], [VCs], lambda e, mt=mt, pk=pk: e.copy(out=VCs[:, mt, :, 0:64], in_=pk[:, 0:256].rearrange("p (g d) -> p g d", d=64)))
            if SST <= 5:
                return
            pSh = [nxt("S"), nxt("S")]
            for hf in range(2):
                for mt in range(NMS):
                    for g in (hf, hf + 2):
                        cq = 4 * (g // 2)
                        mm(pSh[hf], pSh[hf][:, mt * 16 + 4 * g:mt * 16 + 4 * g + 4], KCs, KCs[hf * 64:hf * 64 + 64, g // 2, mt * 128:(mt + 1) * 128],
                           QTs, QTs[hf * 64:hf * 64 + 64, cq:cq + 4, b], True, True)
            pat = "p (m gi hf r) -> p m gi hf r"
            S16v = S16[:, :, :].rearrange("p m (gi hf r) -> p m gi hf r", hf=2, r=4)
            absv = csb[:, o_abs:o_abs + NMS * 16].rearrange(pat, gi=2, hf=2, r=4)
            for hf in range(2):
                pv_ = pSh[hf][:, 0:NMS * 16].rearrange(pat, gi=2, hf=2, r=4)
                dve.op([pSh[hf], csb], [S16], lambda e, hf=hf, pv_=pv_: e.tensor_tensor(out=S16v[:, :, :, hf, :], in0=pv_[:, :, :, hf, :], in1=absv[:, :, :, hf, :], op=ALU.add))
            act.op([S16], [P16], lambda e: e.activation(out=P16[:, :, :], in_=S16[:, :, :], func=AF.Exp))
            pD = nxt("O")
            pO = nxt("O")
            for mt in range(NMS):
                mm(pD, pD[:, 0:16], ones_bf, ones_bf[:, :], P16, P16[:, mt, :], mt == 0, mt == NMS - 1)
            for mt in range(NMS):
                mm(pO, pO[0:16, 0:260], P16, P16[:, mt, :], VCs, VCs[:, mt, :, :], mt == 0, mt == NMS - 1)
            act.op([pO], [O16sb], lambda e: e.copy(out=O16sb[:, 0, :], in_=pO[0:16, 0:260]))
            dve.op([pD], [rd16], lambda e: e.tensor_scalar(out=rd16[:, :], in0=pD[:, 0:16], scalar1=1e-30, scalar2=None, op0=ALU.max))
            dve.op([rd16], [rd16], lambda e: e.reciprocal(out=rd16[:, :], in_=rd16[:, :]))
            for mt in range(NMS):
                dve.op([P16, rd16], [S16], lambda e, mt=mt: e.tensor_tensor(out=S16[:, mt, :], in0=P16[:, mt, :], in1=rd16[:, :], op=ALU.mult))
            dve.op([S16], [Pn4f], lambda e: e.tensor_reduce(out=Pn4f[:, :, :], in_=S16[:, :, :].rearrange("p m (g r) -> p m g r", r=4), axis=AX.X, op=ALU.add))
            act.op([Pn4f], [Pn4], lambda e: e.copy(out=Pn4[:, :, :], in_=Pn4f[:, :, :]))
            p4 = nxt("S")
            for mt in range(NMS):
                mm(p4, p4[0:4, 0:128], Pn4, Pn4[:, mt, :], gsb, gsb[:, mt * 128:(mt + 1) * 128], mt == 0, mt == NMS - 1)
            if SST <= 6:
                return
            dve.op([p4, csb], [sc4], lambda e: e.tensor_tensor(out=sc4[:, :], in0=p4[0:4, 0:128], in1=csb[0:4, o_fb:o_fb + 128], op=ALU.add))
            NBs = NPG * 2
            dve.op([sc4], [m84], lambda e: e.max(out=m84[:, 0:8], in_=sc4[:, 0:NBs]))
            dve.op([sc4, m84], [wk4], lambda e: e.match_replace(out=wk4[:, 0:NBs], in_to_replace=m84[:, 0:8], in_values=sc4[:, 0:NBs], imm_value=-3.0e38))
            dve.op([wk4], [m84], lambda e: e.max(out=m84[:, 8:16], in_=wk4[:, 0:NBs]))
            dve.op([], [mb4], lambda e: e.memset(mb4[:, :], 0.0))
            dve.op([sc4, m84], [mb4], lambda e: e.tensor_scalar(out=mb4[:, 0:NBs], in0=sc4[:, 0:NBs], scalar1=m84[:, 14:15], scalar2=NEGB, op0=ALU.is_lt, op1=ALU.mult))
            pe.op([mb4, cf], [psM], lambda e: e.transpose(out=psM[:, 0:4], in_=mb4[0:4, :], identity=ident_f(4)))
            act.op([psM], [MB4T], lambda e: e.copy(out=MB4T[:, :], in_=psM[:, 0:4]))
            if SST <= 7:
                return
            for hh in range(2):
                pq = nxt("A")
                mm(pq, pq[:, :], cf, cf[0:NS, o_id + b:o_id + b + 1].to_broadcast([NS, 128]), qtok, qtok[0:NS, hh * 512:(hh + 1) * 512], True, True)
                act.op([pq], [QBs], lambda e, pq=pq, hh=hh: e.copy(out=QBs[:, hh * 512:(hh + 1) * 512], in_=pq[:, :]))

            def keytile(kp, vp, bias_ap, mask_pg, pO_, first, last_):
                dve.op([kp, QBs], [prod], lambda e: e.tensor_tensor(out=prod[:, :].rearrange("p (g r d) -> p g r d", r=4, d=64),
                                                                    in0=kp[:, :].rearrange("p (g o d) -> p g o d", o=1, d=64).to_broadcast([128, 4, 4, 64]),
                                                                    in1=QBs[:, :].rearrange("p (g r d) -> p g r d", r=4, d=64), op=ALU.mult))
                dve.op([prod], [s16], lambda e: e.tensor_reduce(out=s16[:, :], in_=prod[:, :].rearrange("p (h d) -> p h d", d=64), axis=AX.X, op=ALU.add))
                dve.op([s16, csb], [s16], lambda e: e.tensor_tensor(out=s16[:, :], in0=s16[:, :], in1=bias_ap, op=ALU.add))
                if mask_pg is not None:
                    a2, v = ((2 * mask_pg) // 64) % 2, mask_pg % 32
                    pm_ = nxt("S")
                    mm(pm_, pm_[:, 0:4], cb, cb[a2 * 64:a2 * 64 + 64, o_selh + v * 128:o_selh + (v + 1) * 128], MB4T, MB4T[a2 * 64:a2 * 64 + 64, :], True, True)
                    act.op([pm_], [mkb], lambda e: e.copy(out=mkb[:, :], in_=pm_[:, 0:4]))
                    dve.op([s16, mkb], [s16], lambda e: e.tensor_tensor(out=s16[:, :].rearrange("p (g r) -> p g r", r=4), in0=s16[:, :].rearrange("p (g r) -> p g r", r=4),
                                                                        in1=mkb[:, :].rearrange("p (g o) -> p g o", o=1).to_broadcast([128, 4, 4]), op=ALU.add))
                pk_ = rt("Pk", Pk)
                act.op([s16], [pk_], lambda e: e.activation(out=pk_[:, :], in_=s16[:, :], func=AF.Exp))
                vs_ = rt("Vst", Vst)
                act.op([vp], [vs_], lambda e: e.copy(out=vs_[:, :, 0:64], in_=vp[:, :].rearrange("p (g d) -> p g d", d=64)))
                mm(pO_, pO_[0:16, 0:260], pk_, pk_[:, :], vs_, vs_[:, :, :], first, last_)

            if SST <= 8:
                return
            pOs = nxt("O")
            for pg in range(NPG):
                kp, vp = rt("KP", KP), rt("VP", VP)
                idma(pool, kp[:, :], pools[2][:, :], IDX[:, pg:pg + 1], [pools[2], IDX], [kp], kp)
                idma(pool, vp[:, :], pools[3][:, :], IDX[:, pg:pg + 1], [pools[3], IDX], [vp], vp)
                keytile(kp, vp, csb[:, o_abk + pg * 16:o_abk + (pg + 1) * 16], pg, pOs, pg == 0, pg == NPG - 1)
            act.op([pOs], [O16sb], lambda e: e.copy(out=O16sb[:, 1, :], in_=pOs[0:16, 0:260]))
            if SST <= 9:
                return
            pOw = nxt("O")
            NWT = WBUF // 128
            for t in range(NWT):
                kp, vp = rt("KPw", KPw), rt("VPw", VPw)
                sp.dma(kp[:, :], winc[0][l, b, t * 128:(t + 1) * 128, :], [winc[0]], [kp], kp)
                sp.dma(vp[:, :], winc[1][l, b, t * 128:(t + 1) * 128, :], [winc[1]], [vp], vp)
                keytile(kp, vp, csb[:, o_abw + t * 16:o_abw + (t + 1) * 16], None, pOw, t == 0, t == NWT - 1)
            act.op([pOw], [O16sb], lambda e: e.copy(out=O16sb[:, 2, :], in_=pOw[0:16, 0:260]))
            for br in range(3):
                for g in range(4):
                    sp.dma(OS[b:b + 1, br, 4 * g:4 * g + 4, :], O16sb[4 * g:4 * g + 4, br, g * 65:(g + 1) * 65], [O16sb], [OS], OS)

        def sample_combine(l):
            q4 = qtok[:, :].rearrange("p (g r d) -> p g r d", r=4, d=64)
            for i, (ko, vo) in enumerate(((512, 768), (1024, 1280))):
                kn = kvtok[:, ko:ko + 256].rearrange("p (g o d) -> p g o d", o=1, d=64).to_broadcast([NS, 4, 4, 64])
                dve.op([qtok, kvtok], [tt[0]], lambda e, kn=kn: e.tensor_tensor(out=tt[0][:, :, :].rearrange("p (g r) d -> p g r d", r=4), in0=q4, in1=kn, op=ALU.mult))
                dve.op([tt[0]], [pnew], lambda e, i=i: e.tensor_reduce(out=pnew[:, i, :], in_=tt[0][:, :, :], axis=AX.X, op=ALU.add))
            act.op([pnew], [pnew], lambda e: e.activation(out=pnew[:, :, :], in_=pnew[:, :, :], func=AF.Exp))
            g3 = gat_t[:, :].rearrange("p (h b) -> p h b", b=3)
            for br in range(3):
                if br == 0:
                    dve.op([OS], [cden], lambda e: e.tensor_scalar(out=cden[:, 0, :], in0=OS[:, 0, :, 64], scalar1=1e-30, scalar2=None, op0=ALU.max))
                else:
                    dve.op([OS, pnew], [cden], lambda e, br=br: e.tensor_tensor(out=cden[:, br, :], in0=OS[:, br, :, 64], in1=pnew[:, br - 1, :], op=ALU.add))
                dve.op([cden], [cden], lambda e, br=br: e.reciprocal(out=cden[:, br, :], in_=cden[:, br, :]))
                dve.op([cden, gat_t], [cden], lambda e, br=br: e.tensor_tensor(out=cden[:, br, :], in0=cden[:, br, :], in1=g3[:, :, br], op=ALU.mult))
                num = tt[0]
                if br == 0:
                    dve.op([OS], [num], lambda e: e.tensor_copy(out=num[:, :, :], in_=OS[:, 0, :, 0:64]))
                else:
                    vo = 768 if br == 1 else 1280
                    vn = kvtok[:, vo:vo + 256].rearrange("p (g o d) -> p g o d", o=1, d=64).to_broadcast([NS, 4, 4, 64])
                    dve.op([kvtok, pnew], [num], lambda e, br=br, vn=vn: e.tensor_tensor(out=num[:, :, :].rearrange("p (g r) d -> p g r d", r=4), in0=vn,
                                                                                  in1=pnew[:, br - 1, :].rearrange("p (g r o) -> p g r o", r=4, o=1).to_broadcast([NS, 4, 4, 64]), op=ALU.mult))
                    dve.op([num, OS], [num], lambda e, br=br: e.tensor_tensor(out=num[:, :, :], in0=num[:, :, :], in1=OS[:, br, :, 0:64], op=ALU.add))
                dve.op([num, cden], [tt[1]], lambda e, br=br: e.tensor_tensor(out=tt[1][:, :, :], in0=num[:, :, :],
                                                                           in1=cden[:, br, :].rearrange("p (h o) -> p h o", o=1).to_broadcast([NS, 16, 64]), op=ALU.mult))
                if br == 0:
                    dve.op([tt[1]], [onat], lambda e: e.tensor_copy(out=onat[:, :, :], in_=tt[1][:, :, :]))
                else:
                    dve.op([tt[1], onat], [onat], lambda e: e.tensor_tensor(out=onat[:, :, :], in0=onat[:, :, :], in1=tt[1][:, :, :], op=ALU.add))
            dve.op([onat], [operm], lambda e: e.tensor_copy(out=operm[:, :, :].rearrange("p (hi r e) d -> p hi r e d", r=4, e=2),
                                                           in_=onat[:, :, :].rearrange("p (hi e r) d -> p hi r e d", e=2, r=4)))
            for c in range(8):
                pe.op([operm, cbm], [psT], lambda e, c=c: e.transpose(out=psT[:, c * NS:(c + 1) * NS], in_=operm[0:NS, 2 * c:2 * c + 2, :].rearrange("p h d -> p (h d)"), identity=cbm[0:NS, 0:NS]))
            act.op([FM["ng"]], [FM["t3"]], lambda e: e.activation(out=FM["t3"][:, :, :], in_=FM["ng"][:, :, :], func=AF.Silu))
            dve.op([psT, FM["t3"]], [zbs[2]], lambda e: e.tensor_tensor(out=zbs[2][:, :, :], in0=psT[:, 0:8 * NS].rearrange("p (c t) -> p c t", t=NS), in1=FM["t3"][:, :, :], op=ALU.mult))

        def merge_out(l, xsrc, xdst):
            for n_ in range(3):
                projS(l, [n_ * 8 + dc for dc in range(8)], FM["t0"], Wsrc=Wb[l], rhs_b=zbs[n_])
                if n_ == 0:
                    dve.op([FM["t0"], mgs], [maccs], lambda e: e.tensor_tensor(out=maccs[:, :, :], in0=FM["t0"][:, :, :], in1=mgs[:, 0:8, :], op=ALU.mult))
                else:
                    dve.op([FM["t0"], mgs], [FM["t1"]], lambda e, n_=n_: e.tensor_tensor(out=FM["t1"][:, :, :], in0=FM["t0"][:, :, :], in1=mgs[:, n_ * 8:(n_ + 1) * 8, :], op=ALU.mult))
                    dve.op([FM["t1"], maccs], [maccs], lambda e: e.tensor_tensor(out=maccs[:, :, :], in0=maccs[:, :, :], in1=FM["t1"][:, :, :], op=ALU.add))
            act.op([maccs], [mergs], lambda e: e.copy(out=mergs[:, :, :], in_=maccs[:, :, :]))
            sp.dma(WKs[:], Wo[l][:, :, 0:512], [Wo[l]], [WKs], WKs)
            sp.dma(WK2[:], Wo[l][:, :, 512:1024], [Wo[l]], [WK2], WK2)
            wo = [WKs, WK2]
            po = [nxt("A"), nxt("A")]
            dve.op([], [sms], lambda e: e.memset(sms[:, 16:18], 0.0))
            for eh in range(2):
                for dc in range(8):
                    mm(po[eh], po[eh][0:NS, :], mergs, mergs[:, dc, :], wo[eh], wo[eh][:, dc, :], dc == 0, dc == 7)
                act.op([po[eh], sms], [ubs, sms], lambda e, eh=eh: e.activation(out=ubs[:, eh * 512:(eh + 1) * 512], in_=po[eh][0:NS, :], func=AF.Square, accum_out=sms[0:NS, 16 + eh:17 + eh]))
            dve.op([sms], [sms], lambda e: e.tensor_tensor(out=sms[0:NS, 18:19], in0=sms[0:NS, 16:17], in1=sms[0:NS, 17:18], op=ALU.add))
            dve.op([sms], [sms], lambda e: e.tensor_scalar(out=sms[0:NS, 19:20], in0=sms[0:NS, 18:19], scalar1=1.0 / D, scalar2=EPS, op0=ALU.mult, op1=ALU.add))
            act.op([sms], [sms], lambda e: e.sqrt(out=sms[0:NS, 19:20], in_=sms[0:NS, 19:20]))
            dve.op([sms], [sms], lambda e: e.reciprocal(out=sms[0:NS, 20:21], in_=sms[0:NS, 19:20]))
            for eh in range(2):
                dve.op([po[eh], sms, gbc[1]], [ysb], lambda e, eh=eh: e.scalar_tensor_tensor(out=ysb[:, eh * 512:(eh + 1) * 512], in0=po[eh][0:NS, :], scalar=sms[0:NS, 20:21], in1=gbc[1][0:NS, eh * 512:(eh + 1) * 512], op0=ALU.mult, op1=ALU.mult))
            dve.op([ysb, xsb], [ysb], lambda e: e.tensor_tensor(out=ysb[:, :], in0=ysb[:, :], in1=xsb[:, :], op=ALU.add))
            sp.dma(xdst[:, :], ysb[:, :], [ysb], [xdst], ysb)

        layer_sample(0, xs_in, xs1)
        layer_sample(1, xs1, ys_out)

    init_state()
    stage = getattr(cfg, "STAGE", 99)
    do_prompt = stage >= 2 and stage != 45
    if do_prompt:
        layer_prompt(0, x_in, x1 if stage >= 3 else y_out)
    if do_prompt and stage >= 3:
        layer_prompt(1, x1, y_out)
    if stage >= 40:
        K.barrier()
        K.release_to(MARK_PROMPT)
        sample_phase()
    K.finish()
    return K


def _core_inputs(cfg, inp, b):
    f = lambda a: np.ascontiguousarray(np.asarray(a), dtype=np.float32)
    vec = []
    for l in range(2):
        vec += [f(inp["conv_w"])[l, k] for k in range(4)]
        vec += [f(inp["conv_b"])[l], f(inp["b_rg_a"])[l], f(inp["b_rg_x"])[l], f(inp["lru_lambda"])[l], f(inp["pool_scale"])[l]]
    NS = cfg.NS
    sl = slice(b * NS, (b + 1) * NS)
    m = {
        "x": f(inp["x_prompt"])[b],
        "g_pre": f(inp["g_pre"]), "g_post": f(inp["g_post"]), "w_in": f(inp["w_in"]),
        "vecs": np.ascontiguousarray(np.stack(vec, 0)),
        "w_rg": np.ascontiguousarray(np.stack([f(inp["w_rg_a"]), f(inp["w_rg_x"])], axis=1)),
        "w_pool": f(inp["w_pool"]),
        "cmp_pos": np.ascontiguousarray(np.stack([f(inp["cmp_pos_k"]), f(inp["cmp_pos_v"])], axis=1)),
        "cmp_phi": np.ascontiguousarray(np.stack([f(inp["cmp_phi_k"]), f(inp["cmp_phi_v"])], axis=1)),
        "w_branch": f(inp["w_branch"]), "w_out": f(inp["w_out"]),
        "xs_in": np.ascontiguousarray(f(inp["x_sample"])[sl, 0, :]),
        "st_conv": np.ascontiguousarray(f(inp["state_conv"])[:, sl].reshape(2, NS * 3, D)),
        "st_lru": np.ascontiguousarray(f(inp["state_lru"])[:, sl]),
        "st_pool": np.ascontiguousarray(f(inp["state_pool"])[:, sl].reshape(2, NS * 15, D)),
        "ptab": np.ascontiguousarray(np.asarray(inp["page_table"])[sl].astype(np.int32)),
    }
    for i, k in enumerate(("cache_cmp_k", "cache_cmp_v", "cache_sel_k", "cache_sel_v")):
        m["pool%d" % i] = f(inp[k]).reshape(-1, 256)
    for i, k in enumerate(("cache_win_k", "cache_win_v")):
        a = f(inp[k])
        m["winc%d" % i] = np.ascontiguousarray(a[:, sl].reshape(2, NS, a.shape[2], 256))
    m.update(make_consts(cfg))
    return m


def run_cfg(cfg, inp):
    K = build(cfg)
    in_maps = [_core_inputs(cfg, inp, b) for b in range(cfg.NCORES)]
    res = run_bass_kernel_spmd(K.nc, in_maps, core_ids=list(range(cfg.NCORES)))
    r = res.results
    S = cfg.S
    B = cfg.NCORES
    NS = cfg.NS
    wb = min(512, cfg.P)
    cat = lambda name: np.concatenate([r[b][name] for b in range(B)], axis=0)
    cat1 = lambda name: np.concatenate([r[b][name] for b in range(B)], axis=1)
    y_p = np.stack([r[b]["y"] for b in range(B)], 0)
    outs = [y_p, cat("ys")[:, None, :]]
    skv = cat1("skv")
    for i in range(4):
        outs.append(np.stack([r[b]["kvo%d" % i].reshape(2, S, 4, 64) for b in range(B)], 1))
        outs.append(np.ascontiguousarray(skv[:, :, i * 256:(i + 1) * 256]).reshape(2, -1, 1, 4, 64))
    for i in range(2):
        outs.append(np.stack([r[b]["wino%d" % i].reshape(2, 512, 4, 64) for b in range(B)], 1))
        outs.append(cat1("swin%d" % i).reshape(2, -1, wb, 4, 64))
    st = np.stack([r[b]["st"] for b in range(B)], 1)
    outs.append(np.ascontiguousarray(st[:, :, 0:3]))
    outs.append(cat1("sconv").reshape(2, -1, 3, D))
    outs.append(np.ascontiguousarray(st[:, :, 3]))
    outs.append(cat1("slru"))
    outs.append(np.ascontiguousarray(st[:, :, 4:19]))
    outs.append(cat1("spool").reshape(2, -1, 15, D))
    return tuple(np.ascontiguousarray(o, dtype=np.float32) for o in outs)


def kernel(**inputs):
    cfg = Cfg()
    return run_cfg(cfg, inputs)
```

```python
import numpy as np
import concourse.bass as bass
import concourse.mybir as mybir
from concourse.bass_utils import run_bass_kernel_spmd

F32 = mybir.dt.float32
BF16 = mybir.dt.bfloat16
I32 = mybir.dt.int32
AF = mybir.ActivationFunctionType
ALU = mybir.AluOpType
AX = mybir.AxisListType

D = 1024
IN_W = 10800
NEGB = -30000.0
THR_SKIP = 100.0
HPERM = [0, 4, 1, 5, 2, 6, 3, 7, 8, 12, 9, 13, 10, 14, 11, 15]
SLOPES = [2.0 ** (-8.0 * (h + 1) / 16.0) for h in range(16)]
POOL_WIN = (2, 4, 8, 16)


class Ev:
    __slots__ = ("sem", "val", "home")

    def __init__(self, sem, val, home=None):
        self.sem = sem
        self.val = val
        self.home = home


class Buf:
    __slots__ = ("name", "t", "w", "r", "dsem", "dcnt", "waited")

    def __init__(self, name, t=None):
        self.name = name
        self.t = t
        self.w = None
        self.r = {}
        self.dsem = None
        self.dcnt = 0
        self.waited = 0

    def __getitem__(self, idx):
        return self.t[idx]


class Eng:
    def __init__(self, K, eng, name, is_pe=False):
        self.K = K
        self.e = eng
        self.name = name
        self.sem = K.newsem("e_" + name)
        self.cnt = 0
        self.seen = {}
        self.is_pe = is_pe

    def wait(self, ev):
        if ev is None:
            return
        if self.is_pe and ev.sem is self.sem:
            return
        k = ev.sem.num
        val = ev.val if ev.home is None else ev.home.dcnt
        if ev.home is not None:
            ev.home.waited = max(ev.home.waited, val)
        if self.seen.get(k, 0) >= val:
            return
        self.e.wait_ge(ev.sem, val)
        self.seen[k] = val

    def pre(self, reads, writes):
        for b in reads:
            self.wait(b.w)
        for b in writes:
            self.wait(b.w)
            for ev in b.r.values():
                self.wait(ev)

    def post(self, ins, reads, writes):
        self.cnt += 1
        ins.then_inc(self.sem, 1)
        ev = Ev(self.sem, self.cnt)
        for b in reads:
            b.r[self.sem.num] = ev
        for b in writes:
            b.w = ev
            b.r = {}
        return ev

    def op(self, reads, writes, fn):
        self.pre(reads, writes)
        ins = fn(self.e)
        return self.post(ins, reads, writes)

    def dma(self, out_ap, in_ap, reads, writes, home, **kw):
        self.pre(reads, writes)
        if home.dsem is None:
            home.dsem = self.K.newsem("d_" + home.name)
            self.K.homes.append(home)
        if home.waited:
            self.wait(Ev(home.dsem, home.dcnt, home))
        ins = self.e.dma_start(out=out_ap, in_=in_ap, **kw)
        home.dcnt += 16
        ins.then_inc(home.dsem, 16)
        ev = Ev(home.dsem, home.dcnt, home)
        for b in reads:
            b.r[("d", home.dsem.num)] = ev
        for b in writes:
            b.w = ev
            b.r = {}
        return ev


def idma(eng, out_ap, in_ap, idx_ap, reads, writes, home):
    eng.pre(reads, writes)
    if home.dsem is None:
        home.dsem = eng.K.newsem("d_" + home.name)
        eng.K.homes.append(home)
    if home.waited:
        eng.wait(Ev(home.dsem, home.dcnt, home))
    ins = eng.e.indirect_dma_start(out=out_ap, out_offset=None, in_=in_ap, in_offset=bass.IndirectOffsetOnAxis(ap=idx_ap, axis=0))
    home.dcnt += 16
    ins.then_inc(home.dsem, 16)
    ev = Ev(home.dsem, home.dcnt, home)
    for b in reads:
        b.r[("d", home.dsem.num)] = ev
    for b in writes:
        b.w = ev
        b.r = {}
    return ev


class Kern:
    def __init__(self):
        self.nc = bass.Bass("TRN2", target_bir_lowering=False)
        nc = self.nc
        self.nsem = 0
        self.pe = Eng(self, nc.tensor, "pe", is_pe=True)
        self.dve = Eng(self, nc.vector, "dve")
        self.act = Eng(self, nc.scalar, "act")
        self.pool = Eng(self, nc.gpsimd, "pool")
        self.sp = Eng(self, nc.sync, "sp")
        self.outs = []
        self.cms = []
        self.homes = []

    def newsem(self, name):
        self.nsem += 1
        return self.nc.alloc_semaphore(name="%s_%d" % (name, self.nsem))

    def sb(self, name, shape, dt):
        cm = self.nc.sbuf_tensor(name, list(shape), dt)
        t = cm.__enter__()
        self.cms.append(cm)
        return Buf(name, t)

    def release_to(self, mark):
        while len(self.cms) > mark:
            self.cms.pop().__exit__(None, None, None)

    def barrier(self):
        engs = (self.pe, self.dve, self.act, self.pool, self.sp)
        for e in engs:
            for f in engs:
                if f.cnt and not (f is e and e.is_pe):
                    if f is e:
                        e.e.wait_ge(f.sem, f.cnt)
                        e.seen[f.sem.num] = f.cnt
                    else:
                        e.wait(Ev(f.sem, f.cnt))
            for hb in self.homes:
                e.wait(Ev(hb.dsem, hb.dcnt, hb))

    def ps(self, name, shape, dt=F32):
        t = self.nc.psum_tensor(name, list(shape), dt).__enter__()
        return Buf(name, t)

    def dram_in(self, name, shape, dt):
        t = self.nc.dram_tensor(name, list(shape), dt, kind="ExternalInput")
        return Buf(name, t.ap())

    def dram_out(self, name, shape, dt):
        t = self.nc.dram_tensor(name, list(shape), dt, kind="ExternalOutput")
        b = Buf(name, t.ap())
        self.outs.append(b)
        return b

    def dram_tmp(self, name, shape, dt):
        t = self.nc.dram_tensor(name, list(shape), dt, kind="Internal")
        return Buf(name, t.ap())

    def finish(self):
        for b in self.outs:
            self.sp.wait(b.w)
        for e in (self.pe, self.dve, self.act, self.pool):
            if e.cnt:
                self.sp.wait(Ev(e.sem, e.cnt))


class Cfg:
    def __init__(self, S=8192, NS=16, P=8192, NPHYS=2560, NCORES=2):
        self.S = S
        self.NS = NS
        self.P = P
        self.NPHYS = NPHYS
        self.NCORES = NCORES


CH_LRUX, CH_LRUG, CH_POOLX, CH_POOLG, CH_Q, CH_NSAG, CH_KVF, CH_BG, CH_MG = 0, 8, 16, 24, 32, 40, 48, 56, 57
NCHUNK = 81


def make_consts(cfg):
    S = cfg.S
    NKT = S // 128
    NMT = max(1, (S // 16) // 128)
    p = np.arange(128)
    ident = np.eye(128, dtype=np.float32)
    kk, qq = np.meshgrid(p, p, indexing="ij")
    tric = np.where(kk <= qq, 0.0, NEGB).astype(np.float32)
    triw = np.where(kk > qq, 0.0, NEGB).astype(np.float32)
    row0b = np.zeros((128, 512), np.float32)
    row0b[0, :] = NEGB
    x = np.arange(256)
    m = x[None, :] - 128
    jbrel = (p // 64)[:, None]
    patw = np.where(m > jbrel, -1.0e30, np.where((m == jbrel) | (m == jbrel - 1), 1.0e4, 0.0)).astype(np.float32)
    ratio = np.zeros((128, 4, 15), np.float32)
    for g, w in enumerate(POOL_WIN):
        ratio[:, g, :] = (w / np.minimum(w, np.arange(15) + 1.0))[None, :]
    qrel = np.arange(512)
    cmpb = np.zeros((128, 4, 512), np.float32)
    for v in range(4):
        cmpb[:, v, :] = np.where((16 * p[:, None] + 15 - 512 * v) <= qrel[None, :], 0.0, NEGB)
    selh = np.zeros((128, 32, 128), np.float32)
    k = np.arange(128)
    for v in range(32):
        selh[:, v, :] = ((p % 64)[:, None] == (2 * v + k // 64)[None, :]).astype(np.float32)
    gagg = np.zeros((128, NMT, 128), np.float32)
    j = np.arange(128)
    for mt in range(NMT):
        mm_ = 128 * mt + p
        gagg[:, mt, :] = ((mm_[:, None] >= 1) & (((mm_[:, None] - 1) // 4) == j[None, :])).astype(np.float32)
    NAB = NKT + 3
    OFF = NKT - 1
    NABC = NKT + 16 * (NMT - 1) + 1
    ab = np.zeros((128, 16, NAB), np.float64)
    abc = np.zeros((128, 16, NABC), np.float64)
    for h in range(16):
        W = 128 if h <= 3 else 512
        idx = np.arange(NAB)
        ab[:, h, :] = SLOPES[h] * (p[:, None] + 128.0 * (idx[None, :] - OFF) - W / 2.0)
        idx = np.arange(NABC)
        abc[:, h, :] = SLOPES[h] * (16.0 * p[:, None] + 15.0 + 128.0 * (idx[None, :] - OFF) - W / 2.0)
    f32c = np.concatenate([ident, tric, triw, row0b, patw, ratio.reshape(128, -1),
                           ab.reshape(128, -1).astype(np.float32), abc.reshape(128, -1).astype(np.float32)], axis=1)
    bfc = np.concatenate([cmpb.reshape(128, -1), selh.reshape(128, -1), gagg.reshape(128, -1)], axis=1)
    P_ = cfg.P
    NPG = P_ // 128
    NMS = max(1, (P_ // 16) // 128)
    absb = np.zeros((128, NMS, 16), np.float64)
    abk = np.zeros((128, NPG, 16), np.float64)
    abw = np.zeros((128, 4, 16), np.float64)
    for h in range(16):
        for mt in range(NMS):
            mm_ = 128 * mt + p
            absb[:, mt, h] = -SLOPES[h] * (P_ - (16.0 * mm_ + 15.0))
        for pg in range(NPG):
            abk[:, pg, h] = -SLOPES[h] * (P_ - (128.0 * pg + p))
        for t in range(4):
            abw[:, t, h] = -SLOPES[h] * (512.0 - (128.0 * t + p))
    absb[0, 0, :] = NEGB
    abw[0, 0, :] = NEGB
    ind = (p[:, None] // 16 == np.arange(8)[None, :]).astype(np.float32)
    fb = np.zeros((128, 128), np.float32)
    fb[:, 0] = 1.0e4
    fb[:, NPG * 2 - 1] = 1.0e4
    gs = np.zeros((128, NMS, 128), np.float32)
    for mt in range(NMS):
        mm_ = 128 * mt + p
        gs[:, mt, :] = ((mm_[:, None] >= 1) & (((mm_[:, None] - 1) // 4) == j[None, :])).astype(np.float32)
    c_s = np.concatenate([p[:, None].astype(np.float32), absb.reshape(128, -1), abk.reshape(128, -1), abw.reshape(128, -1),
                          ind, fb, gs.reshape(128, -1)], axis=1)
    return {"c_f32": np.ascontiguousarray(f32c, dtype=np.float32), "c_bf": np.ascontiguousarray(bfc, dtype=np.float32),
            "c_s": np.ascontiguousarray(c_s, dtype=np.float32)}


def build(cfg):
    S = cfg.S
    NT = S // 512
    NKT = S // 128
    NBk = min(S // 64, 128)
    NMT = max(1, (S // 16) // 128)
    NAB = NKT + 3
    OFF = NKT - 1
    NABC = NKT + 16 * (NMT - 1) + 1
    NSL = [min(NKT, 24), NKT]

    K = Kern()
    nc = K.nc
    pe, dve, act, pool, sp = K.pe, K.dve, K.act, K.pool, K.sp

    x_in = K.dram_in("x", [S, D], F32)
    g_pre = K.dram_in("g_pre", [2, D], F32)
    g_post = K.dram_in("g_post", [2, D], F32)
    w_in = K.dram_in("w_in", [2, D, IN_W], F32)
    vecs = K.dram_in("vecs", [18, D], F32)
    w_rg = K.dram_in("w_rg", [2, 2, 16, 64, 64], F32)
    w_pool = K.dram_in("w_pool", [2, 4, 256, 256], F32)
    cmp_pos = K.dram_in("cmp_pos", [2, 2, 32, 64], F32)
    cmp_phi = K.dram_in("cmp_phi", [2, 2, 64, 64], F32)
    w_branch = K.dram_in("w_branch", [2, 3, D, D], F32)
    w_out = K.dram_in("w_out", [2, D, D], F32)
    c_f32 = K.dram_in("c_f32", [128, 128 * 3 + 512 + 256 + 60 + 16 * NAB + 16 * NABC], F32)
    c_bf = K.dram_in("c_bf", [128, 2048 + 4096 + NMT * 128], F32)

    y_out = K.dram_out("y", [S, D], F32)
    kv_out = [K.dram_out("kvo%d" % i, [2, S, 256], F32) for i in range(4)]
    win_out = [K.dram_out("wino%d" % i, [2, 512, 256], F32) for i in range(2)]
    st_out = K.dram_out("st", [2, 19, D], F32)

    NS, PL, NPHYS = cfg.NS, cfg.P, cfg.NPHYS
    NPG = PL // 128
    NMS = max(1, (PL // 16) // 128)
    WBUF = min(512, PL)
    xs_in = K.dram_in("xs_in", [NS, D], F32)
    pools = [K.dram_in("pool%d" % i, [2 * NPHYS * 128, 256], F32) for i in range(4)]
    winc = [K.dram_in("winc%d" % i, [2, NS, WBUF, 256], F32) for i in range(2)]
    st_conv = K.dram_in("st_conv", [2, NS * 3, D], F32)
    st_lru = K.dram_in("st_lru", [2, NS, D], F32)
    st_pool = K.dram_in("st_pool", [2, NS * 15, D], F32)
    ptab = K.dram_in("ptab", [NS, NPG], I32)
    c_s = K.dram_in("c_s", [128, 1 + NMS * 16 + NPG * 16 + 64 + 8 + 128 + NMS * 128], F32)
    ys_out = K.dram_out("ys", [NS, D], F32)
    skv_out = K.dram_out("skv", [2, NS, 1536], F32)
    swin_out = [K.dram_out("swin%d" % i, [2, NS, WBUF, 256], F32) for i in range(2)]
    sconv_out = K.dram_out("sconv", [2, NS * 3, D], F32)
    slru_out = K.dram_out("slru", [2, NS, D], F32)
    spool_out = K.dram_out("spool", [2, NS * 15, D], F32)
    xs1 = K.dram_tmp("xsamp1", [NS, D], F32)
    Wqt = [K.dram_tmp("Wqt%d" % l, [2, 128, 8, 512], BF16) for l in range(2)]

    x1 = K.dram_tmp("x1", [S, D], F32)
    Wc = [K.dram_tmp("Wc%d" % l, [NCHUNK, 128, 8, 128], BF16) for l in range(2)]
    Wkv = [K.dram_tmp("Wkv%d" % l, [3, 128, 8, 512], BF16) for l in range(2)]
    Wb = [K.dram_tmp("Wb%d" % l, [24, 128, 8, 128], BF16) for l in range(2)]
    Wo = [K.dram_tmp("Wo%d" % l, [128, 8, D], BF16) for l in range(2)]

    def prep_cols(l, ch, scol, n, dcol):
        src = w_in[l, :, scol:scol + n].rearrange("(k p) n -> p k n", p=128)
        pool.dma(Wc[l][ch, :, :, dcol:dcol + n], src, [w_in], [Wc[l]], Wc[l])

    def prep_layer(l):
        for c in range(8):
            prep_cols(l, CH_LRUX + c, 0 + c * 128, 128, 0)
            prep_cols(l, CH_LRUG + c, 1024 + c * 128, 128, 0)
            prep_cols(l, CH_POOLX + c, 2048 + c * 128, 128, 0)
            prep_cols(l, CH_POOLG + c, 3072 + c * 128, 128, 0)
            for e in range(2):
                h = HPERM[2 * c + e]
                prep_cols(l, CH_Q + c, 4096 + h * 64, 64, e * 64)
                prep_cols(l, CH_NSAG + c, 5120 + h * 64, 64, e * 64)
        for i, base in enumerate((6144, 6400, 6656, 7168)):
            for cc in range(2):
                prep_cols(l, CH_KVF + 2 * i + cc, base + cc * 128, 128, 0)
        prep_cols(l, CH_BG, 7680, 128, 0)
        for n in range(3):
            for dc in range(8):
                prep_cols(l, CH_MG + n * 8 + dc, 7728 + n * 1024 + dc * 128, 128, 0)
        for blk in range(3):
            src = w_in[l, :, 6144 + blk * 512:6144 + (blk + 1) * 512].rearrange("(k p) n -> p k n", p=128)
            pool.dma(Wkv[l][blk], src, [w_in], [Wkv[l]], Wkv[l])
        for blk in range(2):
            src = w_in[l, :, 4096 + blk * 512:4096 + (blk + 1) * 512].rearrange("(k p) n -> p k n", p=128)
            pool.dma(Wqt[l][blk], src, [w_in], [Wqt[l]], Wqt[l])
        for n in range(2):
            for dc in range(8):
                src = w_branch[l, n, :, dc * 128:(dc + 1) * 128].rearrange("(k p) n -> p k n", p=128)
                pool.dma(Wb[l][n * 8 + dc], src, [w_branch], [Wb[l]], Wb[l])
        for fc in range(8):
            for e in range(2):
                h = HPERM[2 * fc + e]
                src = w_branch[l, 2, h * 64:(h + 1) * 64, :].rearrange("p (dc n) -> dc p n", n=128)
                pool.dma(Wb[l][16:24, e * 64:(e + 1) * 64, fc, :], src, [w_branch], [Wb[l]], Wb[l])
        src = w_out[l].rearrange("(k p) n -> p k n", p=128)
        pool.dma(Wo[l][:, :, :], src, [w_out], [Wo[l]], Wo[l])

    cf = K.sb("cf", [128, 128 + 256 + 60 + 16 * NAB + 16 * NABC], F32)
    sp.dma(cf[:, 0:128], c_f32[:, 0:128], [c_f32], [cf], cf)
    sp.dma(cf[:, 128:], c_f32[:, 896:], [c_f32], [cf], cf)
    o_ = [0]

    def take(n):
        a = o_[0]
        o_[0] += n
        return a

    o_id, o_pw, o_ratio = take(128), take(256), take(60)
    o_ab, o_abc = take(16 * NAB), take(16 * NABC)
    cb = K.sb("cb", [128, 2048 + 4096 + NMT * 128], BF16)
    pool.dma(cb[:], c_bf[:, :], [c_bf], [cb], cb)
    o_cmpb, o_selh, o_gagg = 0, 2048, 2048 + 4096
    cbm = K.sb("cbm", [128, 128 * 3 + 512], BF16)
    pool.dma(cbm[:], c_f32[:, 0:896], [c_f32], [cbm], cbm)
    ones_bf = K.sb("ones_bf", [128, 128], BF16)
    dve.op([], [ones_bf], lambda e: e.memset(ones_bf[:], 1.0))
    zer_bf = K.sb("zer_bf", [128, 128], BF16)
    dve.op([], [zer_bf], lambda e: e.memset(zer_bf[:], 0.0))

    def ident_f(n):
        return cf[0:n, o_id:o_id + n]

    ident_b = cbm[:, 0:128]
    tric_b = cbm[:, 128:256]
    triw_b = cbm[:, 256:384]
    row0b_b = cbm[:, 384:896]

    psA = [K.ps("psA%d" % i, [128, 512]) for i in range(2)]
    psS = [K.ps("psS%d" % i, [128, 512]) for i in range(2)]
    psO = [K.ps("psO%d" % i, [128, 512]) for i in range(2)]
    psM = K.ps("psM", [128, 512])
    psT = K.ps("psT", [128, 1024], BF16)
    rr = {"A": 0, "S": 0, "O": 0}

    def nxt(kind):
        lst = {"A": psA, "S": psS, "O": psO}[kind]
        rr[kind] += 1
        return lst[rr[kind] % 2]

    vr = K.sb("vr", [18, D], F32)
    sp.dma(vr[:], vecs[:, :], [vecs], [vr], vr)
    CV = K.sb("CV", [128, 8, 20], F32)
    for c in range(8):
        pe.op([vr, cf], [psM], lambda e, c=c: e.transpose(out=psM[:, c * 18:(c + 1) * 18], in_=vr[0:18, c * 128:(c + 1) * 128], identity=ident_f(18)))
    act.op([psM], [CV], lambda e: e.copy(out=CV[:, :, 0:18], in_=psM[:, 0:144].rearrange("p (c v) -> p c v", v=18)))
    tmpc = K.sb("tmpc", [128, 8, 2], F32)
    for l in range(2):
        act.op([CV], [tmpc], lambda e, l=l: e.activation(out=tmpc[:, :, l], in_=CV[:, :, l * 9 + 7], func=AF.Exp, scale=-1.0))
        act.op([tmpc], [tmpc], lambda e, l=l: e.activation(out=tmpc[:, :, l], in_=tmpc[:, :, l], func=AF.Ln, bias=1.0, scale=1.0))
        dve.op([tmpc], [CV], lambda e, l=l: e.tensor_scalar(out=CV[:, :, 18 + l], in0=tmpc[:, :, l], scalar1=-8.0, scalar2=None, op0=ALU.mult))

    wpr = K.sb("wpr", [32, 4, 128], F32)
    for l in range(2):
        for kv in range(2):
            for half in range(2):
                sp.dma(wpr[:, l * 2 + kv, half * 64:(half + 1) * 64], cmp_pos[l, kv, :, :], [cmp_pos], [wpr], wpr)
    CWP = K.sb("CWP", [128, 4, 32], F32)
    for i in range(4):
        pe.op([wpr, cf], [psM], lambda e, i=i: e.transpose(out=psM[:, 256 + i * 32:256 + (i + 1) * 32], in_=wpr[0:32, i, :], identity=ident_f(32)))
    act.op([psM], [CWP], lambda e: e.copy(out=CWP[:, :, :], in_=psM[:, 256:384].rearrange("p (i j) -> p i j", j=32)))

    WRG = K.sb("WRG", [128, 2, 8, 128], BF16)
    PHI = K.sb("PHI", [128, 4, 128], BF16)
    WPL = K.sb("WPL", [128, 4, 2, 256], BF16)
    pool.op([], [WRG], lambda e: e.memset(WRG[:], 0.0))
    pool.op([], [PHI], lambda e: e.memset(PHI[:], 0.0))
    for l in range(2):
        for kv in range(2):
            for half in range(2):
                pool.dma(PHI[half * 64:(half + 1) * 64, l * 2 + kv, half * 64:(half + 1) * 64], cmp_phi[l, kv, :, :], [cmp_phi], [PHI], PHI)

    def load_layer_small(l):
        for ax in range(2):
            for c in range(8):
                for half in range(2):
                    pool.dma(WRG[half * 64:(half + 1) * 64, ax, c, half * 64:(half + 1) * 64], w_rg[l, ax, 2 * c + half, :, :], [w_rg], [WRG], WRG)
        for g in range(4):
            pool.dma(WPL[:, g, :, :], w_pool[l, g, :, :].rearrange("(dh p) e -> p dh e", p=128), [w_pool], [WPL], WPL)
        sp.dma(gbc[0][:], g_pre[l:l + 1, :].partition_broadcast(128), [g_pre], [gbc[0]], gbc[0])
        sp.dma(gbc[1][:], g_post[l:l + 1, :].partition_broadcast(128), [g_post], [gbc[1]], gbc[1])

    gbc = [K.sb("gbc%d" % i, [128, D], F32) for i in range(2)]

    prep_layer(0)
    prep_layer(1)

    MARK_PROMPT = len(K.cms)
    KT = [K.sb("KT%d" % cc, [128, NSL[cc] * 128], BF16) for cc in range(2)]
    VS = [K.sb("VS%d" % cc, [128, NSL[cc], 2, 65], BF16) for cc in range(2)]
    KWT = K.sb("KWT", [128, 2, 8 * 128], BF16)
    VW = K.sb("VW", [128, 8, 4, 65], BF16)
    KCT = K.sb("KCT", [128, 2, NMT * 128], BF16)
    VC = K.sb("VC", [128, NMT, 4, 65], BF16)
    lruc = K.sb("lruc", [128, 8, 4], F32)
    poolc = K.sb("poolc", [128, 8, 15], F32)
    cmpc = K.sb("cmpc", [128, 4, 1], F32)

    xs = [K.sb("xs%d" % i, [128, D], F32) for i in range(2)]
    ub = K.sb("ub", [128, D], BF16)
    uT = K.sb("uT", [128, 8, 512], BF16)
    WB = [K.sb("WB%d" % i, [128, 1, 8, 128], BF16) for i in range(4)]
    WKB = K.sb("WKB", [128, 8, 512], BF16)
    QT = K.sb("QT", [128, 8, 512], BF16)
    zb = K.sb("zb", [128, 8, 512], BF16)
    ACC = K.sb("ACC", [128, 8, 512], F32)
    gates = K.sb("gates", [48, 512], F32)
    MbT = K.sb("MbT", [128, 4, 512], BF16)
    MbS = K.sb("MbS", [128, 4, 512], BF16)
    merged = QT
    SC = [K.sb("SC%d" % i, [128, 528], F32) for i in range(6)]
    PT = [K.sb("PT%d" % i, [128, 512], BF16) for i in range(2)]
    PN = K.sb("PN", [128, NMT, 512], BF16)
    kvst = [K.sb("kvst%d" % i, [128, 512], F32) for i in range(1)]
    sm = K.sb("sm", [128, 64], F32)
    rrb = {"WB": 0, "PT": 0, "kvst": 0, "xs": 0}

    def rot(name, lst):
        rrb[name] += 1
        return lst[rrb[name] % len(lst)]

    wb_dma = {"n": 0}

    def mm(out_b, out_ap, l_b, l_ap, r_b, r_ap, start, stop):
        pe.op([l_b, r_b], [out_b], lambda e: e.matmul(out_ap, lhsT=l_ap, rhs=r_ap, start=start, stop=stop))

    PREF = 3

    def proj(l, chunks, consume, Wsrc=None, ncols=128):
        Wsrc = Wsrc if Wsrc is not None else Wc[l]
        bufs = {}

        def load(i):
            buf = rot("WB", WB)
            sp.dma(buf[:, 0, :, :], Wsrc[chunks[i]], [Wsrc], [buf], buf)
            bufs[i] = buf
        for i in range(min(PREF, len(chunks))):
            load(i)
        for i, ch in enumerate(chunks):
            if i + PREF < len(chunks):
                load(i + PREF)
            cur = bufs.pop(i)
            ps = nxt("A")
            for kc in range(8):
                mm(ps, ps[0:ncols, :], cur, cur[:, 0, kc, 0:ncols], uT, uT[:, kc, :], kc == 0, kc == 7)
            consume(i, ch, ps)

    def attn_pairs(h, q0, W, branch, J):
        res = []
        qlo_t = q0 // 128
        nsub = W // 128
        if branch == "s":
            kts = range(0, qlo_t + nsub)
        else:
            kts = range(max(0, qlo_t - 4), qlo_t + nsub)
        for kt in kts:
            i_lo = max(0, kt - qlo_t)
            i_hi = nsub - 1
            if branch == "w":
                i_hi = min(nsub - 1, kt + 4 - qlo_t)
            if i_lo > i_hi:
                continue
            c0, c1 = i_lo * 128, (i_hi + 1) * 128
            mind = (q0 + c0) - (kt * 128 + 127)
            if mind > 0 and SLOPES[h] * mind > THR_SKIP:
                continue
            masks = []
            if kt - qlo_t >= 0:
                masks.append((tric_b, (kt - qlo_t) * 128))
            if branch == "w" and 0 <= kt + 4 - qlo_t <= nsub - 1:
                masks.append((triw_b, (kt + 4 - qlo_t) * 128))
            res.append((kt, c0, c1, masks))
        return res

    EPS = 1e-6
    m8 = K.sb("m8", [128, 16], F32)
    wk = K.sb("wk", [128, 128], F32)
    Mbq = K.sb("Mbq", [128, 128], BF16)
    blkb = K.sb("blkb", [128, 32], BF16)
    vcst = K.sb("vcst", [32, 128], BF16)

    def init_state():
        for b_ in (KCT, VC, VW, VS[0], VS[1], KT[0], KT[1], KWT, MbT, MbS):
            pool.op([], [b_], lambda e, b_=b_: e.memset(b_[:], 0.0))
        pool.op([], [VC], lambda e: e.memset(VC[:, :, :, 64:65], 1.0))
        pool.op([], [VW], lambda e: e.memset(VW[:, :, :, 64:65], 1.0))
        for cc in range(2):
            pool.op([], [VS[cc]], lambda e, cc=cc: e.memset(VS[cc][:, :, :, 64:65], 1.0))

    def compress(l, J, kv, cc, ps):
        i = l * 2 + kv
        kcs, fs = SC[0], SC[1]
        act.op([ps], [kcs], lambda e: e.copy(out=kcs[:, 0:512], in_=ps[:, :]))
        v3 = kcs[:, 0:512].rearrange("p (c j) -> p c j", j=16)
        for half, o in ((0, 0), (1, 32)):
            dve.op([kcs, CWP], [fs], lambda e: e.tensor_scalar(out=fs[:, o:o + 32], in0=v3[:, :, 0], scalar1=CWP[:, i, half * 16:half * 16 + 1], scalar2=None, op0=ALU.mult))
            for j in range(1, 16):
                dve.op([kcs, CWP, fs], [fs], lambda e, j=j: e.scalar_tensor_tensor(out=fs[:, o:o + 32], in0=v3[:, :, j], scalar=CWP[:, i, half * 16 + j:half * 16 + j + 1], in1=fs[:, o:o + 32], op0=ALU.mult, op1=ALU.add))
        ci = kv * 2 + cc
        dve.op([cmpc, fs], [blkb], lambda e: e.tensor_tensor(out=blkb[:, 0:1], in0=cmpc[:, ci, :], in1=fs[:, 32:33], op=ALU.add))
        dve.op([fs], [blkb], lambda e: e.tensor_tensor(out=blkb[:, 1:32], in0=fs[:, 0:31], in1=fs[:, 33:64], op=ALU.add))
        dve.op([fs], [cmpc], lambda e: e.tensor_copy(out=cmpc[:, ci, :], in_=fs[:, 31:32]))
        if kv == 0:
            mm(psM, psM[:, 0:32], PHI, PHI[:, i, :], blkb, blkb[:, :], True, True)
            act.op([psM], [KCT], lambda e: e.copy(out=KCT[:, cc, 32 * J:32 * J + 32], in_=psM[:, 0:32]))
        else:
            mm(psM, psM[0:32, 0:128], blkb, blkb[:, :], PHI, PHI[:, i, :], True, True)
            act.op([psM], [vcst], lambda e: e.copy(out=vcst[:, :], in_=psM[0:32, 0:128]))
            r0 = 32 * (J % 4)
            sp.dma(VC[r0:r0 + 32, J // 4, 2 * cc:2 * cc + 2, 0:64], vcst[:, :].rearrange("p (g d) -> p g d", d=64), [vcst], [VC], VC)

    def combine(h, pos, O, qc0, W, b):
        cp, half = pos // 2, pos % 2
        Oa, rec = SC[2], SC[3]
        act.op([O], [Oa], lambda e: e.copy(out=Oa[0:65, 0:W], in_=O[0:65, 0:W]))
        mm(psM, psM[0:64, 0:W], cf, cf[0:65, o_id + 64:o_id + 65].to_broadcast([65, 64]), Oa, Oa[0:65, 0:W], True, True)
        gb = nxt("S")
        gi = 3 * h + b
        mm(gb, gb[0:64, 0:W], cf, cf[0:48, o_id + gi:o_id + gi + 1].to_broadcast([48, 64]), gates, gates[0:48, qc0:qc0 + W], True, True)
        dve.op([psM], [rec], lambda e: e.tensor_scalar(out=rec[0:64, 0:W], in0=psM[0:64, 0:W], scalar1=1e-30, scalar2=None, op0=ALU.max))
        dve.op([rec], [rec], lambda e: e.reciprocal(out=rec[0:64, 0:W], in_=rec[0:64, 0:W]))
        dve.op([gb, rec], [rec], lambda e: e.tensor_tensor(out=rec[0:64, 0:W], in0=gb[0:64, 0:W], in1=rec[0:64, 0:W], op=ALU.mult))
        T2 = SC[4]
        hs_ = slice(half * 64, half * 64 + 64)
        dve.op([Oa, rec], [T2], lambda e: e.tensor_tensor(out=T2[hs_, 0:W], in0=Oa[0:64, 0:W], in1=rec[0:64, 0:W], op=ALU.mult))
        dve.op([T2, ACC], [ACC], lambda e: e.tensor_tensor(out=ACC[hs_, cp, qc0:qc0 + W], in0=T2[hs_, 0:W], in1=ACC[hs_, cp, qc0:qc0 + W], op=ALU.add))

    DBG = getattr(cfg, 'DBG', '')

    def attend(h, pos, q0, W, branch, J):
        t0 = J * 512
        qc0 = q0 - t0
        cp, half = pos // 2, pos % 2
        g = h // 4
        cc, gi = g // 2, g % 2
        pairs = attn_pairs(h, q0, W, branch, J)
        full = [p_ for p_ in pairs if p_[1] == 0 and p_[2] == W]
        assert full, (h, q0, W, branch)
        pairs = [full[0]] + [p_ for p_ in pairs if p_ is not full[0]]
        O = nxt("O")

        def kv_of(kt):
            if branch == "s":
                sl = kt % NSL[cc]
                return KT[cc], KT[cc][half * 64:half * 64 + 64, sl * 128:(sl + 1) * 128], VS[cc], VS[cc][:, sl, gi, :]
            sl = kt % 8
            return KWT, KWT[half * 64:half * 64 + 64, cc, sl * 128:(sl + 1) * 128], VW, VW[:, sl, g, :]

        def emit_scores(pi):
            kt, c0, c1, masks = pairs[pi]
            S_ = nxt("S")
            lk_b, lk, _, _ = kv_of(kt)
            if 't' in DBG:
                masks = []
            nmore = len(masks) + (1 if (branch == "s" and "m" not in DBG) else 0)
            mm(S_, S_[:, c0:c1], lk_b, lk, QT, QT[half * 64:half * 64 + 64, cp, qc0 + c0:qc0 + c1], True, nmore == 0)
            if branch == "s" and "m" not in DBG:
                a2 = ((2 * kt) // 64) % 2
                v = kt % 32
                nmore -= 1
                Mb_ = MbT if a2 == half else MbS
                mm(S_, S_[:, c0:c1], cb, cb[half * 64:half * 64 + 64, o_selh + v * 128:o_selh + (v + 1) * 128],
                   Mb_, Mb_[half * 64:half * 64 + 64, g, qc0 + c0:qc0 + c1], False, nmore == 0)
            for (mb_, mc0) in masks:
                nmore -= 1
                mm(S_, S_[:, mc0:mc0 + 128], cbm, ident_b, cbm, mb_, False, nmore == 0)
            return S_

        S_next = emit_scores(0)
        for pi, (kt, c0, c1, masks) in enumerate(pairs):
            S_ = S_next
            if pi + 1 < len(pairs):
                S_next = emit_scores(pi + 1)
            P_ = rot("PT", PT)
            _, _, vb, vap = kv_of(kt)
            idx = kt - q0 // 128 + OFF
            act.op([S_, cf], [P_], lambda e, S_=S_, P_=P_, c0=c0, c1=c1, idx=idx: e.activation(
                out=P_[:, c0:c1], in_=S_[:, c0:c1], func=AF.Exp, bias=cf[:, o_ab + h * NAB + idx:o_ab + h * NAB + idx + 1], scale=1.0))
            if 'p' not in DBG:
                mm(O, O[0:65, c0:c1], vb, vap, P_, P_[:, c0:c1], pi == 0, pi == len(pairs) - 1)
        if 'c' not in DBG:
            combine(h, pos, O, qc0, W, 1 if branch == "s" else 2)

    def attend_cmp(h, pos, q0, W, J, psacc, written, last_head):
        t0 = J * 512
        qc0 = q0 - t0
        cp, half = pos // 2, pos % 2
        g = h // 4
        cc = g // 2
        mtd = J // 4
        mts = []
        for mt in range(mtd + 1):
            mind = q0 - (16 * (128 * mt + 127) + 15)
            if mind > 0 and SLOPES[h] * mind > THR_SKIP:
                continue
            mts.append(mt)
        O = nxt("O")
        Dp = nxt("O")
        for mi, mt in enumerate(mts):
            S_ = nxt("S")
            nmore = (1 if mt == 0 else 0) + (1 if mt == mtd else 0)
            mm(S_, S_[:, 0:W], KCT, KCT[half * 64:half * 64 + 64, cc, mt * 128:(mt + 1) * 128], QT, QT[half * 64:half * 64 + 64, cp, qc0:qc0 + W], True, nmore == 0)
            if mt == 0:
                nmore -= 1
                mm(S_, S_[:, 0:W], cbm, ident_b, cbm, row0b_b[:, 0:W], False, nmore == 0)
            if mt == mtd:
                nmore -= 1
                v = J % 4
                mm(S_, S_[:, 0:W], cbm, ident_b, cb, cb[:, o_cmpb + v * 512 + qc0:o_cmpb + v * 512 + qc0 + W], False, nmore == 0)
            idx = 16 * mt - q0 // 128 + OFF
            act.op([S_, cf], [PN], lambda e, S_=S_, mt=mt, idx=idx: e.activation(
                out=PN[:, mt, qc0:qc0 + W], in_=S_[:, 0:W], func=AF.Exp, bias=cf[:, o_abc + h * NABC + idx:o_abc + h * NABC + idx + 1], scale=1.0))
            mm(Dp, Dp[:, 0:W], ones_bf, ones_bf[:, :], PN, PN[:, mt, qc0:qc0 + W], mi == 0, mi == len(mts) - 1)
            mm(O, O[0:65, 0:W], VC, VC[:, mt, g, :], PN, PN[:, mt, qc0:qc0 + W], mi == 0, mi == len(mts) - 1)
        rden = SC[4]
        dve.op([Dp], [rden], lambda e: e.tensor_scalar(out=rden[:, 0:W], in0=Dp[:, 0:W], scalar1=1e-30, scalar2=None, op0=ALU.max))
        dve.op([rden], [rden], lambda e: e.reciprocal(out=rden[:, 0:W], in_=rden[:, 0:W]))
        for mt in mts:
            dve.op([PN, rden], [PN], lambda e, mt=mt: e.tensor_tensor(out=PN[:, mt, qc0:qc0 + W], in0=PN[:, mt, qc0:qc0 + W], in1=rden[:, 0:W], op=ALU.mult))
        nsub = W // 128
        for si in range(nsub):
            qs = qc0 // 128 + si
            for mi, mt in enumerate(mts):
                mm(psacc, psacc[:, qs * 128:qs * 128 + NBk], PN, PN[:, mt, qs * 128:(qs + 1) * 128], cb, cb[:, o_gagg + mt * 128:o_gagg + mt * 128 + NBk],
                   False, last_head and si == nsub - 1 and mi == len(mts) - 1)
        if getattr(cfg, 'STAGE', 99) > 2.31:
            combine(h, pos, O, qc0, W, 0)

    def topk_group(g, J, psacc):
        t0 = J * 512
        for qs in range(4):
            s0 = t0 + qs * 128
            sc = SC[5]
            po = o_pw + 128 - s0 // 64
            dve.op([psacc, cf], [sc], lambda e: e.tensor_tensor(out=sc[:, 0:NBk], in0=psacc[:, qs * 128:qs * 128 + NBk], in1=cf[:, po:po + NBk], op=ALU.add))
            dve.op([sc], [sc], lambda e: e.tensor_scalar(out=sc[:, 0:1], in0=sc[:, 0:1], scalar1=1.0e4, scalar2=None, op0=ALU.add))
            dve.op([sc], [m8], lambda e: e.max(out=m8[:, 0:8], in_=sc[:, 0:NBk]))
            dve.op([sc, m8], [wk], lambda e: e.match_replace(out=wk[:, 0:NBk], in_to_replace=m8[:, 0:8], in_values=sc[:, 0:NBk], imm_value=-3.0e38))
            dve.op([wk], [m8], lambda e: e.max(out=m8[:, 8:16], in_=wk[:, 0:NBk]))
            dve.op([sc, m8], [Mbq], lambda e: e.tensor_scalar(out=Mbq[:, 0:NBk], in0=sc[:, 0:NBk], scalar1=m8[:, 15:16], scalar2=NEGB, op0=ALU.is_lt, op1=ALU.mult))
            pe.op([Mbq, cbm], [psT], lambda e, qs=qs: e.transpose(out=psT[0:NBk, qs * 128:(qs + 1) * 128], in_=Mbq[:, 0:NBk], identity=ident_b))
        act.op([psT], [MbT], lambda e: e.copy(out=MbT[0:NBk, g, :], in_=psT[0:NBk, 0:512]))
        n0 = min(NBk, 64)
        act.op([psT], [MbS], lambda e: e.copy(out=MbS[64:64 + n0, g, :], in_=psT[0:n0, 0:512]))
        if NBk > 64:
            act.op([psT], [MbS], lambda e: e.copy(out=MbS[0:NBk - 64, g, :], in_=psT[64:NBk, 0:512]))

    def merge_branch(l, n, first):
        seq = []
        for dc in range(8):
            seq.append((Wb[l], n * 8 + dc))
            seq.append((Wc[l], CH_MG + n * 8 + dc))
        bufs = {}

        def load(i):
            buf = rot("WB", WB)
            sp.dma(buf[:, 0, :, :], seq[i][0][seq[i][1]], [seq[i][0]], [buf], buf)
            bufs[i] = buf
        load(0)
        load(1)
        for dc in range(8):
            if dc < 7:
                load(2 * dc + 2)
                load(2 * dc + 3)
            wbr, wmg = bufs.pop(2 * dc), bufs.pop(2 * dc + 1)
            pbr, pmg = nxt("A"), nxt("A")
            for fc in range(8):
                mm(pbr, pbr[:, :], wbr, wbr[:, 0, fc, :], zb, zb[:, fc, :], fc == 0, fc == 7)
            for kc in range(8):
                mm(pmg, pmg[:, :], wmg, wmg[:, 0, kc, :], uT, uT[:, kc, :], kc == 0, kc == 7)
            mt_ = SC[0]
            act.op([pmg], [mt_], lambda e: e.activation(out=mt_[:, 0:512], in_=pmg[:, :], func=AF.Sigmoid))
            if first:
                dve.op([pbr, mt_], [ACC], lambda e, dc=dc: e.tensor_tensor(out=ACC[:, dc, :], in0=pbr[:, :], in1=mt_[:, 0:512], op=ALU.mult))
            else:
                dve.op([pbr, mt_], [mt_], lambda e: e.tensor_tensor(out=mt_[:, 0:512], in0=pbr[:, :], in1=mt_[:, 0:512], op=ALU.mult))
                dve.op([ACC, mt_], [ACC], lambda e, dc=dc: e.tensor_tensor(out=ACC[:, dc, :], in0=ACC[:, dc, :], in1=mt_[:, 0:512], op=ALU.add))

    def tile_prompt(l, J, x_src, x_dst):
        stage = getattr(cfg, 'STAGE', 99)
        t0 = J * 512
        last = (J == NT - 1)
        for s in range(4):
            xb = rot("xs", xs)
            sp.dma(xb[:], x_src[t0 + s * 128:t0 + (s + 1) * 128, :], [x_src], [xb], xb)
            dve.op([], [sm], lambda e, s=s: e.memset(sm[:, s:s + 1], 0.0))
            act.op([xb, sm], [ub, sm], lambda e, s=s, xb=xb: e.activation(out=ub[:, :], in_=xb[:, :], func=AF.Square, accum_out=sm[:, s:s + 1]))
            dve.op([sm], [sm], lambda e, s=s: e.tensor_scalar(out=sm[:, 4 + s:5 + s], in0=sm[:, s:s + 1], scalar1=1.0 / D, scalar2=EPS, op0=ALU.mult, op1=ALU.add))
            act.op([sm], [sm], lambda e, s=s: e.sqrt(out=sm[:, 4 + s:5 + s], in_=sm[:, 4 + s:5 + s]))
            dve.op([sm], [sm], lambda e, s=s: e.reciprocal(out=sm[:, 8 + s:9 + s], in_=sm[:, 4 + s:5 + s]))
            dve.op([xb, sm, gbc[0]], [ub], lambda e, s=s, xb=xb: e.scalar_tensor_tensor(out=ub[:, :], in0=xb[:, :], scalar=sm[:, 8 + s:9 + s], in1=gbc[0][:, :], op0=ALU.mult, op1=ALU.mult))
            for c in range(8):
                pe.op([ub, cbm], [psT], lambda e, c=c: e.transpose(out=psT[:, c * 128:(c + 1) * 128], in_=ub[:, c * 128:(c + 1) * 128], identity=ident_b))
            act.op([psT], [uT], lambda e, s=s: e.copy(out=uT[:, :, s * 128:(s + 1) * 128], in_=psT[:, :].rearrange("p (c t) -> p c t", t=128)))

        if stage <= 2.1:
            return
        for blk in range(3):
            sp.dma(WKB[:], Wkv[l][blk], [Wkv[l]], [WKB], WKB)
            for s in range(4):
                ps = nxt("A")
                for kc in range(8):
                    mm(ps, ps[:, :], uT, uT[:, kc, s * 128:(s + 1) * 128], WKB, WKB[:, kc, :], kc == 0, kc == 7)
                st = rot("kvst", kvst)
                act.op([ps], [st], lambda e, ps=ps, st=st: e.copy(out=st[:, :], in_=ps[:, :]))
                r0 = t0 + s * 128
                kt = r0 // 128
                if blk < 2:
                    sp.dma(kv_out[2 * blk][l, r0:r0 + 128, :], st[:, 0:256], [st], [kv_out[2 * blk]], st)
                    sp.dma(kv_out[2 * blk + 1][l, r0:r0 + 128, :], st[:, 256:512], [st], [kv_out[2 * blk + 1]], st)
                elif r0 >= S - 512:
                    w0 = r0 - (S - 512)
                    sp.dma(win_out[0][l, w0:w0 + 128, :], st[:, 0:256], [st], [win_out[0]], st)
                    sp.dma(win_out[1][l, w0:w0 + 128, :], st[:, 256:512], [st], [win_out[1]], st)
                if blk == 1:
                    for cc in range(2):
                        dve.op([st], [VS[cc]], lambda e, cc=cc, st=st, kt=kt: e.tensor_copy(
                            out=VS[cc][:, kt % NSL[cc], :, 0:64], in_=st[:, 256 + cc * 128:256 + (cc + 1) * 128].rearrange("p (g d) -> p g d", d=64)))
                if blk == 2:
                    dve.op([st], [VW], lambda e, st=st, kt=kt: e.tensor_copy(out=VW[:, kt % 8, :, 0:64], in_=st[:, 256:512].rearrange("p (g d) -> p g d", d=64)))

        if stage <= 2.2:
            return
        def c_q(i, ch, ps):
            act.op([ps], [QT], lambda e: e.mul(out=QT[:, i, :], in_=ps[:, :], mul=0.125))
        proj(l, [CH_Q + c for c in range(8)], c_q)

        def c_kv(i, ch, ps):
            j = ch - CH_KVF
            cc = j % 2
            if j < 2:
                compress(l, J, 0, cc, ps)
            elif j < 4:
                compress(l, J, 1, cc, ps)
            elif j < 6:
                sl = (4 * J) % NSL[cc]
                act.op([ps], [KT[cc]], lambda e: e.copy(out=KT[cc][:, sl * 128:sl * 128 + 512], in_=ps[:, :]))
            else:
                sl = (4 * J) % 8
                act.op([ps], [KWT], lambda e: e.copy(out=KWT[:, cc, sl * 128:sl * 128 + 512], in_=ps[:, :]))
        proj(l, [CH_KVF + j for j in range(8)], c_kv)

        def c_bg(i, ch, ps):
            act.op([ps], [gates], lambda e: e.activation(out=gates[:, :], in_=ps[0:48, :], func=AF.Sigmoid))
        proj(l, [CH_BG], c_bg, ncols=48)

        if stage <= 2.3:
            return
        pool.op([], [ACC], lambda e: e.memset(ACC[:], 0.0))
        for g in range(4):
            heads = [(h, HPERM.index(h)) for h in range(4 * g, 4 * g + 4)]
            psacc = psA[g % 2]
            written = set()
            mm(psacc, psacc[:, :], zer_bf, zer_bf[:, :], cb, cb[:, 0:512], True, False)
            for hi, (h, pos) in enumerate(heads):
                W = 128 if h <= 3 else 512
                for q0 in range(t0, t0 + 512, W):
                    attend_cmp(h, pos, q0, W, J, psacc, written, hi == 3 and q0 + W == t0 + 512)
            if stage <= 2.32:
                continue
            topk_group(g, J, psacc)
            if stage <= 2.33:
                continue
            for (h, pos) in heads:
                W = 128 if h <= 3 else 512
                if 'h' in DBG and pos % 2 == 1:
                    continue
                for q0 in range(t0, t0 + 512, W):
                    attend(h, pos, q0, W, "s", J)
                    if stage > 2.34:
                        attend(h, pos, q0, W, "w", J)

        if stage <= 2.4:
            return
        def c_nsag(i, ch, ps):
            sg = SC[0]
            act.op([ps], [sg], lambda e: e.activation(out=sg[:, 0:512], in_=ps[:, :], func=AF.Silu))
            dve.op([ACC, sg], [zb], lambda e: e.tensor_tensor(out=zb[:, i, :], in0=ACC[:, i, :], in1=sg[:, 0:512], op=ALU.mult))
        proj(l, [CH_NSAG + c for c in range(8)], c_nsag)
        merge_branch(l, 2, True)

        if stage <= 2.5:
            return
        def c_lru(i, ch, ps):
            c = i // 2
            if i % 2 == 0:
                xp, xc, r_, i_, a_, hs = SC[0], SC[1], SC[2], SC[3], SC[4], SC[5]
                act.op([lruc], [xp], lambda e: e.copy(out=xp[:, 0:3], in_=lruc[:, c, 0:3]))
                act.op([ps], [xp], lambda e: e.copy(out=xp[:, 3:515], in_=ps[:, :]))
                act.op([xp], [lruc], lambda e: e.copy(out=lruc[:, c, 0:3], in_=xp[:, 512:515]))
                cv = l * 9
                dve.op([xp, CV], [xc], lambda e: e.tensor_scalar(out=xc[:, 0:512], in0=xp[:, 0:512], scalar1=CV[:, c, cv:cv + 1], scalar2=CV[:, c, cv + 4:cv + 5], op0=ALU.mult, op1=ALU.add))
                for k in range(1, 4):
                    dve.op([xp, CV, xc], [xc], lambda e, k=k: e.scalar_tensor_tensor(out=xc[:, 0:512], in0=xp[:, k:k + 512], scalar=CV[:, c, cv + k:cv + k + 1], in1=xc[:, 0:512], op0=ALU.mult, op1=ALU.add))
                xcb = rot("PT", PT)
                act.op([xc], [xcb], lambda e: e.copy(out=xcb[:, :], in_=xc[:, 0:512]))
                pr, pi_ = nxt("S"), nxt("S")
                mm(pr, pr[:, :], WRG, WRG[:, 0, c, :], xcb, xcb[:, :], True, True)
                mm(pi_, pi_[:, :], WRG, WRG[:, 1, c, :], xcb, xcb[:, :], True, True)
                act.op([pr, CV], [r_], lambda e: e.activation(out=r_[:, 0:512], in_=pr[:, :], func=AF.Sigmoid, bias=CV[:, c, cv + 5:cv + 6], scale=1.0))
                act.op([pi_, CV], [i_], lambda e: e.activation(out=i_[:, 0:512], in_=pi_[:, :], func=AF.Sigmoid, bias=CV[:, c, cv + 6:cv + 7], scale=1.0))
                act.op([r_, CV], [a_], lambda e: e.activation(out=a_[:, 0:512], in_=r_[:, 0:512], func=AF.Exp, scale=CV[:, c, 18 + l:19 + l]))
                dve.op([a_], [r_], lambda e: e.scalar_tensor_tensor(out=r_[:, 0:512], in0=a_[:, 0:512], scalar=-1.0, in1=a_[:, 0:512], op0=ALU.mult, op1=ALU.mult))
                act.op([r_], [r_], lambda e: e.activation(out=r_[:, 0:512], in_=r_[:, 0:512], func=AF.Sqrt, bias=1.0, scale=1.0))
                dve.op([i_, xc], [i_], lambda e: e.tensor_tensor(out=i_[:, 0:512], in0=i_[:, 0:512], in1=xc[:, 0:512], op=ALU.mult))
                dve.op([i_, r_], [i_], lambda e: e.tensor_tensor(out=i_[:, 0:512], in0=i_[:, 0:512], in1=r_[:, 0:512], op=ALU.mult))
                dve.op([a_, i_, lruc], [hs], lambda e: e.tensor_tensor_scan(out=hs[:, 0:512], data0=a_[:, 0:512], data1=i_[:, 0:512], initial=lruc[:, c, 3:4], op0=ALU.mult, op1=ALU.add))
                act.op([hs], [lruc], lambda e: e.copy(out=lruc[:, c, 3:4], in_=hs[:, 511:512]))
            else:
                sg, hs = SC[1], SC[5]
                act.op([ps], [sg], lambda e: e.activation(out=sg[:, 0:512], in_=ps[:, :], func=AF.Silu))
                dve.op([hs, sg], [zb], lambda e: e.tensor_tensor(out=zb[:, c, :], in0=hs[:, 0:512], in1=sg[:, 0:512], op=ALU.mult))
        chl = []
        for c in range(8):
            chl += [CH_LRUX + c, CH_LRUG + c]
        proj(l, chl, c_lru)
        merge_branch(l, 0, False)

        if stage <= 2.6:
            return
        PD = [PT[0], PT[1]]

        def c_pool(i, ch, ps):
            gq, k = i // 4, i % 4
            w = POOL_WIN[gq]
            if k < 2:
                c = 2 * gq + k
                X = SC[k]
                act.op([poolc], [X], lambda e: e.copy(out=X[:, 0:15], in_=poolc[:, c, :]))
                act.op([ps], [X], lambda e: e.copy(out=X[:, 15:527], in_=ps[:, :]))
                act.op([X], [poolc], lambda e: e.copy(out=poolc[:, c, :], in_=X[:, 512:527]))
                cur = X
                sh = 1
                tmps = [SC[4], SC[5]]
                ti = 0
                while sh < w:
                    nx = tmps[ti % 2]
                    ti += 1
                    dve.op([cur], [nx], lambda e, cur=cur, nx=nx, sh=sh: e.tensor_tensor(out=nx[:, sh:527], in0=cur[:, sh:527], in1=cur[:, 0:527 - sh], op=ALU.add))
                    cur = nx
                    sh *= 2
                if J == 0:
                    dve.op([cur, cf], [cur], lambda e, cur=cur: e.tensor_tensor(out=cur[:, 15:30], in0=cur[:, 15:30], in1=cf[:, o_ratio + gq * 15:o_ratio + gq * 15 + 15], op=ALU.mult))
                dve.op([cur, X], [PD[k]], lambda e, cur=cur: e.scalar_tensor_tensor(out=PD[k][:, :], in0=cur[:, 15:527], scalar=1.0 / w, in1=X[:, 15:527], op0=ALU.mult, op1=ALU.subtract))
            else:
                eh = k - 2
                c = 2 * gq + eh
                pm = nxt("S")
                for dh in range(2):
                    mm(pm, pm[:, :], WPL, WPL[:, gq, dh, eh * 128:(eh + 1) * 128], PD[dh], PD[dh][:, :], dh == 0, dh == 1)
                sg = SC[2]
                act.op([ps], [sg], lambda e: e.activation(out=sg[:, 0:512], in_=ps[:, :], func=AF.Silu))
                dve.op([pm, CV, sg], [zb], lambda e: e.scalar_tensor_tensor(out=zb[:, c, :], in0=pm[:, :], scalar=CV[:, c, l * 9 + 8:l * 9 + 9], in1=sg[:, 0:512], op0=ALU.mult, op1=ALU.mult))
        chl = []
        for gq in range(4):
            chl += [CH_POOLX + 2 * gq, CH_POOLX + 2 * gq + 1, CH_POOLG + 2 * gq, CH_POOLG + 2 * gq + 1]
        proj(l, chl, c_pool)
        merge_branch(l, 1, False)
        for dc in range(8):
            act.op([ACC], [merged], lambda e, dc=dc: e.copy(out=merged[:, dc, :], in_=ACC[:, dc, :]))

        if stage <= 2.7:
            return
        sp.dma(WKB[:], Wo[l][:, :, 0:512], [Wo[l]], [WKB], WKB)
        sp.dma(zb[:], Wo[l][:, :, 512:1024], [Wo[l]], [zb], zb)
        wo = [WKB, zb]
        for s in range(4):
            xr, yo = xs[0], xs[1]
            sp.dma(xr[:], x_src[t0 + s * 128:t0 + (s + 1) * 128, :], [x_src], [xr], xr)
            po = [nxt("A"), nxt("A")]
            dve.op([], [sm], lambda e: e.memset(sm[:, 16:18], 0.0))
            for eh in range(2):
                for dc in range(8):
                    mm(po[eh], po[eh][:, :], merged, merged[:, dc, s * 128:(s + 1) * 128], wo[eh], wo[eh][:, dc, :], dc == 0, dc == 7)
                act.op([po[eh], sm], [ub, sm], lambda e, eh=eh: e.activation(out=ub[:, eh * 512:(eh + 1) * 512], in_=po[eh][:, :], func=AF.Square, accum_out=sm[:, 16 + eh:17 + eh]))
            dve.op([sm], [sm], lambda e: e.tensor_tensor(out=sm[:, 18:19], in0=sm[:, 16:17], in1=sm[:, 17:18], op=ALU.add))
            dve.op([sm], [sm], lambda e: e.tensor_scalar(out=sm[:, 19:20], in0=sm[:, 18:19], scalar1=1.0 / D, scalar2=EPS, op0=ALU.mult, op1=ALU.add))
            act.op([sm], [sm], lambda e: e.sqrt(out=sm[:, 19:20], in_=sm[:, 19:20]))
            dve.op([sm], [sm], lambda e: e.reciprocal(out=sm[:, 20:21], in_=sm[:, 19:20]))
            for eh in range(2):
                dve.op([po[eh], sm, gbc[1]], [yo], lambda e, eh=eh: e.scalar_tensor_tensor(out=yo[:, eh * 512:(eh + 1) * 512], in0=po[eh][:, :], scalar=sm[:, 20:21], in1=gbc[1][:, eh * 512:(eh + 1) * 512], op0=ALU.mult, op1=ALU.mult))
            dve.op([yo, xr], [yo], lambda e: e.tensor_tensor(out=yo[:, :], in0=yo[:, :], in1=xr[:, :], op=ALU.add))
            sp.dma(x_dst[t0 + s * 128:t0 + (s + 1) * 128, :], yo[:], [yo], [x_dst], yo)

    def layer_prompt(l, x_src, x_dst):
        load_layer_small(l)
        if getattr(cfg, "STAGE", 99) == 1:
            return
        dve.op([], [lruc], lambda e: e.memset(lruc[:], 0.0))
        dve.op([], [poolc], lambda e: e.memset(poolc[:], 0.0))
        dve.op([], [cmpc], lambda e: e.memset(cmpc[:], 0.0))
        for J in range(NT):
            tile_prompt(l, J, x_src, x_dst)
        stt = SC[0]
        for c in range(8):
            act.op([lruc, poolc], [stt], lambda e, c=c: e.copy(out=stt[:, c * 19:c * 19 + 4], in_=lruc[:, c, :]))
            act.op([poolc], [stt], lambda e, c=c: e.copy(out=stt[:, c * 19 + 4:c * 19 + 19], in_=poolc[:, c, :]))
        psx = [psA[0], psA[1]]
        for c in range(8):
            pb = psx[c // 4]
            pe.op([stt, cf], [pb], lambda e, c=c, pb=pb: e.transpose(out=pb[0:19, (c % 4) * 128:(c % 4 + 1) * 128], in_=stt[:, c * 19:c * 19 + 19], identity=ident_f(128)))
        so = xs[0]
        for hh in range(2):
            act.op([psx[hh]], [so], lambda e, hh=hh: e.copy(out=so[0:19, hh * 512:(hh + 1) * 512], in_=psx[hh][0:19, :]))
        sp.dma(st_out[l, :, :], so[0:19, :], [so], [st_out], so)

    SST = getattr(cfg, 'SST', 99)

    def sample_phase():
        NCS = 1 + NMS * 16 + NPG * 16 + 64 + 8 + 128 + NMS * 128
        csb = K.sb("csb", [128, NCS - NMS * 128], F32)
        sp.dma(csb[:], c_s[:, 0:NCS - NMS * 128], [c_s], [csb], csb)
        oo = [0]

        def tk(n):
            a = oo[0]
            oo[0] += n
            return a
        o_p, o_abs, o_abk, o_abw, o_ind, o_fb = tk(1), tk(NMS * 16), tk(NPG * 16), tk(64), tk(8), tk(128)
        gsb = K.sb("gsb", [128, NMS * 128], BF16)
        pool.dma(gsb[:], c_s[:, NCS - NMS * 128:NCS], [c_s], [gsb], gsb)
        indb = K.sb("indb", [128, 8], BF16)
        pool.dma(indb[:], c_s[:, o_ind:o_ind + 8], [c_s], [indb], indb)
        NSB = 32
        xsb = K.sb("xsb", [NS, D], F32)
        ysb = K.sb("ysb", [NS, D], F32)
        ubs = K.sb("ubs", [NS, D], BF16)
        uTs = K.sb("uTs", [128, 8, NS], BF16)
        kvtok = K.sb("kvtok", [NS, 1536], F32)
        qtok = K.sb("qtok", [NS, D], F32)
        FM = {n_: K.sb("fm_" + n_, [128, 8, NS], F32) for n_ in ("xl", "lg", "px", "pg", "ng", "h", "t0", "t1", "t2", "t3")}
        mgs = K.sb("mgs", [128, 24, NS], F32)
        CS = K.sb("CS", [128, 8, NS * 3], F32)
        HS = K.sb("HS", [128, 8, NS], F32)
        PSs = K.sb("PSs", [128, 8, NS * 15], F32)
        stl = K.sb("stl", [128, D], F32)
        zbs = [K.sb("zbs%d" % n_, [128, 8, NS], BF16) for n_ in range(3)]
        xcbs = K.sb("xcbs", [128, 8, NS], BF16)
        pdb = K.sb("pdb", [128, 8, NS], BF16)
        gat_f = K.sb("gat_f", [48, NS], F32)
        gat_t = K.sb("gat_t", [NS, 48], F32)
        WBs = [K.sb("WBs%d" % i, [128, 2, 8, 128], BF16) for i in range(2)]
        WKs = K.sb("WKs", [128, 8, 512], BF16)
        WK2 = K.sb("WK2", [128, 8, 512], BF16)
        sms = K.sb("sms", [128, 32], F32)
        mergs = K.sb("mergs", [128, 8, NS], BF16)
        maccs = K.sb("maccs", [128, 8, NS], F32)
        ptb = K.sb("ptb", [128, NPG], I32)
        ptf = K.sb("ptf", [128, NPG], F32)
        IDX = K.sb("IDX", [128, NPG], I32)
        KP = [K.sb("KP%d" % i, [128, 256], F32) for i in range(4)]
        VP = [K.sb("VP%d" % i, [128, 256], F32) for i in range(4)]
        KPw = [K.sb("KPw%d" % i, [128, 256], F32) for i in range(2)]
        VPw = [K.sb("VPw%d" % i, [128, 256], F32) for i in range(2)]
        WT = K.sb("WT", [128, 2, 2, 256], F32)
        p12 = [K.sb("p12_%d" % i, [128, 2, 256], BF16) for i in range(3)]
        fsb = K.sb("fsb", [128, 4, NMS * 128], F32)
        blkT = K.sb("blkT", [128, 2, NMS * 128], BF16)
        KCs = K.sb("KCs", [128, 2, NMS * 128], BF16)
        VCs = K.sb("VCs", [128, NMS, 4, 65], BF16)
        QTs = K.sb("QTs", [128, 8, NS], BF16)
        S16 = K.sb("S16", [128, NMS, 16], F32)
        P16 = K.sb("P16", [128, NMS, 16], BF16)
        rd16 = K.sb("rd16", [128, 16], F32)
        Pn4 = K.sb("Pn4", [128, NMS, 4], BF16)
        Pn4f = K.sb("Pn4f", [128, NMS, 4], F32)
        sc4 = K.sb("sc4", [4, 128], F32)
        wk4 = K.sb("wk4", [4, 128], F32)
        m84 = K.sb("m84", [4, 16], F32)
        mb4 = K.sb("mb4", [4, 128], F32)
        MB4T = K.sb("MB4T", [128, 4], BF16)
        QBs = K.sb("QBs", [128, D], F32)
        prod = K.sb("prod", [128, D], F32)
        s16 = K.sb("s16", [128, 16], F32)
        mkb = K.sb("mkb", [128, 4], F32)
        Pk = [K.sb("Pk%d" % i, [128, 16], BF16) for i in range(3)]
        Vst = [K.sb("Vst%d" % i, [128, 4, 65], BF16) for i in range(3)]
        O16sb = K.sb("O16sb", [16, 3, 260], F32)
        OS = K.sb("OS", [NS, 3, 16, 65], F32)
        onat = K.sb("onat", [NS, 16, 64], F32)
        operm = K.sb("operm", [NS, 16, 64], BF16)
        tt = [K.sb("tt%d" % i, [NS, 16, 64], F32) for i in range(2)]
        pnew = K.sb("pnew", [NS, 2, 16], F32)
        cden = K.sb("cden", [NS, 3, 16], F32)
        for b_ in [VCs] + Vst:
            pool.op([], [b_], lambda e, b_=b_: e.memset(b_[:], 0.0))
        pool.op([], [VCs], lambda e: e.memset(VCs[:, :, :, 64:65], 1.0))
        for i in range(len(Vst)):
            pool.op([], [Vst[i]], lambda e, i=i: e.memset(Vst[i][:, :, 64:65], 1.0))
        pool.op([], [blkT], lambda e: e.memset(blkT[:], 0.0))
        pool.op([], [IDX], lambda e: e.memset(IDX[:], 0))
        rs = {"WBs": 0, "KP": 0, "VP": 0, "KPw": 0, "VPw": 0, "p12": 0, "Pk": 0, "Vst": 0}

        def rt(name, lst):
            rs[name] += 1
            return lst[rs[name] % len(lst)]

        def projS(l, chunks, dst, ncols=128, Wsrc=None, rhs_b=None):
            Wsrc = Wsrc if Wsrc is not None else Wc[l]
            rhs_b = rhs_b if rhs_b is not None else uTs
            groups = []
            i = 0
            while i < len(chunks):
                n = 2 if (i + 1 < len(chunks) and chunks[i + 1] == chunks[i] + 1) else 1
                groups.append((i, chunks[i], n))
                i += n

            def load(g):
                buf = rt("WBs", WBs)
                sp.dma(buf[:, 0:g[2], :, :], Wsrc[g[1]:g[1] + g[2]].rearrange("c p k n -> p c k n"), [Wsrc], [buf], buf)
                return buf
            ps = nxt("A")
            cur = load(groups[0])
            for gi, g in enumerate(groups):
                nb_ = load(groups[gi + 1]) if gi + 1 < len(groups) else None
                for j in range(g[2]):
                    ii = g[0] + j
                    for kc in range(8):
                        mm(ps, ps[0:ncols, ii * NS:(ii + 1) * NS], cur, cur[:, j, kc, 0:ncols], rhs_b, rhs_b[:, kc, :], kc == 0, kc == 7)
                cur = nb_
            n = len(chunks)
            act.op([ps], [dst], lambda e: e.copy(out=dst[0:ncols, 0:n, :], in_=ps[0:ncols, 0:n * NS].rearrange("p (c t) -> p c t", t=NS)))

        def bc3(ap2, n):
            return ap2.to_broadcast([128, 8, n])

        def tokmajor(src_fm, dst_rows):
            pb = nxt("A")
            for c in range(8):
                pe.op([src_fm, cf], [pb], lambda e, c=c: e.transpose(out=pb[0:NS, c * 128:(c + 1) * 128], in_=src_fm[:, c, :], identity=ident_f(128)))
            pb2 = nxt("A")
            for c in range(4, 8):
                pass
            act.op([pb], [stl], lambda e: e.copy(out=stl[0:NS, 0:512], in_=pb[0:NS, 0:512]))
            return pb

        def layer_sample(l, xsrc, xdst):
            load_layer_small(l)
            cv = l * 9
            sp.dma(xsb[:], xsrc[:, :], [xsrc], [xsb], xsb)
            dve.op([], [sms], lambda e: e.memset(sms[:, 0:1], 0.0))
            act.op([xsb, sms], [ubs, sms], lambda e: e.activation(out=ubs[:, :], in_=xsb[:, :], func=AF.Square, accum_out=sms[0:NS, 0:1]))
            dve.op([sms], [sms], lambda e: e.tensor_scalar(out=sms[0:NS, 1:2], in0=sms[0:NS, 0:1], scalar1=1.0 / D, scalar2=EPS, op0=ALU.mult, op1=ALU.add))
            act.op([sms], [sms], lambda e: e.sqrt(out=sms[0:NS, 1:2], in_=sms[0:NS, 1:2]))
            dve.op([sms], [sms], lambda e: e.reciprocal(out=sms[0:NS, 2:3], in_=sms[0:NS, 1:2]))
            dve.op([xsb, sms, gbc[0]], [ubs], lambda e: e.scalar_tensor_tensor(out=ubs[:, :], in0=xsb[:, :], scalar=sms[0:NS, 2:3], in1=gbc[0][0:NS, :], op0=ALU.mult, op1=ALU.mult))
            for c in range(8):
                pe.op([ubs, cbm], [psT], lambda e, c=c: e.transpose(out=psT[:, c * NS:(c + 1) * NS], in_=ubs[0:NS, c * 128:(c + 1) * 128], identity=cbm[0:NS, 0:NS]))
            act.op([psT], [uTs], lambda e: e.copy(out=uTs[:, :, :], in_=psT[:, 0:8 * NS].rearrange("p (c t) -> p c t", t=NS)))
            for blk in range(3):
                sp.dma(WKs[:], Wkv[l][blk], [Wkv[l]], [WKs], WKs)
                ps = nxt("A")
                for kc in range(8):
                    mm(ps, ps[0:NS, :], uTs, uTs[:, kc, :], WKs, WKs[:, kc, :], kc == 0, kc == 7)
                act.op([ps], [kvtok], lambda e, ps=ps, blk=blk: e.copy(out=kvtok[:, blk * 512:(blk + 1) * 512], in_=ps[0:NS, :]))
            sp.dma(skv_out[l, :, :], kvtok[:, :], [kvtok], [skv_out], kvtok)
            for blk in range(2):
                sp.dma(WKs[:], Wqt[l][blk], [Wqt[l]], [WKs], WKs)
                ps = nxt("A")
                for kc in range(8):
                    mm(ps, ps[0:NS, :], uTs, uTs[:, kc, :], WKs, WKs[:, kc, :], kc == 0, kc == 7)
                act.op([ps], [qtok], lambda e, ps=ps, blk=blk: e.mul(out=qtok[:, blk * 512:(blk + 1) * 512], in_=ps[0:NS, :], mul=0.125))
            projS(l, [CH_Q + c for c in range(8)], FM["t0"])
            act.op([FM["t0"]], [QTs], lambda e: e.mul(out=QTs[:, :, :], in_=FM["t0"][:, :, :], mul=0.125))
            projS(l, [CH_LRUX + c for c in range(8)], FM["xl"])
            projS(l, [CH_LRUG + c for c in range(8)], FM["lg"])
            projS(l, [CH_POOLX + c for c in range(8)], FM["px"])
            projS(l, [CH_POOLG + c for c in range(8)], FM["pg"])
            projS(l, [CH_NSAG + c for c in range(8)], FM["ng"])
            projS(l, [CH_BG], gat_f, ncols=48) if False else None
            psg = nxt("A")
            bw = rt("WBs", WBs)
            sp.dma(bw[:, 0:1, :, :], Wc[l][CH_BG:CH_BG + 1].rearrange("c p k n -> p c k n"), [Wc[l]], [bw], bw)
            for kc in range(8):
                mm(psg, psg[0:48, 0:NS], bw, bw[:, 0, kc, 0:48], uTs, uTs[:, kc, :], kc == 0, kc == 7)
            act.op([psg], [gat_f], lambda e: e.activation(out=gat_f[:, :], in_=psg[0:48, 0:NS], func=AF.Sigmoid))
            pe.op([gat_f, cf], [psM], lambda e: e.transpose(out=psM[0:NS, 0:48], in_=gat_f[0:48, 0:NS], identity=ident_f(48)))
            act.op([psM], [gat_t], lambda e: e.copy(out=gat_t[:, :], in_=psM[0:NS, 0:48]))
            for n_ in range(3):
                projS(l, [CH_MG + n_ * 8 + dc for dc in range(8)], FM["t0"])
                act.op([FM["t0"]], [mgs], lambda e, n_=n_: e.activation(out=mgs[:, n_ * 8:(n_ + 1) * 8, :], in_=FM["t0"][:, :, :], func=AF.Sigmoid))

            if SST <= 1:
                return
            def load_fm(src_ap, rows, dst, width):
                sp.dma(stl[0:rows, :], src_ap, [st_conv, st_lru, st_pool], [stl], stl)
                for c in range(8):
                    pe.op([stl, cf], [psM], lambda e, c=c: e.transpose(out=psM[:, 0:rows], in_=stl[0:rows, c * 128:(c + 1) * 128], identity=ident_f(rows)))
                    act.op([psM], [dst], lambda e, c=c: e.copy(out=dst[:, c, width[0]:width[0] + rows], in_=psM[:, 0:rows]))
            load_fm(st_conv[l, :, :], NS * 3, CS, (0,))
            load_fm(st_lru[l, :, :], NS, HS, (0,))
            half_ = (NS * 15 + 1) // 2 if NS * 15 > 128 else NS * 15
            r0_ = 0
            while r0_ < NS * 15:
                rws = min(half_, NS * 15 - r0_)
                load_fm(st_pool[l, r0_:r0_ + rws, :], rws, PSs, (r0_,))
                r0_ += rws

            xl, t0_, t1_, t2_, t3_, hN = FM["xl"], FM["t0"], FM["t1"], FM["t2"], FM["t3"], FM["h"]
            CS4 = CS[:, :, :].rearrange("p c (b k) -> p c b k", k=3)
            dve.op([xl, CV], [t0_], lambda e: e.tensor_tensor(out=t0_[:, :, :], in0=xl[:, :, :], in1=bc3(CV[:, :, cv + 3:cv + 4], NS), op=ALU.mult))
            for k in range(3):
                dve.op([CS, CV], [t1_], lambda e, k=k: e.tensor_tensor(out=t1_[:, :, :], in0=CS4[:, :, :, k], in1=bc3(CV[:, :, cv + k:cv + k + 1], NS), op=ALU.mult))
                dve.op([t0_, t1_], [t0_], lambda e: e.tensor_tensor(out=t0_[:, :, :], in0=t0_[:, :, :], in1=t1_[:, :, :], op=ALU.add))
            dve.op([t0_, CV], [t0_], lambda e: e.tensor_tensor(out=t0_[:, :, :], in0=t0_[:, :, :], in1=bc3(CV[:, :, cv + 4:cv + 5], NS), op=ALU.add))
            act.op([t0_], [xcbs], lambda e: e.copy(out=xcbs[:, :, :], in_=t0_[:, :, :]))
            pr = nxt("S")
            for ax in range(2):
                for c in range(8):
                    mm(pr, pr[:, (ax * 8 + c) * NS:(ax * 8 + c + 1) * NS], WRG, WRG[:, ax, c, :], xcbs, xcbs[:, c, :], True, True)
            prv = pr[:, 0:16 * NS].rearrange("p (a c t) -> p a c t", a=2, t=NS)
            dve.op([pr, CV], [t1_], lambda e: e.tensor_tensor(out=t1_[:, :, :], in0=prv[:, 0, :, :], in1=bc3(CV[:, :, cv + 5:cv + 6], NS), op=ALU.add))
            dve.op([pr, CV], [t2_], lambda e: e.tensor_tensor(out=t2_[:, :, :], in0=prv[:, 1, :, :], in1=bc3(CV[:, :, cv + 6:cv + 7], NS), op=ALU.add))
            act.op([t1_], [t1_], lambda e: e.activation(out=t1_[:, :, :], in_=t1_[:, :, :], func=AF.Sigmoid))
            act.op([t2_], [t2_], lambda e: e.activation(out=t2_[:, :, :], in_=t2_[:, :, :], func=AF.Sigmoid))
            dve.op([t1_, CV], [t1_], lambda e: e.tensor_tensor(out=t1_[:, :, :], in0=t1_[:, :, :], in1=bc3(CV[:, :, 18 + l:19 + l], NS), op=ALU.mult))
            act.op([t1_], [t1_], lambda e: e.activation(out=t1_[:, :, :], in_=t1_[:, :, :], func=AF.Exp))
            dve.op([t1_], [t3_], lambda e: e.scalar_tensor_tensor(out=t3_[:, :, :], in0=t1_[:, :, :], scalar=-1.0, in1=t1_[:, :, :], op0=ALU.mult, op1=ALU.mult))
            act.op([t3_], [t3_], lambda e: e.activation(out=t3_[:, :, :], in_=t3_[:, :, :], func=AF.Sqrt, bias=1.0, scale=1.0))
            dve.op([t2_, t0_], [t2_], lambda e: e.tensor_tensor(out=t2_[:, :, :], in0=t2_[:, :, :], in1=t0_[:, :, :], op=ALU.mult))
            dve.op([t2_, t3_], [t2_], lambda e: e.tensor_tensor(out=t2_[:, :, :], in0=t2_[:, :, :], in1=t3_[:, :, :], op=ALU.mult))
            dve.op([t1_, HS], [t1_], lambda e: e.tensor_tensor(out=t1_[:, :, :], in0=t1_[:, :, :], in1=HS[:, :, :], op=ALU.mult))
            dve.op([t1_, t2_], [hN], lambda e: e.tensor_tensor(out=hN[:, :, :], in0=t1_[:, :, :], in1=t2_[:, :, :], op=ALU.add))
            act.op([FM["lg"]], [t3_], lambda e: e.activation(out=t3_[:, :, :], in_=FM["lg"][:, :, :], func=AF.Silu))
            dve.op([hN, t3_], [zbs[0]], lambda e: e.tensor_tensor(out=zbs[0][:, :, :], in0=hN[:, :, :], in1=t3_[:, :, :], op=ALU.mult))

            px = FM["px"]
            PS4 = PSs[:, :, :].rearrange("p c (b k) -> p c b k", k=15)
            for gq, w in enumerate(POOL_WIN):
                cs_ = slice(2 * gq, 2 * gq + 2)
                dve.op([PSs], [t0_], lambda e, cs_=cs_, w=w: e.tensor_reduce(out=t0_[:, cs_, :], in_=PS4[:, cs_, :, 15 - (w - 1):15], axis=AX.X, op=ALU.add))
                dve.op([t0_, px], [t0_], lambda e, cs_=cs_: e.tensor_tensor(out=t0_[:, cs_, :], in0=t0_[:, cs_, :], in1=px[:, cs_, :], op=ALU.add))
                dve.op([t0_, px], [pdb], lambda e, cs_=cs_, w=w: e.scalar_tensor_tensor(out=pdb[:, cs_, :], in0=t0_[:, cs_, :], scalar=1.0 / w, in1=px[:, cs_, :], op0=ALU.mult, op1=ALU.subtract))
            pm = nxt("S")
            for c in range(8):
                gq, eh = c // 2, c % 2
                for dh in range(2):
                    mm(pm, pm[:, c * NS:(c + 1) * NS], WPL, WPL[:, gq, dh, eh * 128:(eh + 1) * 128], pdb, pdb[:, 2 * gq + dh, :], dh == 0, dh == 1)
            act.op([FM["pg"]], [t3_], lambda e: e.activation(out=t3_[:, :, :], in_=FM["pg"][:, :, :], func=AF.Silu))
            dve.op([pm, CV], [t0_], lambda e: e.tensor_tensor(out=t0_[:, :, :], in0=pm[:, 0:8 * NS].rearrange("p (c t) -> p c t", t=NS), in1=bc3(CV[:, :, cv + 8:cv + 9], NS), op=ALU.mult))
            dve.op([t0_, t3_], [zbs[1]], lambda e: e.tensor_tensor(out=zbs[1][:, :, :], in0=t0_[:, :, :], in1=t3_[:, :, :], op=ALU.mult))

            def rows_out(src_fm, dst_ap, dst_buf):
                for hh in range(2):
                    pb = nxt("A")
                    for c4 in range(4):
                        c = hh * 4 + c4
                        pe.op([src_fm, cf], [pb], lambda e, c=c, c4=c4, pb=pb: e.transpose(out=pb[0:NS, c4 * 128:(c4 + 1) * 128], in_=src_fm[:, c, :], identity=ident_f(128)))
                    act.op([pb], [stl], lambda e, pb=pb, hh=hh: e.copy(out=stl[0:NS, hh * 512:(hh + 1) * 512], in_=pb[0:NS, :]))
                sp.dma(dst_ap, stl[0:NS, :], [stl], [dst_buf], stl)
            sconv3 = sconv_out[l, :, :].rearrange("(b k) d -> b k d", k=3)
            sp.dma(sconv3[:, 0:2, :], st_conv[l, :, :].rearrange("(b k) d -> b k d", k=3)[:, 1:3, :], [st_conv], [sconv_out], stl)
            rows_out(xl, sconv3[:, 2, :], sconv_out)
            rows_out(hN, slru_out[l, :, :], slru_out)
            spool3 = spool_out[l, :, :].rearrange("(b k) d -> b k d", k=15)
            sp.dma(spool3[:, 0:14, :], st_pool[l, :, :].rearrange("(b k) d -> b k d", k=15)[:, 1:15, :], [st_pool], [spool_out], stl)
            rows_out(px, spool3[:, 14, :], spool_out)
            for i in range(2):
                sp.dma(swin_out[i][l, :, 0:WBUF - 1, :], winc[i][l, :, 1:WBUF, :], [winc[i]], [swin_out[i]], stl)
                sp.dma(swin_out[i][l, :, WBUF - 1, :], kvtok[:, 1024 + i * 256:1280 + i * 256], [kvtok], [swin_out[i]], kvtok)

            if SST <= 2:
                return
            for kvi in range(2):
                i_ = l * 2 + kvi
                for fsx in range(2):
                    src = cmp_pos[l, kvi, fsx * 16:(fsx + 1) * 16, :]
                    for ii in range(8):
                        for g in range(4):
                            sp.dma(WT[ii * 16:(ii + 1) * 16, kvi, fsx, g * 64:(g + 1) * 64], src, [cmp_pos], [WT], WT)
            if SST <= 3:
                return
            for b in range(NS):
                sample_attn(l, b)
            if SST <= 10:
                return
            sample_combine(l)
            merge_out(l, xsrc, xdst)

        def sample_attn(l, b):
            sp.dma(ptb[:], ptab[b:b + 1, :].partition_broadcast(128), [ptab], [ptb], ptb)
            dve.op([ptb], [ptf], lambda e: e.tensor_copy(out=ptf[:], in_=ptb[:]))
            dve.op([csb], [sms], lambda e: e.tensor_scalar(out=sms[:, 8:9], in0=csb[:, o_p:o_p + 1], scalar1=float(l * NPHYS * 128), scalar2=None, op0=ALU.add))
            dve.op([ptf, sms], [ptf], lambda e: e.tensor_scalar(out=ptf[:], in0=ptf[:], scalar1=128.0, scalar2=sms[:, 8:9], op0=ALU.mult, op1=ALU.add))
            dve.op([ptf], [IDX], lambda e: e.tensor_copy(out=IDX[:], in_=ptf[:]))
            if SST <= 4:
                return
            for kvi in range(2):
                i_ = l * 2 + kvi
                banks = [psA[0], psA[1], psS[0], psS[1]]
                for pg in range(NPG):
                    kp = rt("KP", KP)
                    idma(pool, kp[:, :], pools[kvi][:, :], IDX[:, pg:pg + 1], [pools[kvi], IDX], [kp], kp)
                    pp = rt("p12", p12)
                    for fsx in range(2):
                        dve.op([kp, WT], [pp], lambda e, fsx=fsx, kp=kp, pp=pp: e.tensor_tensor(out=pp[:, fsx, :], in0=kp[:, :], in1=WT[:, kvi, fsx, :], op=ALU.mult))
                    for fsx in range(2):
                        for cc in range(2):
                            bk = banks[fsx * 2 + cc]
                            mm(bk, bk[:, pg * 8:(pg + 1) * 8], pp, pp[:, fsx, cc * 128:(cc + 1) * 128], indb, indb[:, :], True, True)
                for q_ in range(4):
                    act.op([banks[q_]], [fsb], lambda e, q_=q_: e.copy(out=fsb[:, q_, :], in_=banks[q_][:, 0:NMS * 128]))
                NC_ = NMS * 128
                for cc in range(2):
                    dve.op([fsb], [blkT], lambda e, cc=cc: e.tensor_tensor(out=blkT[:, cc, 1:NC_], in0=fsb[:, cc, 0:NC_ - 1], in1=fsb[:, 2 + cc, 1:NC_], op=ALU.add))
                if kvi == 0:
                    for cc in range(2):
                        pk = nxt("O")
                        mm(pk, pk[:, 0:NC_], PHI, PHI[:, i_, :], blkT, blkT[:, cc, :], True, True)
                        act.op([pk], [KCs], lambda e, cc=cc, pk=pk: e.copy(out=KCs[:, cc, :], in_=pk[:, 0:NC_]))
                else:
                    for mt in range(NMS):
                        pk = nxt("O")
                        for cc in range(2):
                            mm(pk, pk[:, cc * 128:(cc + 1) * 128], blkT, blkT[:, cc, mt * 128:(mt + 1) * 128], PHI, PHI[:, i_, :], True, True)
                        act.op([pk], [VCs], lambda e, mt=mt, pk=pk: e.copy(out=VCs[:, mt, :, 0:64], in_=pk[:, 0:256].rearrange("p (g d) -> p g d", d=64)))
            if SST <= 5:
                return
            pSh = [nxt("S"), nxt("S")]
            for hf in range(2):
                for mt in range(NMS):
                    for g in (hf, hf + 2):
                        cq = 4 * (g // 2)
                        mm(pSh[hf], pSh[hf][:, mt * 16 + 4 * g:mt * 16 + 4 * g + 4], KCs, KCs[hf * 64:hf * 64 + 64, g // 2, mt * 128:(mt + 1) * 128],
                           QTs, QTs[hf * 64:hf * 64 + 64, cq:cq + 4, b], True, True)
            pat = "p (m gi hf r) -> p m gi hf r"
            S16v = S16[:, :, :].rearrange("p m (gi hf r) -> p m gi hf r", hf=2, r=4)
            absv = csb[:, o_abs:o_abs + NMS * 16].rearrange(pat, gi=2, hf=2, r=4)
            for hf in range(2):
                pv_ = pSh[hf][:, 0:NMS * 16].rearrange(pat, gi=2, hf=2, r=4)
                dve.op([pSh[hf], csb], [S16], lambda e, hf=hf, pv_=pv_: e.tensor_tensor(out=S16v[:, :, :, hf, :], in0=pv_[:, :, :, hf, :], in1=absv[:, :, :, hf, :], op=ALU.add))
            act.op([S16], [P16], lambda e: e.activation(out=P16[:, :, :], in_=S16[:, :, :], func=AF.Exp))
            pD = nxt("O")
            pO = nxt("O")
            for mt in range(NMS):
                mm(pD, pD[:, 0:16], ones_bf, ones_bf[:, :], P16, P16[:, mt, :], mt == 0, mt == NMS - 1)
            for mt in range(NMS):
                mm(pO, pO[0:16, 0:260], P16, P16[:, mt, :], VCs, VCs[:, mt, :, :], mt == 0, mt == NMS - 1)
            act.op([pO], [O16sb], lambda e: e.copy(out=O16sb[:, 0, :], in_=pO[0:16, 0:260]))
            dve.op([pD], [rd16], lambda e: e.tensor_scalar(out=rd16[:, :], in0=pD[:, 0:16], scalar1=1e-30, scalar2=None, op0=ALU.max))
            dve.op([rd16], [rd16], lambda e: e.reciprocal(out=rd16[:, :], in_=rd16[:, :]))
            for mt in range(NMS):
                dve.op([P16, rd16], [S16], lambda e, mt=mt: e.tensor_tensor(out=S16[:, mt, :], in0=P16[:, mt, :], in1=rd16[:, :], op=ALU.mult))
            dve.op([S16], [Pn4f], lambda e: e.tensor_reduce(out=Pn4f[:, :, :], in_=S16[:, :, :].rearrange("p m (g r) -> p m g r", r=4), axis=AX.X, op=ALU.add))
            act.op([Pn4f], [Pn4], lambda e: e.copy(out=Pn4[:, :, :], in_=Pn4f[:, :, :]))
            p4 = nxt("S")
            for mt in range(NMS):
                mm(p4, p4[0:4, 0:128], Pn4, Pn4[:, mt, :], gsb, gsb[:, mt * 128:(mt + 1) * 128], mt == 0, mt == NMS - 1)
            if SST <= 6:
                return
            dve.op([p4, csb], [sc4], lambda e: e.tensor_tensor(out=sc4[:, :], in0=p4[0:4, 0:128], in1=csb[0:4, o_fb:o_fb + 128], op=ALU.add))
            NBs = NPG * 2
            dve.op([sc4], [m84], lambda e: e.max(out=m84[:, 0:8], in_=sc4[:, 0:NBs]))
            dve.op([sc4, m84], [wk4], lambda e: e.match_replace(out=wk4[:, 0:NBs], in_to_replace=m84[:, 0:8], in_values=sc4[:, 0:NBs], imm_value=-3.0e38))
            dve.op([wk4], [m84], lambda e: e.max(out=m84[:, 8:16], in_=wk4[:, 0:NBs]))
            dve.op([], [mb4], lambda e: e.memset(mb4[:, :], 0.0))
            dve.op([sc4, m84], [mb4], lambda e: e.tensor_scalar(out=mb4[:, 0:NBs], in0=sc4[:, 0:NBs], scalar1=m84[:, 14:15], scalar2=NEGB, op0=ALU.is_lt, op1=ALU.mult))
            pe.op([mb4, cf], [psM], lambda e: e.transpose(out=psM[:, 0:4], in_=mb4[0:4, :], identity=ident_f(4)))
            act.op([psM], [MB4T], lambda e: e.copy(out=MB4T[:, :], in_=psM[:, 0:4]))
            if SST <= 7:
                return
            for hh in range(2):
                pq = nxt("A")
                mm(pq, pq[:, :], cf, cf[0:NS, o_id + b:o_id + b + 1].to_broadcast([NS, 128]), qtok, qtok[0:NS, hh * 512:(hh + 1) * 512], True, True)
                act.op([pq], [QBs], lambda e, pq=pq, hh=hh: e.copy(out=QBs[:, hh * 512:(hh + 1) * 512], in_=pq[:, :]))

            def keytile(kp, vp, bias_ap, mask_pg, pO_, first, last_):
                dve.op([kp, QBs], [prod], lambda e: e.tensor_tensor(out=prod[:, :].rearrange("p (g r d) -> p g r d", r=4, d=64),
                                                                    in0=kp[:, :].rearrange("p (g o d) -> p g o d", o=1, d=64).to_broadcast([128, 4, 4, 64]),
                                                                    in1=QBs[:, :].rearrange("p (g r d) -> p g r d", r=4, d=64), op=ALU.mult))
                dve.op([prod], [s16], lambda e: e.tensor_reduce(out=s16[:, :], in_=prod[:, :].rearrange("p (h d) -> p h d", d=64), axis=AX.X, op=ALU.add))
                dve.op([s16, csb], [s16], lambda e: e.tensor_tensor(out=s16[:, :], in0=s16[:, :], in1=bias_ap, op=ALU.add))
                if mask_pg is not None:
                    a2, v = ((2 * mask_pg) // 64) % 2, mask_pg % 32
                    pm_ = nxt("S")
                    mm(pm_, pm_[:, 0:4], cb, cb[a2 * 64:a2 * 64 + 64, o_selh + v * 128:o_selh + (v + 1) * 128], MB4T, MB4T[a2 * 64:a2 * 64 + 64, :], True, True)
                    act.op([pm_], [mkb], lambda e: e.copy(out=mkb[:, :], in_=pm_[:, 0:4]))
                    dve.op([s16, mkb], [s16], lambda e: e.tensor_tensor(out=s16[:, :].rearrange("p (g r) -> p g r", r=4), in0=s16[:, :].rearrange("p (g r) -> p g r", r=4),
                                                                        in1=mkb[:, :].rearrange("p (g o) -> p g o", o=1).to_broadcast([128, 4, 4]), op=ALU.add))
                pk_ = rt("Pk", Pk)
                act.op([s16], [pk_], lambda e: e.activation(out=pk_[:, :], in_=s16[:, :], func=AF.Exp))
                vs_ = rt("Vst", Vst)
                act.op([vp], [vs_], lambda e: e.copy(out=vs_[:, :, 0:64], in_=vp[:, :].rearrange("p (g d) -> p g d", d=64)))
                mm(pO_, pO_[0:16, 0:260], pk_, pk_[:, :], vs_, vs_[:, :, :], first, last_)

            if SST <= 8:
                return
            pOs = nxt("O")
            for pg in range(NPG):
                kp, vp = rt("KP", KP), rt("VP", VP)
                idma(pool, kp[:, :], pools[2][:, :], IDX[:, pg:pg + 1], [pools[2], IDX], [kp], kp)
                idma(pool, vp[:, :], pools[3][:, :], IDX[:, pg:pg + 1], [pools[3], IDX], [vp], vp)
                keytile(kp, vp, csb[:, o_abk + pg * 16:o_abk + (pg + 1) * 16], pg, pOs, pg == 0, pg == NPG - 1)
            act.op([pOs], [O16sb], lambda e: e.copy(out=O16sb[:, 1, :], in_=pOs[0:16, 0:260]))
            if SST <= 9:
                return
            pOw = nxt("O")
            NWT = WBUF // 128
            for t in range(NWT):
                kp, vp = rt("KPw", KPw), rt("VPw", VPw)
                sp.dma(kp[:, :], winc[0][l, b, t * 128:(t + 1) * 128, :], [winc[0]], [kp], kp)
                sp.dma(vp[:, :], winc[1][l, b, t * 128:(t + 1) * 128, :], [winc[1]], [vp], vp)
                keytile(kp, vp, csb[:, o_abw + t * 16:o_abw + (t + 1) * 16], None, pOw, t == 0, t == NWT - 1)
            act.op([pOw], [O16sb], lambda e: e.copy(out=O16sb[:, 2, :], in_=pOw[0:16, 0:260]))
            for br in range(3):
                for g in range(4):
                    sp.dma(OS[b:b + 1, br, 4 * g:4 * g + 4, :], O16sb[4 * g:4 * g + 4, br, g * 65:(g + 1) * 65], [O16sb], [OS], OS)

        def sample_combine(l):
            q4 = qtok[:, :].rearrange("p (g r d) -> p g r d", r=4, d=64)
            for i, (ko, vo) in enumerate(((512, 768), (1024, 1280))):
                kn = kvtok[:, ko:ko + 256].rearrange("p (g o d) -> p g o d", o=1, d=64).to_broadcast([NS, 4, 4, 64])
                dve.op([qtok, kvtok], [tt[0]], lambda e, kn=kn: e.tensor_tensor(out=tt[0][:, :, :].rearrange("p (g r) d -> p g r d", r=4), in0=q4, in1=kn, op=ALU.mult))
                dve.op([tt[0]], [pnew], lambda e, i=i: e.tensor_reduce(out=pnew[:, i, :], in_=tt[0][:, :, :], axis=AX.X, op=ALU.add))
            act.op([pnew], [pnew], lambda e: e.activation(out=pnew[:, :, :], in_=pnew[:, :, :], func=AF.Exp))
            g3 = gat_t[:, :].rearrange("p (h b) -> p h b", b=3)
            for br in range(3):
                if br == 0:
                    dve.op([OS], [cden], lambda e: e.tensor_scalar(out=cden[:, 0, :], in0=OS[:, 0, :, 64], scalar1=1e-30, scalar2=None, op0=ALU.max))
                else:
                    dve.op([OS, pnew], [cden], lambda e, br=br: e.tensor_tensor(out=cden[:, br, :], in0=OS[:, br, :, 64], in1=pnew[:, br - 1, :], op=ALU.add))
                dve.op([cden], [cden], lambda e, br=br: e.reciprocal(out=cden[:, br, :], in_=cden[:, br, :]))
                dve.op([cden, gat_t], [cden], lambda e, br=br: e.tensor_tensor(out=cden[:, br, :], in0=cden[:, br, :], in1=g3[:, :, br], op=ALU.mult))
                num = tt[0]
                if br == 0:
                    dve.op([OS], [num], lambda e: e.tensor_copy(out=num[:, :, :], in_=OS[:, 0, :, 0:64]))
                else:
                    vo = 768 if br == 1 else 1280
                    vn = kvtok[:, vo:vo + 256].rearrange("p (g o d) -> p g o d", o=1, d=64).to_broadcast([NS, 4, 4, 64])
                    dve.op([kvtok, pnew], [num], lambda e, br=br, vn=vn: e.tensor_tensor(out=num[:, :, :].rearrange("p (g r) d -> p g r d", r=4), in0=vn,
                                                                                  in1=pnew[:, br - 1, :].rearrange("p (g r o) -> p g r o", r=4, o=1).to_broadcast([NS, 4, 4, 64]), op=ALU.mult))
                    dve.op([num, OS], [num], lambda e, br=br: e.tensor_tensor(out=num[:, :, :], in0=num[:, :, :], in1=OS[:, br, :, 0:64], op=ALU.add))
                dve.op([num, cden], [tt[1]], lambda e, br=br: e.tensor_tensor(out=tt[1][:, :, :], in0=num[:, :, :],
                                                                           in1=cden[:, br, :].rearrange("p (h o) -> p h o", o=1).to_broadcast([NS, 16, 64]), op=ALU.mult))
                if br == 0:
                    dve.op([tt[1]], [onat], lambda e: e.tensor_copy(out=onat[:, :, :], in_=tt[1][:, :, :]))
                else:
                    dve.op([tt[1], onat], [onat], lambda e: e.tensor_tensor(out=onat[:, :, :], in0=onat[:, :, :], in1=tt[1][:, :, :], op=ALU.add))
            dve.op([onat], [operm], lambda e: e.tensor_copy(out=operm[:, :, :].rearrange("p (hi r e) d -> p hi r e d", r=4, e=2),
                                                           in_=onat[:, :, :].rearrange("p (hi e r) d -> p hi r e d", e=2, r=4)))
            for c in range(8):
                pe.op([operm, cbm], [psT], lambda e, c=c: e.transpose(out=psT[:, c * NS:(c + 1) * NS], in_=operm[0:NS, 2 * c:2 * c + 2, :].rearrange("p h d -> p (h d)"), identity=cbm[0:NS, 0:NS]))
            act.op([FM["ng"]], [FM["t3"]], lambda e: e.activation(out=FM["t3"][:, :, :], in_=FM["ng"][:, :, :], func=AF.Silu))
            dve.op([psT, FM["t3"]], [zbs[2]], lambda e: e.tensor_tensor(out=zbs[2][:, :, :], in0=psT[:, 0:8 * NS].rearrange("p (c t) -> p c t", t=NS), in1=FM["t3"][:, :, :], op=ALU.mult))

        def merge_out(l, xsrc, xdst):
            for n_ in range(3):
                projS(l, [n_ * 8 + dc for dc in range(8)], FM["t0"], Wsrc=Wb[l], rhs_b=zbs[n_])
                if n_ == 0:
                    dve.op([FM["t0"], mgs], [maccs], lambda e: e.tensor_tensor(out=maccs[:, :, :], in0=FM["t0"][:, :, :], in1=mgs[:, 0:8, :], op=ALU.mult))
                else:
                    dve.op([FM["t0"], mgs], [FM["t1"]], lambda e, n_=n_: e.tensor_tensor(out=FM["t1"][:, :, :], in0=FM["t0"][:, :, :], in1=mgs[:, n_ * 8:(n_ + 1) * 8, :], op=ALU.mult))
                    dve.op([FM["t1"], maccs], [maccs], lambda e: e.tensor_tensor(out=maccs[:, :, :], in0=maccs[:, :, :], in1=FM["t1"][:, :, :], op=ALU.add))
            act.op([maccs], [mergs], lambda e: e.copy(out=mergs[:, :, :], in_=maccs[:, :, :]))
            sp.dma(WKs[:], Wo[l][:, :, 0:512], [Wo[l]], [WKs], WKs)
            sp.dma(WK2[:], Wo[l][:, :, 512:1024], [Wo[l]], [WK2], WK2)
            wo = [WKs, WK2]
            po = [nxt("A"), nxt("A")]
            dve.op([], [sms], lambda e: e.memset(sms[:, 16:18], 0.0))
            for eh in range(2):
                for dc in range(8):
                    mm(po[eh], po[eh][0:NS, :], mergs, mergs[:, dc, :], wo[eh], wo[eh][:, dc, :], dc == 0, dc == 7)
                act.op([po[eh], sms], [ubs, sms], lambda e, eh=eh: e.activation(out=ubs[:, eh * 512:(eh + 1) * 512], in_=po[eh][0:NS, :], func=AF.Square, accum_out=sms[0:NS, 16 + eh:17 + eh]))
            dve.op([sms], [sms], lambda e: e.tensor_tensor(out=sms[0:NS, 18:19], in0=sms[0:NS, 16:17], in1=sms[0:NS, 17:18], op=ALU.add))
            dve.op([sms], [sms], lambda e: e.tensor_scalar(out=sms[0:NS, 19:20], in0=sms[0:NS, 18:19], scalar1=1.0 / D, scalar2=EPS, op0=ALU.mult, op1=ALU.add))
            act.op([sms], [sms], lambda e: e.sqrt(out=sms[0:NS, 19:20], in_=sms[0:NS, 19:20]))
            dve.op([sms], [sms], lambda e: e.reciprocal(out=sms[0:NS, 20:21], in_=sms[0:NS, 19:20]))
            for eh in range(2):
                dve.op([po[eh], sms, gbc[1]], [ysb], lambda e, eh=eh: e.scalar_tensor_tensor(out=ysb[:, eh * 512:(eh + 1) * 512], in0=po[eh][0:NS, :], scalar=sms[0:NS, 20:21], in1=gbc[1][0:NS, eh * 512:(eh + 1) * 512], op0=ALU.mult, op1=ALU.mult))
            dve.op([ysb, xsb], [ysb], lambda e: e.tensor_tensor(out=ysb[:, :], in0=ysb[:, :], in1=xsb[:, :], op=ALU.add))
            sp.dma(xdst[:, :], ysb[:, :], [ysb], [xdst], ysb)

        layer_sample(0, xs_in, xs1)
        layer_sample(1, xs1, ys_out)

    init_state()
    stage = getattr(cfg, "STAGE", 99)
    do_prompt = stage >= 2 and stage != 45
    if do_prompt:
        layer_prompt(0, x_in, x1 if stage >= 3 else y_out)
    if do_prompt and stage >= 3:
        layer_prompt(1, x1, y_out)
    if stage >= 40:
        K.barrier()
        K.release_to(MARK_PROMPT)
        sample_phase()
    K.finish()
    return K


def _core_inputs(cfg, inp, b):
    f = lambda a: np.ascontiguousarray(np.asarray(a), dtype=np.float32)
    vec = []
    for l in range(2):
        vec += [f(inp["conv_w"])[l, k] for k in range(4)]
        vec += [f(inp["conv_b"])[l], f(inp["b_rg_a"])[l], f(inp["b_rg_x"])[l], f(inp["lru_lambda"])[l], f(inp["pool_scale"])[l]]
    NS = cfg.NS
    sl = slice(b * NS, (b + 1) * NS)
    m = {
        "x": f(inp["x_prompt"])[b],
        "g_pre": f(inp["g_pre"]), "g_post": f(inp["g_post"]), "w_in": f(inp["w_in"]),
        "vecs": np.ascontiguousarray(np.stack(vec, 0)),
        "w_rg": np.ascontiguousarray(np.stack([f(inp["w_rg_a"]), f(inp["w_rg_x"])], axis=1)),
        "w_pool": f(inp["w_pool"]),
        "cmp_pos": np.ascontiguousarray(np.stack([f(inp["cmp_pos_k"]), f(inp["cmp_pos_v"])], axis=1)),
        "cmp_phi": np.ascontiguousarray(np.stack([f(inp["cmp_phi_k"]), f(inp["cmp_phi_v"])], axis=1)),
        "w_branch": f(inp["w_branch"]), "w_out": f(inp["w_out"]),
        "xs_in": np.ascontiguousarray(f(inp["x_sample"])[sl, 0, :]),
        "st_conv": np.ascontiguousarray(f(inp["state_conv"])[:, sl].reshape(2, NS * 3, D)),
        "st_lru": np.ascontiguousarray(f(inp["state_lru"])[:, sl]),
        "st_pool": np.ascontiguousarray(f(inp["state_pool"])[:, sl].reshape(2, NS * 15, D)),
        "ptab": np.ascontiguousarray(np.asarray(inp["page_table"])[sl].astype(np.int32)),
    }
    for i, k in enumerate(("cache_cmp_k", "cache_cmp_v", "cache_sel_k", "cache_sel_v")):
        m["pool%d" % i] = f(inp[k]).reshape(-1, 256)
    for i, k in enumerate(("cache_win_k", "cache_win_v")):
        a = f(inp[k])
        m["winc%d" % i] = np.ascontiguousarray(a[:, sl].reshape(2, NS, a.shape[2], 256))
    m.update(make_consts(cfg))
    return m


def run_cfg(cfg, inp):
    K = build(cfg)
    in_maps = [_core_inputs(cfg, inp, b) for b in range(cfg.NCORES)]
    res = run_bass_kernel_spmd(K.nc, in_maps, core_ids=list(range(cfg.NCORES)))
    r = res.results
    S = cfg.S
    B = cfg.NCORES
    NS = cfg.NS
    wb = min(512, cfg.P)
    cat = lambda name: np.concatenate([r[b][name] for b in range(B)], axis=0)
    cat1 = lambda name: np.concatenate([r[b][name] for b in range(B)], axis=1)
    y_p = np.stack([r[b]["y"] for b in range(B)], 0)
    outs = [y_p, cat("ys")[:, None, :]]
    skv = cat1("skv")
    for i in range(4):
        outs.append(np.stack([r[b]["kvo%d" % i].reshape(2, S, 4, 64) for b in range(B)], 1))
        outs.append(np.ascontiguousarray(skv[:, :, i * 256:(i + 1) * 256]).reshape(2, -1, 1, 4, 64))
    for i in range(2):
        outs.append(np.stack([r[b]["wino%d" % i].reshape(2, 512, 4, 64) for b in range(B)], 1))
        outs.append(cat1("swin%d" % i).reshape(2, -1, wb, 4, 64))
    st = np.stack([r[b]["st"] for b in range(B)], 1)
    outs.append(np.ascontiguousarray(st[:, :, 0:3]))
    outs.append(cat1("sconv").reshape(2, -1, 3, D))
    outs.append(np.ascontiguousarray(st[:, :, 3]))
    outs.append(cat1("slru"))
    outs.append(np.ascontiguousarray(st[:, :, 4:19]))
    outs.append(cat1("spool").reshape(2, -1, 15, D))
    return tuple(np.ascontiguousarray(o, dtype=np.float32) for o in outs)


def kernel(**inputs):
    cfg = Cfg()
    return run_cfg(cfg, inputs)
```

```python
import numpy as np
import concourse.bass as bass
import concourse.mybir as mybir
from concourse.bass_utils import run_bass_kernel_spmd

F32 = mybir.dt.float32
BF16 = mybir.dt.bfloat16
I32 = mybir.dt.int32
AF = mybir.ActivationFunctionType
ALU = mybir.AluOpType
AX = mybir.AxisListType

D = 1024
IN_W = 10800
NEGB = -30000.0
THR_SKIP = 80.0
HPERM = [0, 4, 1, 5, 2, 6, 3, 7, 8, 12, 9, 13, 10, 14, 11, 15]
SLOPES = [2.0 ** (-8.0 * (h + 1) / 16.0) for h in range(16)]
POOL_WIN = (2, 4, 8, 16)


class Ev:
    __slots__ = ("sem", "val", "home")

    def __init__(self, sem, val, home=None):
        self.sem = sem
        self.val = val
        self.home = home


class Buf:
    __slots__ = ("name", "t", "w", "r", "dsem", "dcnt", "waited")

    def __init__(self, name, t=None):
        self.name = name
        self.t = t
        self.w = None
        self.r = {}
        self.dsem = None
        self.dcnt = 0
        self.waited = 0

    def __getitem__(self, idx):
        return self.t[idx]


class Eng:
    def __init__(self, K, eng, name, is_pe=False):
        self.K = K
        self.e = eng
        self.name = name
        self.sem = K.newsem("e_" + name)
        self.cnt = 0
        self.seen = {}
        self.is_pe = is_pe

    def wait(self, ev):
        if ev is None:
            return
        if self.is_pe and ev.sem is self.sem:
            return
        k = ev.sem.num
        val = ev.val if ev.home is None else ev.home.dcnt
        if ev.home is not None:
            ev.home.waited = max(ev.home.waited, val)
        if self.seen.get(k, 0) >= val:
            return
        self.e.wait_ge(ev.sem, val)
        self.seen[k] = val

    def pre(self, reads, writes):
        for b in reads:
            self.wait(b.w)
        for b in writes:
            self.wait(b.w)
            for ev in b.r.values():
                self.wait(ev)

    def post(self, ins, reads, writes):
        self.cnt += 1
        ins.then_inc(self.sem, 1)
        ev = Ev(self.sem, self.cnt)
        for b in reads:
            b.r[self.sem.num] = ev
        for b in writes:
            b.w = ev
            b.r = {}
        return ev

    def op(self, reads, writes, fn):
        self.pre(reads, writes)
        ins = fn(self.e)
        return self.post(ins, reads, writes)

    def dma(self, out_ap, in_ap, reads, writes, home, **kw):
        self.pre(reads, writes)
        if home.dsem is None:
            home.dsem = self.K.newsem("d_" + home.name)
            self.K.homes.append(home)
        if home.waited:
            self.wait(Ev(home.dsem, home.dcnt, home))
        ins = self.e.dma_start(out=out_ap, in_=in_ap, **kw)
        home.dcnt += 16
        ins.then_inc(home.dsem, 16)
        ev = Ev(home.dsem, home.dcnt, home)
        for b in reads:
            b.r[("d", home.dsem.num)] = ev
        for b in writes:
            b.w = ev
            b.r = {}
        return ev


def idma(eng, out_ap, in_ap, idx_ap, reads, writes, home):
    eng.pre(reads, writes)
    if home.dsem is None:
        home.dsem = eng.K.newsem("d_" + home.name)
        eng.K.homes.append(home)
    if home.waited:
        eng.wait(Ev(home.dsem, home.dcnt, home))
    ins = eng.e.indirect_dma_start(out=out_ap, out_offset=None, in_=in_ap, in_offset=bass.IndirectOffsetOnAxis(ap=idx_ap, axis=0))
    home.dcnt += 16
    ins.then_inc(home.dsem, 16)
    ev = Ev(home.dsem, home.dcnt, home)
    for b in reads:
        b.r[("d", home.dsem.num)] = ev
    for b in writes:
        b.w = ev
        b.r = {}
    return ev


class Kern:
    def __init__(self):
        self.nc = bass.Bass("TRN2", target_bir_lowering=False)
        nc = self.nc
        self.nsem = 0
        self.pe = Eng(self, nc.tensor, "pe", is_pe=True)
        self.dve = Eng(self, nc.vector, "dve")
        self.act = Eng(self, nc.scalar, "act")
        self.pool = Eng(self, nc.gpsimd, "pool")
        self.sp = Eng(self, nc.sync, "sp")
        self.outs = []
        self.cms = []
        self.homes = []

    def newsem(self, name):
        self.nsem += 1
        return self.nc.alloc_semaphore(name="%s_%d" % (name, self.nsem))

    def sb(self, name, shape, dt):
        cm = self.nc.sbuf_tensor(name, list(shape), dt)
        t = cm.__enter__()
        self.cms.append(cm)
        return Buf(name, t)

    def release_to(self, mark):
        while len(self.cms) > mark:
            self.cms.pop().__exit__(None, None, None)

    def barrier(self):
        engs = (self.pe, self.dve, self.act, self.pool, self.sp)
        for e in engs:
            for f in engs:
                if f.cnt and not (f is e and e.is_pe):
                    if f is e:
                        e.e.wait_ge(f.sem, f.cnt)
                        e.seen[f.sem.num] = f.cnt
                    else:
                        e.wait(Ev(f.sem, f.cnt))
            for hb in self.homes:
                e.wait(Ev(hb.dsem, hb.dcnt, hb))

    def ps(self, name, shape, dt=F32):
        t = self.nc.psum_tensor(name, list(shape), dt).__enter__()
        return Buf(name, t)

    def dram_in(self, name, shape, dt):
        t = self.nc.dram_tensor(name, list(shape), dt, kind="ExternalInput")
        return Buf(name, t.ap())

    def dram_out(self, name, shape, dt):
        t = self.nc.dram_tensor(name, list(shape), dt, kind="ExternalOutput")
        b = Buf(name, t.ap())
        self.outs.append(b)
        return b

    def dram_tmp(self, name, shape, dt):
        t = self.nc.dram_tensor(name, list(shape), dt, kind="Internal")
        return Buf(name, t.ap())

    def finish(self):
        for b in self.outs:
            self.sp.wait(b.w)
        for e in (self.pe, self.dve, self.act, self.pool):
            if e.cnt:
                self.sp.wait(Ev(e.sem, e.cnt))


class Cfg:
    def __init__(self, S=8192, NS=16, P=8192, NPHYS=2560, NCORES=2):
        self.S = S
        self.NS = NS
        self.P = P
        self.NPHYS = NPHYS
        self.NCORES = NCORES


CH_LRUX, CH_LRUG, CH_POOLX, CH_POOLG, CH_Q, CH_NSAG, CH_KVF, CH_BG, CH_MG = 0, 8, 16, 24, 32, 40, 48, 56, 57
NCHUNK = 81


def make_consts(cfg):
    S = cfg.S
    NKT = S // 128
    NMT = max(1, (S // 16) // 128)
    p = np.arange(128)
    ident = np.eye(128, dtype=np.float32)
    kk, qq = np.meshgrid(p, p, indexing="ij")
    tric = np.where(kk <= qq, 0.0, NEGB).astype(np.float32)
    triw = np.where(kk > qq, 0.0, NEGB).astype(np.float32)
    row0b = np.zeros((128, 512), np.float32)
    row0b[0, :] = NEGB
    x = np.arange(256)
    m = x[None, :] - 128
    jbrel = (p // 64)[:, None]
    patw = np.where(m > jbrel, -1.0e30, np.where((m == jbrel) | (m == jbrel - 1), 1.0e4, 0.0)).astype(np.float32)
    ratio = np.zeros((128, 4, 15), np.float32)
    for g, w in enumerate(POOL_WIN):
        ratio[:, g, :] = (w / np.minimum(w, np.arange(15) + 1.0))[None, :]
    qrel = np.arange(512)
    cmpb = np.zeros((128, 4, 512), np.float32)
    for v in range(4):
        cmpb[:, v, :] = np.where((16 * p[:, None] + 15 - 512 * v) <= qrel[None, :], 0.0, NEGB)
    selh = np.zeros((128, 32, 128), np.float32)
    k = np.arange(128)
    for v in range(32):
        selh[:, v, :] = ((p % 64)[:, None] == (2 * v + k // 64)[None, :]).astype(np.float32)
    gagg = np.zeros((128, NMT, 128), np.float32)
    j = np.arange(128)
    for mt in range(NMT):
        mm_ = 128 * mt + p
        gagg[:, mt, :] = ((mm_[:, None] >= 1) & (((mm_[:, None] - 1) // 4) == j[None, :])).astype(np.float32)
    NAB = NKT + 3
    OFF = NKT - 1
    NABC = NKT + 16 * (NMT - 1) + 1
    ab = np.zeros((128, 16, NAB), np.float64)
    abc = np.zeros((128, 16, NABC), np.float64)
    for h in range(16):
        W = 128 if h <= 3 else 512
        idx = np.arange(NAB)
        ab[:, h, :] = SLOPES[h] * (p[:, None] + 128.0 * (idx[None, :] - OFF) - W / 2.0)
        idx = np.arange(NABC)
        abc[:, h, :] = SLOPES[h] * (16.0 * p[:, None] + 15.0 + 128.0 * (idx[None, :] - OFF) - W / 2.0)
    f32c = np.concatenate([ident, tric, triw, row0b, patw, ratio.reshape(128, -1),
                           ab.reshape(128, -1).astype(np.float32), abc.reshape(128, -1).astype(np.float32)], axis=1)
    bfc = np.concatenate([cmpb.reshape(128, -1), selh.reshape(128, -1), gagg.reshape(128, -1)], axis=1)
    P_ = cfg.P
    NPG = P_ // 128
    NMS = max(1, (P_ // 16) // 128)
    absb = np.zeros((128, NMS, 16), np.float64)
    abk = np.zeros((128, NPG, 16), np.float64)
    abw = np.zeros((128, 4, 16), np.float64)
    for h in range(16):
        for mt in range(NMS):
            mm_ = 128 * mt + p
            absb[:, mt, h] = -SLOPES[h] * (P_ - (16.0 * mm_ + 15.0))
        for pg in range(NPG):
            abk[:, pg, h] = -SLOPES[h] * (P_ - (128.0 * pg + p))
        for t in range(4):
            abw[:, t, h] = -SLOPES[h] * (512.0 - (128.0 * t + p))
    absb[0, 0, :] = NEGB
    abw[0, 0, :] = NEGB
    ind = (p[:, None] // 16 == np.arange(8)[None, :]).astype(np.float32)
    fb = np.zeros((128, 128), np.float32)
    fb[:, 0] = 1.0e4
    fb[:, NPG * 2 - 1] = 1.0e4
    gs = np.zeros((128, NMS, 128), np.float32)
    for mt in range(NMS):
        mm_ = 128 * mt + p
        gs[:, mt, :] = ((mm_[:, None] >= 1) & (((mm_[:, None] - 1) // 4) == j[None, :])).astype(np.float32)
    c_s = np.concatenate([p[:, None].astype(np.float32), absb.reshape(128, -1), abk.reshape(128, -1), abw.reshape(128, -1),
                          ind, fb, gs.reshape(128, -1)], axis=1)
    return {"c_f32": np.ascontiguousarray(f32c, dtype=np.float32), "c_bf": np.ascontiguousarray(bfc, dtype=np.float32),
            "c_s": np.ascontiguousarray(c_s, dtype=np.float32)}


def build(cfg):
    S = cfg.S
    NT = S // 512
    NKT = S // 128
    NBk = min(S // 64, 128)
    NMT = max(1, (S // 16) // 128)
    NAB = NKT + 3
    OFF = NKT - 1
    NABC = NKT + 16 * (NMT - 1) + 1
    NSL = [min(NKT, 24), NKT]

    K = Kern()
    nc = K.nc
    pe, dve, act, pool, sp = K.pe, K.dve, K.act, K.pool, K.sp

    x_in = K.dram_in("x", [S, D], F32)
    g_pre = K.dram_in("g_pre", [2, D], F32)
    g_post = K.dram_in("g_post", [2, D], F32)
    w_in = K.dram_in("w_in", [2, D, IN_W], F32)
    vecs = K.dram_in("vecs", [18, D], F32)
    w_rg = K.dram_in("w_rg", [2, 2, 16, 64, 64], F32)
    w_pool = K.dram_in("w_pool", [2, 4, 256, 256], F32)
    cmp_pos = K.dram_in("cmp_pos", [2, 2, 32, 64], F32)
    cmp_phi = K.dram_in("cmp_phi", [2, 2, 64, 64], F32)
    w_branch = K.dram_in("w_branch", [2, 3, D, D], F32)
    w_out = K.dram_in("w_out", [2, D, D], F32)
    c_f32 = K.dram_in("c_f32", [128, 128 * 3 + 512 + 256 + 60 + 16 * NAB + 16 * NABC], F32)
    c_bf = K.dram_in("c_bf", [128, 2048 + 4096 + NMT * 128], F32)

    y_out = K.dram_out("y", [S, D], F32)
    kv_out = [K.dram_out("kvo%d" % i, [2, S, 256], F32) for i in range(4)]
    win_out = [K.dram_out("wino%d" % i, [2, 512, 256], F32) for i in range(2)]
    st_out = K.dram_out("st", [2, 19, D], F32)

    NS, PL, NPHYS = cfg.NS, cfg.P, cfg.NPHYS
    NPG = PL // 128
    NMS = max(1, (PL // 16) // 128)
    WBUF = min(512, PL)
    xs_in = K.dram_in("xs_in", [NS, D], F32)
    pools = [K.dram_in("pool%d" % i, [2 * NPHYS * 128, 256], F32) for i in range(4)]
    winc = [K.dram_in("winc%d" % i, [2, NS, WBUF, 256], F32) for i in range(2)]
    st_conv = K.dram_in("st_conv", [2, NS * 3, D], F32)
    st_lru = K.dram_in("st_lru", [2, NS, D], F32)
    st_pool = K.dram_in("st_pool", [2, NS * 15, D], F32)
    ptab = K.dram_in("ptab", [NS, NPG], I32)
    c_s = K.dram_in("c_s", [128, 1 + NMS * 16 + NPG * 16 + 64 + 8 + 128 + NMS * 128], F32)
    ys_out = K.dram_out("ys", [NS, D], F32)
    skv_out = K.dram_out("skv", [2, NS, 1536], F32)
    swin_out = [K.dram_out("swin%d" % i, [2, NS, WBUF, 256], F32) for i in range(2)]
    sconv_out = K.dram_out("sconv", [2, NS * 3, D], F32)
    slru_out = K.dram_out("slru", [2, NS, D], F32)
    spool_out = K.dram_out("spool", [2, NS * 15, D], F32)
    xs1 = K.dram_tmp("xsamp1", [NS, D], F32)
    Wqt = [K.dram_tmp("Wqt%d" % l, [2, 128, 8, 512], BF16) for l in range(2)]

    x1 = K.dram_tmp("x1", [S, D], F32)
    Wc = [K.dram_tmp("Wc%d" % l, [NCHUNK, 128, 8, 128], BF16) for l in range(2)]
    Wkv = [K.dram_tmp("Wkv%d" % l, [3, 128, 8, 512], BF16) for l in range(2)]
    Wb = [K.dram_tmp("Wb%d" % l, [24, 128, 8, 128], BF16) for l in range(2)]
    Wo = [K.dram_tmp("Wo%d" % l, [128, 8, D], BF16) for l in range(2)]

    def prep_cols(l, ch, scol, n, dcol):
        src = w_in[l, :, scol:scol + n].rearrange("(k p) n -> p k n", p=128)
        pool.dma(Wc[l][ch, :, :, dcol:dcol + n], src, [w_in], [Wc[l]], Wc[l])

    def prep_layer(l):
        for c in range(8):
            prep_cols(l, CH_LRUX + c, 0 + c * 128, 128, 0)
            prep_cols(l, CH_LRUG + c, 1024 + c * 128, 128, 0)
            prep_cols(l, CH_POOLX + c, 2048 + c * 128, 128, 0)
            prep_cols(l, CH_POOLG + c, 3072 + c * 128, 128, 0)
            for e in range(2):
                h = HPERM[2 * c + e]
                prep_cols(l, CH_Q + c, 4096 + h * 64, 64, e * 64)
                prep_cols(l, CH_NSAG + c, 5120 + h * 64, 64, e * 64)
        for i, base in enumerate((6144, 6400, 6656, 7168)):
            for cc in range(2):
                prep_cols(l, CH_KVF + 2 * i + cc, base + cc * 128, 128, 0)
        prep_cols(l, CH_BG, 7680, 128, 0)
        for n in range(3):
            for dc in range(8):
                prep_cols(l, CH_MG + n * 8 + dc, 7728 + n * 1024 + dc * 128, 128, 0)
        for blk in range(3):
            src = w_in[l, :, 6144 + blk * 512:6144 + (blk + 1) * 512].rearrange("(k p) n -> p k n", p=128)
            pool.dma(Wkv[l][blk], src, [w_in], [Wkv[l]], Wkv[l])
        for blk in range(2):
            src = w_in[l, :, 4096 + blk * 512:4096 + (blk + 1) * 512].rearrange("(k p) n -> p k n", p=128)
            pool.dma(Wqt[l][blk], src, [w_in], [Wqt[l]], Wqt[l])
        for n in range(2):
            for dc in range(8):
                src = w_branch[l, n, :, dc * 128:(dc + 1) * 128].rearrange("(k p) n -> p k n", p=128)
                pool.dma(Wb[l][n * 8 + dc], src, [w_branch], [Wb[l]], Wb[l])
        for fc in range(8):
            for e in range(2):
                h = HPERM[2 * fc + e]
                src = w_branch[l, 2, h * 64:(h + 1) * 64, :].rearrange("p (dc n) -> dc p n", n=128)
                pool.dma(Wb[l][16:24, e * 64:(e + 1) * 64, fc, :], src, [w_branch], [Wb[l]], Wb[l])
        src = w_out[l].rearrange("(k p) n -> p k n", p=128)
        pool.dma(Wo[l][:, :, :], src, [w_out], [Wo[l]], Wo[l])

    cf = K.sb("cf", [128, 128 + 256 + 60 + 16 * NAB + 16 * NABC], F32)
    sp.dma(cf[:, 0:128], c_f32[:, 0:128], [c_f32], [cf], cf)
    sp.dma(cf[:, 128:], c_f32[:, 896:], [c_f32], [cf], cf)
    o_ = [0]

    def take(n):
        a = o_[0]
        o_[0] += n
        return a

    o_id, o_pw, o_ratio = take(128), take(256), take(60)
    o_ab, o_abc = take(16 * NAB), take(16 * NABC)
    cb = K.sb("cb", [128, 2048 + 4096 + NMT * 128], BF16)
    pool.dma(cb[:], c_bf[:, :], [c_bf], [cb], cb)
    o_cmpb, o_selh, o_gagg = 0, 2048, 2048 + 4096
    cbm = K.sb("cbm", [128, 128 * 3 + 512], BF16)
    pool.dma(cbm[:], c_f32[:, 0:896], [c_f32], [cbm], cbm)
    ones_bf = K.sb("ones_bf", [128, 128], BF16)
    dve.op([], [ones_bf], lambda e: e.memset(ones_bf[:], 1.0))
    zer_bf = K.sb("zer_bf", [128, 128], BF16)
    dve.op([], [zer_bf], lambda e: e.memset(zer_bf[:], 0.0))

    def ident_f(n):
        return cf[0:n, o_id:o_id + n]

    ident_b = cbm[:, 0:128]
    tric_b = cbm[:, 128:256]
    triw_b = cbm[:, 256:384]
    row0b_b = cbm[:, 384:896]

    psA = [K.ps("psA%d" % i, [128, 512]) for i in range(2)]
    psS = [K.ps("psS%d" % i, [128, 512]) for i in range(2)]
    psO = [K.ps("psO%d" % i, [128, 512]) for i in range(2)]
    psM = K.ps("psM", [128, 512])
    psT = K.ps("psT", [128, 1024], BF16)
    rr = {"A": 0, "S": 0, "O": 0}

    def nxt(kind):
        lst = {"A": psA, "S": psS, "O": psO}[kind]
        rr[kind] += 1
        return lst[rr[kind] % 2]

    vr = K.sb("vr", [18, D], F32)
    sp.dma(vr[:], vecs[:, :], [vecs], [vr], vr)
    CV = K.sb("CV", [128, 8, 20], F32)
    for c in range(8):
        pe.op([vr, cf], [psM], lambda e, c=c: e.transpose(out=psM[:, c * 18:(c + 1) * 18], in_=vr[0:18, c * 128:(c + 1) * 128], identity=ident_f(18)))
    act.op([psM], [CV], lambda e: e.copy(out=CV[:, :, 0:18], in_=psM[:, 0:144].rearrange("p (c v) -> p c v", v=18)))
    tmpc = K.sb("tmpc", [128, 8, 2], F32)
    for l in range(2):
        act.op([CV], [tmpc], lambda e, l=l: e.activation(out=tmpc[:, :, l], in_=CV[:, :, l * 9 + 7], func=AF.Exp, scale=-1.0))
        act.op([tmpc], [tmpc], lambda e, l=l: e.activation(out=tmpc[:, :, l], in_=tmpc[:, :, l], func=AF.Ln, bias=1.0, scale=1.0))
        dve.op([tmpc], [CV], lambda e, l=l: e.tensor_scalar(out=CV[:, :, 18 + l], in0=tmpc[:, :, l], scalar1=-8.0, scalar2=None, op0=ALU.mult))

    wpr = K.sb("wpr", [32, 4, 128], F32)
    for l in range(2):
        for kv in range(2):
            for half in range(2):
                sp.dma(wpr[:, l * 2 + kv, half * 64:(half + 1) * 64], cmp_pos[l, kv, :, :], [cmp_pos], [wpr], wpr)
    CWP = K.sb("CWP", [128, 4, 32], F32)
    for i in range(4):
        pe.op([wpr, cf], [psM], lambda e, i=i: e.transpose(out=psM[:, 256 + i * 32:256 + (i + 1) * 32], in_=wpr[0:32, i, :], identity=ident_f(32)))
    act.op([psM], [CWP], lambda e: e.copy(out=CWP[:, :, :], in_=psM[:, 256:384].rearrange("p (i j) -> p i j", j=32)))

    WRG = K.sb("WRG", [128, 2, 8, 128], BF16)
    PHI = K.sb("PHI", [128, 4, 128], BF16)
    WPL = K.sb("WPL", [128, 4, 2, 256], BF16)
    pool.op([], [WRG], lambda e: e.memset(WRG[:], 0.0))
    pool.op([], [PHI], lambda e: e.memset(PHI[:], 0.0))
    for l in range(2):
        for kv in range(2):
            for half in range(2):
                pool.dma(PHI[half * 64:(half + 1) * 64, l * 2 + kv, half * 64:(half + 1) * 64], cmp_phi[l, kv, :, :], [cmp_phi], [PHI], PHI)

    def load_layer_small(l):
        for ax in range(2):
            for c in range(8):
                for half in range(2):
                    pool.dma(WRG[half * 64:(half + 1) * 64, ax, c, half * 64:(half + 1) * 64], w_rg[l, ax, 2 * c + half, :, :], [w_rg], [WRG], WRG)
        for g in range(4):
            pool.dma(WPL[:, g, :, :], w_pool[l, g, :, :].rearrange("(dh p) e -> p dh e", p=128), [w_pool], [WPL], WPL)
        sp.dma(gbc[0][:], g_pre[l:l + 1, :].partition_broadcast(128), [g_pre], [gbc[0]], gbc[0])
        sp.dma(gbc[1][:], g_post[l:l + 1, :].partition_broadcast(128), [g_post], [gbc[1]], gbc[1])

    gbc = [K.sb("gbc%d" % i, [128, D], F32) for i in range(2)]

    prep_layer(0)
    prep_layer(1)

    MARK_PROMPT = len(K.cms)
    KT = [K.sb("KT%d" % cc, [128, NSL[cc] * 128], BF16) for cc in range(2)]
    VS = [K.sb("VS%d" % cc, [128, NSL[cc], 2, 65], BF16) for cc in range(2)]
    KWT = K.sb("KWT", [128, 2, 8 * 128], BF16)
    VW = K.sb("VW", [128, 8, 4, 65], BF16)
    KCT = K.sb("KCT", [128, 2, NMT * 128], BF16)
    VC = K.sb("VC", [128, NMT, 4, 65], BF16)
    lruc = K.sb("lruc", [128, 8, 4], F32)
    poolc = K.sb("poolc", [128, 8, 15], F32)
    cmpc = K.sb("cmpc", [128, 4, 1], F32)

    xs = [K.sb("xs%d" % i, [128, D], F32) for i in range(2)]
    ub = K.sb("ub", [128, D], BF16)
    uT = K.sb("uT", [128, 8, 512], BF16)
    WB = [K.sb("WB%d" % i, [128, 1, 8, 128], BF16) for i in range(4)]
    WKB = K.sb("WKB", [128, 8, 512], BF16)
    QT = K.sb("QT", [128, 8, 512], BF16)
    zb = K.sb("zb", [128, 8, 512], BF16)
    ACC = K.sb("ACC", [128, 8, 512], F32)
    gates = K.sb("gates", [48, 512], F32)
    MbT = K.sb("MbT", [128, 4, 512], BF16)
    MbS = K.sb("MbS", [128, 4, 512], BF16)
    merged = QT
    SC = [K.sb("SC%d" % i, [128, 528], F32) for i in range(6)]
    PT = [K.sb("PT%d" % i, [128, 512], BF16) for i in range(2)]
    PN = K.sb("PN", [128, NMT, 512], BF16)
    kvst = [K.sb("kvst%d" % i, [128, 512], F32) for i in range(1)]
    sm = K.sb("sm", [128, 64], F32)
    rrb = {"WB": 0, "PT": 0, "kvst": 0, "xs": 0}

    def rot(name, lst):
        rrb[name] += 1
        return lst[rrb[name] % len(lst)]

    wb_dma = {"n": 0}

    def mm(out_b, out_ap, l_b, l_ap, r_b, r_ap, start, stop):
        pe.op([l_b, r_b], [out_b], lambda e: e.matmul(out_ap, lhsT=l_ap, rhs=r_ap, start=start, stop=stop))

    PREF = 3

    def proj(l, chunks, consume, Wsrc=None, ncols=128):
        Wsrc = Wsrc if Wsrc is not None else Wc[l]
        bufs = {}

        def load(i):
            buf = rot("WB", WB)
            sp.dma(buf[:, 0, :, :], Wsrc[chunks[i]], [Wsrc], [buf], buf)
            bufs[i] = buf
        for i in range(min(PREF, len(chunks))):
            load(i)
        for i, ch in enumerate(chunks):
            if i + PREF < len(chunks):
                load(i + PREF)
            cur = bufs.pop(i)
            ps = nxt("A")
            for kc in range(8):
                mm(ps, ps[0:ncols, :], cur, cur[:, 0, kc, 0:ncols], uT, uT[:, kc, :], kc == 0, kc == 7)
            consume(i, ch, ps)

    def attn_pairs(h, q0, W, branch, J):
        res = []
        qlo_t = q0 // 128
        nsub = W // 128
        if branch == "s":
            kts = range(0, qlo_t + nsub)
        else:
            kts = range(max(0, qlo_t - 4), qlo_t + nsub)
        for kt in kts:
            i_lo = max(0, kt - qlo_t)
            i_hi = nsub - 1
            if branch == "w":
                i_hi = min(nsub - 1, kt + 4 - qlo_t)
            if i_lo > i_hi:
                continue
            c0, c1 = i_lo * 128, (i_hi + 1) * 128
            mind = (q0 + c0) - (kt * 128 + 127)
            if mind > 0 and SLOPES[h] * mind > THR_SKIP:
                continue
            masks = []
            if kt - qlo_t >= 0:
                masks.append((tric_b, (kt - qlo_t) * 128))
            if branch == "w" and 0 <= kt + 4 - qlo_t <= nsub - 1:
                masks.append((triw_b, (kt + 4 - qlo_t) * 128))
            res.append((kt, c0, c1, masks))
        return res

    EPS = 1e-6
    m8 = K.sb("m8", [128, 16], F32)
    wk = K.sb("wk", [128, 128], F32)
    Mbq = K.sb("Mbq", [128, 128], BF16)
    blkb = K.sb("blkb", [128, 32], BF16)
    vcst = K.sb("vcst", [32, 128], BF16)

    def init_state():
        for b_ in (KCT, VC, VW, VS[0], VS[1], KT[0], KT[1], KWT, MbT, MbS):
            pool.op([], [b_], lambda e, b_=b_: e.memset(b_[:], 0.0))
        pool.op([], [VC], lambda e: e.memset(VC[:, :, :, 64:65], 1.0))
        pool.op([], [VW], lambda e: e.memset(VW[:, :, :, 64:65], 1.0))
        for cc in range(2):
            pool.op([], [VS[cc]], lambda e, cc=cc: e.memset(VS[cc][:, :, :, 64:65], 1.0))

    def compress(l, J, kv, cc, ps):
        i = l * 2 + kv
        kcs, fs = SC[0], SC[1]
        act.op([ps], [kcs], lambda e: e.copy(out=kcs[:, 0:512], in_=ps[:, :]))
        v3 = kcs[:, 0:512].rearrange("p (c j) -> p c j", j=16)
        for half, o in ((0, 0), (1, 32)):
            dve.op([kcs, CWP], [fs], lambda e: e.tensor_scalar(out=fs[:, o:o + 32], in0=v3[:, :, 0], scalar1=CWP[:, i, half * 16:half * 16 + 1], scalar2=None, op0=ALU.mult))
            for j in range(1, 16):
                dve.op([kcs, CWP, fs], [fs], lambda e, j=j: e.scalar_tensor_tensor(out=fs[:, o:o + 32], in0=v3[:, :, j], scalar=CWP[:, i, half * 16 + j:half * 16 + j + 1], in1=fs[:, o:o + 32], op0=ALU.mult, op1=ALU.add))
        ci = kv * 2 + cc
        dve.op([cmpc, fs], [blkb], lambda e: e.tensor_tensor(out=blkb[:, 0:1], in0=cmpc[:, ci, :], in1=fs[:, 32:33], op=ALU.add))
        dve.op([fs], [blkb], lambda e: e.tensor_tensor(out=blkb[:, 1:32], in0=fs[:, 0:31], in1=fs[:, 33:64], op=ALU.add))
        dve.op([fs], [cmpc], lambda e: e.tensor_copy(out=cmpc[:, ci, :], in_=fs[:, 31:32]))
        if kv == 0:
            mm(psM, psM[:, 0:32], PHI, PHI[:, i, :], blkb, blkb[:, :], True, True)
            act.op([psM], [KCT], lambda e: e.copy(out=KCT[:, cc, 32 * J:32 * J + 32], in_=psM[:, 0:32]))
        else:
            mm(psM, psM[0:32, 0:128], blkb, blkb[:, :], PHI, PHI[:, i, :], True, True)
            act.op([psM], [vcst], lambda e: e.copy(out=vcst[:, :], in_=psM[0:32, 0:128]))
            r0 = 32 * (J % 4)
            sp.dma(VC[r0:r0 + 32, J // 4, 2 * cc:2 * cc + 2, 0:64], vcst[:, :].rearrange("p (g d) -> p g d", d=64), [vcst], [VC], VC)

    def combine(h, pos, O, qc0, W, b):
        cp, half = pos // 2, pos % 2
        Oa, rec = SC[2], SC[3]
        act.op([O], [Oa], lambda e: e.copy(out=Oa[0:65, 0:W], in_=O[0:65, 0:W]))
        mm(psM, psM[0:64, 0:W], cf, cf[0:65, o_id + 64:o_id + 65].to_broadcast([65, 64]), Oa, Oa[0:65, 0:W], True, True)
        gb = nxt("S")
        gi = 3 * h + b
        mm(gb, gb[0:64, 0:W], cf, cf[0:48, o_id + gi:o_id + gi + 1].to_broadcast([48, 64]), gates, gates[0:48, qc0:qc0 + W], True, True)
        dve.op([psM], [rec], lambda e: e.tensor_scalar(out=rec[0:64, 0:W], in0=psM[0:64, 0:W], scalar1=1e-30, scalar2=None, op0=ALU.max))
        dve.op([rec], [rec], lambda e: e.reciprocal(out=rec[0:64, 0:W], in_=rec[0:64, 0:W]))
        dve.op([gb, rec], [rec], lambda e: e.tensor_tensor(out=rec[0:64, 0:W], in0=gb[0:64, 0:W], in1=rec[0:64, 0:W], op=ALU.mult))
        T2 = SC[4]
        hs_ = slice(half * 64, half * 64 + 64)
        dve.op([Oa, rec], [T2], lambda e: e.tensor_tensor(out=T2[hs_, 0:W], in0=Oa[0:64, 0:W], in1=rec[0:64, 0:W], op=ALU.mult))
        dve.op([T2, ACC], [ACC], lambda e: e.tensor_tensor(out=ACC[hs_, cp, qc0:qc0 + W], in0=T2[hs_, 0:W], in1=ACC[hs_, cp, qc0:qc0 + W], op=ALU.add))

    DBG = getattr(cfg, 'DBG', '')

    def attend(h, pos, q0, W, branch, J, Osh=None):
        t0 = J * 512
        qc0 = q0 - t0
        cp, half = pos // 2, pos % 2
        g = h // 4
        cc, gi = g // 2, g % 2
        pairs = attn_pairs(h, q0, W, branch, J)
        full = [p_ for p_ in pairs if p_[1] == 0 and p_[2] == W]
        assert full, (h, q0, W, branch)
        pairs = [full[0]] + [p_ for p_ in pairs if p_ is not full[0]]
        O = nxt("O") if Osh is None else Osh
        ob = 0 if Osh is None else qc0

        def kv_of(kt):
            if branch == "s":
                sl = kt % NSL[cc]
                return KT[cc], KT[cc][half * 64:half * 64 + 64, sl * 128:(sl + 1) * 128], VS[cc], VS[cc][:, sl, gi, :]
            sl = kt % 8
            return KWT, KWT[half * 64:half * 64 + 64, cc, sl * 128:(sl + 1) * 128], VW, VW[:, sl, g, :]

        def emit_scores(pi):
            kt, c0, c1, masks = pairs[pi]
            S_ = nxt("S")
            lk_b, lk, _, _ = kv_of(kt)
            if 't' in DBG:
                masks = []
            nmore = len(masks) + (1 if (branch == "s" and "m" not in DBG) else 0)
            mm(S_, S_[:, c0:c1], lk_b, lk, QT, QT[half * 64:half * 64 + 64, cp, qc0 + c0:qc0 + c1], True, nmore == 0)
            if branch == "s" and "m" not in DBG:
                a2 = ((2 * kt) // 64) % 2
                v = kt % 32
                nmore -= 1
                Mb_ = MbT if a2 == half else MbS
                mm(S_, S_[:, c0:c1], cb, cb[half * 64:half * 64 + 64, o_selh + v * 128:o_selh + (v + 1) * 128],
                   Mb_, Mb_[half * 64:half * 64 + 64, g, qc0 + c0:qc0 + c1], False, nmore == 0)
            for (mb_, mc0) in masks:
                nmore -= 1
                mm(S_, S_[:, mc0:mc0 + 128], cbm, ident_b, cbm, mb_, False, nmore == 0)
            return S_

        S_next = emit_scores(0)
        for pi, (kt, c0, c1, masks) in enumerate(pairs):
            S_ = S_next
            if pi + 1 < len(pairs):
                S_next = emit_scores(pi + 1)
            P_ = rot("PT", PT)
            _, _, vb, vap = kv_of(kt)
            idx = kt - q0 // 128 + OFF
            act.op([S_, cf], [P_], lambda e, S_=S_, P_=P_, c0=c0, c1=c1, idx=idx: e.activation(
                out=P_[:, c0:c1], in_=S_[:, c0:c1], func=AF.Exp, bias=cf[:, o_ab + h * NAB + idx:o_ab + h * NAB + idx + 1], scale=1.0))
            if 'p' not in DBG:
                mm(O, O[0:65, ob + c0:ob + c1], vb, vap, P_, P_[:, c0:c1], pi == 0, pi == len(pairs) - 1)
        if 'c' not in DBG and Osh is None:
            combine(h, pos, O, qc0, W, 1 if branch == "s" else 2)

    def attend_cmp(h, pos, q0, W, J, psacc, written, last_head, Osh=None):
        t0 = J * 512
        qc0 = q0 - t0
        cp, half = pos // 2, pos % 2
        g = h // 4
        cc = g // 2
        mtd = J // 4
        mts = []
        for mt in range(mtd + 1):
            mind = q0 - (16 * (128 * mt + 127) + 15)
            if mind > 0 and SLOPES[h] * mind > THR_SKIP:
                continue
            mts.append(mt)
        if Osh is None:
            O = nxt("O")
            Dp = nxt("O")
        else:
            O = Osh
            Dp = psO[0] if Osh is psO[1] else psO[1]
        ob = 0 if Osh is None else qc0
        for mi, mt in enumerate(mts):
            S_ = nxt("S")
            nmore = (1 if mt == 0 else 0) + (1 if mt == mtd else 0)
            mm(S_, S_[:, 0:W], KCT, KCT[half * 64:half * 64 + 64, cc, mt * 128:(mt + 1) * 128], QT, QT[half * 64:half * 64 + 64, cp, qc0:qc0 + W], True, nmore == 0)
            if mt == 0:
                nmore -= 1
                mm(S_, S_[:, 0:W], cbm, ident_b, cbm, row0b_b[:, 0:W], False, nmore == 0)
            if mt == mtd:
                nmore -= 1
                v = J % 4
                mm(S_, S_[:, 0:W], cbm, ident_b, cb, cb[:, o_cmpb + v * 512 + qc0:o_cmpb + v * 512 + qc0 + W], False, nmore == 0)
            idx = 16 * mt - q0 // 128 + OFF
            act.op([S_, cf], [PN], lambda e, S_=S_, mt=mt, idx=idx: e.activation(
                out=PN[:, mt, qc0:qc0 + W], in_=S_[:, 0:W], func=AF.Exp, bias=cf[:, o_abc + h * NABC + idx:o_abc + h * NABC + idx + 1], scale=1.0))
            mm(Dp, Dp[:, 0:W], ones_bf, ones_bf[:, :], PN, PN[:, mt, qc0:qc0 + W], mi == 0, mi == len(mts) - 1)
            mm(O, O[0:65, ob:ob + W], VC, VC[:, mt, g, :], PN, PN[:, mt, qc0:qc0 + W], mi == 0, mi == len(mts) - 1)
        rden = SC[4]
        dve.op([Dp], [rden], lambda e: e.tensor_scalar(out=rden[:, 0:W], in0=Dp[:, 0:W], scalar1=1e-30, scalar2=None, op0=ALU.max))
        dve.op([rden], [rden], lambda e: e.reciprocal(out=rden[:, 0:W], in_=rden[:, 0:W]))
        for mt in mts:
            dve.op([PN, rden], [PN], lambda e, mt=mt: e.tensor_tensor(out=PN[:, mt, qc0:qc0 + W], in0=PN[:, mt, qc0:qc0 + W], in1=rden[:, 0:W], op=ALU.mult))
        nsub = W // 128
        for si in range(nsub):
            qs = qc0 // 128 + si
            for mi, mt in enumerate(mts):
                mm(psacc, psacc[:, qs * 128:qs * 128 + NBk], PN, PN[:, mt, qs * 128:(qs + 1) * 128], cb, cb[:, o_gagg + mt * 128:o_gagg + mt * 128 + NBk],
                   False, last_head and si == nsub - 1 and mi == len(mts) - 1)
        if getattr(cfg, 'STAGE', 99) > 2.31 and Osh is None:
            combine(h, pos, O, qc0, W, 0)

    def topk_group(g, J, psacc):
        t0 = J * 512
        for qs in range(4):
            s0 = t0 + qs * 128
            sc = SC[5]
            po = o_pw + 128 - s0 // 64
            dve.op([psacc, cf], [sc], lambda e: e.tensor_tensor(out=sc[:, 0:NBk], in0=psacc[:, qs * 128:qs * 128 + NBk], in1=cf[:, po:po + NBk], op=ALU.add))
            dve.op([sc], [sc], lambda e: e.tensor_scalar(out=sc[:, 0:1], in0=sc[:, 0:1], scalar1=1.0e4, scalar2=None, op0=ALU.add))
            dve.op([sc], [m8], lambda e: e.max(out=m8[:, 0:8], in_=sc[:, 0:NBk]))
            dve.op([sc, m8], [wk], lambda e: e.match_replace(out=wk[:, 0:NBk], in_to_replace=m8[:, 0:8], in_values=sc[:, 0:NBk], imm_value=-3.0e38))
            dve.op([wk], [m8], lambda e: e.max(out=m8[:, 8:16], in_=wk[:, 0:NBk]))
            dve.op([sc, m8], [Mbq], lambda e: e.tensor_scalar(out=Mbq[:, 0:NBk], in0=sc[:, 0:NBk], scalar1=m8[:, 15:16], scalar2=NEGB, op0=ALU.is_lt, op1=ALU.mult))
            pe.op([Mbq, cbm], [psT], lambda e, qs=qs: e.transpose(out=psT[0:NBk, qs * 128:(qs + 1) * 128], in_=Mbq[:, 0:NBk], identity=ident_b))
        act.op([psT], [MbT], lambda e: e.copy(out=MbT[0:NBk, g, :], in_=psT[0:NBk, 0:512]))
        n0 = min(NBk, 64)
        act.op([psT], [MbS], lambda e: e.copy(out=MbS[64:64 + n0, g, :], in_=psT[0:n0, 0:512]))
        if NBk > 64:
            act.op([psT], [MbS], lambda e: e.copy(out=MbS[0:NBk - 64, g, :], in_=psT[64:NBk, 0:512]))

    def merge_branch(l, n, first):
        seq = []
        for dc in range(8):
            seq.append((Wb[l], n * 8 + dc))
            seq.append((Wc[l], CH_MG + n * 8 + dc))
        bufs = {}

        def load(i):
            buf = rot("WB", WB)
            sp.dma(buf[:, 0, :, :], seq[i][0][seq[i][1]], [seq[i][0]], [buf], buf)
            bufs[i] = buf
        load(0)
        load(1)
        for dc in range(8):
            if dc < 7:
                load(2 * dc + 2)
                load(2 * dc + 3)
            wbr, wmg = bufs.pop(2 * dc), bufs.pop(2 * dc + 1)
            pbr, pmg = nxt("A"), nxt("A")
            for fc in range(8):
                mm(pbr, pbr[:, :], wbr, wbr[:, 0, fc, :], zb, zb[:, fc, :], fc == 0, fc == 7)
            for kc in range(8):
                mm(pmg, pmg[:, :], wmg, wmg[:, 0, kc, :], uT, uT[:, kc, :], kc == 0, kc == 7)
            mt_ = SC[0]
            act.op([pmg], [mt_], lambda e: e.activation(out=mt_[:, 0:512], in_=pmg[:, :], func=AF.Sigmoid))
            if first:
                dve.op([pbr, mt_], [ACC], lambda e, dc=dc: e.tensor_tensor(out=ACC[:, dc, :], in0=pbr[:, :], in1=mt_[:, 0:512], op=ALU.mult))
            else:
                dve.op([pbr, mt_], [mt_], lambda e: e.tensor_tensor(out=mt_[:, 0:512], in0=pbr[:, :], in1=mt_[:, 0:512], op=ALU.mult))
                dve.op([ACC, mt_], [ACC], lambda e, dc=dc: e.tensor_tensor(out=ACC[:, dc, :], in0=ACC[:, dc, :], in1=mt_[:, 0:512], op=ALU.add))

    def tile_prompt(l, J, x_src, x_dst):
        stage = getattr(cfg, 'STAGE', 99)
        t0 = J * 512
        last = (J == NT - 1)
        for s in range(4):
            xb = rot("xs", xs)
            sp.dma(xb[:], x_src[t0 + s * 128:t0 + (s + 1) * 128, :], [x_src], [xb], xb)
            dve.op([], [sm], lambda e, s=s: e.memset(sm[:, s:s + 1], 0.0))
            act.op([xb, sm], [ub, sm], lambda e, s=s, xb=xb: e.activation(out=ub[:, :], in_=xb[:, :], func=AF.Square, accum_out=sm[:, s:s + 1]))
            dve.op([sm], [sm], lambda e, s=s: e.tensor_scalar(out=sm[:, 4 + s:5 + s], in0=sm[:, s:s + 1], scalar1=1.0 / D, scalar2=EPS, op0=ALU.mult, op1=ALU.add))
            act.op([sm], [sm], lambda e, s=s: e.sqrt(out=sm[:, 4 + s:5 + s], in_=sm[:, 4 + s:5 + s]))
            dve.op([sm], [sm], lambda e, s=s: e.reciprocal(out=sm[:, 8 + s:9 + s], in_=sm[:, 4 + s:5 + s]))
            dve.op([xb, sm, gbc[0]], [ub], lambda e, s=s, xb=xb: e.scalar_tensor_tensor(out=ub[:, :], in0=xb[:, :], scalar=sm[:, 8 + s:9 + s], in1=gbc[0][:, :], op0=ALU.mult, op1=ALU.mult))
            for c in range(8):
                pe.op([ub, cbm], [psT], lambda e, c=c: e.transpose(out=psT[:, c * 128:(c + 1) * 128], in_=ub[:, c * 128:(c + 1) * 128], identity=ident_b))
            act.op([psT], [uT], lambda e, s=s: e.copy(out=uT[:, :, s * 128:(s + 1) * 128], in_=psT[:, :].rearrange("p (c t) -> p c t", t=128)))

        if stage <= 2.1:
            return
        for blk in range(3):
            sp.dma(WKB[:], Wkv[l][blk], [Wkv[l]], [WKB], WKB)
            for s in range(4):
                ps = nxt("A")
                for kc in range(8):
                    mm(ps, ps[:, :], uT, uT[:, kc, s * 128:(s + 1) * 128], WKB, WKB[:, kc, :], kc == 0, kc == 7)
                st = rot("kvst", kvst)
                act.op([ps], [st], lambda e, ps=ps, st=st: e.copy(out=st[:, :], in_=ps[:, :]))
                r0 = t0 + s * 128
                kt = r0 // 128
                if blk < 2:
                    sp.dma(kv_out[2 * blk][l, r0:r0 + 128, :], st[:, 0:256], [st], [kv_out[2 * blk]], st)
                    sp.dma(kv_out[2 * blk + 1][l, r0:r0 + 128, :], st[:, 256:512], [st], [kv_out[2 * blk + 1]], st)
                elif r0 >= S - 512:
                    w0 = r0 - (S - 512)
                    sp.dma(win_out[0][l, w0:w0 + 128, :], st[:, 0:256], [st], [win_out[0]], st)
                    sp.dma(win_out[1][l, w0:w0 + 128, :], st[:, 256:512], [st], [win_out[1]], st)
                if blk == 1:
                    for cc in range(2):
                        dve.op([st], [VS[cc]], lambda e, cc=cc, st=st, kt=kt: e.tensor_copy(
                            out=VS[cc][:, kt % NSL[cc], :, 0:64], in_=st[:, 256 + cc * 128:256 + (cc + 1) * 128].rearrange("p (g d) -> p g d", d=64)))
                if blk == 2:
                    dve.op([st], [VW], lambda e, st=st, kt=kt: e.tensor_copy(out=VW[:, kt % 8, :, 0:64], in_=st[:, 256:512].rearrange("p (g d) -> p g d", d=64)))

        if stage <= 2.2:
            return
        def c_q(i, ch, ps):
            act.op([ps], [QT], lambda e: e.mul(out=QT[:, i, :], in_=ps[:, :], mul=0.125))
        proj(l, [CH_Q + c for c in range(8)], c_q)

        def c_kv(i, ch, ps):
            j = ch - CH_KVF
            cc = j % 2
            if j < 2:
                compress(l, J, 0, cc, ps)
            elif j < 4:
                compress(l, J, 1, cc, ps)
            elif j < 6:
                sl = (4 * J) % NSL[cc]
                act.op([ps], [KT[cc]], lambda e: e.copy(out=KT[cc][:, sl * 128:sl * 128 + 512], in_=ps[:, :]))
            else:
                sl = (4 * J) % 8
                act.op([ps], [KWT], lambda e: e.copy(out=KWT[:, cc, sl * 128:sl * 128 + 512], in_=ps[:, :]))
        proj(l, [CH_KVF + j for j in range(8)], c_kv)

        def c_bg(i, ch, ps):
            act.op([ps], [gates], lambda e: e.activation(out=gates[:, :], in_=ps[0:48, :], func=AF.Sigmoid))
        proj(l, [CH_BG], c_bg, ncols=48)

        if stage <= 2.3:
            return
        pool.op([], [ACC], lambda e: e.memset(ACC[:], 0.0))
        for g in range(4):
            heads = [(h, HPERM.index(h)) for h in range(4 * g, 4 * g + 4)]
            psacc = psA[g % 2]
            written = set()
            mm(psacc, psacc[:, :], zer_bf, zer_bf[:, :], cb, cb[:, 0:512], True, False)
            for hi, (h, pos) in enumerate(heads):
                W = 128 if h <= 3 else 512
                Osh = nxt("O") if W == 128 else None
                for q0 in range(t0, t0 + 512, W):
                    attend_cmp(h, pos, q0, W, J, psacc, written, hi == 3 and q0 + W == t0 + 512, Osh)
                if Osh is not None and stage > 2.31:
                    combine(h, pos, Osh, 0, 512, 0)
            if stage <= 2.32:
                continue
            topk_group(g, J, psacc)
            if stage <= 2.33:
                continue
            for (h, pos) in heads:
                W = 128 if h <= 3 else 512
                if 'h' in DBG and pos % 2 == 1:
                    continue
                if W == 512:
                    attend(h, pos, t0, W, "s", J)
                    if stage > 2.34:
                        attend(h, pos, t0, W, "w", J)
                else:
                    for br_, bi_ in (("s", 1), ("w", 2)):
                        if br_ == "w" and stage <= 2.34:
                            continue
                        Osh = nxt("O")
                        for q0 in range(t0, t0 + 512, W):
                            attend(h, pos, q0, W, br_, J, Osh)
                        if 'c' not in DBG:
                            combine(h, pos, Osh, 0, 512, bi_)

        if stage <= 2.4:
            return
        def c_nsag(i, ch, ps):
            sg = SC[0]
            act.op([ps], [sg], lambda e: e.activation(out=sg[:, 0:512], in_=ps[:, :], func=AF.Silu))
            dve.op([ACC, sg], [zb], lambda e: e.tensor_tensor(out=zb[:, i, :], in0=ACC[:, i, :], in1=sg[:, 0:512], op=ALU.mult))
        proj(l, [CH_NSAG + c for c in range(8)], c_nsag)
        merge_branch(l, 2, True)

        if stage <= 2.5:
            return
        def c_lru(i, ch, ps):
            c = i // 2
            if i % 2 == 0:
                xp, xc, r_, i_, a_, hs = SC[0], SC[1], SC[2], SC[3], SC[4], SC[5]
                act.op([lruc], [xp], lambda e: e.copy(out=xp[:, 0:3], in_=lruc[:, c, 0:3]))
                act.op([ps], [xp], lambda e: e.copy(out=xp[:, 3:515], in_=ps[:, :]))
                act.op([xp], [lruc], lambda e: e.copy(out=lruc[:, c, 0:3], in_=xp[:, 512:515]))
                cv = l * 9
                dve.op([xp, CV], [xc], lambda e: e.tensor_scalar(out=xc[:, 0:512], in0=xp[:, 0:512], scalar1=CV[:, c, cv:cv + 1], scalar2=CV[:, c, cv + 4:cv + 5], op0=ALU.mult, op1=ALU.add))
                for k in range(1, 4):
                    dve.op([xp, CV, xc], [xc], lambda e, k=k: e.scalar_tensor_tensor(out=xc[:, 0:512], in0=xp[:, k:k + 512], scalar=CV[:, c, cv + k:cv + k + 1], in1=xc[:, 0:512], op0=ALU.mult, op1=ALU.add))
                xcb = rot("PT", PT)
                act.op([xc], [xcb], lambda e: e.copy(out=xcb[:, :], in_=xc[:, 0:512]))
                pr, pi_ = nxt("S"), nxt("S")
                mm(pr, pr[:, :], WRG, WRG[:, 0, c, :], xcb, xcb[:, :], True, True)
                mm(pi_, pi_[:, :], WRG, WRG[:, 1, c, :], xcb, xcb[:, :], True, True)
                act.op([pr, CV], [r_], lambda e: e.activation(out=r_[:, 0:512], in_=pr[:, :], func=AF.Sigmoid, bias=CV[:, c, cv + 5:cv + 6], scale=1.0))
                act.op([pi_, CV], [i_], lambda e: e.activation(out=i_[:, 0:512], in_=pi_[:, :], func=AF.Sigmoid, bias=CV[:, c, cv + 6:cv + 7], scale=1.0))
                act.op([r_, CV], [a_], lambda e: e.activation(out=a_[:, 0:512], in_=r_[:, 0:512], func=AF.Exp, scale=CV[:, c, 18 + l:19 + l]))
                dve.op([a_], [r_], lambda e: e.scalar_tensor_tensor(out=r_[:, 0:512], in0=a_[:, 0:512], scalar=-1.0, in1=a_[:, 0:512], op0=ALU.mult, op1=ALU.mult))
                act.op([r_], [r_], lambda e: e.activation(out=r_[:, 0:512], in_=r_[:, 0:512], func=AF.Sqrt, bias=1.0, scale=1.0))
                dve.op([i_, xc], [i_], lambda e: e.tensor_tensor(out=i_[:, 0:512], in0=i_[:, 0:512], in1=xc[:, 0:512], op=ALU.mult))
                dve.op([i_, r_], [i_], lambda e: e.tensor_tensor(out=i_[:, 0:512], in0=i_[:, 0:512], in1=r_[:, 0:512], op=ALU.mult))
                dve.op([a_, i_, lruc], [hs], lambda e: e.tensor_tensor_scan(out=hs[:, 0:512], data0=a_[:, 0:512], data1=i_[:, 0:512], initial=lruc[:, c, 3:4], op0=ALU.mult, op1=ALU.add))
                act.op([hs], [lruc], lambda e: e.copy(out=lruc[:, c, 3:4], in_=hs[:, 511:512]))
            else:
                sg, hs = SC[1], SC[5]
                act.op([ps], [sg], lambda e: e.activation(out=sg[:, 0:512], in_=ps[:, :], func=AF.Silu))
                dve.op([hs, sg], [zb], lambda e: e.tensor_tensor(out=zb[:, c, :], in0=hs[:, 0:512], in1=sg[:, 0:512], op=ALU.mult))
        chl = []
        for c in range(8):
            chl += [CH_LRUX + c, CH_LRUG + c]
        proj(l, chl, c_lru)
        merge_branch(l, 0, False)

        if stage <= 2.6:
            return
        PD = [PT[0], PT[1]]

        def c_pool(i, ch, ps):
            gq, k = i // 4, i % 4
            w = POOL_WIN[gq]
            if k < 2:
                c = 2 * gq + k
                X = SC[k]
                act.op([poolc], [X], lambda e: e.copy(out=X[:, 0:15], in_=poolc[:, c, :]))
                act.op([ps], [X], lambda e: e.copy(out=X[:, 15:527], in_=ps[:, :]))
                act.op([X], [poolc], lambda e: e.copy(out=poolc[:, c, :], in_=X[:, 512:527]))
                cur = X
                sh = 1
                tmps = [SC[4], SC[5]]
                ti = 0
                while sh < w:
                    nx = tmps[ti % 2]
                    ti += 1
                    dve.op([cur], [nx], lambda e, cur=cur, nx=nx, sh=sh: e.tensor_tensor(out=nx[:, sh:527], in0=cur[:, sh:527], in1=cur[:, 0:527 - sh], op=ALU.add))
                    cur = nx
                    sh *= 2
                if J == 0:
                    dve.op([cur, cf], [cur], lambda e, cur=cur: e.tensor_tensor(out=cur[:, 15:30], in0=cur[:, 15:30], in1=cf[:, o_ratio + gq * 15:o_ratio + gq * 15 + 15], op=ALU.mult))
                dve.op([cur, X], [PD[k]], lambda e, cur=cur: e.scalar_tensor_tensor(out=PD[k][:, :], in0=cur[:, 15:527], scalar=1.0 / w, in1=X[:, 15:527], op0=ALU.mult, op1=ALU.subtract))
            else:
                eh = k - 2
                c = 2 * gq + eh
                pm = nxt("S")
                for dh in range(2):
                    mm(pm, pm[:, :], WPL, WPL[:, gq, dh, eh * 128:(eh + 1) * 128], PD[dh], PD[dh][:, :], dh == 0, dh == 1)
                sg = SC[2]
                act.op([ps], [sg], lambda e: e.activation(out=sg[:, 0:512], in_=ps[:, :], func=AF.Silu))
                dve.op([pm, CV, sg], [zb], lambda e: e.scalar_tensor_tensor(out=zb[:, c, :], in0=pm[:, :], scalar=CV[:, c, l * 9 + 8:l * 9 + 9], in1=sg[:, 0:512], op0=ALU.mult, op1=ALU.mult))
        chl = []
        for gq in range(4):
            chl += [CH_POOLX + 2 * gq, CH_POOLX + 2 * gq + 1, CH_POOLG + 2 * gq, CH_POOLG + 2 * gq + 1]
        proj(l, chl, c_pool)
        merge_branch(l, 1, False)
        for dc in range(8):
            act.op([ACC], [merged], lambda e, dc=dc: e.copy(out=merged[:, dc, :], in_=ACC[:, dc, :]))

        if stage <= 2.7:
            return
        sp.dma(WKB[:], Wo[l][:, :, 0:512], [Wo[l]], [WKB], WKB)
        sp.dma(zb[:], Wo[l][:, :, 512:1024], [Wo[l]], [zb], zb)
        wo = [WKB, zb]
        for s in range(4):
            xr, yo = xs[0], xs[1]
            sp.dma(xr[:], x_src[t0 + s * 128:t0 + (s + 1) * 128, :], [x_src], [xr], xr)
            po = [nxt("A"), nxt("A")]
            dve.op([], [sm], lambda e: e.memset(sm[:, 16:18], 0.0))
            for eh in range(2):
                for dc in range(8):
                    mm(po[eh], po[eh][:, :], merged, merged[:, dc, s * 128:(s + 1) * 128], wo[eh], wo[eh][:, dc, :], dc == 0, dc == 7)
                act.op([po[eh], sm], [ub, sm], lambda e, eh=eh: e.activation(out=ub[:, eh * 512:(eh + 1) * 512], in_=po[eh][:, :], func=AF.Square, accum_out=sm[:, 16 + eh:17 + eh]))
            dve.op([sm], [sm], lambda e: e.tensor_tensor(out=sm[:, 18:19], in0=sm[:, 16:17], in1=sm[:, 17:18], op=ALU.add))
            dve.op([sm], [sm], lambda e: e.tensor_scalar(out=sm[:, 19:20], in0=sm[:, 18:19], scalar1=1.0 / D, scalar2=EPS, op0=ALU.mult, op1=ALU.add))
            act.op([sm], [sm], lambda e: e.sqrt(out=sm[:, 19:20], in_=sm[:, 19:20]))
            dve.op([sm], [sm], lambda e: e.reciprocal(out=sm[:, 20:21], in_=sm[:, 19:20]))
            for eh in range(2):
                dve.op([po[eh], sm, gbc[1]], [yo], lambda e, eh=eh: e.scalar_tensor_tensor(out=yo[:, eh * 512:(eh + 1) * 512], in0=po[eh][:, :], scalar=sm[:, 20:21], in1=gbc[1][:, eh * 512:(eh + 1) * 512], op0=ALU.mult, op1=ALU.mult))
            dve.op([yo, xr], [yo], lambda e: e.tensor_tensor(out=yo[:, :], in0=yo[:, :], in1=xr[:, :], op=ALU.add))
            sp.dma(x_dst[t0 + s * 128:t0 + (s + 1) * 128, :], yo[:], [yo], [x_dst], yo)

    def layer_prompt(l, x_src, x_dst):
        load_layer_small(l)
        if getattr(cfg, "STAGE", 99) == 1:
            return
        dve.op([], [lruc], lambda e: e.memset(lruc[:], 0.0))
        dve.op([], [poolc], lambda e: e.memset(poolc[:], 0.0))
        dve.op([], [cmpc], lambda e: e.memset(cmpc[:], 0.0))
        for J in range(NT):
            tile_prompt(l, J, x_src, x_dst)
        stt = SC[0]
        for c in range(8):
            act.op([lruc, poolc], [stt], lambda e, c=c: e.copy(out=stt[:, c * 19:c * 19 + 4], in_=lruc[:, c, :]))
            act.op([poolc], [stt], lambda e, c=c: e.copy(out=stt[:, c * 19 + 4:c * 19 + 19], in_=poolc[:, c, :]))
        psx = [psA[0], psA[1]]
        for c in range(8):
            pb = psx[c // 4]
            pe.op([stt, cf], [pb], lambda e, c=c, pb=pb: e.transpose(out=pb[0:19, (c % 4) * 128:(c % 4 + 1) * 128], in_=stt[:, c * 19:c * 19 + 19], identity=ident_f(128)))
        so = xs[0]
        for hh in range(2):
            act.op([psx[hh]], [so], lambda e, hh=hh: e.copy(out=so[0:19, hh * 512:(hh + 1) * 512], in_=psx[hh][0:19, :]))
        sp.dma(st_out[l, :, :], so[0:19, :], [so], [st_out], so)

    SST = getattr(cfg, 'SST', 99)

    def sample_phase():
        NCS = 1 + NMS * 16 + NPG * 16 + 64 + 8 + 128 + NMS * 128
        csb = K.sb("csb", [128, NCS - NMS * 128], F32)
        sp.dma(csb[:], c_s[:, 0:NCS - NMS * 128], [c_s], [csb], csb)
        oo = [0]

        def tk(n):
            a = oo[0]
            oo[0] += n
            return a
        o_p, o_abs, o_abk, o_abw, o_ind, o_fb = tk(1), tk(NMS * 16), tk(NPG * 16), tk(64), tk(8), tk(128)
        gsb = K.sb("gsb", [128, NMS * 128], BF16)
        pool.dma(gsb[:], c_s[:, NCS - NMS * 128:NCS], [c_s], [gsb], gsb)
        indb = K.sb("indb", [128, 8], BF16)
        pool.dma(indb[:], c_s[:, o_ind:o_ind + 8], [c_s], [indb], indb)
        NSB = 32
        xsb = K.sb("xsb", [NS, D], F32)
        ysb = K.sb("ysb", [NS, D], F32)
        ubs = K.sb("ubs", [NS, D], BF16)
        uTs = K.sb("uTs", [128, 8, NS], BF16)
        kvtok = K.sb("kvtok", [NS, 1536], F32)
        qtok = K.sb("qtok", [NS, D], F32)
        FM = {n_: K.sb("fm_" + n_, [128, 8, NS], F32) for n_ in ("xl", "lg", "px", "pg", "ng", "h", "t0", "t1", "t2", "t3")}
        mgs = K.sb("mgs", [128, 24, NS], F32)
        CS = K.sb("CS", [128, 8, NS * 3], F32)
        HS = K.sb("HS", [128, 8, NS], F32)
        PSs = K.sb("PSs", [128, 8, NS * 15], F32)
        stl = K.sb("stl", [128, D], F32)
        zbs = [K.sb("zbs%d" % n_, [128, 8, NS], BF16) for n_ in range(3)]
        xcbs = K.sb("xcbs", [128, 8, NS], BF16)
        pdb = K.sb("pdb", [128, 8, NS], BF16)
        gat_f = K.sb("gat_f", [48, NS], F32)
        gat_t = K.sb("gat_t", [NS, 48], F32)
        WBs = [K.sb("WBs%d" % i, [128, 2, 8, 128], BF16) for i in range(2)]
        WKs = K.sb("WKs", [128, 8, 512], BF16)
        WK2 = K.sb("WK2", [128, 8, 512], BF16)
        sms = K.sb("sms", [128, 32], F32)
        mergs = K.sb("mergs", [128, 8, NS], BF16)
        maccs = K.sb("maccs", [128, 8, NS], F32)
        ptb = K.sb("ptb", [128, NPG], I32)
        ptf = K.sb("ptf", [128, NPG], F32)
        IDX = K.sb("IDX", [128, NPG], I32)
        KP = [K.sb("KP%d" % i, [128, 256], F32) for i in range(4)]
        VP = [K.sb("VP%d" % i, [128, 256], F32) for i in range(4)]
        KPw = [K.sb("KPw%d" % i, [128, 256], F32) for i in range(2)]
        VPw = [K.sb("VPw%d" % i, [128, 256], F32) for i in range(2)]
        WT = K.sb("WT", [128, 2, 2, 256], F32)
        p12 = [K.sb("p12_%d" % i, [128, 2, 256], BF16) for i in range(3)]
        fsb = K.sb("fsb", [128, 4, NMS * 128], F32)
        blkT = K.sb("blkT", [128, 2, NMS * 128], BF16)
        KCs = K.sb("KCs", [128, 2, NMS * 128], BF16)
        VCs = K.sb("VCs", [128, NMS, 4, 65], BF16)
        QTs = K.sb("QTs", [128, 8, NS], BF16)
        S16 = K.sb("S16", [128, NMS, 16], F32)
        P16 = K.sb("P16", [128, NMS, 16], BF16)
        rd16 = K.sb("rd16", [128, 16], F32)
        Pn4 = K.sb("Pn4", [128, NMS, 4], BF16)
        Pn4f = K.sb("Pn4f", [128, NMS, 4], F32)
        sc4 = K.sb("sc4", [4, 128], F32)
        wk4 = K.sb("wk4", [4, 128], F32)
        m84 = K.sb("m84", [4, 16], F32)
        mb4 = K.sb("mb4", [4, 128], F32)
        MB4T = K.sb("MB4T", [128, 4], BF16)
        QBs = K.sb("QBs", [128, D], F32)
        prod = K.sb("prod", [128, D], F32)
        s16 = K.sb("s16", [128, 16], F32)
        mkb = K.sb("mkb", [128, 4], F32)
        Pk = [K.sb("Pk%d" % i, [128, 16], BF16) for i in range(3)]
        Vst = [K.sb("Vst%d" % i, [128, 4, 65], BF16) for i in range(3)]
        O16sb = K.sb("O16sb", [16, 3, 260], F32)
        OS = K.sb("OS", [NS, 3, 16, 65], F32)
        onat = K.sb("onat", [NS, 16, 64], F32)
        operm = K.sb("operm", [NS, 16, 64], BF16)
        tt = [K.sb("tt%d" % i, [NS, 16, 64], F32) for i in range(2)]
        pnew = K.sb("pnew", [NS, 2, 16], F32)
        cden = K.sb("cden", [NS, 3, 16], F32)
        for b_ in [VCs] + Vst:
            pool.op([], [b_], lambda e, b_=b_: e.memset(b_[:], 0.0))
        pool.op([], [VCs], lambda e: e.memset(VCs[:, :, :, 64:65], 1.0))
        for i in range(len(Vst)):
            pool.op([], [Vst[i]], lambda e, i=i: e.memset(Vst[i][:, :, 64:65], 1.0))
        pool.op([], [blkT], lambda e: e.memset(blkT[:], 0.0))
        pool.op([], [IDX], lambda e: e.memset(IDX[:], 0))
        rs = {"WBs": 0, "KP": 0, "VP": 0, "KPw": 0, "VPw": 0, "p12": 0, "Pk": 0, "Vst": 0}

        def rt(name, lst):
            rs[name] += 1
            return lst[rs[name] % len(lst)]

        def projS(l, chunks, dst, ncols=128, Wsrc=None, rhs_b=None):
            Wsrc = Wsrc if Wsrc is not None else Wc[l]
            rhs_b = rhs_b if rhs_b is not None else uTs
            groups = []
            i = 0
            while i < len(chunks):
                n = 2 if (i + 1 < len(chunks) and chunks[i + 1] == chunks[i] + 1) else 1
                groups.append((i, chunks[i], n))
                i += n

            def load(g):
                buf = rt("WBs", WBs)
                sp.dma(buf[:, 0:g[2], :, :], Wsrc[g[1]:g[1] + g[2]].rearrange("c p k n -> p c k n"), [Wsrc], [buf], buf)
                return buf
            ps = nxt("A")
            cur = load(groups[0])
            for gi, g in enumerate(groups):
                nb_ = load(groups[gi + 1]) if gi + 1 < len(groups) else None
                for j in range(g[2]):
                    ii = g[0] + j
                    for kc in range(8):
                        mm(ps, ps[0:ncols, ii * NS:(ii + 1) * NS], cur, cur[:, j, kc, 0:ncols], rhs_b, rhs_b[:, kc, :], kc == 0, kc == 7)
                cur = nb_
            n = len(chunks)
            act.op([ps], [dst], lambda e: e.copy(out=dst[0:ncols, 0:n, :], in_=ps[0:ncols, 0:n * NS].rearrange("p (c t) -> p c t", t=NS)))

        def bc3(ap2, n):
            return ap2.to_broadcast([128, 8, n])

        def tokmajor(src_fm, dst_rows):
            pb = nxt("A")
            for c in range(8):
                pe.op([src_fm, cf], [pb], lambda e, c=c: e.transpose(out=pb[0:NS, c * 128:(c + 1) * 128], in_=src_fm[:, c, :], identity=ident_f(128)))
            pb2 = nxt("A")
            for c in range(4, 8):
                pass
            act.op([pb], [stl], lambda e: e.copy(out=stl[0:NS, 0:512], in_=pb[0:NS, 0:512]))
            return pb

        def layer_sample(l, xsrc, xdst):
            load_layer_small(l)
            cv = l * 9
            sp.dma(xsb[:], xsrc[:, :], [xsrc], [xsb], xsb)
            dve.op([], [sms], lambda e: e.memset(sms[:, 0:1], 0.0))
            act.op([xsb, sms], [ubs, sms], lambda e: e.activation(out=ubs[:, :], in_=xsb[:, :], func=AF.Square, accum_out=sms[0:NS, 0:1]))
            dve.op([sms], [sms], lambda e: e.tensor_scalar(out=sms[0:NS, 1:2], in0=sms[0:NS, 0:1], scalar1=1.0 / D, scalar2=EPS, op0=ALU.mult, op1=ALU.add))
            act.op([sms], [sms], lambda e: e.sqrt(out=sms[0:NS, 1:2], in_=sms[0:NS, 1:2]))
            dve.op([sms], [sms], lambda e: e.reciprocal(out=sms[0:NS, 2:3], in_=sms[0:NS, 1:2]))
            dve.op([xsb, sms, gbc[0]], [ubs], lambda e: e.scalar_tensor_tensor(out=ubs[:, :], in0=xsb[:, :], scalar=sms[0:NS, 2:3], in1=gbc[0][0:NS, :], op0=ALU.mult, op1=ALU.mult))
            for c in range(8):
                pe.op([ubs, cbm], [psT], lambda e, c=c: e.transpose(out=psT[:, c * NS:(c + 1) * NS], in_=ubs[0:NS, c * 128:(c + 1) * 128], identity=cbm[0:NS, 0:NS]))
            act.op([psT], [uTs], lambda e: e.copy(out=uTs[:, :, :], in_=psT[:, 0:8 * NS].rearrange("p (c t) -> p c t", t=NS)))
            for blk in range(3):
                sp.dma(WKs[:], Wkv[l][blk], [Wkv[l]], [WKs], WKs)
                ps = nxt("A")
                for kc in range(8):
                    mm(ps, ps[0:NS, :], uTs, uTs[:, kc, :], WKs, WKs[:, kc, :], kc == 0, kc == 7)
                act.op([ps], [kvtok], lambda e, ps=ps, blk=blk: e.copy(out=kvtok[:, blk * 512:(blk + 1) * 512], in_=ps[0:NS, :]))
            sp.dma(skv_out[l, :, :], kvtok[:, :], [kvtok], [skv_out], kvtok)
            for blk in range(2):
                sp.dma(WKs[:], Wqt[l][blk], [Wqt[l]], [WKs], WKs)
                ps = nxt("A")
                for kc in range(8):
                    mm(ps, ps[0:NS, :], uTs, uTs[:, kc, :], WKs, WKs[:, kc, :], kc == 0, kc == 7)
                act.op([ps], [qtok], lambda e, ps=ps, blk=blk: e.mul(out=qtok[:, blk * 512:(blk + 1) * 512], in_=ps[0:NS, :], mul=0.125))
            projS(l, [CH_Q + c for c in range(8)], FM["t0"])
            act.op([FM["t0"]], [QTs], lambda e: e.mul(out=QTs[:, :, :], in_=FM["t0"][:, :, :], mul=0.125))
            projS(l, [CH_LRUX + c for c in range(8)], FM["xl"])
            projS(l, [CH_LRUG + c for c in range(8)], FM["lg"])
            projS(l, [CH_POOLX + c for c in range(8)], FM["px"])
            projS(l, [CH_POOLG + c for c in range(8)], FM["pg"])
            projS(l, [CH_NSAG + c for c in range(8)], FM["ng"])
            projS(l, [CH_BG], gat_f, ncols=48) if False else None
            psg = nxt("A")
            bw = rt("WBs", WBs)
            sp.dma(bw[:, 0:1, :, :], Wc[l][CH_BG:CH_BG + 1].rearrange("c p k n -> p c k n"), [Wc[l]], [bw], bw)
            for kc in range(8):
                mm(psg, psg[0:48, 0:NS], bw, bw[:, 0, kc, 0:48], uTs, uTs[:, kc, :], kc == 0, kc == 7)
            act.op([psg], [gat_f], lambda e: e.activation(out=gat_f[:, :], in_=psg[0:48, 0:NS], func=AF.Sigmoid))
            pe.op([gat_f, cf], [psM], lambda e: e.transpose(out=psM[0:NS, 0:48], in_=gat_f[0:48, 0:NS], identity=ident_f(48)))
            act.op([psM], [gat_t], lambda e: e.copy(out=gat_t[:, :], in_=psM[0:NS, 0:48]))
            for n_ in range(3):
                projS(l, [CH_MG + n_ * 8 + dc for dc in range(8)], FM["t0"])
                act.op([FM["t0"]], [mgs], lambda e, n_=n_: e.activation(out=mgs[:, n_ * 8:(n_ + 1) * 8, :], in_=FM["t0"][:, :, :], func=AF.Sigmoid))

            if SST <= 1:
                return
            def load_fm(src_ap, rows, dst, width):
                sp.dma(stl[0:rows, :], src_ap, [st_conv, st_lru, st_pool], [stl], stl)
                for c in range(8):
                    pe.op([stl, cf], [psM], lambda e, c=c: e.transpose(out=psM[:, 0:rows], in_=stl[0:rows, c * 128:(c + 1) * 128], identity=ident_f(rows)))
                    act.op([psM], [dst], lambda e, c=c: e.copy(out=dst[:, c, width[0]:width[0] + rows], in_=psM[:, 0:rows]))
            load_fm(st_conv[l, :, :], NS * 3, CS, (0,))
            load_fm(st_lru[l, :, :], NS, HS, (0,))
            half_ = (NS * 15 + 1) // 2 if NS * 15 > 128 else NS * 15
            r0_ = 0
            while r0_ < NS * 15:
                rws = min(half_, NS * 15 - r0_)
                load_fm(st_pool[l, r0_:r0_ + rws, :], rws, PSs, (r0_,))
                r0_ += rws

            xl, t0_, t1_, t2_, t3_, hN = FM["xl"], FM["t0"], FM["t1"], FM["t2"], FM["t3"], FM["h"]
            CS4 = CS[:, :, :].rearrange("p c (b k) -> p c b k", k=3)
            dve.op([xl, CV], [t0_], lambda e: e.tensor_tensor(out=t0_[:, :, :], in0=xl[:, :, :], in1=bc3(CV[:, :, cv + 3:cv + 4], NS), op=ALU.mult))
            for k in range(3):
                dve.op([CS, CV], [t1_], lambda e, k=k: e.tensor_tensor(out=t1_[:, :, :], in0=CS4[:, :, :, k], in1=bc3(CV[:, :, cv + k:cv + k + 1], NS), op=ALU.mult))
                dve.op([t0_, t1_], [t0_], lambda e: e.tensor_tensor(out=t0_[:, :, :], in0=t0_[:, :, :], in1=t1_[:, :, :], op=ALU.add))
            dve.op([t0_, CV], [t0_], lambda e: e.tensor_tensor(out=t0_[:, :, :], in0=t0_[:, :, :], in1=bc3(CV[:, :, cv + 4:cv + 5], NS), op=ALU.add))
            act.op([t0_], [xcbs], lambda e: e.copy(out=xcbs[:, :, :], in_=t0_[:, :, :]))
            pr = nxt("S")
            for ax in range(2):
                for c in range(8):
                    mm(pr, pr[:, (ax * 8 + c) * NS:(ax * 8 + c + 1) * NS], WRG, WRG[:, ax, c, :], xcbs, xcbs[:, c, :], True, True)
            prv = pr[:, 0:16 * NS].rearrange("p (a c t) -> p a c t", a=2, t=NS)
            dve.op([pr, CV], [t1_], lambda e: e.tensor_tensor(out=t1_[:, :, :], in0=prv[:, 0, :, :], in1=bc3(CV[:, :, cv + 5:cv + 6], NS), op=ALU.add))
            dve.op([pr, CV], [t2_], lambda e: e.tensor_tensor(out=t2_[:, :, :], in0=prv[:, 1, :, :], in1=bc3(CV[:, :, cv + 6:cv + 7], NS), op=ALU.add))
            act.op([t1_], [t1_], lambda e: e.activation(out=t1_[:, :, :], in_=t1_[:, :, :], func=AF.Sigmoid))
            act.op([t2_], [t2_], lambda e: e.activation(out=t2_[:, :, :], in_=t2_[:, :, :], func=AF.Sigmoid))
            dve.op([t1_, CV], [t1_], lambda e: e.tensor_tensor(out=t1_[:, :, :], in0=t1_[:, :, :], in1=bc3(CV[:, :, 18 + l:19 + l], NS), op=ALU.mult))
            act.op([t1_], [t1_], lambda e: e.activation(out=t1_[:, :, :], in_=t1_[:, :, :], func=AF.Exp))
            dve.op([t1_], [t3_], lambda e: e.scalar_tensor_tensor(out=t3_[:, :, :], in0=t1_[:, :, :], scalar=-1.0, in1=t1_[:, :, :], op0=ALU.mult, op1=ALU.mult))
            act.op([t3_], [t3_], lambda e: e.activation(out=t3_[:, :, :], in_=t3_[:, :, :], func=AF.Sqrt, bias=1.0, scale=1.0))
            dve.op([t2_, t0_], [t2_], lambda e: e.tensor_tensor(out=t2_[:, :, :], in0=t2_[:, :, :], in1=t0_[:, :, :], op=ALU.mult))
            dve.op([t2_, t3_], [t2_], lambda e: e.tensor_tensor(out=t2_[:, :, :], in0=t2_[:, :, :], in1=t3_[:, :, :], op=ALU.mult))
            dve.op([t1_, HS], [t1_], lambda e: e.tensor_tensor(out=t1_[:, :, :], in0=t1_[:, :, :], in1=HS[:, :, :], op=ALU.mult))
            dve.op([t1_, t2_], [hN], lambda e: e.tensor_tensor(out=hN[:, :, :], in0=t1_[:, :, :], in1=t2_[:, :, :], op=ALU.add))
            act.op([FM["lg"]], [t3_], lambda e: e.activation(out=t3_[:, :, :], in_=FM["lg"][:, :, :], func=AF.Silu))
            dve.op([hN, t3_], [zbs[0]], lambda e: e.tensor_tensor(out=zbs[0][:, :, :], in0=hN[:, :, :], in1=t3_[:, :, :], op=ALU.mult))

            px = FM["px"]
            PS4 = PSs[:, :, :].rearrange("p c (b k) -> p c b k", k=15)
            for gq, w in enumerate(POOL_WIN):
                cs_ = slice(2 * gq, 2 * gq + 2)
                dve.op([PSs], [t0_], lambda e, cs_=cs_, w=w: e.tensor_reduce(out=t0_[:, cs_, :], in_=PS4[:, cs_, :, 15 - (w - 1):15], axis=AX.X, op=ALU.add))
                dve.op([t0_, px], [t0_], lambda e, cs_=cs_: e.tensor_tensor(out=t0_[:, cs_, :], in0=t0_[:, cs_, :], in1=px[:, cs_, :], op=ALU.add))
                dve.op([t0_, px], [pdb], lambda e, cs_=cs_, w=w: e.scalar_tensor_tensor(out=pdb[:, cs_, :], in0=t0_[:, cs_, :], scalar=1.0 / w, in1=px[:, cs_, :], op0=ALU.mult, op1=ALU.subtract))
            pm = nxt("S")
            for c in range(8):
                gq, eh = c // 2, c % 2
                for dh in range(2):
                    mm(pm, pm[:, c * NS:(c + 1) * NS], WPL, WPL[:, gq, dh, eh * 128:(eh + 1) * 128], pdb, pdb[:, 2 * gq + dh, :], dh == 0, dh == 1)
            act.op([FM["pg"]], [t3_], lambda e: e.activation(out=t3_[:, :, :], in_=FM["pg"][:, :, :], func=AF.Silu))
            dve.op([pm, CV], [t0_], lambda e: e.tensor_tensor(out=t0_[:, :, :], in0=pm[:, 0:8 * NS].rearrange("p (c t) -> p c t", t=NS), in1=bc3(CV[:, :, cv + 8:cv + 9], NS), op=ALU.mult))
            dve.op([t0_, t3_], [zbs[1]], lambda e: e.tensor_tensor(out=zbs[1][:, :, :], in0=t0_[:, :, :], in1=t3_[:, :, :], op=ALU.mult))

            def rows_out(src_fm, dst_ap, dst_buf):
                for hh in range(2):
                    pb = nxt("A")
                    for c4 in range(4):
                        c = hh * 4 + c4
                        pe.op([src_fm, cf], [pb], lambda e, c=c, c4=c4, pb=pb: e.transpose(out=pb[0:NS, c4 * 128:(c4 + 1) * 128], in_=src_fm[:, c, :], identity=ident_f(128)))
                    act.op([pb], [stl], lambda e, pb=pb, hh=hh: e.copy(out=stl[0:NS, hh * 512:(hh + 1) * 512], in_=pb[0:NS, :]))
                sp.dma(dst_ap, stl[0:NS, :], [stl], [dst_buf], stl)
            sconv3 = sconv_out[l, :, :].rearrange("(b k) d -> b k d", k=3)
            sp.dma(sconv3[:, 0:2, :], st_conv[l, :, :].rearrange("(b k) d -> b k d", k=3)[:, 1:3, :], [st_conv], [sconv_out], stl)
            rows_out(xl, sconv3[:, 2, :], sconv_out)
            rows_out(hN, slru_out[l, :, :], slru_out)
            spool3 = spool_out[l, :, :].rearrange("(b k) d -> b k d", k=15)
            sp.dma(spool3[:, 0:14, :], st_pool[l, :, :].rearrange("(b k) d -> b k d", k=15)[:, 1:15, :], [st_pool], [spool_out], stl)
            rows_out(px, spool3[:, 14, :], spool_out)
            for i in range(2):
                sp.dma(swin_out[i][l, :, 0:WBUF - 1, :], winc[i][l, :, 1:WBUF, :], [winc[i]], [swin_out[i]], stl)
                sp.dma(swin_out[i][l, :, WBUF - 1, :], kvtok[:, 1024 + i * 256:1280 + i * 256], [kvtok], [swin_out[i]], kvtok)

            if SST <= 2:
                return
            for kvi in range(2):
                i_ = l * 2 + kvi
                for fsx in range(2):
                    src = cmp_pos[l, kvi, fsx * 16:(fsx + 1) * 16, :]
                    for ii in range(8):
                        for g in range(4):
                            sp.dma(WT[ii * 16:(ii + 1) * 16, kvi, fsx, g * 64:(g + 1) * 64], src, [cmp_pos], [WT], WT)
            if SST <= 3:
                return
            for b in range(NS):
                sample_attn(l, b)
            if SST <= 10:
                return
            sample_combine(l)
            merge_out(l, xsrc, xdst)

        def sample_attn(l, b):
            sp.dma(ptb[:], ptab[b:b + 1, :].partition_broadcast(128), [ptab], [ptb], ptb)
            dve.op([ptb], [ptf], lambda e: e.tensor_copy(out=ptf[:], in_=ptb[:]))
            dve.op([csb], [sms], lambda e: e.tensor_scalar(out=sms[:, 8:9], in0=csb[:, o_p:o_p + 1], scalar1=float(l * NPHYS * 128), scalar2=None, op0=ALU.add))
            dve.op([ptf, sms], [ptf], lambda e: e.tensor_scalar(out=ptf[:], in0=ptf[:], scalar1=128.0, scalar2=sms[:, 8:9], op0=ALU.mult, op1=ALU.add))
            dve.op([ptf], [IDX], lambda e: e.tensor_copy(out=IDX[:], in_=ptf[:]))
            if SST <= 4:
                return
            for kvi in range(2):
                i_ = l * 2 + kvi
                banks = [psA[0], psA[1], psS[0], psS[1]]
                for pg in range(NPG):
                    kp = rt("KP", KP)
                    idma(pool, kp[:, :], pools[kvi][:, :], IDX[:, pg:pg + 1], [pools[kvi], IDX], [kp], kp)
                    pp = rt("p12", p12)
                    for fsx in range(2):
                        dve.op([kp, WT], [pp], lambda e, fsx=fsx, kp=kp, pp=pp: e.tensor_tensor(out=pp[:, fsx, :], in0=kp[:, :], in1=WT[:, kvi, fsx, :], op=ALU.mult))
                    for fsx in range(2):
                        for cc in range(2):
                            bk = banks[fsx * 2 + cc]
                            mm(bk, bk[:, pg * 8:(pg + 1) * 8], pp, pp[:, fsx, cc * 128:(cc + 1) * 128], indb, indb[:, :], True, True)
                for q_ in range(4):
                    act.op([banks[q_]], [fsb], lambda e, q_=q_: e.copy(out=fsb[:, q_, :], in_=banks[q_][:, 0:NMS * 128]))
                NC_ = NMS * 128
                for cc in range(2):
                    dve.op([fsb], [blkT], lambda e, cc=cc: e.tensor_tensor(out=blkT[:, cc, 1:NC_], in0=fsb[:, cc, 0:NC_ - 1], in1=fsb[:, 2 + cc, 1:NC_], op=ALU.add))
                if kvi == 0:
                    for cc in range(2):
                        pk = nxt("O")
                        mm(pk, pk[:, 0:NC_], PHI, PHI[:, i_, :], blkT, blkT[:, cc, :], True, True)
                        act.op([pk], [KCs], lambda e, cc=cc, pk=pk: e.copy(out=KCs[:, cc, :], in_=pk[:, 0:NC_]))
                else:
                    for mt in range(NMS):
                        pk = nxt("O")
                        for cc in range(2):
                            mm(pk, pk[:, cc * 128:(cc + 1) * 128], blkT, blkT[:, cc, mt * 128:(mt + 1) * 128], PHI, PHI[:, i_, :], True, True)
                        act.op([pk], [VCs], lambda e, mt=mt, pk=pk: e.copy(out=VCs[:, mt, :, 0:64], in_=pk[:, 0:256].rearrange("p (g d) -> p g d", d=64)))
            if SST <= 5:
                return
            pSh = [nxt("S"), nxt("S")]
            for hf in range(2):
                for mt in range(NMS):
                    for g in (hf, hf + 2):
                        cq = 4 * (g // 2)
                        mm(pSh[hf], pSh[hf][:, mt * 16 + 4 * g:mt * 16 + 4 * g + 4], KCs, KCs[hf * 64:hf * 64 + 64, g // 2, mt * 128:(mt + 1) * 128],
                           QTs, QTs[hf * 64:hf * 64 + 64, cq:cq + 4, b], True, True)
            pat = "p (m gi hf r) -> p m gi hf r"
            S16v = S16[:, :, :].rearrange("p m (gi hf r) -> p m gi hf r", hf=2, r=4)
            absv = csb[:, o_abs:o_abs + NMS * 16].rearrange(pat, gi=2, hf=2, r=4)
            for hf in range(2):
                pv_ = pSh[hf][:, 0:NMS * 16].rearrange(pat, gi=2, hf=2, r=4)
                dve.op([pSh[hf], csb], [S16], lambda e, hf=hf, pv_=pv_: e.tensor_tensor(out=S16v[:, :, :, hf, :], in0=pv_[:, :, :, hf, :], in1=absv[:, :, :, hf, :], op=ALU.add))
            act.op([S16], [P16], lambda e: e.activation(out=P16[:, :, :], in_=S16[:, :, :], func=AF.Exp))
            pD = nxt("O")
            pO = nxt("O")
            for mt in range(NMS):
                mm(pD, pD[:, 0:16], ones_bf, ones_bf[:, :], P16, P16[:, mt, :], mt == 0, mt == NMS - 1)
            for mt in range(NMS):
                mm(pO, pO[0:16, 0:260], P16, P16[:, mt, :], VCs, VCs[:, mt, :, :], mt == 0, mt == NMS - 1)
            act.op([pO], [O16sb], lambda e: e.copy(out=O16sb[:, 0, :], in_=pO[0:16, 0:260]))
            dve.op([pD], [rd16], lambda e: e.tensor_scalar(out=rd16[:, :], in0=pD[:, 0:16], scalar1=1e-30, scalar2=None, op0=ALU.max))
            dve.op([rd16], [rd16], lambda e: e.reciprocal(out=rd16[:, :], in_=rd16[:, :]))
            for mt in range(NMS):
                dve.op([P16, rd16], [S16], lambda e, mt=mt: e.tensor_tensor(out=S16[:, mt, :], in0=P16[:, mt, :], in1=rd16[:, :], op=ALU.mult))
            dve.op([S16], [Pn4f], lambda e: e.tensor_reduce(out=Pn4f[:, :, :], in_=S16[:, :, :].rearrange("p m (g r) -> p m g r", r=4), axis=AX.X, op=ALU.add))
            act.op([Pn4f], [Pn4], lambda e: e.copy(out=Pn4[:, :, :], in_=Pn4f[:, :, :]))
            p4 = nxt("S")
            for mt in range(NMS):
                mm(p4, p4[0:4, 0:128], Pn4, Pn4[:, mt, :], gsb, gsb[:, mt * 128:(mt + 1) * 128], mt == 0, mt == NMS - 1)
            if SST <= 6:
                return
            dve.op([p4, csb], [sc4], lambda e: e.tensor_tensor(out=sc4[:, :], in0=p4[0:4, 0:128], in1=csb[0:4, o_fb:o_fb + 128], op=ALU.add))
            NBs = NPG * 2
            dve.op([sc4], [m84], lambda e: e.max(out=m84[:, 0:8], in_=sc4[:, 0:NBs]))
            dve.op([sc4, m84], [wk4], lambda e: e.match_replace(out=wk4[:, 0:NBs], in_to_replace=m84[:, 0:8], in_values=sc4[:, 0:NBs], imm_value=-3.0e38))
            dve.op([wk4], [m84], lambda e: e.max(out=m84[:, 8:16], in_=wk4[:, 0:NBs]))
            dve.op([], [mb4], lambda e: e.memset(mb4[:, :], 0.0))
            dve.op([sc4, m84], [mb4], lambda e: e.tensor_scalar(out=mb4[:, 0:NBs], in0=sc4[:, 0:NBs], scalar1=m84[:, 14:15], scalar2=NEGB, op0=ALU.is_lt, op1=ALU.mult))
            pe.op([mb4, cf], [psM], lambda e: e.transpose(out=psM[:, 0:4], in_=mb4[0:4, :], identity=ident_f(4)))
            act.op([psM], [MB4T], lambda e: e.copy(out=MB4T[:, :], in_=psM[:, 0:4]))
            if SST <= 7:
                return
            for hh in range(2):
                pq = nxt("A")
                mm(pq, pq[:, :], cf, cf[0:NS, o_id + b:o_id + b + 1].to_broadcast([NS, 128]), qtok, qtok[0:NS, hh * 512:(hh + 1) * 512], True, True)
                act.op([pq], [QBs], lambda e, pq=pq, hh=hh: e.copy(out=QBs[:, hh * 512:(hh + 1) * 512], in_=pq[:, :]))

            def keytile(kp, vp, bias_ap, mask_pg, pO_, first, last_):
                dve.op([kp, QBs], [prod], lambda e: e.tensor_tensor(out=prod[:, :].rearrange("p (g r d) -> p g r d", r=4, d=64),
                                                                    in0=kp[:, :].rearrange("p (g o d) -> p g o d", o=1, d=64).to_broadcast([128, 4, 4, 64]),
                                                                    in1=QBs[:, :].rearrange("p (g r d) -> p g r d", r=4, d=64), op=ALU.mult))
                dve.op([prod], [s16], lambda e: e.tensor_reduce(out=s16[:, :], in_=prod[:, :].rearrange("p (h d) -> p h d", d=64), axis=AX.X, op=ALU.add))
                dve.op([s16, csb], [s16], lambda e: e.tensor_tensor(out=s16[:, :], in0=s16[:, :], in1=bias_ap, op=ALU.add))
                if mask_pg is not None:
                    a2, v = ((2 * mask_pg) // 64) % 2, mask_pg % 32
                    pm_ = nxt("S")
                    mm(pm_, pm_[:, 0:4], cb, cb[a2 * 64:a2 * 64 + 64, o_selh + v * 128:o_selh + (v + 1) * 128], MB4T, MB4T[a2 * 64:a2 * 64 + 64, :], True, True)
                    act.op([pm_], [mkb], lambda e: e.copy(out=mkb[:, :], in_=pm_[:, 0:4]))
                    dve.op([s16, mkb], [s16], lambda e: e.tensor_tensor(out=s16[:, :].rearrange("p (g r) -> p g r", r=4), in0=s16[:, :].rearrange("p (g r) -> p g r", r=4),
                                                                        in1=mkb[:, :].rearrange("p (g o) -> p g o", o=1).to_broadcast([128, 4, 4]), op=ALU.add))
                pk_ = rt("Pk", Pk)
                act.op([s16], [pk_], lambda e: e.activation(out=pk_[:, :], in_=s16[:, :], func=AF.Exp))
                vs_ = rt("Vst", Vst)
                act.op([vp], [vs_], lambda e: e.copy(out=vs_[:, :, 0:64], in_=vp[:, :].rearrange("p (g d) -> p g d", d=64)))
                mm(pO_, pO_[0:16, 0:260], pk_, pk_[:, :], vs_, vs_[:, :, :], first, last_)

            if SST <= 8:
                return
            pOs = nxt("O")
            for pg in range(NPG):
                kp, vp = rt("KP", KP), rt("VP", VP)
                idma(pool, kp[:, :], pools[2][:, :], IDX[:, pg:pg + 1], [pools[2], IDX], [kp], kp)
                idma(pool, vp[:, :], pools[3][:, :], IDX[:, pg:pg + 1], [pools[3], IDX], [vp], vp)
                keytile(kp, vp, csb[:, o_abk + pg * 16:o_abk + (pg + 1) * 16], pg, pOs, pg == 0, pg == NPG - 1)
            act.op([pOs], [O16sb], lambda e: e.copy(out=O16sb[:, 1, :], in_=pOs[0:16, 0:260]))
            if SST <= 9:
                return
            pOw = nxt("O")
            NWT = WBUF // 128
            for t in range(NWT):
                kp, vp = rt("KPw", KPw), rt("VPw", VPw)
                sp.dma(kp[:, :], winc[0][l, b, t * 128:(t + 1) * 128, :], [winc[0]], [kp], kp)
                sp.dma(vp[:, :], winc[1][l, b, t * 128:(t + 1) * 128, :], [winc[1]], [vp], vp)
                keytile(kp, vp, csb[:, o_abw + t * 16:o_abw + (t + 1) * 16], None, pOw, t == 0, t == NWT - 1)
            act.op([pOw], [O16sb], lambda e: e.copy(out=O16sb[:, 2, :], in_=pOw[0:16, 0:260]))
            for br in range(3):
                for g in range(4):
                    sp.dma(OS[b:b + 1, br, 4 * g:4 * g + 4, :], O16sb[4 * g:4 * g + 4, br, g * 65:(g + 1) * 65], [O16sb], [OS], OS)

        def sample_combine(l):
            q4 = qtok[:, :].rearrange("p (g r d) -> p g r d", r=4, d=64)
            for i, (ko, vo) in enumerate(((512, 768), (1024, 1280))):
                kn = kvtok[:, ko:ko + 256].rearrange("p (g o d) -> p g o d", o=1, d=64).to_broadcast([NS, 4, 4, 64])
                dve.op([qtok, kvtok], [tt[0]], lambda e, kn=kn: e.tensor_tensor(out=tt[0][:, :, :].rearrange("p (g r) d -> p g r d", r=4), in0=q4, in1=kn, op=ALU.mult))
                dve.op([tt[0]], [pnew], lambda e, i=i: e.tensor_reduce(out=pnew[:, i, :], in_=tt[0][:, :, :], axis=AX.X, op=ALU.add))
            act.op([pnew], [pnew], lambda e: e.activation(out=pnew[:, :, :], in_=pnew[:, :, :], func=AF.Exp))
            g3 = gat_t[:, :].rearrange("p (h b) -> p h b", b=3)
            for br in range(3):
                if br == 0:
                    dve.op([OS], [cden], lambda e: e.tensor_scalar(out=cden[:, 0, :], in0=OS[:, 0, :, 64], scalar1=1e-30, scalar2=None, op0=ALU.max))
                else:
                    dve.op([OS, pnew], [cden], lambda e, br=br: e.tensor_tensor(out=cden[:, br, :], in0=OS[:, br, :, 64], in1=pnew[:, br - 1, :], op=ALU.add))
                dve.op([cden], [cden], lambda e, br=br: e.reciprocal(out=cden[:, br, :], in_=cden[:, br, :]))
                dve.op([cden, gat_t], [cden], lambda e, br=br: e.tensor_tensor(out=cden[:, br, :], in0=cden[:, br, :], in1=g3[:, :, br], op=ALU.mult))
                num = tt[0]
                if br == 0:
                    dve.op([OS], [num], lambda e: e.tensor_copy(out=num[:, :, :], in_=OS[:, 0, :, 0:64]))
                else:
                    vo = 768 if br == 1 else 1280
                    vn = kvtok[:, vo:vo + 256].rearrange("p (g o d) -> p g o d", o=1, d=64).to_broadcast([NS, 4, 4, 64])
                    dve.op([kvtok, pnew], [num], lambda e, br=br, vn=vn: e.tensor_tensor(out=num[:, :, :].rearrange("p (g r) d -> p g r d", r=4), in0=vn,
                                                                                  in1=pnew[:, br - 1, :].rearrange("p (g r o) -> p g r o", r=4, o=1).to_broadcast([NS, 4, 4, 64]), op=ALU.mult))
                    dve.op([num, OS], [num], lambda e, br=br: e.tensor_tensor(out=num[:, :, :], in0=num[:, :, :], in1=OS[:, br, :, 0:64], op=ALU.add))
                dve.op([num, cden], [tt[1]], lambda e, br=br: e.tensor_tensor(out=tt[1][:, :, :], in0=num[:, :, :],
                                                                           in1=cden[:, br, :].rearrange("p (h o) -> p h o", o=1).to_broadcast([NS, 16, 64]), op=ALU.mult))
                if br == 0:
                    dve.op([tt[1]], [onat], lambda e: e.tensor_copy(out=onat[:, :, :], in_=tt[1][:, :, :]))
                else:
                    dve.op([tt[1], onat], [onat], lambda e: e.tensor_tensor(out=onat[:, :, :], in0=onat[:, :, :], in1=tt[1][:, :, :], op=ALU.add))
            dve.op([onat], [operm], lambda e: e.tensor_copy(out=operm[:, :, :].rearrange("p (hi r e) d -> p hi r e d", r=4, e=2),
                                                           in_=onat[:, :, :].rearrange("p (hi e r) d -> p hi r e d", e=2, r=4)))
            for c in range(8):
                pe.op([operm, cbm], [psT], lambda e, c=c: e.transpose(out=psT[:, c * NS:(c + 1) * NS], in_=operm[0:NS, 2 * c:2 * c + 2, :].rearrange("p h d -> p (h d)"), identity=cbm[0:NS, 0:NS]))
            act.op([FM["ng"]], [FM["t3"]], lambda e: e.activation(out=FM["t3"][:, :, :], in_=FM["ng"][:, :, :], func=AF.Silu))
            dve.op([psT, FM["t3"]], [zbs[2]], lambda e: e.tensor_tensor(out=zbs[2][:, :, :], in0=psT[:, 0:8 * NS].rearrange("p (c t) -> p c t", t=NS), in1=FM["t3"][:, :, :], op=ALU.mult))

        def merge_out(l, xsrc, xdst):
            for n_ in range(3):
                projS(l, [n_ * 8 + dc for dc in range(8)], FM["t0"], Wsrc=Wb[l], rhs_b=zbs[n_])
                if n_ == 0:
                    dve.op([FM["t0"], mgs], [maccs], lambda e: e.tensor_tensor(out=maccs[:, :, :], in0=FM["t0"][:, :, :], in1=mgs[:, 0:8, :], op=ALU.mult))
                else:
                    dve.op([FM["t0"], mgs], [FM["t1"]], lambda e, n_=n_: e.tensor_tensor(out=FM["t1"][:, :, :], in0=FM["t0"][:, :, :], in1=mgs[:, n_ * 8:(n_ + 1) * 8, :], op=ALU.mult))
                    dve.op([FM["t1"], maccs], [maccs], lambda e: e.tensor_tensor(out=maccs[:, :, :], in0=maccs[:, :, :], in1=FM["t1"][:, :, :], op=ALU.add))
            act.op([maccs], [mergs], lambda e: e.copy(out=mergs[:, :, :], in_=maccs[:, :, :]))
            sp.dma(WKs[:], Wo[l][:, :, 0:512], [Wo[l]], [WKs], WKs)
            sp.dma(WK2[:], Wo[l][:, :, 512:1024], [Wo[l]], [WK2], WK2)
            wo = [WKs, WK2]
            po = [nxt("A"), nxt("A")]
            dve.op([], [sms], lambda e: e.memset(sms[:, 16:18], 0.0))
            for eh in range(2):
                for dc in range(8):
                    mm(po[eh], po[eh][0:NS, :], mergs, mergs[:, dc, :], wo[eh], wo[eh][:, dc, :], dc == 0, dc == 7)
                act.op([po[eh], sms], [ubs, sms], lambda e, eh=eh: e.activation(out=ubs[:, eh * 512:(eh + 1) * 512], in_=po[eh][0:NS, :], func=AF.Square, accum_out=sms[0:NS, 16 + eh:17 + eh]))
            dve.op([sms], [sms], lambda e: e.tensor_tensor(out=sms[0:NS, 18:19], in0=sms[0:NS, 16:17], in1=sms[0:NS, 17:18], op=ALU.add))
            dve.op([sms], [sms], lambda e: e.tensor_scalar(out=sms[0:NS, 19:20], in0=sms[0:NS, 18:19], scalar1=1.0 / D, scalar2=EPS, op0=ALU.mult, op1=ALU.add))
            act.op([sms], [sms], lambda e: e.sqrt(out=sms[0:NS, 19:20], in_=sms[0:NS, 19:20]))
            dve.op([sms], [sms], lambda e: e.reciprocal(out=sms[0:NS, 20:21], in_=sms[0:NS, 19:20]))
            for eh in range(2):
                dve.op([po[eh], sms, gbc[1]], [ysb], lambda e, eh=eh: e.scalar_tensor_tensor(out=ysb[:, eh * 512:(eh + 1) * 512], in0=po[eh][0:NS, :], scalar=sms[0:NS, 20:21], in1=gbc[1][0:NS, eh * 512:(eh + 1) * 512], op0=ALU.mult, op1=ALU.mult))
            dve.op([ysb, xsb], [ysb], lambda e: e.tensor_tensor(out=ysb[:, :], in0=ysb[:, :], in1=xsb[:, :], op=ALU.add))
            sp.dma(xdst[:, :], ysb[:, :], [ysb], [xdst], ysb)

        layer_sample(0, xs_in, xs1)
        layer_sample(1, xs1, ys_out)

    init_state()
    stage = getattr(cfg, "STAGE", 99)
    do_prompt = stage >= 2 and stage != 45
    if do_prompt:
        layer_prompt(0, x_in, x1 if stage >= 3 else y_out)
    if do_prompt and stage >= 3:
        layer_prompt(1, x1, y_out)
    if stage >= 40:
        K.barrier()
        K.release_to(MARK_PROMPT)
        sample_phase()
    K.finish()
    return K


def _core_inputs(cfg, inp, b):
    f = lambda a: np.ascontiguousarray(np.asarray(a), dtype=np.float32)
    vec = []
    for l in range(2):
        vec += [f(inp["conv_w"])[l, k] for k in range(4)]
        vec += [f(inp["conv_b"])[l], f(inp["b_rg_a"])[l], f(inp["b_rg_x"])[l], f(inp["lru_lambda"])[l], f(inp["pool_scale"])[l]]
    NS = cfg.NS
    sl = slice(b * NS, (b + 1) * NS)
    m = {
        "x": f(inp["x_prompt"])[b],
        "g_pre": f(inp["g_pre"]), "g_post": f(inp["g_post"]), "w_in": f(inp["w_in"]),
        "vecs": np.ascontiguousarray(np.stack(vec, 0)),
        "w_rg": np.ascontiguousarray(np.stack([f(inp["w_rg_a"]), f(inp["w_rg_x"])], axis=1)),
        "w_pool": f(inp["w_pool"]),
        "cmp_pos": np.ascontiguousarray(np.stack([f(inp["cmp_pos_k"]), f(inp["cmp_pos_v"])], axis=1)),
        "cmp_phi": np.ascontiguousarray(np.stack([f(inp["cmp_phi_k"]), f(inp["cmp_phi_v"])], axis=1)),
        "w_branch": f(inp["w_branch"]), "w_out": f(inp["w_out"]),
        "xs_in": np.ascontiguousarray(f(inp["x_sample"])[sl, 0, :]),
        "st_conv": np.ascontiguousarray(f(inp["state_conv"])[:, sl].reshape(2, NS * 3, D)),
        "st_lru": np.ascontiguousarray(f(inp["state_lru"])[:, sl]),
        "st_pool": np.ascontiguousarray(f(inp["state_pool"])[:, sl].reshape(2, NS * 15, D)),
        "ptab": np.ascontiguousarray(np.asarray(inp["page_table"])[sl].astype(np.int32)),
    }
    for i, k in enumerate(("cache_cmp_k", "cache_cmp_v", "cache_sel_k", "cache_sel_v")):
        m["pool%d" % i] = f(inp[k]).reshape(-1, 256)
    for i, k in enumerate(("cache_win_k", "cache_win_v")):
        a = f(inp[k])
        m["winc%d" % i] = np.ascontiguousarray(a[:, sl].reshape(2, NS, a.shape[2], 256))
    m.update(make_consts(cfg))
    return m


def run_cfg(cfg, inp):
    K = build(cfg)
    in_maps = [_core_inputs(cfg, inp, b) for b in range(cfg.NCORES)]
    res = run_bass_kernel_spmd(K.nc, in_maps, core_ids=list(range(cfg.NCORES)))
    r = res.results
    S = cfg.S
    B = cfg.NCORES
    NS = cfg.NS
    wb = min(512, cfg.P)
    cat = lambda name: np.concatenate([r[b][name] for b in range(B)], axis=0)
    cat1 = lambda name: np.concatenate([r[b][name] for b in range(B)], axis=1)
    y_p = np.stack([r[b]["y"] for b in range(B)], 0)
    outs = [y_p, cat("ys")[:, None, :]]
    skv = cat1("skv")
    for i in range(4):
        outs.append(np.stack([r[b]["kvo%d" % i].reshape(2, S, 4, 64) for b in range(B)], 1))
        outs.append(np.ascontiguousarray(skv[:, :, i * 256:(i + 1) * 256]).reshape(2, -1, 1, 4, 64))
    for i in range(2):
        outs.append(np.stack([r[b]["wino%d" % i].reshape(2, 512, 4, 64) for b in range(B)], 1))
        outs.append(cat1("swin%d" % i).reshape(2, -1, wb, 4, 64))
    st = np.stack([r[b]["st"] for b in range(B)], 1)
    outs.append(np.ascontiguousarray(st[:, :, 0:3]))
    outs.append(cat1("sconv").reshape(2, -1, 3, D))
    outs.append(np.ascontiguousarray(st[:, :, 3]))
    outs.append(cat1("slru"))
    outs.append(np.ascontiguousarray(st[:, :, 4:19]))
    outs.append(cat1("spool").reshape(2, -1, 15, D))
    return tuple(np.ascontiguousarray(o, dtype=np.float32) for o in outs)


def kernel(**inputs):
    cfg = Cfg()
    return run_cfg(cfg, inputs)
```

```python
import numpy as np
import concourse.bass as bass
import concourse.mybir as mybir
from concourse.bass_utils import run_bass_kernel_spmd

F32 = mybir.dt.float32
BF16 = mybir.dt.bfloat16
I32 = mybir.dt.int32
AF = mybir.ActivationFunctionType
ALU = mybir.AluOpType
AX = mybir.AxisListType

D = 1024
IN_W = 10800
NEGB = -30000.0
THR_SKIP = 64.0
HPERM = [0, 4, 1, 5, 2, 6, 3, 7, 8, 12, 9, 13, 10, 14, 11, 15]
SLOPES = [2.0 ** (-8.0 * (h + 1) / 16.0) for h in range(16)]
POOL_WIN = (2, 4, 8, 16)


class Ev:
    __slots__ = ("sem", "val", "home")

    def __init__(self, sem, val, home=None):
        self.sem = sem
        self.val = val
        self.home = home


class Buf:
    __slots__ = ("name", "t", "w", "r", "dsem", "dcnt", "waited")

    def __init__(self, name, t=None):
        self.name = name
        self.t = t
        self.w = None
        self.r = {}
        self.dsem = None
        self.dcnt = 0
        self.waited = 0

    def __getitem__(self, idx):
        return self.t[idx]


class Eng:
    def __init__(self, K, eng, name, is_pe=False):
        self.K = K
        self.e = eng
        self.name = name
        self.sem = K.newsem("e_" + name)
        self.cnt = 0
        self.seen = {}
        self.is_pe = is_pe

    def wait(self, ev):
        if ev is None:
            return
        if self.is_pe and ev.sem is self.sem:
            return
        k = ev.sem.num
        val = ev.val if ev.home is None else ev.home.dcnt
        if ev.home is not None:
            ev.home.waited = max(ev.home.waited, val)
        if self.seen.get(k, 0) >= val:
            return
        self.e.wait_ge(ev.sem, val)
        self.seen[k] = val

    def pre(self, reads, writes):
        for b in reads:
            self.wait(b.w)
        for b in writes:
            self.wait(b.w)
            for ev in b.r.values():
                self.wait(ev)

    def post(self, ins, reads, writes):
        self.cnt += 1
        ins.then_inc(self.sem, 1)
        ev = Ev(self.sem, self.cnt)
        for b in reads:
            b.r[self.sem.num] = ev
        for b in writes:
            b.w = ev
            b.r = {}
        return ev

    def op(self, reads, writes, fn):
        self.pre(reads, writes)
        ins = fn(self.e)
        return self.post(ins, reads, writes)

    def dma(self, out_ap, in_ap, reads, writes, home, **kw):
        self.pre(reads, writes)
        if home.dsem is None:
            home.dsem = self.K.newsem("d_" + home.name)
            self.K.homes.append(home)
        if home.waited:
            self.wait(Ev(home.dsem, home.dcnt, home))
        ins = self.e.dma_start(out=out_ap, in_=in_ap, **kw)
        home.dcnt += 16
        ins.then_inc(home.dsem, 16)
        ev = Ev(home.dsem, home.dcnt, home)
        for b in reads:
            b.r[("d", home.dsem.num)] = ev
        for b in writes:
            b.w = ev
            b.r = {}
        return ev


def idma(eng, out_ap, in_ap, idx_ap, reads, writes, home):
    eng.pre(reads, writes)
    if home.dsem is None:
        home.dsem = eng.K.newsem("d_" + home.name)
        eng.K.homes.append(home)
    if home.waited:
        eng.wait(Ev(home.dsem, home.dcnt, home))
    ins = eng.e.indirect_dma_start(out=out_ap, out_offset=None, in_=in_ap, in_offset=bass.IndirectOffsetOnAxis(ap=idx_ap, axis=0))
    home.dcnt += 16
    ins.then_inc(home.dsem, 16)
    ev = Ev(home.dsem, home.dcnt, home)
    for b in reads:
        b.r[("d", home.dsem.num)] = ev
    for b in writes:
        b.w = ev
        b.r = {}
    return ev


class Kern:
    def __init__(self):
        self.nc = bass.Bass("TRN2", target_bir_lowering=False)
        nc = self.nc
        self.nsem = 0
        self.pe = Eng(self, nc.tensor, "pe", is_pe=True)
        self.dve = Eng(self, nc.vector, "dve")
        self.act = Eng(self, nc.scalar, "act")
        self.pool = Eng(self, nc.gpsimd, "pool")
        self.sp = Eng(self, nc.sync, "sp")
        self.outs = []
        self.cms = []
        self.homes = []

    def newsem(self, name):
        self.nsem += 1
        return self.nc.alloc_semaphore(name="%s_%d" % (name, self.nsem))

    def sb(self, name, shape, dt):
        cm = self.nc.sbuf_tensor(name, list(shape), dt)
        t = cm.__enter__()
        self.cms.append(cm)
        return Buf(name, t)

    def release_to(self, mark):
        while len(self.cms) > mark:
            self.cms.pop().__exit__(None, None, None)

    def barrier(self):
        engs = (self.pe, self.dve, self.act, self.pool, self.sp)
        for e in engs:
            for f in engs:
                if f.cnt and not (f is e and e.is_pe):
                    if f is e:
                        e.e.wait_ge(f.sem, f.cnt)
                        e.seen[f.sem.num] = f.cnt
                    else:
                        e.wait(Ev(f.sem, f.cnt))
            for hb in self.homes:
                e.wait(Ev(hb.dsem, hb.dcnt, hb))

    def ps(self, name, shape, dt=F32):
        t = self.nc.psum_tensor(name, list(shape), dt).__enter__()
        return Buf(name, t)

    def dram_in(self, name, shape, dt):
        t = self.nc.dram_tensor(name, list(shape), dt, kind="ExternalInput")
        return Buf(name, t.ap())

    def dram_out(self, name, shape, dt):
        t = self.nc.dram_tensor(name, list(shape), dt, kind="ExternalOutput")
        b = Buf(name, t.ap())
        self.outs.append(b)
        return b

    def dram_tmp(self, name, shape, dt):
        t = self.nc.dram_tensor(name, list(shape), dt, kind="Internal")
        return Buf(name, t.ap())

    def finish(self):
        for b in self.outs:
            self.sp.wait(b.w)
        for e in (self.pe, self.dve, self.act, self.pool):
            if e.cnt:
                self.sp.wait(Ev(e.sem, e.cnt))


class Cfg:
    def __init__(self, S=8192, NS=16, P=8192, NPHYS=2560, NCORES=2):
        self.S = S
        self.NS = NS
        self.P = P
        self.NPHYS = NPHYS
        self.NCORES = NCORES


CH_LRUX, CH_LRUG, CH_POOLX, CH_POOLG, CH_Q, CH_NSAG, CH_KVF, CH_BG, CH_MG = 0, 8, 16, 24, 32, 40, 48, 56, 57
NCHUNK = 81


def make_consts(cfg):
    S = cfg.S
    NKT = S // 128
    NMT = max(1, (S // 16) // 128)
    p = np.arange(128)
    ident = np.eye(128, dtype=np.float32)
    kk, qq = np.meshgrid(p, p, indexing="ij")
    tric = np.where(kk <= qq, 0.0, NEGB).astype(np.float32)
    triw = np.where(kk > qq, 0.0, NEGB).astype(np.float32)
    row0b = np.zeros((128, 512), np.float32)
    row0b[0, :] = NEGB
    x = np.arange(256)
    m = x[None, :] - 128
    jbrel = (p // 64)[:, None]
    patw = np.where(m > jbrel, -1.0e30, np.where((m == jbrel) | (m == jbrel - 1), 1.0e4, 0.0)).astype(np.float32)
    ratio = np.zeros((128, 4, 15), np.float32)
    for g, w in enumerate(POOL_WIN):
        ratio[:, g, :] = (w / np.minimum(w, np.arange(15) + 1.0))[None, :]
    qrel = np.arange(512)
    cmpb = np.zeros((128, 4, 512), np.float32)
    for v in range(4):
        cmpb[:, v, :] = np.where((16 * p[:, None] + 15 - 512 * v) <= qrel[None, :], 0.0, NEGB)
    selh = np.zeros((128, 32, 128), np.float32)
    k = np.arange(128)
    for v in range(32):
        selh[:, v, :] = ((p % 64)[:, None] == (2 * v + k // 64)[None, :]).astype(np.float32)
    gagg = np.zeros((128, NMT, 128), np.float32)
    j = np.arange(128)
    for mt in range(NMT):
        mm_ = 128 * mt + p
        gagg[:, mt, :] = ((mm_[:, None] >= 1) & (((mm_[:, None] - 1) // 4) == j[None, :])).astype(np.float32)
    NAB = NKT + 3
    OFF = NKT - 1
    NABC = NKT + 16 * (NMT - 1) + 1
    ab = np.zeros((128, 16, NAB), np.float64)
    abc = np.zeros((128, 16, NABC), np.float64)
    for h in range(16):
        W = 128 if h <= 3 else 512
        idx = np.arange(NAB)
        ab[:, h, :] = SLOPES[h] * (p[:, None] + 128.0 * (idx[None, :] - OFF) - W / 2.0)
        idx = np.arange(NABC)
        abc[:, h, :] = SLOPES[h] * (16.0 * p[:, None] + 15.0 + 128.0 * (idx[None, :] - OFF) - W / 2.0)
    f32c = np.concatenate([ident, tric, triw, row0b, patw, ratio.reshape(128, -1),
                           ab.reshape(128, -1).astype(np.float32), abc.reshape(128, -1).astype(np.float32)], axis=1)
    bfc = np.concatenate([cmpb.reshape(128, -1), selh.reshape(128, -1), gagg.reshape(128, -1)], axis=1)
    P_ = cfg.P
    NPG = P_ // 128
    NMS = max(1, (P_ // 16) // 128)
    absb = np.zeros((128, NMS, 16), np.float64)
    abk = np.zeros((128, NPG, 16), np.float64)
    abw = np.zeros((128, 4, 16), np.float64)
    for h in range(16):
        for mt in range(NMS):
            mm_ = 128 * mt + p
            absb[:, mt, h] = -SLOPES[h] * (P_ - (16.0 * mm_ + 15.0))
        for pg in range(NPG):
            abk[:, pg, h] = -SLOPES[h] * (P_ - (128.0 * pg + p))
        for t in range(4):
            abw[:, t, h] = -SLOPES[h] * (512.0 - (128.0 * t + p))
    absb[0, 0, :] = NEGB
    abw[0, 0, :] = NEGB
    ind = (p[:, None] // 16 == np.arange(8)[None, :]).astype(np.float32)
    fb = np.zeros((128, 128), np.float32)
    fb[:, 0] = 1.0e4
    fb[:, NPG * 2 - 1] = 1.0e4
    gs = np.zeros((128, NMS, 128), np.float32)
    for mt in range(NMS):
        mm_ = 128 * mt + p
        gs[:, mt, :] = ((mm_[:, None] >= 1) & (((mm_[:, None] - 1) // 4) == j[None, :])).astype(np.float32)
    c_s = np.concatenate([p[:, None].astype(np.float32), absb.reshape(128, -1), abk.reshape(128, -1), abw.reshape(128, -1),
                          ind, fb, gs.reshape(128, -1)], axis=1)
    return {"c_f32": np.ascontiguousarray(f32c, dtype=np.float32), "c_bf": np.ascontiguousarray(bfc, dtype=np.float32),
            "c_s": np.ascontiguousarray(c_s, dtype=np.float32)}


def build(cfg):
    S = cfg.S
    NT = S // 512
    NKT = S // 128
    NBk = min(S // 64, 128)
    NMT = max(1, (S // 16) // 128)
    NAB = NKT + 3
    OFF = NKT - 1
    NABC = NKT + 16 * (NMT - 1) + 1
    NSL = [min(NKT, 24), NKT]

    K = Kern()
    nc = K.nc
    pe, dve, act, pool, sp = K.pe, K.dve, K.act, K.pool, K.sp

    x_in = K.dram_in("x", [S, D], F32)
    g_pre = K.dram_in("g_pre", [2, D], F32)
    g_post = K.dram_in("g_post", [2, D], F32)
    w_in = K.dram_in("w_in", [2, D, IN_W], F32)
    vecs = K.dram_in("vecs", [18, D], F32)
    w_rg = K.dram_in("w_rg", [2, 2, 16, 64, 64], F32)
    w_pool = K.dram_in("w_pool", [2, 4, 256, 256], F32)
    cmp_pos = K.dram_in("cmp_pos", [2, 2, 32, 64], F32)
    cmp_phi = K.dram_in("cmp_phi", [2, 2, 64, 64], F32)
    w_branch = K.dram_in("w_branch", [2, 3, D, D], F32)
    w_out = K.dram_in("w_out", [2, D, D], F32)
    c_f32 = K.dram_in("c_f32", [128, 128 * 3 + 512 + 256 + 60 + 16 * NAB + 16 * NABC], F32)
    c_bf = K.dram_in("c_bf", [128, 2048 + 4096 + NMT * 128], F32)

    y_out = K.dram_out("y", [S, D], F32)
    kv_out = [K.dram_out("kvo%d" % i, [2, S, 256], F32) for i in range(4)]
    win_out = [K.dram_out("wino%d" % i, [2, 512, 256], F32) for i in range(2)]
    st_out = K.dram_out("st", [2, 19, D], F32)

    NS, PL, NPHYS = cfg.NS, cfg.P, cfg.NPHYS
    NPG = PL // 128
    NMS = max(1, (PL // 16) // 128)
    WBUF = min(512, PL)
    xs_in = K.dram_in("xs_in", [NS, D], F32)
    pools = [K.dram_in("pool%d" % i, [2 * NPHYS * 128, 256], F32) for i in range(4)]
    winc = [K.dram_in("winc%d" % i, [2, NS, WBUF, 256], F32) for i in range(2)]
    st_conv = K.dram_in("st_conv", [2, NS * 3, D], F32)
    st_lru = K.dram_in("st_lru", [2, NS, D], F32)
    st_pool = K.dram_in("st_pool", [2, NS * 15, D], F32)
    ptab = K.dram_in("ptab", [NS, NPG], I32)
    c_s = K.dram_in("c_s", [128, 1 + NMS * 16 + NPG * 16 + 64 + 8 + 128 + NMS * 128], F32)
    ys_out = K.dram_out("ys", [NS, D], F32)
    skv_out = K.dram_out("skv", [2, NS, 1536], F32)
    swin_out = [K.dram_out("swin%d" % i, [2, NS, WBUF, 256], F32) for i in range(2)]
    sconv_out = K.dram_out("sconv", [2, NS * 3, D], F32)
    slru_out = K.dram_out("slru", [2, NS, D], F32)
    spool_out = K.dram_out("spool", [2, NS * 15, D], F32)
    xs1 = K.dram_tmp("xsamp1", [NS, D], F32)
    Wqt = [K.dram_tmp("Wqt%d" % l, [2, 128, 8, 512], BF16) for l in range(2)]

    x1 = K.dram_tmp("x1", [S, D], F32)
    Wc = [K.dram_tmp("Wc%d" % l, [NCHUNK, 128, 8, 128], BF16) for l in range(2)]
    Wkv = [K.dram_tmp("Wkv%d" % l, [3, 128, 8, 512], BF16) for l in range(2)]
    Wb = [K.dram_tmp("Wb%d" % l, [24, 128, 8, 128], BF16) for l in range(2)]
    Wo = [K.dram_tmp("Wo%d" % l, [128, 8, D], BF16) for l in range(2)]

    def prep_cols(l, ch, scol, n, dcol):
        src = w_in[l, :, scol:scol + n].rearrange("(k p) n -> p k n", p=128)
        pool.dma(Wc[l][ch, :, :, dcol:dcol + n], src, [w_in], [Wc[l]], Wc[l])

    def prep_layer(l):
        for c in range(8):
            prep_cols(l, CH_LRUX + c, 0 + c * 128, 128, 0)
            prep_cols(l, CH_LRUG + c, 1024 + c * 128, 128, 0)
            prep_cols(l, CH_POOLX + c, 2048 + c * 128, 128, 0)
            prep_cols(l, CH_POOLG + c, 3072 + c * 128, 128, 0)
            for e in range(2):
                h = HPERM[2 * c + e]
                prep_cols(l, CH_Q + c, 4096 + h * 64, 64, e * 64)
                prep_cols(l, CH_NSAG + c, 5120 + h * 64, 64, e * 64)
        for i, base in enumerate((6144, 6400, 6656, 7168)):
            for cc in range(2):
                prep_cols(l, CH_KVF + 2 * i + cc, base + cc * 128, 128, 0)
        prep_cols(l, CH_BG, 7680, 128, 0)
        for n in range(3):
            for dc in range(8):
                prep_cols(l, CH_MG + n * 8 + dc, 7728 + n * 1024 + dc * 128, 128, 0)
        for blk in range(3):
            src = w_in[l, :, 6144 + blk * 512:6144 + (blk + 1) * 512].rearrange("(k p) n -> p k n", p=128)
            pool.dma(Wkv[l][blk], src, [w_in], [Wkv[l]], Wkv[l])
        for blk in range(2):
            src = w_in[l, :, 4096 + blk * 512:4096 + (blk + 1) * 512].rearrange("(k p) n -> p k n", p=128)
            pool.dma(Wqt[l][blk], src, [w_in], [Wqt[l]], Wqt[l])
        for n in range(2):
            for dc in range(8):
                src = w_branch[l, n, :, dc * 128:(dc + 1) * 128].rearrange("(k p) n -> p k n", p=128)
                pool.dma(Wb[l][n * 8 + dc], src, [w_branch], [Wb[l]], Wb[l])
        for fc in range(8):
            for e in range(2):
                h = HPERM[2 * fc + e]
                src = w_branch[l, 2, h * 64:(h + 1) * 64, :].rearrange("p (dc n) -> dc p n", n=128)
                pool.dma(Wb[l][16:24, e * 64:(e + 1) * 64, fc, :], src, [w_branch], [Wb[l]], Wb[l])
        src = w_out[l].rearrange("(k p) n -> p k n", p=128)
        pool.dma(Wo[l][:, :, :], src, [w_out], [Wo[l]], Wo[l])

    cf = K.sb("cf", [128, 128 + 256 + 60 + 16 * NAB + 16 * NABC], F32)
    sp.dma(cf[:, 0:128], c_f32[:, 0:128], [c_f32], [cf], cf)
    sp.dma(cf[:, 128:], c_f32[:, 896:], [c_f32], [cf], cf)
    o_ = [0]

    def take(n):
        a = o_[0]
        o_[0] += n
        return a

    o_id, o_pw, o_ratio = take(128), take(256), take(60)
    o_ab, o_abc = take(16 * NAB), take(16 * NABC)
    cb = K.sb("cb", [128, 2048 + 4096 + NMT * 128], BF16)
    pool.dma(cb[:], c_bf[:, :], [c_bf], [cb], cb)
    o_cmpb, o_selh, o_gagg = 0, 2048, 2048 + 4096
    cbm = K.sb("cbm", [128, 128 * 3 + 512], BF16)
    pool.dma(cbm[:], c_f32[:, 0:896], [c_f32], [cbm], cbm)
    ones_bf = K.sb("ones_bf", [128, 128], BF16)
    dve.op([], [ones_bf], lambda e: e.memset(ones_bf[:], 1.0))
    zer_bf = K.sb("zer_bf", [128, 128], BF16)
    dve.op([], [zer_bf], lambda e: e.memset(zer_bf[:], 0.0))

    def ident_f(n):
        return cf[0:n, o_id:o_id + n]

    ident_b = cbm[:, 0:128]
    tric_b = cbm[:, 128:256]
    triw_b = cbm[:, 256:384]
    row0b_b = cbm[:, 384:896]

    psA = [K.ps("psA%d" % i, [128, 512]) for i in range(2)]
    psS = [K.ps("psS%d" % i, [128, 512]) for i in range(2)]
    psO = [K.ps("psO%d" % i, [128, 512]) for i in range(2)]
    psM = K.ps("psM", [128, 512])
    psT = K.ps("psT", [128, 1024], BF16)
    rr = {"A": 0, "S": 0, "O": 0}

    def nxt(kind):
        lst = {"A": psA, "S": psS, "O": psO}[kind]
        rr[kind] += 1
        return lst[rr[kind] % 2]

    vr = K.sb("vr", [18, D], F32)
    sp.dma(vr[:], vecs[:, :], [vecs], [vr], vr)
    CV = K.sb("CV", [128, 8, 20], F32)
    for c in range(8):
        pe.op([vr, cf], [psM], lambda e, c=c: e.transpose(out=psM[:, c * 18:(c + 1) * 18], in_=vr[0:18, c * 128:(c + 1) * 128], identity=ident_f(18)))
    act.op([psM], [CV], lambda e: e.copy(out=CV[:, :, 0:18], in_=psM[:, 0:144].rearrange("p (c v) -> p c v", v=18)))
    tmpc = K.sb("tmpc", [128, 8, 2], F32)
    for l in range(2):
        act.op([CV], [tmpc], lambda e, l=l: e.activation(out=tmpc[:, :, l], in_=CV[:, :, l * 9 + 7], func=AF.Exp, scale=-1.0))
        act.op([tmpc], [tmpc], lambda e, l=l: e.activation(out=tmpc[:, :, l], in_=tmpc[:, :, l], func=AF.Ln, bias=1.0, scale=1.0))
        dve.op([tmpc], [CV], lambda e, l=l: e.tensor_scalar(out=CV[:, :, 18 + l], in0=tmpc[:, :, l], scalar1=-8.0, scalar2=None, op0=ALU.mult))

    wpr = K.sb("wpr", [32, 4, 128], F32)
    for l in range(2):
        for kv in range(2):
            for half in range(2):
                sp.dma(wpr[:, l * 2 + kv, half * 64:(half + 1) * 64], cmp_pos[l, kv, :, :], [cmp_pos], [wpr], wpr)
    CWP = K.sb("CWP", [128, 4, 32], F32)
    for i in range(4):
        pe.op([wpr, cf], [psM], lambda e, i=i: e.transpose(out=psM[:, 256 + i * 32:256 + (i + 1) * 32], in_=wpr[0:32, i, :], identity=ident_f(32)))
    act.op([psM], [CWP], lambda e: e.copy(out=CWP[:, :, :], in_=psM[:, 256:384].rearrange("p (i j) -> p i j", j=32)))

    WRG = K.sb("WRG", [128, 2, 8, 128], BF16)
    PHI = K.sb("PHI", [128, 4, 128], BF16)
    WPL = K.sb("WPL", [128, 4, 2, 256], BF16)
    pool.op([], [WRG], lambda e: e.memset(WRG[:], 0.0))
    pool.op([], [PHI], lambda e: e.memset(PHI[:], 0.0))
    for l in range(2):
        for kv in range(2):
            for half in range(2):
                pool.dma(PHI[half * 64:(half + 1) * 64, l * 2 + kv, half * 64:(half + 1) * 64], cmp_phi[l, kv, :, :], [cmp_phi], [PHI], PHI)

    def load_layer_small(l):
        for ax in range(2):
            for c in range(8):
                for half in range(2):
                    pool.dma(WRG[half * 64:(half + 1) * 64, ax, c, half * 64:(half + 1) * 64], w_rg[l, ax, 2 * c + half, :, :], [w_rg], [WRG], WRG)
        for g in range(4):
            pool.dma(WPL[:, g, :, :], w_pool[l, g, :, :].rearrange("(dh p) e -> p dh e", p=128), [w_pool], [WPL], WPL)
        sp.dma(gbc[0][:], g_pre[l:l + 1, :].partition_broadcast(128), [g_pre], [gbc[0]], gbc[0])
        sp.dma(gbc[1][:], g_post[l:l + 1, :].partition_broadcast(128), [g_post], [gbc[1]], gbc[1])

    gbc = [K.sb("gbc%d" % i, [128, D], F32) for i in range(2)]

    prep_layer(0)
    prep_layer(1)

    MARK_PROMPT = len(K.cms)
    KT = [K.sb("KT%d" % cc, [128, NSL[cc] * 128], BF16) for cc in range(2)]
    VS = [K.sb("VS%d" % cc, [128, NSL[cc], 2, 65], BF16) for cc in range(2)]
    KWT = K.sb("KWT", [128, 2, 8 * 128], BF16)
    VW = K.sb("VW", [128, 8, 4, 65], BF16)
    KCT = K.sb("KCT", [128, 2, NMT * 128], BF16)
    VC = K.sb("VC", [128, NMT, 4, 65], BF16)
    lruc = K.sb("lruc", [128, 8, 4], F32)
    poolc = K.sb("poolc", [128, 8, 15], F32)
    cmpc = K.sb("cmpc", [128, 4, 1], F32)

    xs = [K.sb("xs%d" % i, [128, D], F32) for i in range(2)]
    ub = K.sb("ub", [128, D], BF16)
    uT = K.sb("uT", [128, 8, 512], BF16)
    WB = [K.sb("WB%d" % i, [128, 1, 8, 128], BF16) for i in range(4)]
    WKB = K.sb("WKB", [128, 8, 512], BF16)
    QT = K.sb("QT", [128, 8, 512], BF16)
    zb = K.sb("zb", [128, 8, 512], BF16)
    ACC = K.sb("ACC", [128, 8, 512], F32)
    gates = K.sb("gates", [48, 512], F32)
    MbT = K.sb("MbT", [128, 4, 512], BF16)
    MbS = K.sb("MbS", [128, 4, 512], BF16)
    merged = QT
    SC = [K.sb("SC%d" % i, [128, 528], F32) for i in range(6)]
    PT = [K.sb("PT%d" % i, [128, 512], BF16) for i in range(2)]
    PN = K.sb("PN", [128, NMT, 512], BF16)
    kvst = [K.sb("kvst%d" % i, [128, 512], F32) for i in range(1)]
    sm = K.sb("sm", [128, 64], F32)
    rrb = {"WB": 0, "PT": 0, "kvst": 0, "xs": 0}

    def rot(name, lst):
        rrb[name] += 1
        return lst[rrb[name] % len(lst)]

    wb_dma = {"n": 0}

    def mm(out_b, out_ap, l_b, l_ap, r_b, r_ap, start, stop):
        pe.op([l_b, r_b], [out_b], lambda e: e.matmul(out_ap, lhsT=l_ap, rhs=r_ap, start=start, stop=stop))

    PREF = 3

    def proj(l, chunks, consume, Wsrc=None, ncols=128):
        Wsrc = Wsrc if Wsrc is not None else Wc[l]
        bufs = {}

        def load(i):
            buf = rot("WB", WB)
            sp.dma(buf[:, 0, :, :], Wsrc[chunks[i]], [Wsrc], [buf], buf)
            bufs[i] = buf
        for i in range(min(PREF, len(chunks))):
            load(i)
        for i, ch in enumerate(chunks):
            if i + PREF < len(chunks):
                load(i + PREF)
            cur = bufs.pop(i)
            ps = nxt("A")
            for kc in range(8):
                mm(ps, ps[0:ncols, :], cur, cur[:, 0, kc, 0:ncols], uT, uT[:, kc, :], kc == 0, kc == 7)
            consume(i, ch, ps)

    def attn_pairs(h, q0, W, branch, J):
        res = []
        qlo_t = q0 // 128
        nsub = W // 128
        if branch == "s":
            kts = range(0, qlo_t + nsub)
        else:
            kts = range(max(0, qlo_t - 4), qlo_t + nsub)
        for kt in kts:
            i_lo = max(0, kt - qlo_t)
            i_hi = nsub - 1
            if branch == "w":
                i_hi = min(nsub - 1, kt + 4 - qlo_t)
            if i_lo > i_hi:
                continue
            c0, c1 = i_lo * 128, (i_hi + 1) * 128
            mind = (q0 + c0) - (kt * 128 + 127)
            if mind > 0 and SLOPES[h] * mind > THR_SKIP:
                continue
            masks = []
            if kt - qlo_t >= 0:
                masks.append((tric_b, (kt - qlo_t) * 128))
            if branch == "w" and 0 <= kt + 4 - qlo_t <= nsub - 1:
                masks.append((triw_b, (kt + 4 - qlo_t) * 128))
            res.append((kt, c0, c1, masks))
        return res

    EPS = 1e-6
    m8 = K.sb("m8", [128, 16], F32)
    wk = K.sb("wk", [128, 128], F32)
    Mbq = K.sb("Mbq", [128, 128], BF16)
    blkb = K.sb("blkb", [128, 32], BF16)
    vcst = K.sb("vcst", [32, 128], BF16)

    def init_state():
        for b_ in (KCT, VC, VW, VS[0], VS[1], KT[0], KT[1], KWT, MbT, MbS):
            pool.op([], [b_], lambda e, b_=b_: e.memset(b_[:], 0.0))
        pool.op([], [VC], lambda e: e.memset(VC[:, :, :, 64:65], 1.0))
        pool.op([], [VW], lambda e: e.memset(VW[:, :, :, 64:65], 1.0))
        for cc in range(2):
            pool.op([], [VS[cc]], lambda e, cc=cc: e.memset(VS[cc][:, :, :, 64:65], 1.0))

    def compress(l, J, kv, cc, ps):
        i = l * 2 + kv
        kcs, fs = SC[0], SC[1]
        act.op([ps], [kcs], lambda e: e.copy(out=kcs[:, 0:512], in_=ps[:, :]))
        v3 = kcs[:, 0:512].rearrange("p (c j) -> p c j", j=16)
        for half, o in ((0, 0), (1, 32)):
            dve.op([kcs, CWP], [fs], lambda e: e.tensor_scalar(out=fs[:, o:o + 32], in0=v3[:, :, 0], scalar1=CWP[:, i, half * 16:half * 16 + 1], scalar2=None, op0=ALU.mult))
            for j in range(1, 16):
                dve.op([kcs, CWP, fs], [fs], lambda e, j=j: e.scalar_tensor_tensor(out=fs[:, o:o + 32], in0=v3[:, :, j], scalar=CWP[:, i, half * 16 + j:half * 16 + j + 1], in1=fs[:, o:o + 32], op0=ALU.mult, op1=ALU.add))
        ci = kv * 2 + cc
        dve.op([cmpc, fs], [blkb], lambda e: e.tensor_tensor(out=blkb[:, 0:1], in0=cmpc[:, ci, :], in1=fs[:, 32:33], op=ALU.add))
        dve.op([fs], [blkb], lambda e: e.tensor_tensor(out=blkb[:, 1:32], in0=fs[:, 0:31], in1=fs[:, 33:64], op=ALU.add))
        dve.op([fs], [cmpc], lambda e: e.tensor_copy(out=cmpc[:, ci, :], in_=fs[:, 31:32]))
        if kv == 0:
            mm(psM, psM[:, 0:32], PHI, PHI[:, i, :], blkb, blkb[:, :], True, True)
            act.op([psM], [KCT], lambda e: e.copy(out=KCT[:, cc, 32 * J:32 * J + 32], in_=psM[:, 0:32]))
        else:
            mm(psM, psM[0:32, 0:128], blkb, blkb[:, :], PHI, PHI[:, i, :], True, True)
            act.op([psM], [vcst], lambda e: e.copy(out=vcst[:, :], in_=psM[0:32, 0:128]))
            r0 = 32 * (J % 4)
            sp.dma(VC[r0:r0 + 32, J // 4, 2 * cc:2 * cc + 2, 0:64], vcst[:, :].rearrange("p (g d) -> p g d", d=64), [vcst], [VC], VC)

    def combine(h, pos, O, qc0, W, b):
        cp, half = pos // 2, pos % 2
        Oa, rec = SC[2], SC[3]
        act.op([O], [Oa], lambda e: e.copy(out=Oa[0:65, 0:W], in_=O[0:65, 0:W]))
        mm(psM, psM[0:64, 0:W], cf, cf[0:65, o_id + 64:o_id + 65].to_broadcast([65, 64]), Oa, Oa[0:65, 0:W], True, True)
        gb = nxt("S")
        gi = 3 * h + b
        mm(gb, gb[0:64, 0:W], cf, cf[0:48, o_id + gi:o_id + gi + 1].to_broadcast([48, 64]), gates, gates[0:48, qc0:qc0 + W], True, True)
        dve.op([psM], [rec], lambda e: e.tensor_scalar(out=rec[0:64, 0:W], in0=psM[0:64, 0:W], scalar1=1e-30, scalar2=None, op0=ALU.max))
        dve.op([rec], [rec], lambda e: e.reciprocal(out=rec[0:64, 0:W], in_=rec[0:64, 0:W]))
        dve.op([gb, rec], [rec], lambda e: e.tensor_tensor(out=rec[0:64, 0:W], in0=gb[0:64, 0:W], in1=rec[0:64, 0:W], op=ALU.mult))
        T2 = SC[4]
        hs_ = slice(half * 64, half * 64 + 64)
        dve.op([Oa, rec], [T2], lambda e: e.tensor_tensor(out=T2[hs_, 0:W], in0=Oa[0:64, 0:W], in1=rec[0:64, 0:W], op=ALU.mult))
        dve.op([T2, ACC], [ACC], lambda e: e.tensor_tensor(out=ACC[hs_, cp, qc0:qc0 + W], in0=T2[hs_, 0:W], in1=ACC[hs_, cp, qc0:qc0 + W], op=ALU.add))

    DBG = getattr(cfg, 'DBG', '')

    def attend(h, pos, q0, W, branch, J, Osh=None):
        t0 = J * 512
        qc0 = q0 - t0
        cp, half = pos // 2, pos % 2
        g = h // 4
        cc, gi = g // 2, g % 2
        pairs = attn_pairs(h, q0, W, branch, J)
        full = [p_ for p_ in pairs if p_[1] == 0 and p_[2] == W]
        assert full, (h, q0, W, branch)
        pairs = [full[0]] + [p_ for p_ in pairs if p_ is not full[0]]
        O = nxt("O") if Osh is None else Osh
        ob = 0 if Osh is None else qc0

        def kv_of(kt):
            if branch == "s":
                sl = kt % NSL[cc]
                return KT[cc], KT[cc][half * 64:half * 64 + 64, sl * 128:(sl + 1) * 128], VS[cc], VS[cc][:, sl, gi, :]
            sl = kt % 8
            return KWT, KWT[half * 64:half * 64 + 64, cc, sl * 128:(sl + 1) * 128], VW, VW[:, sl, g, :]

        def emit_scores(pi):
            kt, c0, c1, masks = pairs[pi]
            S_ = nxt("S")
            lk_b, lk, _, _ = kv_of(kt)
            if 't' in DBG:
                masks = []
            nmore = len(masks) + (1 if (branch == "s" and "m" not in DBG) else 0)
            mm(S_, S_[:, c0:c1], lk_b, lk, QT, QT[half * 64:half * 64 + 64, cp, qc0 + c0:qc0 + c1], True, nmore == 0)
            if branch == "s" and "m" not in DBG:
                a2 = ((2 * kt) // 64) % 2
                v = kt % 32
                nmore -= 1
                Mb_ = MbT if a2 == half else MbS
                mm(S_, S_[:, c0:c1], cb, cb[half * 64:half * 64 + 64, o_selh + v * 128:o_selh + (v + 1) * 128],
                   Mb_, Mb_[half * 64:half * 64 + 64, g, qc0 + c0:qc0 + c1], False, nmore == 0)
            for (mb_, mc0) in masks:
                nmore -= 1
                mm(S_, S_[:, mc0:mc0 + 128], cbm, ident_b, cbm, mb_, False, nmore == 0)
            return S_

        S_next = emit_scores(0)
        for pi, (kt, c0, c1, masks) in enumerate(pairs):
            S_ = S_next
            if pi + 1 < len(pairs):
                S_next = emit_scores(pi + 1)
            P_ = rot("PT", PT)
            _, _, vb, vap = kv_of(kt)
            idx = kt - q0 // 128 + OFF
            act.op([S_, cf], [P_], lambda e, S_=S_, P_=P_, c0=c0, c1=c1, idx=idx: e.activation(
                out=P_[:, c0:c1], in_=S_[:, c0:c1], func=AF.Exp, bias=cf[:, o_ab + h * NAB + idx:o_ab + h * NAB + idx + 1], scale=1.0))
            if 'p' not in DBG:
                mm(O, O[0:65, ob + c0:ob + c1], vb, vap, P_, P_[:, c0:c1], pi == 0, pi == len(pairs) - 1)
        if 'c' not in DBG and Osh is None:
            combine(h, pos, O, qc0, W, 1 if branch == "s" else 2)

    def attend_cmp(h, pos, q0, W, J, psacc, written, last_head, Osh=None):
        t0 = J * 512
        qc0 = q0 - t0
        cp, half = pos // 2, pos % 2
        g = h // 4
        cc = g // 2
        mtd = J // 4
        mts = []
        for mt in range(mtd + 1):
            mind = q0 - (16 * (128 * mt + 127) + 15)
            if mind > 0 and SLOPES[h] * mind > THR_SKIP:
                continue
            mts.append(mt)
        if Osh is None:
            O = nxt("O")
            Dp = nxt("O")
        else:
            O = Osh
            Dp = psO[0] if Osh is psO[1] else psO[1]
        ob = 0 if Osh is None else qc0
        for mi, mt in enumerate(mts):
            S_ = nxt("S")
            nmore = (1 if mt == 0 else 0) + (1 if mt == mtd else 0)
            mm(S_, S_[:, 0:W], KCT, KCT[half * 64:half * 64 + 64, cc, mt * 128:(mt + 1) * 128], QT, QT[half * 64:half * 64 + 64, cp, qc0:qc0 + W], True, nmore == 0)
            if mt == 0:
                nmore -= 1
                mm(S_, S_[:, 0:W], cbm, ident_b, cbm, row0b_b[:, 0:W], False, nmore == 0)
            if mt == mtd:
                nmore -= 1
                v = J % 4
                mm(S_, S_[:, 0:W], cbm, ident_b, cb, cb[:, o_cmpb + v * 512 + qc0:o_cmpb + v * 512 + qc0 + W], False, nmore == 0)
            idx = 16 * mt - q0 // 128 + OFF
            act.op([S_, cf], [PN], lambda e, S_=S_, mt=mt, idx=idx: e.activation(
                out=PN[:, mt, qc0:qc0 + W], in_=S_[:, 0:W], func=AF.Exp, bias=cf[:, o_abc + h * NABC + idx:o_abc + h * NABC + idx + 1], scale=1.0))
            mm(Dp, Dp[:, 0:W], ones_bf, ones_bf[:, :], PN, PN[:, mt, qc0:qc0 + W], mi == 0, mi == len(mts) - 1)
            mm(O, O[0:65, ob:ob + W], VC, VC[:, mt, g, :], PN, PN[:, mt, qc0:qc0 + W], mi == 0, mi == len(mts) - 1)
        rden = SC[4]
        dve.op([Dp], [rden], lambda e: e.tensor_scalar(out=rden[:, 0:W], in0=Dp[:, 0:W], scalar1=1e-30, scalar2=None, op0=ALU.max))
        dve.op([rden], [rden], lambda e: e.reciprocal(out=rden[:, 0:W], in_=rden[:, 0:W]))
        for mt in mts:
            dve.op([PN, rden], [PN], lambda e, mt=mt: e.tensor_tensor(out=PN[:, mt, qc0:qc0 + W], in0=PN[:, mt, qc0:qc0 + W], in1=rden[:, 0:W], op=ALU.mult))
        nsub = W // 128
        for si in range(nsub):
            qs = qc0 // 128 + si
            for mi, mt in enumerate(mts):
                mm(psacc, psacc[:, qs * 128:qs * 128 + NBk], PN, PN[:, mt, qs * 128:(qs + 1) * 128], cb, cb[:, o_gagg + mt * 128:o_gagg + mt * 128 + NBk],
                   False, last_head and si == nsub - 1 and mi == len(mts) - 1)
        if getattr(cfg, 'STAGE', 99) > 2.31 and Osh is None:
            combine(h, pos, O, qc0, W, 0)

    def topk_group(g, J, psacc):
        t0 = J * 512
        for qs in range(4):
            s0 = t0 + qs * 128
            sc = SC[5]
            po = o_pw + 128 - s0 // 64
            dve.op([psacc, cf], [sc], lambda e: e.tensor_tensor(out=sc[:, 0:NBk], in0=psacc[:, qs * 128:qs * 128 + NBk], in1=cf[:, po:po + NBk], op=ALU.add))
            dve.op([sc], [sc], lambda e: e.tensor_scalar(out=sc[:, 0:1], in0=sc[:, 0:1], scalar1=1.0e4, scalar2=None, op0=ALU.add))
            dve.op([sc], [m8], lambda e: e.max(out=m8[:, 0:8], in_=sc[:, 0:NBk]))
            dve.op([sc, m8], [wk], lambda e: e.match_replace(out=wk[:, 0:NBk], in_to_replace=m8[:, 0:8], in_values=sc[:, 0:NBk], imm_value=-3.0e38))
            dve.op([wk], [m8], lambda e: e.max(out=m8[:, 8:16], in_=wk[:, 0:NBk]))
            dve.op([sc, m8], [Mbq], lambda e: e.tensor_scalar(out=Mbq[:, 0:NBk], in0=sc[:, 0:NBk], scalar1=m8[:, 15:16], scalar2=NEGB, op0=ALU.is_lt, op1=ALU.mult))
            pe.op([Mbq, cbm], [psT], lambda e, qs=qs: e.transpose(out=psT[0:NBk, qs * 128:(qs + 1) * 128], in_=Mbq[:, 0:NBk], identity=ident_b))
        act.op([psT], [MbT], lambda e: e.copy(out=MbT[0:NBk, g, :], in_=psT[0:NBk, 0:512]))
        n0 = min(NBk, 64)
        act.op([psT], [MbS], lambda e: e.copy(out=MbS[64:64 + n0, g, :], in_=psT[0:n0, 0:512]))
        if NBk > 64:
            act.op([psT], [MbS], lambda e: e.copy(out=MbS[0:NBk - 64, g, :], in_=psT[64:NBk, 0:512]))

    def merge_branch(l, n, first):
        seq = []
        for dc in range(8):
            seq.append((Wb[l], n * 8 + dc))
            seq.append((Wc[l], CH_MG + n * 8 + dc))
        bufs = {}

        def load(i):
            buf = rot("WB", WB)
            sp.dma(buf[:, 0, :, :], seq[i][0][seq[i][1]], [seq[i][0]], [buf], buf)
            bufs[i] = buf
        load(0)
        load(1)
        for dc in range(8):
            if dc < 7:
                load(2 * dc + 2)
                load(2 * dc + 3)
            wbr, wmg = bufs.pop(2 * dc), bufs.pop(2 * dc + 1)
            pbr, pmg = nxt("A"), nxt("A")
            for fc in range(8):
                mm(pbr, pbr[:, :], wbr, wbr[:, 0, fc, :], zb, zb[:, fc, :], fc == 0, fc == 7)
            for kc in range(8):
                mm(pmg, pmg[:, :], wmg, wmg[:, 0, kc, :], uT, uT[:, kc, :], kc == 0, kc == 7)
            mt_ = SC[0]
            act.op([pmg], [mt_], lambda e: e.activation(out=mt_[:, 0:512], in_=pmg[:, :], func=AF.Sigmoid))
            if first:
                dve.op([pbr, mt_], [ACC], lambda e, dc=dc: e.tensor_tensor(out=ACC[:, dc, :], in0=pbr[:, :], in1=mt_[:, 0:512], op=ALU.mult))
            else:
                dve.op([pbr, mt_], [mt_], lambda e: e.tensor_tensor(out=mt_[:, 0:512], in0=pbr[:, :], in1=mt_[:, 0:512], op=ALU.mult))
                dve.op([ACC, mt_], [ACC], lambda e, dc=dc: e.tensor_tensor(out=ACC[:, dc, :], in0=ACC[:, dc, :], in1=mt_[:, 0:512], op=ALU.add))

    def tile_prompt(l, J, x_src, x_dst):
        stage = getattr(cfg, 'STAGE', 99)
        t0 = J * 512
        last = (J == NT - 1)
        for s in range(4):
            xb = rot("xs", xs)
            sp.dma(xb[:], x_src[t0 + s * 128:t0 + (s + 1) * 128, :], [x_src], [xb], xb)
            dve.op([], [sm], lambda e, s=s: e.memset(sm[:, s:s + 1], 0.0))
            act.op([xb, sm], [ub, sm], lambda e, s=s, xb=xb: e.activation(out=ub[:, :], in_=xb[:, :], func=AF.Square, accum_out=sm[:, s:s + 1]))
            dve.op([sm], [sm], lambda e, s=s: e.tensor_scalar(out=sm[:, 4 + s:5 + s], in0=sm[:, s:s + 1], scalar1=1.0 / D, scalar2=EPS, op0=ALU.mult, op1=ALU.add))
            act.op([sm], [sm], lambda e, s=s: e.sqrt(out=sm[:, 4 + s:5 + s], in_=sm[:, 4 + s:5 + s]))
            dve.op([sm], [sm], lambda e, s=s: e.reciprocal(out=sm[:, 8 + s:9 + s], in_=sm[:, 4 + s:5 + s]))
            dve.op([xb, sm, gbc[0]], [ub], lambda e, s=s, xb=xb: e.scalar_tensor_tensor(out=ub[:, :], in0=xb[:, :], scalar=sm[:, 8 + s:9 + s], in1=gbc[0][:, :], op0=ALU.mult, op1=ALU.mult))
            for c in range(8):
                pe.op([ub, cbm], [psT], lambda e, c=c: e.transpose(out=psT[:, c * 128:(c + 1) * 128], in_=ub[:, c * 128:(c + 1) * 128], identity=ident_b))
            act.op([psT], [uT], lambda e, s=s: e.copy(out=uT[:, :, s * 128:(s + 1) * 128], in_=psT[:, :].rearrange("p (c t) -> p c t", t=128)))

        if stage <= 2.1:
            return
        for blk in range(3):
            sp.dma(WKB[:], Wkv[l][blk], [Wkv[l]], [WKB], WKB)
            for s in range(4):
                ps = nxt("A")
                for kc in range(8):
                    mm(ps, ps[:, :], uT, uT[:, kc, s * 128:(s + 1) * 128], WKB, WKB[:, kc, :], kc == 0, kc == 7)
                st = rot("kvst", kvst)
                act.op([ps], [st], lambda e, ps=ps, st=st: e.copy(out=st[:, :], in_=ps[:, :]))
                r0 = t0 + s * 128
                kt = r0 // 128
                if blk < 2:
                    sp.dma(kv_out[2 * blk][l, r0:r0 + 128, :], st[:, 0:256], [st], [kv_out[2 * blk]], st)
                    sp.dma(kv_out[2 * blk + 1][l, r0:r0 + 128, :], st[:, 256:512], [st], [kv_out[2 * blk + 1]], st)
                elif r0 >= S - 512:
                    w0 = r0 - (S - 512)
                    sp.dma(win_out[0][l, w0:w0 + 128, :], st[:, 0:256], [st], [win_out[0]], st)
                    sp.dma(win_out[1][l, w0:w0 + 128, :], st[:, 256:512], [st], [win_out[1]], st)
                if blk == 1:
                    for cc in range(2):
                        dve.op([st], [VS[cc]], lambda e, cc=cc, st=st, kt=kt: e.tensor_copy(
                            out=VS[cc][:, kt % NSL[cc], :, 0:64], in_=st[:, 256 + cc * 128:256 + (cc + 1) * 128].rearrange("p (g d) -> p g d", d=64)))
                if blk == 2:
                    dve.op([st], [VW], lambda e, st=st, kt=kt: e.tensor_copy(out=VW[:, kt % 8, :, 0:64], in_=st[:, 256:512].rearrange("p (g d) -> p g d", d=64)))

        if stage <= 2.2:
            return
        def c_q(i, ch, ps):
            act.op([ps], [QT], lambda e: e.mul(out=QT[:, i, :], in_=ps[:, :], mul=0.125))
        proj(l, [CH_Q + c for c in range(8)], c_q)

        def c_kv(i, ch, ps):
            j = ch - CH_KVF
            cc = j % 2
            if j < 2:
                compress(l, J, 0, cc, ps)
            elif j < 4:
                compress(l, J, 1, cc, ps)
            elif j < 6:
                sl = (4 * J) % NSL[cc]
                act.op([ps], [KT[cc]], lambda e: e.copy(out=KT[cc][:, sl * 128:sl * 128 + 512], in_=ps[:, :]))
            else:
                sl = (4 * J) % 8
                act.op([ps], [KWT], lambda e: e.copy(out=KWT[:, cc, sl * 128:sl * 128 + 512], in_=ps[:, :]))
        proj(l, [CH_KVF + j for j in range(8)], c_kv)

        def c_bg(i, ch, ps):
            act.op([ps], [gates], lambda e: e.activation(out=gates[:, :], in_=ps[0:48, :], func=AF.Sigmoid))
        proj(l, [CH_BG], c_bg, ncols=48)

        if stage <= 2.3:
            return
        pool.op([], [ACC], lambda e: e.memset(ACC[:], 0.0))
        for g in range(4):
            heads = [(h, HPERM.index(h)) for h in range(4 * g, 4 * g + 4)]
            psacc = psA[g % 2]
            written = set()
            mm(psacc, psacc[:, :], zer_bf, zer_bf[:, :], cb, cb[:, 0:512], True, False)
            for hi, (h, pos) in enumerate(heads):
                W = 128 if h <= 3 else 512
                Osh = nxt("O") if W == 128 else None
                for q0 in range(t0, t0 + 512, W):
                    attend_cmp(h, pos, q0, W, J, psacc, written, hi == 3 and q0 + W == t0 + 512, Osh)
                if Osh is not None and stage > 2.31:
                    combine(h, pos, Osh, 0, 512, 0)
            if stage <= 2.32:
                continue
            topk_group(g, J, psacc)
            if stage <= 2.33:
                continue
            for (h, pos) in heads:
                W = 128 if h <= 3 else 512
                if 'h' in DBG and pos % 2 == 1:
                    continue
                if W == 512:
                    attend(h, pos, t0, W, "s", J)
                    if stage > 2.34:
                        attend(h, pos, t0, W, "w", J)
                else:
                    for br_, bi_ in (("s", 1), ("w", 2)):
                        if br_ == "w" and stage <= 2.34:
                            continue
                        Osh = nxt("O")
                        for q0 in range(t0, t0 + 512, W):
                            attend(h, pos, q0, W, br_, J, Osh)
                        if 'c' not in DBG:
                            combine(h, pos, Osh, 0, 512, bi_)

        if stage <= 2.4:
            return
        def c_nsag(i, ch, ps):
            sg = SC[0]
            act.op([ps], [sg], lambda e: e.activation(out=sg[:, 0:512], in_=ps[:, :], func=AF.Silu))
            dve.op([ACC, sg], [zb], lambda e: e.tensor_tensor(out=zb[:, i, :], in0=ACC[:, i, :], in1=sg[:, 0:512], op=ALU.mult))
        proj(l, [CH_NSAG + c for c in range(8)], c_nsag)
        merge_branch(l, 2, True)

        if stage <= 2.5:
            return
        def c_lru(i, ch, ps):
            c = i // 2
            if i % 2 == 0:
                xp, xc, r_, i_, a_, hs = SC[0], SC[1], SC[2], SC[3], SC[4], SC[5]
                act.op([lruc], [xp], lambda e: e.copy(out=xp[:, 0:3], in_=lruc[:, c, 0:3]))
                act.op([ps], [xp], lambda e: e.copy(out=xp[:, 3:515], in_=ps[:, :]))
                act.op([xp], [lruc], lambda e: e.copy(out=lruc[:, c, 0:3], in_=xp[:, 512:515]))
                cv = l * 9
                dve.op([xp, CV], [xc], lambda e: e.tensor_scalar(out=xc[:, 0:512], in0=xp[:, 0:512], scalar1=CV[:, c, cv:cv + 1], scalar2=CV[:, c, cv + 4:cv + 5], op0=ALU.mult, op1=ALU.add))
                for k in range(1, 4):
                    dve.op([xp, CV, xc], [xc], lambda e, k=k: e.scalar_tensor_tensor(out=xc[:, 0:512], in0=xp[:, k:k + 512], scalar=CV[:, c, cv + k:cv + k + 1], in1=xc[:, 0:512], op0=ALU.mult, op1=ALU.add))
                xcb = rot("PT", PT)
                act.op([xc], [xcb], lambda e: e.copy(out=xcb[:, :], in_=xc[:, 0:512]))
                pr, pi_ = nxt("S"), nxt("S")
                mm(pr, pr[:, :], WRG, WRG[:, 0, c, :], xcb, xcb[:, :], True, True)
                mm(pi_, pi_[:, :], WRG, WRG[:, 1, c, :], xcb, xcb[:, :], True, True)
                act.op([pr, CV], [r_], lambda e: e.activation(out=r_[:, 0:512], in_=pr[:, :], func=AF.Sigmoid, bias=CV[:, c, cv + 5:cv + 6], scale=1.0))
                act.op([pi_, CV], [i_], lambda e: e.activation(out=i_[:, 0:512], in_=pi_[:, :], func=AF.Sigmoid, bias=CV[:, c, cv + 6:cv + 7], scale=1.0))
                act.op([r_, CV], [a_], lambda e: e.activation(out=a_[:, 0:512], in_=r_[:, 0:512], func=AF.Exp, scale=CV[:, c, 18 + l:19 + l]))
                dve.op([a_], [r_], lambda e: e.scalar_tensor_tensor(out=r_[:, 0:512], in0=a_[:, 0:512], scalar=-1.0, in1=a_[:, 0:512], op0=ALU.mult, op1=ALU.mult))
                act.op([r_], [r_], lambda e: e.activation(out=r_[:, 0:512], in_=r_[:, 0:512], func=AF.Sqrt, bias=1.0, scale=1.0))
                dve.op([i_, xc], [i_], lambda e: e.tensor_tensor(out=i_[:, 0:512], in0=i_[:, 0:512], in1=xc[:, 0:512], op=ALU.mult))
                dve.op([i_, r_], [i_], lambda e: e.tensor_tensor(out=i_[:, 0:512], in0=i_[:, 0:512], in1=r_[:, 0:512], op=ALU.mult))
                dve.op([a_, i_, lruc], [hs], lambda e: e.tensor_tensor_scan(out=hs[:, 0:512], data0=a_[:, 0:512], data1=i_[:, 0:512], initial=lruc[:, c, 3:4], op0=ALU.mult, op1=ALU.add))
                act.op([hs], [lruc], lambda e: e.copy(out=lruc[:, c, 3:4], in_=hs[:, 511:512]))
            else:
                sg, hs = SC[1], SC[5]
                act.op([ps], [sg], lambda e: e.activation(out=sg[:, 0:512], in_=ps[:, :], func=AF.Silu))
                dve.op([hs, sg], [zb], lambda e: e.tensor_tensor(out=zb[:, c, :], in0=hs[:, 0:512], in1=sg[:, 0:512], op=ALU.mult))
        chl = []
        for c in range(8):
            chl += [CH_LRUX + c, CH_LRUG + c]
        proj(l, chl, c_lru)
        merge_branch(l, 0, False)

        if stage <= 2.6:
            return
        PD = [PT[0], PT[1]]

        def c_pool(i, ch, ps):
            gq, k = i // 4, i % 4
            w = POOL_WIN[gq]
            if k < 2:
                c = 2 * gq + k
                X = SC[k]
                act.op([poolc], [X], lambda e: e.copy(out=X[:, 0:15], in_=poolc[:, c, :]))
                act.op([ps], [X], lambda e: e.copy(out=X[:, 15:527], in_=ps[:, :]))
                act.op([X], [poolc], lambda e: e.copy(out=poolc[:, c, :], in_=X[:, 512:527]))
                cur = X
                sh = 1
                tmps = [SC[4], SC[5]]
                ti = 0
                while sh < w:
                    nx = tmps[ti % 2]
                    ti += 1
                    dve.op([cur], [nx], lambda e, cur=cur, nx=nx, sh=sh: e.tensor_tensor(out=nx[:, sh:527], in0=cur[:, sh:527], in1=cur[:, 0:527 - sh], op=ALU.add))
                    cur = nx
                    sh *= 2
                if J == 0:
                    dve.op([cur, cf], [cur], lambda e, cur=cur: e.tensor_tensor(out=cur[:, 15:30], in0=cur[:, 15:30], in1=cf[:, o_ratio + gq * 15:o_ratio + gq * 15 + 15], op=ALU.mult))
                dve.op([cur, X], [PD[k]], lambda e, cur=cur: e.scalar_tensor_tensor(out=PD[k][:, :], in0=cur[:, 15:527], scalar=1.0 / w, in1=X[:, 15:527], op0=ALU.mult, op1=ALU.subtract))
            else:
                eh = k - 2
                c = 2 * gq + eh
                pm = nxt("S")
                for dh in range(2):
                    mm(pm, pm[:, :], WPL, WPL[:, gq, dh, eh * 128:(eh + 1) * 128], PD[dh], PD[dh][:, :], dh == 0, dh == 1)
                sg = SC[2]
                act.op([ps], [sg], lambda e: e.activation(out=sg[:, 0:512], in_=ps[:, :], func=AF.Silu))
                dve.op([pm, CV, sg], [zb], lambda e: e.scalar_tensor_tensor(out=zb[:, c, :], in0=pm[:, :], scalar=CV[:, c, l * 9 + 8:l * 9 + 9], in1=sg[:, 0:512], op0=ALU.mult, op1=ALU.mult))
        chl = []
        for gq in range(4):
            chl += [CH_POOLX + 2 * gq, CH_POOLX + 2 * gq + 1, CH_POOLG + 2 * gq, CH_POOLG + 2 * gq + 1]
        proj(l, chl, c_pool)
        merge_branch(l, 1, False)
        for dc in range(8):
            act.op([ACC], [merged], lambda e, dc=dc: e.copy(out=merged[:, dc, :], in_=ACC[:, dc, :]))

        if stage <= 2.7:
            return
        sp.dma(WKB[:], Wo[l][:, :, 0:512], [Wo[l]], [WKB], WKB)
        sp.dma(zb[:], Wo[l][:, :, 512:1024], [Wo[l]], [zb], zb)
        wo = [WKB, zb]
        for s in range(4):
            xr, yo = xs[0], xs[1]
            sp.dma(xr[:], x_src[t0 + s * 128:t0 + (s + 1) * 128, :], [x_src], [xr], xr)
            po = [nxt("A"), nxt("A")]
            dve.op([], [sm], lambda e: e.memset(sm[:, 16:18], 0.0))
            for eh in range(2):
                for dc in range(8):
                    mm(po[eh], po[eh][:, :], merged, merged[:, dc, s * 128:(s + 1) * 128], wo[eh], wo[eh][:, dc, :], dc == 0, dc == 7)
                act.op([po[eh], sm], [ub, sm], lambda e, eh=eh: e.activation(out=ub[:, eh * 512:(eh + 1) * 512], in_=po[eh][:, :], func=AF.Square, accum_out=sm[:, 16 + eh:17 + eh]))
            dve.op([sm], [sm], lambda e: e.tensor_tensor(out=sm[:, 18:19], in0=sm[:, 16:17], in1=sm[:, 17:18], op=ALU.add))
            dve.op([sm], [sm], lambda e: e.tensor_scalar(out=sm[:, 19:20], in0=sm[:, 18:19], scalar1=1.0 / D, scalar2=EPS, op0=ALU.mult, op1=ALU.add))
            act.op([sm], [sm], lambda e: e.sqrt(out=sm[:, 19:20], in_=sm[:, 19:20]))
            dve.op([sm], [sm], lambda e: e.reciprocal(out=sm[:, 20:21], in_=sm[:, 19:20]))
            for eh in range(2):
                dve.op([po[eh], sm, gbc[1]], [yo], lambda e, eh=eh: e.scalar_tensor_tensor(out=yo[:, eh * 512:(eh + 1) * 512], in0=po[eh][:, :], scalar=sm[:, 20:21], in1=gbc[1][:, eh * 512:(eh + 1) * 512], op0=ALU.mult, op1=ALU.mult))
            dve.op([yo, xr], [yo], lambda e: e.tensor_tensor(out=yo[:, :], in0=yo[:, :], in1=xr[:, :], op=ALU.add))
            sp.dma(x_dst[t0 + s * 128:t0 + (s + 1) * 128, :], yo[:], [yo], [x_dst], yo)

    def layer_prompt(l, x_src, x_dst):
        load_layer_small(l)
        if getattr(cfg, "STAGE", 99) == 1:
            return
        dve.op([], [lruc], lambda e: e.memset(lruc[:], 0.0))
        dve.op([], [poolc], lambda e: e.memset(poolc[:], 0.0))
        dve.op([], [cmpc], lambda e: e.memset(cmpc[:], 0.0))
        for J in range(NT):
            tile_prompt(l, J, x_src, x_dst)
        stt = SC[0]
        for c in range(8):
            act.op([lruc, poolc], [stt], lambda e, c=c: e.copy(out=stt[:, c * 19:c * 19 + 4], in_=lruc[:, c, :]))
            act.op([poolc], [stt], lambda e, c=c: e.copy(out=stt[:, c * 19 + 4:c * 19 + 19], in_=poolc[:, c, :]))
        psx = [psA[0], psA[1]]
        for c in range(8):
            pb = psx[c // 4]
            pe.op([stt, cf], [pb], lambda e, c=c, pb=pb: e.transpose(out=pb[0:19, (c % 4) * 128:(c % 4 + 1) * 128], in_=stt[:, c * 19:c * 19 + 19], identity=ident_f(128)))
        so = xs[0]
        for hh in range(2):
            act.op([psx[hh]], [so], lambda e, hh=hh: e.copy(out=so[0:19, hh * 512:(hh + 1) * 512], in_=psx[hh][0:19, :]))
        sp.dma(st_out[l, :, :], so[0:19, :], [so], [st_out], so)

    SST = getattr(cfg, 'SST', 99)

    def sample_phase():
        NCS = 1 + NMS * 16 + NPG * 16 + 64 + 8 + 128 + NMS * 128
        csb = K.sb("csb", [128, NCS - NMS * 128], F32)
        sp.dma(csb[:], c_s[:, 0:NCS - NMS * 128], [c_s], [csb], csb)
        oo = [0]

        def tk(n):
            a = oo[0]
            oo[0] += n
            return a
        o_p, o_abs, o_abk, o_abw, o_ind, o_fb = tk(1), tk(NMS * 16), tk(NPG * 16), tk(64), tk(8), tk(128)
        gsb = K.sb("gsb", [128, NMS * 128], BF16)
        pool.dma(gsb[:], c_s[:, NCS - NMS * 128:NCS], [c_s], [gsb], gsb)
        indb = K.sb("indb", [128, 8], BF16)
        pool.dma(indb[:], c_s[:, o_ind:o_ind + 8], [c_s], [indb], indb)
        NSB = 32
        xsb = K.sb("xsb", [NS, D], F32)
        ysb = K.sb("ysb", [NS, D], F32)
        ubs = K.sb("ubs", [NS, D], BF16)
        uTs = K.sb("uTs", [128, 8, NS], BF16)
        kvtok = K.sb("kvtok", [NS, 1536], F32)
        qtok = K.sb("qtok", [NS, D], F32)
        FM = {n_: K.sb("fm_" + n_, [128, 8, NS], F32) for n_ in ("xl", "lg", "px", "pg", "ng", "h", "t0", "t1", "t2", "t3")}
        mgs = K.sb("mgs", [128, 24, NS], F32)
        CS = K.sb("CS", [128, 8, NS * 3], F32)
        HS = K.sb("HS", [128, 8, NS], F32)
        PSs = K.sb("PSs", [128, 8, NS * 15], F32)
        stl = K.sb("stl", [128, D], F32)
        zbs = [K.sb("zbs%d" % n_, [128, 8, NS], BF16) for n_ in range(3)]
        xcbs = K.sb("xcbs", [128, 8, NS], BF16)
        pdb = K.sb("pdb", [128, 8, NS], BF16)
        gat_f = K.sb("gat_f", [48, NS], F32)
        gat_t = K.sb("gat_t", [NS, 48], F32)
        WBs = [K.sb("WBs%d" % i, [128, 2, 8, 128], BF16) for i in range(2)]
        WKs = K.sb("WKs", [128, 8, 512], BF16)
        WK2 = K.sb("WK2", [128, 8, 512], BF16)
        sms = K.sb("sms", [128, 32], F32)
        mergs = K.sb("mergs", [128, 8, NS], BF16)
        maccs = K.sb("maccs", [128, 8, NS], F32)
        ptb = K.sb("ptb", [128, NPG], I32)
        ptf = K.sb("ptf", [128, NPG], F32)
        IDX = K.sb("IDX", [128, NPG], I32)
        KP = [K.sb("KP%d" % i, [128, 256], F32) for i in range(4)]
        VP = [K.sb("VP%d" % i, [128, 256], F32) for i in range(4)]
        KPw = [K.sb("KPw%d" % i, [128, 256], F32) for i in range(2)]
        VPw = [K.sb("VPw%d" % i, [128, 256], F32) for i in range(2)]
        WT = K.sb("WT", [128, 2, 2, 256], F32)
        p12 = [K.sb("p12_%d" % i, [128, 2, 256], BF16) for i in range(3)]
        fsb = K.sb("fsb", [128, 4, NMS * 128], F32)
        blkT = K.sb("blkT", [128, 2, NMS * 128], BF16)
        KCs = K.sb("KCs", [128, 2, NMS * 128], BF16)
        VCs = K.sb("VCs", [128, NMS, 4, 65], BF16)
        QTs = K.sb("QTs", [128, 8, NS], BF16)
        S16 = K.sb("S16", [128, NMS, 16], F32)
        P16 = K.sb("P16", [128, NMS, 16], BF16)
        rd16 = K.sb("rd16", [128, 16], F32)
        Pn4 = K.sb("Pn4", [128, NMS, 4], BF16)
        Pn4f = K.sb("Pn4f", [128, NMS, 4], F32)
        sc4 = K.sb("sc4", [4, 128], F32)
        wk4 = K.sb("wk4", [4, 128], F32)
        m84 = K.sb("m84", [4, 16], F32)
        mb4 = K.sb("mb4", [4, 128], F32)
        MB4T = K.sb("MB4T", [128, 4], BF16)
        QBs = K.sb("QBs", [128, D], F32)
        prod = K.sb("prod", [128, D], F32)
        s16 = K.sb("s16", [128, 16], F32)
        mkb = K.sb("mkb", [128, 4], F32)
        Pk = [K.sb("Pk%d" % i, [128, 16], BF16) for i in range(3)]
        Vst = [K.sb("Vst%d" % i, [128, 4, 65], BF16) for i in range(3)]
        O16sb = K.sb("O16sb", [16, 3, 260], F32)
        OS = K.sb("OS", [NS, 3, 16, 65], F32)
        onat = K.sb("onat", [NS, 16, 64], F32)
        operm = K.sb("operm", [NS, 16, 64], BF16)
        tt = [K.sb("tt%d" % i, [NS, 16, 64], F32) for i in range(2)]
        pnew = K.sb("pnew", [NS, 2, 16], F32)
        cden = K.sb("cden", [NS, 3, 16], F32)
        for b_ in [VCs] + Vst:
            pool.op([], [b_], lambda e, b_=b_: e.memset(b_[:], 0.0))
        pool.op([], [VCs], lambda e: e.memset(VCs[:, :, :, 64:65], 1.0))
        for i in range(len(Vst)):
            pool.op([], [Vst[i]], lambda e, i=i: e.memset(Vst[i][:, :, 64:65], 1.0))
        pool.op([], [blkT], lambda e: e.memset(blkT[:], 0.0))
        pool.op([], [IDX], lambda e: e.memset(IDX[:], 0))
        rs = {"WBs": 0, "KP": 0, "VP": 0, "KPw": 0, "VPw": 0, "p12": 0, "Pk": 0, "Vst": 0}

        def rt(name, lst):
            rs[name] += 1
            return lst[rs[name] % len(lst)]

        def projS(l, chunks, dst, ncols=128, Wsrc=None, rhs_b=None):
            Wsrc = Wsrc if Wsrc is not None else Wc[l]
            rhs_b = rhs_b if rhs_b is not None else uTs
            groups = []
            i = 0
            while i < len(chunks):
                n = 2 if (i + 1 < len(chunks) and chunks[i + 1] == chunks[i] + 1) else 1
                groups.append((i, chunks[i], n))
                i += n

            def load(g):
                buf = rt("WBs", WBs)
                sp.dma(buf[:, 0:g[2], :, :], Wsrc[g[1]:g[1] + g[2]].rearrange("c p k n -> p c k n"), [Wsrc], [buf], buf)
                return buf
            ps = nxt("A")
            cur = load(groups[0])
            for gi, g in enumerate(groups):
                nb_ = load(groups[gi + 1]) if gi + 1 < len(groups) else None
                for j in range(g[2]):
                    ii = g[0] + j
                    for kc in range(8):
                        mm(ps, ps[0:ncols, ii * NS:(ii + 1) * NS], cur, cur[:, j, kc, 0:ncols], rhs_b, rhs_b[:, kc, :], kc == 0, kc == 7)
                cur = nb_
            n = len(chunks)
            act.op([ps], [dst], lambda e: e.copy(out=dst[0:ncols, 0:n, :], in_=ps[0:ncols, 0:n * NS].rearrange("p (c t) -> p c t", t=NS)))

        def bc3(ap2, n):
            return ap2.to_broadcast([128, 8, n])

        def tokmajor(src_fm, dst_rows):
            pb = nxt("A")
            for c in range(8):
                pe.op([src_fm, cf], [pb], lambda e, c=c: e.transpose(out=pb[0:NS, c * 128:(c + 1) * 128], in_=src_fm[:, c, :], identity=ident_f(128)))
            pb2 = nxt("A")
            for c in range(4, 8):
                pass
            act.op([pb], [stl], lambda e: e.copy(out=stl[0:NS, 0:512], in_=pb[0:NS, 0:512]))
            return pb

        def layer_sample(l, xsrc, xdst):
            load_layer_small(l)
            cv = l * 9
            sp.dma(xsb[:], xsrc[:, :], [xsrc], [xsb], xsb)
            dve.op([], [sms], lambda e: e.memset(sms[:, 0:1], 0.0))
            act.op([xsb, sms], [ubs, sms], lambda e: e.activation(out=ubs[:, :], in_=xsb[:, :], func=AF.Square, accum_out=sms[0:NS, 0:1]))
            dve.op([sms], [sms], lambda e: e.tensor_scalar(out=sms[0:NS, 1:2], in0=sms[0:NS, 0:1], scalar1=1.0 / D, scalar2=EPS, op0=ALU.mult, op1=ALU.add))
            act.op([sms], [sms], lambda e: e.sqrt(out=sms[0:NS, 1:2], in_=sms[0:NS, 1:2]))
            dve.op([sms], [sms], lambda e: e.reciprocal(out=sms[0:NS, 2:3], in_=sms[0:NS, 1:2]))
            dve.op([xsb, sms, gbc[0]], [ubs], lambda e: e.scalar_tensor_tensor(out=ubs[:, :], in0=xsb[:, :], scalar=sms[0:NS, 2:3], in1=gbc[0][0:NS, :], op0=ALU.mult, op1=ALU.mult))
            for c in range(8):
                pe.op([ubs, cbm], [psT], lambda e, c=c: e.transpose(out=psT[:, c * NS:(c + 1) * NS], in_=ubs[0:NS, c * 128:(c + 1) * 128], identity=cbm[0:NS, 0:NS]))
            act.op([psT], [uTs], lambda e: e.copy(out=uTs[:, :, :], in_=psT[:, 0:8 * NS].rearrange("p (c t) -> p c t", t=NS)))
            for blk in range(3):
                sp.dma(WKs[:], Wkv[l][blk], [Wkv[l]], [WKs], WKs)
                ps = nxt("A")
                for kc in range(8):
                    mm(ps, ps[0:NS, :], uTs, uTs[:, kc, :], WKs, WKs[:, kc, :], kc == 0, kc == 7)
                act.op([ps], [kvtok], lambda e, ps=ps, blk=blk: e.copy(out=kvtok[:, blk * 512:(blk + 1) * 512], in_=ps[0:NS, :]))
            sp.dma(skv_out[l, :, :], kvtok[:, :], [kvtok], [skv_out], kvtok)
            for blk in range(2):
                sp.dma(WKs[:], Wqt[l][blk], [Wqt[l]], [WKs], WKs)
                ps = nxt("A")
                for kc in range(8):
                    mm(ps, ps[0:NS, :], uTs, uTs[:, kc, :], WKs, WKs[:, kc, :], kc == 0, kc == 7)
                act.op([ps], [qtok], lambda e, ps=ps, blk=blk: e.mul(out=qtok[:, blk * 512:(blk + 1) * 512], in_=ps[0:NS, :], mul=0.125))
            projS(l, [CH_Q + c for c in range(8)], FM["t0"])
            act.op([FM["t0"]], [QTs], lambda e: e.mul(out=QTs[:, :, :], in_=FM["t0"][:, :, :], mul=0.125))
            projS(l, [CH_LRUX + c for c in range(8)], FM["xl"])
            projS(l, [CH_LRUG + c for c in range(8)], FM["lg"])
            projS(l, [CH_POOLX + c for c in range(8)], FM["px"])
            projS(l, [CH_POOLG + c for c in range(8)], FM["pg"])
            projS(l, [CH_NSAG + c for c in range(8)], FM["ng"])
            projS(l, [CH_BG], gat_f, ncols=48) if False else None
            psg = nxt("A")
            bw = rt("WBs", WBs)
            sp.dma(bw[:, 0:1, :, :], Wc[l][CH_BG:CH_BG + 1].rearrange("c p k n -> p c k n"), [Wc[l]], [bw], bw)
            for kc in range(8):
                mm(psg, psg[0:48, 0:NS], bw, bw[:, 0, kc, 0:48], uTs, uTs[:, kc, :], kc == 0, kc == 7)
            act.op([psg], [gat_f], lambda e: e.activation(out=gat_f[:, :], in_=psg[0:48, 0:NS], func=AF.Sigmoid))
            pe.op([gat_f, cf], [psM], lambda e: e.transpose(out=psM[0:NS, 0:48], in_=gat_f[0:48, 0:NS], identity=ident_f(48)))
            act.op([psM], [gat_t], lambda e: e.copy(out=gat_t[:, :], in_=psM[0:NS, 0:48]))
            for n_ in range(3):
                projS(l, [CH_MG + n_ * 8 + dc for dc in range(8)], FM["t0"])
                act.op([FM["t0"]], [mgs], lambda e, n_=n_: e.activation(out=mgs[:, n_ * 8:(n_ + 1) * 8, :], in_=FM["t0"][:, :, :], func=AF.Sigmoid))

            if SST <= 1:
                return
            def load_fm(src_ap, rows, dst, width):
                sp.dma(stl[0:rows, :], src_ap, [st_conv, st_lru, st_pool], [stl], stl)
                for c in range(8):
                    pe.op([stl, cf], [psM], lambda e, c=c: e.transpose(out=psM[:, 0:rows], in_=stl[0:rows, c * 128:(c + 1) * 128], identity=ident_f(rows)))
                    act.op([psM], [dst], lambda e, c=c: e.copy(out=dst[:, c, width[0]:width[0] + rows], in_=psM[:, 0:rows]))
            load_fm(st_conv[l, :, :], NS * 3, CS, (0,))
            load_fm(st_lru[l, :, :], NS, HS, (0,))
            half_ = (NS * 15 + 1) // 2 if NS * 15 > 128 else NS * 15
            r0_ = 0
            while r0_ < NS * 15:
                rws = min(half_, NS * 15 - r0_)
                load_fm(st_pool[l, r0_:r0_ + rws, :], rws, PSs, (r0_,))
                r0_ += rws

            xl, t0_, t1_, t2_, t3_, hN = FM["xl"], FM["t0"], FM["t1"], FM["t2"], FM["t3"], FM["h"]
            CS4 = CS[:, :, :].rearrange("p c (b k) -> p c b k", k=3)
            dve.op([xl, CV], [t0_], lambda e: e.tensor_tensor(out=t0_[:, :, :], in0=xl[:, :, :], in1=bc3(CV[:, :, cv + 3:cv + 4], NS), op=ALU.mult))
            for k in range(3):
                dve.op([CS, CV], [t1_], lambda e, k=k: e.tensor_tensor(out=t1_[:, :, :], in0=CS4[:, :, :, k], in1=bc3(CV[:, :, cv + k:cv + k + 1], NS), op=ALU.mult))
                dve.op([t0_, t1_], [t0_], lambda e: e.tensor_tensor(out=t0_[:, :, :], in0=t0_[:, :, :], in1=t1_[:, :, :], op=ALU.add))
            dve.op([t0_, CV], [t0_], lambda e: e.tensor_tensor(out=t0_[:, :, :], in0=t0_[:, :, :], in1=bc3(CV[:, :, cv + 4:cv + 5], NS), op=ALU.add))
            act.op([t0_], [xcbs], lambda e: e.copy(out=xcbs[:, :, :], in_=t0_[:, :, :]))
            pr = nxt("S")
            for ax in range(2):
                for c in range(8):
                    mm(pr, pr[:, (ax * 8 + c) * NS:(ax * 8 + c + 1) * NS], WRG, WRG[:, ax, c, :], xcbs, xcbs[:, c, :], True, True)
            prv = pr[:, 0:16 * NS].rearrange("p (a c t) -> p a c t", a=2, t=NS)
            dve.op([pr, CV], [t1_], lambda e: e.tensor_tensor(out=t1_[:, :, :], in0=prv[:, 0, :, :], in1=bc3(CV[:, :, cv + 5:cv + 6], NS), op=ALU.add))
            dve.op([pr, CV], [t2_], lambda e: e.tensor_tensor(out=t2_[:, :, :], in0=prv[:, 1, :, :], in1=bc3(CV[:, :, cv + 6:cv + 7], NS), op=ALU.add))
            act.op([t1_], [t1_], lambda e: e.activation(out=t1_[:, :, :], in_=t1_[:, :, :], func=AF.Sigmoid))
            act.op([t2_], [t2_], lambda e: e.activation(out=t2_[:, :, :], in_=t2_[:, :, :], func=AF.Sigmoid))
            dve.op([t1_, CV], [t1_], lambda e: e.tensor_tensor(out=t1_[:, :, :], in0=t1_[:, :, :], in1=bc3(CV[:, :, 18 + l:19 + l], NS), op=ALU.mult))
            act.op([t1_], [t1_], lambda e: e.activation(out=t1_[:, :, :], in_=t1_[:, :, :], func=AF.Exp))
            dve.op([t1_], [t3_], lambda e: e.scalar_tensor_tensor(out=t3_[:, :, :], in0=t1_[:, :, :], scalar=-1.0, in1=t1_[:, :, :], op0=ALU.mult, op1=ALU.mult))
            act.op([t3_], [t3_], lambda e: e.activation(out=t3_[:, :, :], in_=t3_[:, :, :], func=AF.Sqrt, bias=1.0, scale=1.0))
            dve.op([t2_, t0_], [t2_], lambda e: e.tensor_tensor(out=t2_[:, :, :], in0=t2_[:, :, :], in1=t0_[:, :, :], op=ALU.mult))
            dve.op([t2_, t3_], [t2_], lambda e: e.tensor_tensor(out=t2_[:, :, :], in0=t2_[:, :, :], in1=t3_[:, :, :], op=ALU.mult))
            dve.op([t1_, HS], [t1_], lambda e: e.tensor_tensor(out=t1_[:, :, :], in0=t1_[:, :, :], in1=HS[:, :, :], op=ALU.mult))
            dve.op([t1_, t2_], [hN], lambda e: e.tensor_tensor(out=hN[:, :, :], in0=t1_[:, :, :], in1=t2_[:, :, :], op=ALU.add))
            act.op([FM["lg"]], [t3_], lambda e: e.activation(out=t3_[:, :, :], in_=FM["lg"][:, :, :], func=AF.Silu))
            dve.op([hN, t3_], [zbs[0]], lambda e: e.tensor_tensor(out=zbs[0][:, :, :], in0=hN[:, :, :], in1=t3_[:, :, :], op=ALU.mult))

            px = FM["px"]
            PS4 = PSs[:, :, :].rearrange("p c (b k) -> p c b k", k=15)
            for gq, w in enumerate(POOL_WIN):
                cs_ = slice(2 * gq, 2 * gq + 2)
                dve.op([PSs], [t0_], lambda e, cs_=cs_, w=w: e.tensor_reduce(out=t0_[:, cs_, :], in_=PS4[:, cs_, :, 15 - (w - 1):15], axis=AX.X, op=ALU.add))
                dve.op([t0_, px], [t0_], lambda e, cs_=cs_: e.tensor_tensor(out=t0_[:, cs_, :], in0=t0_[:, cs_, :], in1=px[:, cs_, :], op=ALU.add))
                dve.op([t0_, px], [pdb], lambda e, cs_=cs_, w=w: e.scalar_tensor_tensor(out=pdb[:, cs_, :], in0=t0_[:, cs_, :], scalar=1.0 / w, in1=px[:, cs_, :], op0=ALU.mult, op1=ALU.subtract))
            pm = nxt("S")
            for c in range(8):
                gq, eh = c // 2, c % 2
                for dh in range(2):
                    mm(pm, pm[:, c * NS:(c + 1) * NS], WPL, WPL[:, gq, dh, eh * 128:(eh + 1) * 128], pdb, pdb[:, 2 * gq + dh, :], dh == 0, dh == 1)
            act.op([FM["pg"]], [t3_], lambda e: e.activation(out=t3_[:, :, :], in_=FM["pg"][:, :, :], func=AF.Silu))
            dve.op([pm, CV], [t0_], lambda e: e.tensor_tensor(out=t0_[:, :, :], in0=pm[:, 0:8 * NS].rearrange("p (c t) -> p c t", t=NS), in1=bc3(CV[:, :, cv + 8:cv + 9], NS), op=ALU.mult))
            dve.op([t0_, t3_], [zbs[1]], lambda e: e.tensor_tensor(out=zbs[1][:, :, :], in0=t0_[:, :, :], in1=t3_[:, :, :], op=ALU.mult))

            def rows_out(src_fm, dst_ap, dst_buf):
                for hh in range(2):
                    pb = nxt("A")
                    for c4 in range(4):
                        c = hh * 4 + c4
                        pe.op([src_fm, cf], [pb], lambda e, c=c, c4=c4, pb=pb: e.transpose(out=pb[0:NS, c4 * 128:(c4 + 1) * 128], in_=src_fm[:, c, :], identity=ident_f(128)))
                    act.op([pb], [stl], lambda e, pb=pb, hh=hh: e.copy(out=stl[0:NS, hh * 512:(hh + 1) * 512], in_=pb[0:NS, :]))
                sp.dma(dst_ap, stl[0:NS, :], [stl], [dst_buf], stl)
            sconv3 = sconv_out[l, :, :].rearrange("(b k) d -> b k d", k=3)
            sp.dma(sconv3[:, 0:2, :], st_conv[l, :, :].rearrange("(b k) d -> b k d", k=3)[:, 1:3, :], [st_conv], [sconv_out], stl)
            rows_out(xl, sconv3[:, 2, :], sconv_out)
            rows_out(hN, slru_out[l, :, :], slru_out)
            spool3 = spool_out[l, :, :].rearrange("(b k) d -> b k d", k=15)
            sp.dma(spool3[:, 0:14, :], st_pool[l, :, :].rearrange("(b k) d -> b k d", k=15)[:, 1:15, :], [st_pool], [spool_out], stl)
            rows_out(px, spool3[:, 14, :], spool_out)
            for i in range(2):
                sp.dma(swin_out[i][l, :, 0:WBUF - 1, :], winc[i][l, :, 1:WBUF, :], [winc[i]], [swin_out[i]], stl)
                sp.dma(swin_out[i][l, :, WBUF - 1, :], kvtok[:, 1024 + i * 256:1280 + i * 256], [kvtok], [swin_out[i]], kvtok)

            if SST <= 2:
                return
            for kvi in range(2):
                i_ = l * 2 + kvi
                for fsx in range(2):
                    src = cmp_pos[l, kvi, fsx * 16:(fsx + 1) * 16, :]
                    for ii in range(8):
                        for g in range(4):
                            sp.dma(WT[ii * 16:(ii + 1) * 16, kvi, fsx, g * 64:(g + 1) * 64], src, [cmp_pos], [WT], WT)
            if SST <= 3:
                return
            for b in range(NS):
                sample_attn(l, b)
            if SST <= 10:
                return
            sample_combine(l)
            merge_out(l, xsrc, xdst)

        def sample_attn(l, b):
            sp.dma(ptb[:], ptab[b:b + 1, :].partition_broadcast(128), [ptab], [ptb], ptb)
            dve.op([ptb], [ptf], lambda e: e.tensor_copy(out=ptf[:], in_=ptb[:]))
            dve.op([csb], [sms], lambda e: e.tensor_scalar(out=sms[:, 8:9], in0=csb[:, o_p:o_p + 1], scalar1=float(l * NPHYS * 128), scalar2=None, op0=ALU.add))
            dve.op([ptf, sms], [ptf], lambda e: e.tensor_scalar(out=ptf[:], in0=ptf[:], scalar1=128.0, scalar2=sms[:, 8:9], op0=ALU.mult, op1=ALU.add))
            dve.op([ptf], [IDX], lambda e: e.tensor_copy(out=IDX[:], in_=ptf[:]))
            if SST <= 4:
                return
            for kvi in range(2):
                i_ = l * 2 + kvi
                banks = [psA[0], psA[1], psS[0], psS[1]]
                for pg in range(NPG):
                    kp = rt("KP", KP)
                    idma(pool, kp[:, :], pools[kvi][:, :], IDX[:, pg:pg + 1], [pools[kvi], IDX], [kp], kp)
                    pp = rt("p12", p12)
                    for fsx in range(2):
                        dve.op([kp, WT], [pp], lambda e, fsx=fsx, kp=kp, pp=pp: e.tensor_tensor(out=pp[:, fsx, :], in0=kp[:, :], in1=WT[:, kvi, fsx, :], op=ALU.mult))
                    for fsx in range(2):
                        for cc in range(2):
                            bk = banks[fsx * 2 + cc]
                            mm(bk, bk[:, pg * 8:(pg + 1) * 8], pp, pp[:, fsx, cc * 128:(cc + 1) * 128], indb, indb[:, :], True, True)
                for q_ in range(4):
                    act.op([banks[q_]], [fsb], lambda e, q_=q_: e.copy(out=fsb[:, q_, :], in_=banks[q_][:, 0:NMS * 128]))
                NC_ = NMS * 128
                for cc in range(2):
                    dve.op([fsb], [blkT], lambda e, cc=cc: e.tensor_tensor(out=blkT[:, cc, 1:NC_], in0=fsb[:, cc, 0:NC_ - 1], in1=fsb[:, 2 + cc, 1:NC_], op=ALU.add))
                if kvi == 0:
                    for cc in range(2):
                        pk = nxt("O")
                        mm(pk, pk[:, 0:NC_], PHI, PHI[:, i_, :], blkT, blkT[:, cc, :], True, True)
                        act.op([pk], [KCs], lambda e, cc=cc, pk=pk: e.copy(out=KCs[:, cc, :], in_=pk[:, 0:NC_]))
                else:
                    for mt in range(NMS):
                        pk = nxt("O")
                        for cc in range(2):
                            mm(pk, pk[:, cc * 128:(cc + 1) * 128], blkT, blkT[:, cc, mt * 128:(mt + 1) * 128], PHI, PHI[:, i_, :], True, True)
                        act.op([pk], [VCs], lambda e, mt=mt, pk=pk: e.copy(out=VCs[:, mt, :, 0:64], in_=pk[:, 0:256].rearrange("p (g d) -> p g d", d=64)))
            if SST <= 5:
                return
            pSh = [nxt("S"), nxt("S")]
            for hf in range(2):
                for mt in range(NMS):
                    for g in (hf, hf + 2):
                        cq = 4 * (g // 2)
                        mm(pSh[hf], pSh[hf][:, mt * 16 + 4 * g:mt * 16 + 4 * g + 4], KCs, KCs[hf * 64:hf * 64 + 64, g // 2, mt * 128:(mt + 1) * 128],
                           QTs, QTs[hf * 64:hf * 64 + 64, cq:cq + 4, b], True, True)
            pat = "p (m gi hf r) -> p m gi hf r"
            S16v = S16[:, :, :].rearrange("p m (gi hf r) -> p m gi hf r", hf=2, r=4)
            absv = csb[:, o_abs:o_abs + NMS * 16].rearrange(pat, gi=2, hf=2, r=4)
            for hf in range(2):
                pv_ = pSh[hf][:, 0:NMS * 16].rearrange(pat, gi=2, hf=2, r=4)
                dve.op([pSh[hf], csb], [S16], lambda e, hf=hf, pv_=pv_: e.tensor_tensor(out=S16v[:, :, :, hf, :], in0=pv_[:, :, :, hf, :], in1=absv[:, :, :, hf, :], op=ALU.add))
            act.op([S16], [P16], lambda e: e.activation(out=P16[:, :, :], in_=S16[:, :, :], func=AF.Exp))
            pD = nxt("O")
            pO = nxt("O")
            for mt in range(NMS):
                mm(pD, pD[:, 0:16], ones_bf, ones_bf[:, :], P16, P16[:, mt, :], mt == 0, mt == NMS - 1)
            for mt in range(NMS):
                mm(pO, pO[0:16, 0:260], P16, P16[:, mt, :], VCs, VCs[:, mt, :, :], mt == 0, mt == NMS - 1)
            act.op([pO], [O16sb], lambda e: e.copy(out=O16sb[:, 0, :], in_=pO[0:16, 0:260]))
            dve.op([pD], [rd16], lambda e: e.tensor_scalar(out=rd16[:, :], in0=pD[:, 0:16], scalar1=1e-30, scalar2=None, op0=ALU.max))
            dve.op([rd16], [rd16], lambda e: e.reciprocal(out=rd16[:, :], in_=rd16[:, :]))
            for mt in range(NMS):
                dve.op([P16, rd16], [S16], lambda e, mt=mt: e.tensor_tensor(out=S16[:, mt, :], in0=P16[:, mt, :], in1=rd16[:, :], op=ALU.mult))
            dve.op([S16], [Pn4f], lambda e: e.tensor_reduce(out=Pn4f[:, :, :], in_=S16[:, :, :].rearrange("p m (g r) -> p m g r", r=4), axis=AX.X, op=ALU.add))
            act.op([Pn4f], [Pn4], lambda e: e.copy(out=Pn4[:, :, :], in_=Pn4f[:, :, :]))
            p4 = nxt("S")
            for mt in range(NMS):
                mm(p4, p4[0:4, 0:128], Pn4, Pn4[:, mt, :], gsb, gsb[:, mt * 128:(mt + 1) * 128], mt == 0, mt == NMS - 1)
            if SST <= 6:
                return
            dve.op([p4, csb], [sc4], lambda e: e.tensor_tensor(out=sc4[:, :], in0=p4[0:4, 0:128], in1=csb[0:4, o_fb:o_fb + 128], op=ALU.add))
            NBs = NPG * 2
            dve.op([sc4], [m84], lambda e: e.max(out=m84[:, 0:8], in_=sc4[:, 0:NBs]))
            dve.op([sc4, m84], [wk4], lambda e: e.match_replace(out=wk4[:, 0:NBs], in_to_replace=m84[:, 0:8], in_values=sc4[:, 0:NBs], imm_value=-3.0e38))
            dve.op([wk4], [m84], lambda e: e.max(out=m84[:, 8:16], in_=wk4[:, 0:NBs]))
            dve.op([], [mb4], lambda e: e.memset(mb4[:, :], 0.0))
            dve.op([sc4, m84], [mb4], lambda e: e.tensor_scalar(out=mb4[:, 0:NBs], in0=sc4[:, 0:NBs], scalar1=m84[:, 14:15], scalar2=NEGB, op0=ALU.is_lt, op1=ALU.mult))
            pe.op([mb4, cf], [psM], lambda e: e.transpose(out=psM[:, 0:4], in_=mb4[0:4, :], identity=ident_f(4)))
            act.op([psM], [MB4T], lambda e: e.copy(out=MB4T[:, :], in_=psM[:, 0:4]))
            if SST <= 7:
                return
            for hh in range(2):
                pq = nxt("A")
                mm(pq, pq[:, :], cf, cf[0:NS, o_id + b:o_id + b + 1].to_broadcast([NS, 128]), qtok, qtok[0:NS, hh * 512:(hh + 1) * 512], True, True)
                act.op([pq], [QBs], lambda e, pq=pq, hh=hh: e.copy(out=QBs[:, hh * 512:(hh + 1) * 512], in_=pq[:, :]))

            def keytile(kp, vp, bias_ap, mask_pg, pO_, first, last_):
                dve.op([kp, QBs], [prod], lambda e: e.tensor_tensor(out=prod[:, :].rearrange("p (g r d) -> p g r d", r=4, d=64),
                                                                    in0=kp[:, :].rearrange("p (g o d) -> p g o d", o=1, d=64).to_broadcast([128, 4, 4, 64]),
                                                                    in1=QBs[:, :].rearrange("p (g r d) -> p g r d", r=4, d=64), op=ALU.mult))
                dve.op([prod], [s16], lambda e: e.tensor_reduce(out=s16[:, :], in_=prod[:, :].rearrange("p (h d) -> p h d", d=64), axis=AX.X, op=ALU.add))
                dve.op([s16, csb], [s16], lambda e: e.tensor_tensor(out=s16[:, :], in0=s16[:, :], in1=bias_ap, op=ALU.add))
                if mask_pg is not None:
                    a2, v = ((2 * mask_pg) // 64) % 2, mask_pg % 32
                    pm_ = nxt("S")
                    mm(pm_, pm_[:, 0:4], cb, cb[a2 * 64:a2 * 64 + 64, o_selh + v * 128:o_selh + (v + 1) * 128], MB4T, MB4T[a2 * 64:a2 * 64 + 64, :], True, True)
                    act.op([pm_], [mkb], lambda e: e.copy(out=mkb[:, :], in_=pm_[:, 0:4]))
                    dve.op([s16, mkb], [s16], lambda e: e.tensor_tensor(out=s16[:, :].rearrange("p (g r) -> p g r", r=4), in0=s16[:, :].rearrange("p (g r) -> p g r", r=4),
                                                                        in1=mkb[:, :].rearrange("p (g o) -> p g o", o=1).to_broadcast([128, 4, 4]), op=ALU.add))
                pk_ = rt("Pk", Pk)
                act.op([s16], [pk_], lambda e: e.activation(out=pk_[:, :], in_=s16[:, :], func=AF.Exp))
                vs_ = rt("Vst", Vst)
                act.op([vp], [vs_], lambda e: e.copy(out=vs_[:, :, 0:64], in_=vp[:, :].rearrange("p (g d) -> p g d", d=64)))
                mm(pO_, pO_[0:16, 0:260], pk_, pk_[:, :], vs_, vs_[:, :, :], first, last_)

            if SST <= 8:
                return
            pOs = nxt("O")
            for pg in range(NPG):
                kp, vp = rt("KP", KP), rt("VP", VP)
                idma(pool, kp[:, :], pools[2][:, :], IDX[:, pg:pg + 1], [pools[2], IDX], [kp], kp)
                idma(pool, vp[:, :], pools[3][:, :], IDX[:, pg:pg + 1], [pools[3], IDX], [vp], vp)
                keytile(kp, vp, csb[:, o_abk + pg * 16:o_abk + (pg + 1) * 16], pg, pOs, pg == 0, pg == NPG - 1)
            act.op([pOs], [O16sb], lambda e: e.copy(out=O16sb[:, 1, :], in_=pOs[0:16, 0:260]))
            if SST <= 9:
                return
            pOw = nxt("O")
            NWT = WBUF // 128
            for t in range(NWT):
                kp, vp = rt("KPw", KPw), rt("VPw", VPw)
                sp.dma(kp[:, :], winc[0][l, b, t * 128:(t + 1) * 128, :], [winc[0]], [kp], kp)
                sp.dma(vp[:, :], winc[1][l, b, t * 128:(t + 1) * 128, :], [winc[1]], [vp], vp)
                keytile(kp, vp, csb[:, o_abw + t * 16:o_abw + (t + 1) * 16], None, pOw, t == 0, t == NWT - 1)
            act.op([pOw], [O16sb], lambda e: e.copy(out=O16sb[:, 2, :], in_=pOw[0:16, 0:260]))
            for br in range(3):
                for g in range(4):
                    sp.dma(OS[b:b + 1, br, 4 * g:4 * g + 4, :], O16sb[4 * g:4 * g + 4, br, g * 65:(g + 1) * 65], [O16sb], [OS], OS)

        def sample_combine(l):
            q4 = qtok[:, :].rearrange("p (g r d) -> p g r d", r=4, d=64)
            for i, (ko, vo) in enumerate(((512, 768), (1024, 1280))):
                kn = kvtok[:, ko:ko + 256].rearrange("p (g o d) -> p g o d", o=1, d=64).to_broadcast([NS, 4, 4, 64])
                dve.op([qtok, kvtok], [tt[0]], lambda e, kn=kn: e.tensor_tensor(out=tt[0][:, :, :].rearrange("p (g r) d -> p g r d", r=4), in0=q4, in1=kn, op=ALU.mult))
                dve.op([tt[0]], [pnew], lambda e, i=i: e.tensor_reduce(out=pnew[:, i, :], in_=tt[0][:, :, :], axis=AX.X, op=ALU.add))
            act.op([pnew], [pnew], lambda e: e.activation(out=pnew[:, :, :], in_=pnew[:, :, :], func=AF.Exp))
            g3 = gat_t[:, :].rearrange("p (h b) -> p h b", b=3)
            for br in range(3):
                if br == 0:
                    dve.op([OS], [cden], lambda e: e.tensor_scalar(out=cden[:, 0, :], in0=OS[:, 0, :, 64], scalar1=1e-30, scalar2=None, op0=ALU.max))
                else:
                    dve.op([OS, pnew], [cden], lambda e, br=br: e.tensor_tensor(out=cden[:, br, :], in0=OS[:, br, :, 64], in1=pnew[:, br - 1, :], op=ALU.add))
                dve.op([cden], [cden], lambda e, br=br: e.reciprocal(out=cden[:, br, :], in_=cden[:, br, :]))
                dve.op([cden, gat_t], [cden], lambda e, br=br: e.tensor_tensor(out=cden[:, br, :], in0=cden[:, br, :], in1=g3[:, :, br], op=ALU.mult))
                num = tt[0]
                if br == 0:
                    dve.op([OS], [num], lambda e: e.tensor_copy(out=num[:, :, :], in_=OS[:, 0, :, 0:64]))
                else:
                    vo = 768 if br == 1 else 1280
                    vn = kvtok[:, vo:vo + 256].rearrange("p (g o d) -> p g o d", o=1, d=64).to_broadcast([NS, 4, 4, 64])
                    dve.op([kvtok, pnew], [num], lambda e, br=br, vn=vn: e.tensor_tensor(out=num[:, :, :].rearrange("p (g r) d -> p g r d", r=4), in0=vn,
                                                                                  in1=pnew[:, br - 1, :].rearrange("p (g r o) -> p g r o", r=4, o=1).to_broadcast([NS, 4, 4, 64]), op=ALU.mult))
                    dve.op([num, OS], [num], lambda e, br=br: e.tensor_tensor(out=num[:, :, :], in0=num[:, :, :], in1=OS[:, br, :, 0:64], op=ALU.add))
                dve.op([num, cden], [tt[1]], lambda e, br=br: e.tensor_tensor(out=tt[1][:, :, :], in0=num[:, :, :],
                                                                           in1=cden[:, br, :].rearrange("p (h o) -> p h o", o=1).to_broadcast([NS, 16, 64]), op=ALU.mult))
                if br == 0:
                    dve.op([tt[1]], [onat], lambda e: e.tensor_copy(out=onat[:, :, :], in_=tt[1][:, :, :]))
                else:
                    dve.op([tt[1], onat], [onat], lambda e: e.tensor_tensor(out=onat[:, :, :], in0=onat[:, :, :], in1=tt[1][:, :, :], op=ALU.add))
            dve.op([onat], [operm], lambda e: e.tensor_copy(out=operm[:, :, :].rearrange("p (hi r e) d -> p hi r e d", r=4, e=2),
                                                           in_=onat[:, :, :].rearrange("p (hi e r) d -> p hi r e d", e=2, r=4)))
            for c in range(8):
                pe.op([operm, cbm], [psT], lambda e, c=c: e.transpose(out=psT[:, c * NS:(c + 1) * NS], in_=operm[0:NS, 2 * c:2 * c + 2, :].rearrange("p h d -> p (h d)"), identity=cbm[0:NS, 0:NS]))
            act.op([FM["ng"]], [FM["t3"]], lambda e: e.activation(out=FM["t3"][:, :, :], in_=FM["ng"][:, :, :], func=AF.Silu))
            dve.op([psT, FM["t3"]], [zbs[2]], lambda e: e.tensor_tensor(out=zbs[2][:, :, :], in0=psT[:, 0:8 * NS].rearrange("p (c t) -> p c t", t=NS), in1=FM["t3"][:, :, :], op=ALU.mult))

        def merge_out(l, xsrc, xdst):
            for n_ in range(3):
                projS(l, [n_ * 8 + dc for dc in range(8)], FM["t0"], Wsrc=Wb[l], rhs_b=zbs[n_])
                if n_ == 0:
                    dve.op([FM["t0"], mgs], [maccs], lambda e: e.tensor_tensor(out=maccs[:, :, :], in0=FM["t0"][:, :, :], in1=mgs[:, 0:8, :], op=ALU.mult))
                else:
                    dve.op([FM["t0"], mgs], [FM["t1"]], lambda e, n_=n_: e.tensor_tensor(out=FM["t1"][:, :, :], in0=FM["t0"][:, :, :], in1=mgs[:, n_ * 8:(n_ + 1) * 8, :], op=ALU.mult))
                    dve.op([FM["t1"], maccs], [maccs], lambda e: e.tensor_tensor(out=maccs[:, :, :], in0=maccs[:, :, :], in1=FM["t1"][:, :, :], op=ALU.add))
            act.op([maccs], [mergs], lambda e: e.copy(out=mergs[:, :, :], in_=maccs[:, :, :]))
            sp.dma(WKs[:], Wo[l][:, :, 0:512], [Wo[l]], [WKs], WKs)
            sp.dma(WK2[:], Wo[l][:, :, 512:1024], [Wo[l]], [WK2], WK2)
            wo = [WKs, WK2]
            po = [nxt("A"), nxt("A")]
            dve.op([], [sms], lambda e: e.memset(sms[:, 16:18], 0.0))
            for eh in range(2):
                for dc in range(8):
                    mm(po[eh], po[eh][0:NS, :], mergs, mergs[:, dc, :], wo[eh], wo[eh][:, dc, :], dc == 0, dc == 7)
                act.op([po[eh], sms], [ubs, sms], lambda e, eh=eh: e.activation(out=ubs[:, eh * 512:(eh + 1) * 512], in_=po[eh][0:NS, :], func=AF.Square, accum_out=sms[0:NS, 16 + eh:17 + eh]))
            dve.op([sms], [sms], lambda e: e.tensor_tensor(out=sms[0:NS, 18:19], in0=sms[0:NS, 16:17], in1=sms[0:NS, 17:18], op=ALU.add))
            dve.op([sms], [sms], lambda e: e.tensor_scalar(out=sms[0:NS, 19:20], in0=sms[0:NS, 18:19], scalar1=1.0 / D, scalar2=EPS, op0=ALU.mult, op1=ALU.add))
            act.op([sms], [sms], lambda e: e.sqrt(out=sms[0:NS, 19:20], in_=sms[0:NS, 19:20]))
            dve.op([sms], [sms], lambda e: e.reciprocal(out=sms[0:NS, 20:21], in_=sms[0:NS, 19:20]))
            for eh in range(2):
                dve.op([po[eh], sms, gbc[1]], [ysb], lambda e, eh=eh: e.scalar_tensor_tensor(out=ysb[:, eh * 512:(eh + 1) * 512], in0=po[eh][0:NS, :], scalar=sms[0:NS, 20:21], in1=gbc[1][0:NS, eh * 512:(eh + 1) * 512], op0=ALU.mult, op1=ALU.mult))
            dve.op([ysb, xsb], [ysb], lambda e: e.tensor_tensor(out=ysb[:, :], in0=ysb[:, :], in1=xsb[:, :], op=ALU.add))
            sp.dma(xdst[:, :], ysb[:, :], [ysb], [xdst], ysb)

        layer_sample(0, xs_in, xs1)
        layer_sample(1, xs1, ys_out)

    init_state()
    stage = getattr(cfg, "STAGE", 99)
    do_prompt = stage >= 2 and stage != 45
    if do_prompt:
        layer_prompt(0, x_in, x1 if stage >= 3 else y_out)
    if do_prompt and stage >= 3:
        layer_prompt(1, x1, y_out)
    if stage >= 40:
        K.barrier()
        K.release_to(MARK_PROMPT)
        sample_phase()
    K.finish()
    return K


def _core_inputs(cfg, inp, b):
    f = lambda a: np.ascontiguousarray(np.asarray(a), dtype=np.float32)
    vec = []
    for l in range(2):
        vec += [f(inp["conv_w"])[l, k] for k in range(4)]
        vec += [f(inp["conv_b"])[l], f(inp["b_rg_a"])[l], f(inp["b_rg_x"])[l], f(inp["lru_lambda"])[l], f(inp["pool_scale"])[l]]
    NS = cfg.NS
    sl = slice(b * NS, (b + 1) * NS)
    m = {
        "x": f(inp["x_prompt"])[b],
        "g_pre": f(inp["g_pre"]), "g_post": f(inp["g_post"]), "w_in": f(inp["w_in"]),
        "vecs": np.ascontiguousarray(np.stack(vec, 0)),
        "w_rg": np.ascontiguousarray(np.stack([f(inp["w_rg_a"]), f(inp["w_rg_x"])], axis=1)),
        "w_pool": f(inp["w_pool"]),
        "cmp_pos": np.ascontiguousarray(np.stack([f(inp["cmp_pos_k"]), f(inp["cmp_pos_v"])], axis=1)),
        "cmp_phi": np.ascontiguousarray(np.stack([f(inp["cmp_phi_k"]), f(inp["cmp_phi_v"])], axis=1)),
        "w_branch": f(inp["w_branch"]), "w_out": f(inp["w_out"]),
        "xs_in": np.ascontiguousarray(f(inp["x_sample"])[sl, 0, :]),
        "st_conv": np.ascontiguousarray(f(inp["state_conv"])[:, sl].reshape(2, NS * 3, D)),
        "st_lru": np.ascontiguousarray(f(inp["state_lru"])[:, sl]),
        "st_pool": np.ascontiguousarray(f(inp["state_pool"])[:, sl].reshape(2, NS * 15, D)),
        "ptab": np.ascontiguousarray(np.asarray(inp["page_table"])[sl].astype(np.int32)),
    }
    for i, k in enumerate(("cache_cmp_k", "cache_cmp_v", "cache_sel_k", "cache_sel_v")):
        m["pool%d" % i] = f(inp[k]).reshape(-1, 256)
    for i, k in enumerate(("cache_win_k", "cache_win_v")):
        a = f(inp[k])
        m["winc%d" % i] = np.ascontiguousarray(a[:, sl].reshape(2, NS, a.shape[2], 256))
    m.update(make_consts(cfg))
    return m


def run_cfg(cfg, inp):
    K = build(cfg)
    in_maps = [_core_inputs(cfg, inp, b) for b in range(cfg.NCORES)]
    res = run_bass_kernel_spmd(K.nc, in_maps, core_ids=list(range(cfg.NCORES)))
    r = res.results
    S = cfg.S
    B = cfg.NCORES
    NS = cfg.NS
    wb = min(512, cfg.P)
    cat = lambda name: np.concatenate([r[b][name] for b in range(B)], axis=0)
    cat1 = lambda name: np.concatenate([r[b][name] for b in range(B)], axis=1)
    y_p = np.stack([r[b]["y"] for b in range(B)], 0)
    outs = [y_p, cat("ys")[:, None, :]]
    skv = cat1("skv")
    for i in range(4):
        outs.append(np.stack([r[b]["kvo%d" % i].reshape(2, S, 4, 64) for b in range(B)], 1))
        outs.append(np.ascontiguousarray(skv[:, :, i * 256:(i + 1) * 256]).reshape(2, -1, 1, 4, 64))
    for i in range(2):
        outs.append(np.stack([r[b]["wino%d" % i].reshape(2, 512, 4, 64) for b in range(B)], 1))
        outs.append(cat1("swin%d" % i).reshape(2, -1, wb, 4, 64))
    st = np.stack([r[b]["st"] for b in range(B)], 1)
    outs.append(np.ascontiguousarray(st[:, :, 0:3]))
    outs.append(cat1("sconv").reshape(2, -1, 3, D))
    outs.append(np.ascontiguousarray(st[:, :, 3]))
    outs.append(cat1("slru"))
    outs.append(np.ascontiguousarray(st[:, :, 4:19]))
    outs.append(cat1("spool").reshape(2, -1, 15, D))
    return tuple(np.ascontiguousarray(o, dtype=np.float32) for o in outs)


def kernel(**inputs):
    cfg = Cfg()
    return run_cfg(cfg, inputs)
```
